# Optimizing a Trainium2 kernel written in Bass

```python
import jax, jax.numpy as jnp
from jax import lax
import numpy as np

D_MODEL = 1024
BATCH = 4
SEQ = 4096
DEPTH = 4

GRID_W = 64
CTX_LEN = 256
N_BRANCH = 3
BRANCH_WIDTH = D_MODEL // 2
RET_HEADS = 4
RET_HEAD_DIM = BRANCH_WIDTH // RET_HEADS
RET_WIDTH = RET_HEADS * RET_HEAD_DIM
RET_CHUNK = 128
LRU_WIDTH = BRANCH_WIDTH
LRU_BLOCKS = 8
LRU_BLOCK = LRU_WIDTH // LRU_BLOCKS
LRU_CONV = 4
LRU_CONV_PAD_LEFT = 1
LRU_C = 8.0
ATT_HEAD_DIM = 64
ATT_Q_HEADS = BRANCH_WIDTH // ATT_HEAD_DIM
ATT_KV_HEADS = 2
ATT_GROUP = ATT_Q_HEADS // ATT_KV_HEADS
ATT_WIDTH = ATT_Q_HEADS * ATT_HEAD_DIM
ATT_KV_WIDTH = ATT_KV_HEADS * ATT_HEAD_DIM
Q_BLOCK = 128
ROPE_THETA = 10000.0
D_FF = 256 * ((8 * D_MODEL // 3 + 255) // 256)
MACARON_WEIGHT = 0.5
N_SUB = 3
NORM_EPS = 1e-6
IN_SIZES = (RET_WIDTH, RET_WIDTH, RET_WIDTH, RET_WIDTH, LRU_WIDTH, LRU_WIDTH,
            ATT_WIDTH, ATT_KV_WIDTH, ATT_KV_WIDTH, N_BRANCH * D_MODEL)
D_IN = sum(IN_SIZES)

kernel_name = 'hybrid_retention_rglru_gqa_prefix_dit'


def _rms(x, g):
    xf = x.astype(jnp.float32)
    y = xf * lax.rsqrt(jnp.mean(xf * xf, axis=-1, keepdims=True) + NORM_EPS)
    return (y * g.astype(jnp.float32)).astype(x.dtype)


def _modulate(x, g, shift, scale):
    return _rms(x, g) * (1 + scale) + shift


def _swiglu(h, w_in, w_out):
    a, b = jnp.split(h @ w_in, 2, axis=-1)
    return (jax.nn.silu(a) * b) @ w_out


def _half_ffn(s, m, g, w_in, w_out):
    return MACARON_WEIGHT * m[:, 2] * _swiglu(_modulate(s, g, m[:, 0], m[:, 1]), w_in, w_out)


def _split_cols(z):
    offs = []
    acc = 0
    for s in IN_SIZES[:-1]:
        acc += s
        offs.append(acc)
    return jnp.split(z, offs, axis=-1)


def _heads(t, n_heads):
    return t.reshape(*t.shape[:-1], n_heads, t.shape[-1] // n_heads)


def _flip(t):
    return jnp.flip(t, axis=1)


def _same(t):
    return t


def _rope_half(x, ang):
    x1, x2 = jnp.split(x, 2, axis=-1)
    cos = jnp.cos(ang)[None, :, None, :].astype(x.dtype)
    sin = jnp.sin(ang)[None, :, None, :].astype(x.dtype)
    return jnp.concatenate([x1 * cos - x2 * sin, x1 * sin + x2 * cos], axis=-1)


def _axial_rope(x, rows, cols):
    half = x.shape[-1] // 2
    freqs = ROPE_THETA ** (-jnp.arange(0, half, 2, dtype=jnp.float32) / half)
    xr, xc = jnp.split(x, 2, axis=-1)
    return jnp.concatenate([_rope_half(xr, rows[:, None] * freqs[None]),
                            _rope_half(xc, cols[:, None] * freqs[None])], axis=-1)


def _retention_chunks(q, k, v, log_gamma, s0):
    B, L, H, _ = q.shape
    dv = v.shape[-1]
    n = L // RET_CHUNK
    pos = jnp.arange(RET_CHUNK, dtype=jnp.float32)
    diff = pos[:, None] - pos[None, :]
    lg = log_gamma.astype(jnp.float32)
    intra = jnp.where(diff[None] >= 0,
                      jnp.exp(jnp.maximum(diff, 0.0)[None] * lg[:, None, None]), 0.0).astype(q.dtype)
    q_decay = jnp.exp((pos[:, None] + 1.0) * lg[None]).astype(q.dtype)
    k_decay = jnp.exp((RET_CHUNK - 1.0 - pos)[:, None] * lg[None]).astype(q.dtype)
    s_decay = jnp.exp(RET_CHUNK * lg).astype(q.dtype)

    def to_chunks(t):
        return t.reshape(B, n, RET_CHUNK, *t.shape[2:]).swapaxes(0, 1)

    def step(s, blk):
        qc, kc, vc = blk
        scores = jnp.einsum('bihd,bjhd->bhij', qc, kc) * intra
        inner = jnp.einsum('bhij,bjhe->bihe', scores, vc)
        cross = jnp.einsum('bihd,bhde->bihe', qc, s) * q_decay[None, :, :, None]
        s_new = s * s_decay[None, :, None, None] + jnp.einsum(
            'bjhd,bjhe->bhde', kc * k_decay[None, :, :, None], vc)
        return s_new, inner + cross

    s_fin, out = lax.scan(step, s0, (to_chunks(q), to_chunks(k), to_chunks(v)))
    return out.swapaxes(0, 1).reshape(B, L, H, dv), s_fin


def _bidir_retention(qc, kc, vc, ql, kl, vl, log_gamma):
    s0 = jnp.zeros((ql.shape[0], RET_HEADS, RET_HEAD_DIM, RET_HEAD_DIM), ql.dtype)
    outs_c, outs_l = [], []
    for d in range(2):
        f = _flip if d else _same
        oc, sc = _retention_chunks(f(qc), f(kc), f(vc), log_gamma[d], s0)
        ol, _ = _retention_chunks(f(ql), f(kl), f(vl), log_gamma[d], sc)
        outs_c.append(f(oc))
        outs_l.append(f(ol))
    return outs_c[0] + outs_c[1], outs_l[0] + outs_l[1]


def _head_norm(y, g):
    yf = y.astype(jnp.float32)
    mu = jnp.mean(yf, axis=-1, keepdims=True)
    var = jnp.mean(jnp.square(yf - mu), axis=-1, keepdims=True)
    yn = ((yf - mu) * lax.rsqrt(var + NORM_EPS)).reshape(*y.shape[:2], -1)
    return (yn * g.astype(jnp.float32)).astype(y.dtype)


def _conv_centred(x, w, b):
    L = x.shape[1]
    xp = jnp.pad(x, ((0, 0), (LRU_CONV_PAD_LEFT, LRU_CONV - 1 - LRU_CONV_PAD_LEFT), (0, 0)))
    y = b
    for j in range(LRU_CONV):
        y = y + xp[:, j:j + L] * w[j]
    return y


def _block_diag(x, w, b):
    B, L, _ = x.shape
    y = jnp.einsum('blnc,ncd->blnd', x.reshape(B, L, LRU_BLOCKS, LRU_BLOCK), w)
    return y.reshape(B, L, LRU_WIDTH) + b


def _rglru_scan(x, w_a, b_a, w_x, b_x, lam, h0):
    r = jax.nn.sigmoid(_block_diag(x, w_a, b_a))
    i = jax.nn.sigmoid(_block_diag(x, w_x, b_x))
    log_a = (-LRU_C * r * jax.nn.softplus(-lam)).astype(jnp.float32)
    a = jnp.exp(log_a)
    u = jnp.sqrt(-jnp.expm1(2.0 * log_a)) * (i * x).astype(jnp.float32)

    def combine(p, q):
        a1, b1 = p
        a2, b2 = q
        return a1 * a2, a2 * b1 + b2

    a_cum, b_cum = lax.associative_scan(combine, (a, u), axis=1)
    h = b_cum + a_cum * h0[:, None]
    return h, h[:, -1]


def _bidir_rglru(xc, xl, w_a, b_a, w_x, b_x, lam):
    h0 = jnp.zeros((xl.shape[0], LRU_WIDTH), jnp.float32)
    outs_c, outs_l = [], []
    for d in range(2):
        f = _flip if d else _same
        hc, sc = _rglru_scan(f(xc), w_a[d], b_a[d], w_x[d], b_x[d], lam[d], h0)
        hl, _ = _rglru_scan(f(xl), w_a[d], b_a[d], w_x[d], b_x[d], lam[d], sc)
        outs_c.append(f(hc))
        outs_l.append(f(hl))
    return (outs_c[0] + outs_c[1]).astype(xc.dtype), (outs_l[0] + outs_l[1]).astype(xl.dtype)


def _attend(q, k, v):
    s = jnp.einsum('bqkgd,bskd->bkgqs', q, k).astype(jnp.float32) * (ATT_HEAD_DIM ** -0.5)
    p = jax.nn.softmax(s, axis=-1).astype(v.dtype)
    return jnp.einsum('bkgqs,bskd->bqkgd', p, v)


def _attend_blocks(q, k, v):
    B, T = q.shape[:2]
    nb = T // Q_BLOCK
    qb = q.reshape(B, nb, Q_BLOCK, ATT_KV_HEADS, ATT_GROUP, ATT_HEAD_DIM).swapaxes(0, 1)
    out = lax.map(lambda blk: _attend(blk, k, v), qb)
    return out.swapaxes(0, 1).reshape(B, T, ATT_WIDTH)


def _merge(y_ret, y_lru, y_att, gate_logits, w_branch, w_out):
    y = jnp.stack([y_ret, y_lru, y_att], axis=2)
    u = jnp.einsum('blnw,nwd->blnd', y, w_branch)
    g = jax.nn.sigmoid(gate_logits.reshape(*gate_logits.shape[:2], N_BRANCH, D_MODEL))
    return jnp.sum(g * u, axis=2) @ w_out


def _token_mix(hc, hl, p, rows, cols, ctx_out):
    rq_c, rk_c, rv_c, rg_c, lx_c, lz_c, aq_c, ak_c, av_c, gt_c = _split_cols(hc @ p['w_in'])
    rq_l, rk_l, rv_l, rg_l, lx_l, lz_l, aq_l, ak_l, av_l, gt_l = _split_cols(hl @ p['w_in'])
    k_scale = RET_HEAD_DIM ** -0.5
    ret_c, ret_l = _bidir_retention(
        _heads(rq_c, RET_HEADS), _heads(rk_c, RET_HEADS) * k_scale, _heads(rv_c, RET_HEADS),
        _axial_rope(_heads(rq_l, RET_HEADS), rows, cols),
        _axial_rope(_heads(rk_l, RET_HEADS) * k_scale, rows, cols),
        _heads(rv_l, RET_HEADS),
        jax.nn.log_sigmoid(p['ret_decay_logit']))
    lru_c, lru_l = _bidir_rglru(
        _conv_centred(lx_c, p['lru_conv_w'], p['lru_conv_b']),
        _conv_centred(lx_l, p['lru_conv_w'], p['lru_conv_b']),
        p['lru_w_a'], p['lru_b_a'], p['lru_w_x'], p['lru_b_x'], p['lru_lambda'])
    qc = _rms(_heads(aq_c, ATT_Q_HEADS), p['q_norm_g'])
    kc = _rms(_heads(ak_c, ATT_KV_HEADS), p['k_norm_g'])
    vc = _heads(av_c, ATT_KV_HEADS)
    ql = _axial_rope(_rms(_heads(aq_l, ATT_Q_HEADS), p['q_norm_g']), rows, cols)
    kl = _axial_rope(_rms(_heads(ak_l, ATT_KV_HEADS), p['k_norm_g']), rows, cols)
    vl = _heads(av_l, ATT_KV_HEADS)
    att_l = _attend_blocks(ql, jnp.concatenate([kc, kl], axis=1), jnp.concatenate([vc, vl], axis=1))
    y_l = _merge(_head_norm(ret_l, p['ret_norm_g']) * jax.nn.silu(rg_l),
                 jax.nn.gelu(lz_l) * lru_l, att_l, gt_l, p['w_branch'], p['w_out'])
    if not ctx_out:
        return y_l, None
    B, Lc = hc.shape[:2]
    att_c = _attend(qc.reshape(B, Lc, ATT_KV_HEADS, ATT_GROUP, ATT_HEAD_DIM), kc, vc).reshape(B, Lc, ATT_WIDTH)
    y_c = _merge(_head_norm(ret_c, p['ret_norm_g']) * jax.nn.silu(rg_c),
                 jax.nn.gelu(lz_c) * lru_c, att_c, gt_c, p['w_branch'], p['w_out'])
    return y_l, y_c


def setup_inputs(seed: int = 0) -> dict:
    key = jax.random.key(seed)
    ks = jax.random.split(key, 24)
    f32 = jnp.float32
    D = D_MODEL

    def nrm(k, shape, scale):
        return jax.random.normal(k, shape, f32) * scale

    gamma = 1.0 - 2.0 ** (-5.0 - jnp.arange(RET_HEADS, dtype=f32))
    ret_logit0 = jnp.log(gamma) - jnp.log1p(-gamma)
    u = jax.random.uniform(ks[12], (DEPTH, 2, LRU_WIDTH), f32, 0.9, 0.999)
    a0 = u ** (1.0 / LRU_C)
    lam = jnp.log(a0) - jnp.log1p(-a0)
    return {
        'x': nrm(ks[0], (BATCH, SEQ, D), 1.0),
        'c': nrm(ks[1], (BATCH, D), 1.0),
        'ctx': nrm(ks[2], (BATCH, CTX_LEN, D), 1.0),
        'c_ctx': nrm(ks[3], (D,), 1.0),
        'w_mod': nrm(ks[4], (DEPTH, D, N_SUB * 3 * D), 0.5 * D ** -0.5),
        'b_mod': nrm(ks[5], (DEPTH, N_SUB * 3 * D), 0.02),
        'norm_g': 1.0 + nrm(ks[6], (DEPTH, N_SUB, D), 0.02),
        'ffn_w_in': nrm(ks[7], (DEPTH, 2, D, 2 * D_FF), D ** -0.5),
        'ffn_w_out': nrm(ks[8], (DEPTH, 2, D_FF, D), D_FF ** -0.5),
        'w_in': nrm(ks[9], (DEPTH, D, D_IN), D ** -0.5),
        'ret_decay_logit': ret_logit0 + nrm(ks[10], (DEPTH, 2, RET_HEADS), 0.1),
        'ret_norm_g': 1.0 + nrm(ks[11], (DEPTH, RET_WIDTH), 0.02),
        'lru_conv_w': nrm(ks[13], (DEPTH, LRU_CONV, LRU_WIDTH), LRU_CONV ** -0.5),
        'lru_conv_b': nrm(ks[14], (DEPTH, LRU_WIDTH), 0.02),
        'lru_w_a': nrm(ks[15], (DEPTH, 2, LRU_BLOCKS, LRU_BLOCK, LRU_BLOCK), LRU_BLOCK ** -0.5),
        'lru_b_a': nrm(ks[16], (DEPTH, 2, LRU_WIDTH), 0.1),
        'lru_w_x': nrm(ks[17], (DEPTH, 2, LRU_BLOCKS, LRU_BLOCK, LRU_BLOCK), LRU_BLOCK ** -0.5),
        'lru_b_x': nrm(ks[18], (DEPTH, 2, LRU_WIDTH), 0.1),
        'lru_lambda': lam,
        'attn_q_norm_g': 1.0 + nrm(ks[19], (DEPTH, ATT_HEAD_DIM), 0.02),
        'attn_k_norm_g': 1.0 + nrm(ks[20], (DEPTH, ATT_HEAD_DIM), 0.02),
        'w_branch': nrm(ks[21], (DEPTH, N_BRANCH, BRANCH_WIDTH, D), BRANCH_WIDTH ** -0.5),
        'w_out': nrm(ks[22], (DEPTH, D, D), D ** -0.5),
        'final_norm_g': 1.0 + nrm(ks[23], (D,), 0.02),
    }


def reference(x, c, ctx, c_ctx, w_mod, b_mod, norm_g, ffn_w_in, ffn_w_out, w_in,
              ret_decay_logit, ret_norm_g, lru_conv_w, lru_conv_b, lru_w_a, lru_b_a,
              lru_w_x, lru_b_x, lru_lambda, attn_q_norm_g, attn_k_norm_g, w_branch,
              w_out, final_norm_g):
    seq = x.shape[1]
    n_rows = seq // GRID_W
    rows = jnp.repeat(jnp.arange(n_rows, dtype=jnp.float32), GRID_W)
    cols = jnp.tile(jnp.arange(GRID_W, dtype=jnp.float32), n_rows)
    s_lat = jax.nn.silu(c)
    s_ctx = jax.nn.silu(c_ctx)[None]
    xc = ctx
    for l in range(DEPTH):
        last = l == DEPTH - 1
        m_l = (s_lat @ w_mod[l] + b_mod[l]).reshape(-1, N_SUB, 3, 1, D_MODEL)
        m_c = (s_ctx @ w_mod[l] + b_mod[l]).reshape(1, N_SUB, 3, 1, D_MODEL)
        x = x + _half_ffn(x, m_l[:, 0], norm_g[l, 0], ffn_w_in[l, 0], ffn_w_out[l, 0])
        xc = xc + _half_ffn(xc, m_c[:, 0], norm_g[l, 0], ffn_w_in[l, 0], ffn_w_out[l, 0])
        p = {
            'w_in': w_in[l], 'ret_decay_logit': ret_decay_logit[l], 'ret_norm_g': ret_norm_g[l],
            'lru_conv_w': lru_conv_w[l], 'lru_conv_b': lru_conv_b[l],
            'lru_w_a': lru_w_a[l], 'lru_b_a': lru_b_a[l], 'lru_w_x': lru_w_x[l], 'lru_b_x': lru_b_x[l],
            'lru_lambda': lru_lambda[l], 'q_norm_g': attn_q_norm_g[l], 'k_norm_g': attn_k_norm_g[l],
            'w_branch': w_branch[l], 'w_out': w_out[l],
        }
        h_l = _modulate(x, norm_g[l, 1], m_l[:, 1, 0], m_l[:, 1, 1])
        h_c = _modulate(xc, norm_g[l, 1], m_c[:, 1, 0], m_c[:, 1, 1])
        y_l, y_c = _token_mix(h_c, h_l, p, rows, cols, not last)
        x = x + m_l[:, 1, 2] * y_l
        x = x + _half_ffn(x, m_l[:, 2], norm_g[l, 2], ffn_w_in[l, 1], ffn_w_out[l, 1])
        if not last:
            xc = xc + m_c[:, 1, 2] * y_c
            xc = xc + _half_ffn(xc, m_c[:, 2], norm_g[l, 2], ffn_w_in[l, 1], ffn_w_out[l, 1])
    return _rms(x, final_norm_g)
```

```python
import numpy as np
import concourse.bass as bass
import concourse.mybir as mybir
from concourse.bass_utils import run_bass_kernel_spmd

F32 = mybir.dt.float32
BF16 = mybir.dt.bfloat16
AF = mybir.ActivationFunctionType
ALU = mybir.AluOpType

D = 1024
DEPTH = 4
CTX = 256
LAT = 2048
T = CTX + LAT
TN = 256
NT = T // TN
DFF = 2816
DIN = 6912
EPS = 1e-6
NZ = 38
PL = 156
NPV = DEPTH * PL + 8 + 16 + 2
EX_KA = 0
EX_AV = EX_KA + 128 * LAT
EX_SL = EX_AV + LAT * 128
EX_LX = EX_SL + 8 * 128 * 128
EX_N = EX_LX + 4 * 128 * 4
EX2_N = 2 * 512

DEBUG_STOP = None
N_LAYERS = DEPTH
TRACE = False


class _I:
    __slots__ = ("eng", "fn", "waits", "sig", "idx", "kind", "sem", "target")


class Sched:
    R = 8

    def __init__(self, nc, sems):
        self.nc = nc
        self.sems = sems
        nxt = iter(range(len(sems)))
        self.csem = {e: next(nxt) for e in ("pe", "act", "dve", "pool")}
        self.qsem = {q: [next(nxt) for _ in range(self.R)] for q in ("sp", "pool")}
        self.ccsem = next(nxt)
        self.ncc = 0
        self.qn = {"sp": 0, "pool": 0}
        self.cnt = {e: 0 for e in self.csem}
        self.lists = {e: [] for e in ("pe", "act", "dve", "pool", "sp")}
        self.lastw = {}
        self.rd = {}
        self.seen = {e: {} for e in self.lists}
        self.barrier = []
        self.ninstr = 0

    def add(self, eng, fn, reads=(), writes=(), kind="c"):
        I = _I()
        I.eng, I.fn, I.kind, I.sig, I.idx, I.waits = eng, fn, kind, False, None, []
        deps = []
        for b in reads:
            w = self.lastw.get(b)
            if w is not None:
                deps.append(w)
        for b in writes:
            w = self.lastw.get(b)
            if w is not None:
                deps.append(w)
            r = self.rd.get(b)
            if r:
                deps.extend(r[0].values())
                deps.extend(r[1])
        for J in deps:
            if J.kind in ("dma", "cc"):
                I.waits.append(J)
            elif J.eng == eng and kind == "c":
                if eng == "pe":
                    continue
                I.waits.append(J)
                J.sig = True
            else:
                I.waits.append(J)
                J.sig = True
        if kind == "dma":
            n = self.qn[eng]
            self.qn[eng] += 1
            I.sem = self.qsem[eng][n % self.R]
            I.target = 16 * (n // self.R + 1)
            if n >= self.R:
                I.waits.append((I.sem, 16 * (n // self.R)))
        elif kind == "cc":
            I.sem = self.ccsem
            self.ncc += 1
            I.target = self.ncc
        for b in reads:
            r = self.rd.setdefault(b, ({}, []))
            if kind == "c":
                r[0][eng] = I
            else:
                r[1].append(I)
        for b in writes:
            self.lastw[b] = I
            self.rd[b] = ({}, [])
        self.lists[eng].append(I)
        self.ninstr += 1
        return I

    def dma(self, q, out, in_, reads=(), writes=(), slow=False):
        if slow:
            return self.add(q, lambda e: e.dma_start(out=out, in_=in_, allow_slow_non_contiguous=True), reads, writes,
                            kind="dma")
        return self.add(q, lambda e: e.dma_start(out=out, in_=in_), reads, writes, kind="dma")

    def mm(self, out, lhsT, rhs, start, stop, reads=(), writes=()):
        return self.add("pe", lambda e: e.matmul(out, lhsT=lhsT, rhs=rhs, start=start, stop=stop), reads, writes)

    def tr(self, out, in_, ident, reads=(), writes=()):
        return self.add("pe", lambda e: e.transpose(out, in_, ident), reads, writes)

    def act(self, out, in_, func, reads=(), writes=(), bias=None, scale=None):
        kw = {}
        if bias is not None:
            kw["bias"] = bias
        if scale is not None:
            kw["scale"] = scale
        return self.add("act", lambda e: e.activation(out=out, in_=in_, func=func, **kw), reads, writes)

    def tt(self, eng, out, in0, in1, op, reads=(), writes=()):
        return self.add(eng, lambda e: e.tensor_tensor(out=out, in0=in0, in1=in1, op=op), reads, writes)

    def ts(self, eng, out, in0, s1, s2, op0, op1, reads=(), writes=()):
        return self.add(eng, lambda e: e.tensor_scalar(out=out, in0=in0, scalar1=s1, scalar2=s2, op0=op0, op1=op1),
                        reads, writes)

    def stt(self, out, in0, scalar, in1, op0, op1, reads=(), writes=()):
        return self.add("dve", lambda e: e.scalar_tensor_tensor(out=out, in0=in0, scalar=scalar, in1=in1,
                                                                op0=op0, op1=op1), reads, writes)

    def cp(self, eng, out, in_, reads=(), writes=()):
        if eng == "act":
            return self.add("act", lambda e: e.copy(out=out, in_=in_), reads, writes)
        return self.add(eng, lambda e: e.tensor_copy(out=out, in_=in_), reads, writes)

    def recip(self, out, in_, reads=(), writes=()):
        return self.add("dve", lambda e: e.reciprocal(out=out, in_=in_), reads, writes)

    def memset(self, eng, ap, val, reads=(), writes=()):
        return self.add(eng, lambda e: e.memset(ap, val), reads, writes)

    def scan(self, out, d0, d1, init, reads=(), writes=()):
        return self.add("dve", lambda e: e.tensor_tensor_scan(out=out, data0=d0, data1=d1, initial=init,
                                                              op0=ALU.mult, op1=ALU.add), reads, writes)

    def cc(self, ins_ap, outs_ap, reads=(), writes=()):
        groups = [[0, 1], [2, 3], [4, 5], [6, 7]]
        return self.add("pool", lambda e: e.collective_compute("AllGather", ALU.bypass, replica_groups=groups,
                                                               ins=[ins_ap], outs=[outs_ap]),
                        reads, writes, kind="cc")

    def flush(self):
        nc = self.nc
        for e in self.csem:
            last = None
            for I in self.lists[e]:
                if I.kind == "c":
                    last = I
            if last is not None:
                last.sig = True
            for I in self.lists[e]:
                if I.kind == "c" and I.sig:
                    self.cnt[e] += 1
                    I.idx = self.cnt[e]
        sems = self.sems

        def emit(eh, eng):
            seen = self.seen[eng]

            def wait(si, val):
                if seen.get(si, 0) < val:
                    eh.wait_ge(sems[si], val)
                    seen[si] = val

            for (si, val) in self.barrier:
                wait(si, val)
            for I in self.lists[eng]:
                for w in I.waits:
                    if isinstance(w, tuple):
                        wait(w[0], w[1])
                    elif w.kind in ("dma", "cc"):
                        wait(w.sem, w.target)
                    else:
                        wait(self.csem[w.eng], w.idx)
                ins = I.fn(eh)
                if I.kind == "dma":
                    ins.then_inc(sems[I.sem], 16)
                elif I.kind == "cc":
                    ins.then_inc(sems[I.sem])
                elif I.sig:
                    ins.then_inc(sems[self.csem[eng]], 1)

        with nc.Block() as block:
            @block.tensor
            def _(eh):
                emit(eh, "pe")

            @block.scalar
            def _(eh):
                emit(eh, "act")

            @block.vector
            def _(eh):
                emit(eh, "dve")

            @block.gpsimd
            def _(eh):
                emit(eh, "pool")

            @block.sync
            def _(eh):
                emit(eh, "sp")

        bar = [(self.csem[e], self.cnt[e]) for e in self.csem if self.cnt[e] > 0]
        for q in ("sp", "pool"):
            n = self.qn[q]
            for i in range(self.R):
                if n > i:
                    bar.append((self.qsem[q][i], 16 * ((n - 1 - i) // self.R + 1)))
        if self.ncc:
            bar.append((self.ccsem, self.ncc))
        self.barrier = bar
        self.lists = {e: [] for e in self.lists}
        self.lastw = {}
        self.rd = {}

    def final_wait(self):
        nc = self.nc
        sems = self.sems
        with nc.Block() as block:
            @block.sync
            def _(eh):
                for (si, val) in self.barrier:
                    eh.wait_ge(sems[si], val)

            @block.gpsimd
            def _(eh):
                for (si, val) in self.barrier:
                    eh.wait_ge(sems[si], val)


class Builder:
    def __init__(self, nc):
        self.nc = nc
        self.pscur = 0

    def declare(self):
        nc = self.nc
        di = lambda n, s: nc.dram_tensor(n, s, F32, kind="ExternalInput").ap()
        self.xin = di("xin", [8, 128, T])
        self.pvin = di("pv", [128, NPV])
        self.cst = di("cst", [6, 128, 128])
        self.posin = di("pos", [2, 128, TN])
        self.rope = di("rope", [4, 128, T])
        self.w_mod = di("w_mod", [N_LAYERS, D, 9 * D])
        self.ffn_w_in = di("ffn_w_in", [N_LAYERS, 2, D, 2 * DFF])
        self.ffn_w_out = di("ffn_w_out", [N_LAYERS, 2, DFF, D])
        self.w_in = di("w_in", [N_LAYERS, D, DIN])
        self.w_inp = di("w_inp", [N_LAYERS, D, 13 * 128])
        self.lbd = di("lbd", [N_LAYERS, 2, 2, 4, 128, 128])
        self.w_branch = di("w_branch", [N_LAYERS, 3, 512, D])
        self.w_out = di("w_out", [N_LAYERS, D, D])
        self.out = nc.dram_tensor("out", [8, 128, LAT], F32, kind="ExternalOutput").ap()
        if DEBUG_STOP is not None:
            self.dbg = nc.dram_tensor("dbg", [8, 128, T], F32, kind="ExternalOutput").ap()
        dt = lambda n, s, d=F32: nc.dram_tensor(n, s, d)
        self.X = dt("X", [8, 128, T]).ap()
        self.H = dt("H", [8, 128, T], BF16).ap()
        self.Z = dt("Z", [NZ, 128, T]).ap()
        self.RV = dt("RV", [T, 512], BF16).ap()
        self.AVC = dt("AVC", [CTX, 128]).ap()
        self.KAC = dt("KAC", [128, CTX]).ap()
        self.QA = dt("QA", [4, 128, T], BF16).ap()
        self.QK = dt("QK", [4, 4, 128, T], BF16).ap()
        self.EXP = [dt("EXP%d" % j, [128, 512]) for j in range(10)] + [dt("EXPL", [4, 512])]
        self.EXGP = [dt("EXGP%d" % j, [256, 512]) for j in range(10)] + [dt("EXGPL", [8, 512])]
        self.EX2t = dt("EX2", [EX2_N // 512, 512])
        self.EX2Gt = dt("EX2G", [2 * EX2_N // 512, 512])
        self.EX2 = self.EX2t.ap().rearrange("a b -> (a b)")
        self.EX2G = self.EX2Gt.ap().rearrange("a b -> (a b)")
        self.HS = dt("HS", [4, 128, T]).ap()
        self.ACF = dt("ACF", [4, 128, LAT]).ap()
        self.ACB = dt("ACB", [4, 128, LAT]).ap()
        self.YR = dt("YR", [4, 128, T], BF16).ap()
        self.YL = dt("YL", [4, 128, T], BF16).ap()
        self.YA = dt("YA", [4, 128, T], BF16).ap()

    def un(self, name):
        self.uid = getattr(self, "uid", 0) + 1
        return "%s_%d" % (name, self.uid)

    def nextps(self, n=6):
        i = self.pscur % n
        self.pscur += 1
        return self.ps[i], ("ps", i)

    def pvl(self, l, off, n=1):
        return self.pv[:, l * PL + off:l * PL + off + n]

    def phase_init(self):
        nc, S = self.nc, self.S
        with nc.sbuf_tensor(self.un("ini_c"), [128, 5, 128], F32) as cf, nc.sbuf_tensor(self.un("ini_s"), [128, 16], F32) as sv:
            S.dma("sp", self.pv[:], self.pvin, writes=["pv"])
            S.dma("sp", cf[:], self.cst[0:5].rearrange("c p n -> p c n"), writes=["cf"])
            S.dma("sp", self.pos[:], self.posin.rearrange("c p n -> p c n"), writes=["pos"])
            S.cp("dve", self.ident[:], cf[:, 0, :], reads=["cf"], writes=["ident"])
            S.cp("dve", self.ones[:], cf[:, 1, :], reads=["cf"], writes=["ones"])
            S.cp("dve", self.bones[:], cf[:, 2, :], reads=["cf"], writes=["bones"])
            S.cp("dve", self.maskf[:], cf[:, 3, :], reads=["cf"], writes=["maskf"])
            S.cp("dve", self.maskb[:], cf[:, 4, :], reads=["cf"], writes=["maskb"])
            S.cp("dve", self.ones32[:], cf[:, 1, :], reads=["cf"], writes=["ones32"])
            base = DEPTH * PL + 8
            S.act(sv[:], self.pv[:, base:base + 16], AF.Silu, reads=["pv"], writes=["sv"])
            S.cp("dve", self.sb[:], sv[:].rearrange("p (c k) -> p k c", c=2), reads=["sv"], writes=["sb"])
            for k in range(8):
                S.dma("sp", self.X[k], self.xin[k], writes=[("X", k)])
            S.flush()

    def phase_mod(self, l):
        nc, S = self.nc, self.S
        with (nc.sbuf_tensor(self.un("wm0"), [128, 8, 1024], BF16) as wm0,
              nc.sbuf_tensor(self.un("wm1"), [128, 8, 1024], BF16) as wm1,
              nc.sbuf_tensor(self.un("mraw"), [128, 9, 8, 2], F32) as mraw,
              nc.sbuf_tensor(self.un("lg"), [128, 8], F32) as lg):
            wms = [wm0, wm1]
            src = self.w_mod[l].rearrange("(k p) c -> p k c", p=128)
            for b in range(9):
                wm = wms[b % 2]
                for kk in range(0, 8, 4):
                    S.dma("pool", wm[:, kk:kk + 4, :], src[:, kk:kk + 4, b * 1024:(b + 1) * 1024],
                          writes=[("wm", b % 2, kk)])
                pm, km = self.nextps()
                for i in range(8):
                    for k in range(8):
                        S.mm(pm[:, i * 2:(i + 1) * 2], wm[:, k, i * 128:(i + 1) * 128], self.sb[:, k, :],
                             k == 0, k == 7, reads=[("wm", b % 2, (k // 4) * 4), "sb"], writes=[km])
                bm = self.pvl(l, 24 + b * 8, 8)
                S.tt("dve", mraw[:, b, :, :], pm[:, 0:16].rearrange("p (i c) -> p i c", c=2),
                     bm.unsqueeze(2).broadcast_to([128, 8, 2]), ALU.add, reads=[km, "pv"], writes=[("mraw", b)])
            for s in range(3):
                g = self.pvl(l, s * 8, 8).unsqueeze(2).broadcast_to([128, 8, 2])
                S.stt(self.modA[:, s, :, :], mraw[:, 3 * s + 1, :, :], 1.0, g, ALU.add, ALU.mult,
                      reads=[("mraw", 3 * s + 1), "pv"], writes=[("modA", s)])
                S.cp("dve", self.modB[:, s, :, :], mraw[:, 3 * s, :, :], reads=[("mraw", 3 * s)], writes=[("modB", s)])
                S.ts("dve", self.modG[:, s, :, :], mraw[:, 3 * s + 2, :, :], 0.5 if s != 1 else 1.0, None,
                     ALU.mult, ALU.bypass, reads=[("mraw", 3 * s + 2)], writes=[("modG", s)])
            S.act(lg[:], self.pvl(l, 148, 8), AF.Exp, scale=-1.0, reads=["pv"], writes=["lg"])
            S.act(lg[:], lg[:], AF.Ln, bias=1.0, reads=["lg"], writes=["lg"])
            S.ts("dve", self.lgam[:], lg[:], -1.0, None, ALU.mult, ALU.bypass, reads=["lg"], writes=["lgam"])
            S.ts("dve", self.nlgam[:], lg[:], 1.0, None, ALU.mult, ALU.bypass, reads=["lg"], writes=["nlgam"])
            S.act(self.gC[:], self.lgam[:], AF.Exp, scale=128.0, reads=["lgam"], writes=["gC"])
            hb = DEPTH * PL + 24
            S.ts("dve", lg[:, 0:4], self.lgam[:, 0:4], self.pv[:, hb:hb + 1], 2048.0, ALU.mult, ALU.mult,
                 reads=["lgam", "pv"], writes=["lg"])
            S.ts("dve", lg[:, 4:8], self.lgam[:, 4:8], self.pv[:, hb + 1:hb + 2], 2048.0, ALU.mult, ALU.mult,
                 reads=["lgam", "pv"], writes=["lg"])
            S.act(self.bc1[:], lg[:], AF.Exp, reads=["lg"], writes=["bc1"])
            S.act(self.lcp[:], self.pvl(l, 136, 8), AF.Exp, scale=-1.0, reads=["pv"], writes=["lcp"])
            S.act(self.lcp[:], self.lcp[:], AF.Ln, bias=1.0, reads=["lcp"], writes=["lcp"])
            S.ts("dve", self.lcp2[:], self.lcp[:], -16.0, None, ALU.mult, ALU.bypass, reads=["lcp"], writes=["lcp2"])
            S.ts("dve", self.lcp[:], self.lcp[:], -8.0, None, ALU.mult, ALU.bypass, reads=["lcp"], writes=["lcp"])
            S.flush()

    def load_x(self, xt, t, par):
        S = self.S
        S.dma("sp", xt[:], self.X[:, :, t * TN:(t + 1) * TN].rearrange("k p n -> p k n"),
              reads=[("X", t)], writes=[("xt", par, i) for i in range(8)])

    def store_x(self, xt, t, par, dst=None):
        S = self.S
        dst = self.X if dst is None else dst
        S.dma("sp", dst[:, :, t * TN:(t + 1) * TN].rearrange("k p n -> p k n"), xt[:],
              reads=[("xt", par, i) for i in range(8)], writes=[("X", t)])

    def norm_mod(self, xt, par, sq, rs, tmp, h, sub, c):
        S = self.S
        xk = [("xt", par, i) for i in range(8)]
        S.act(sq[:], xt[:], AF.Square, reads=xk, writes=["sq"])
        pn, kn = self.nextps()
        for k in range(8):
            S.mm(pn[:, 0:TN], self.ones[:], sq[:, k, :], k == 0, k == 7, reads=["sq", "ones"], writes=[kn])
        S.act(rs[:], pn[:, 0:TN], AF.Sqrt, scale=1.0 / D, bias=self.epsb[:, 0:1], reads=[kn], writes=["rs"])
        S.recip(rs[:], rs[:], reads=["rs"], writes=["rs"])
        S.tt("dve", tmp[:], xt[:], rs[:].unsqueeze(1).broadcast_to([128, 8, TN]), ALU.mult,
             reads=xk + ["rs"], writes=["tmp"])
        for k in range(8):
            S.act(h[:, k, :], tmp[:, k, :], AF.Identity, scale=self.modA[:, sub, k, c:c + 1],
                  bias=self.modB[:, sub, k, c:c + 1], reads=["tmp", ("modA", sub), ("modB", sub)],
                  writes=[("h", par)])

    def phase_ffn(self, l, which, sub):
        nc, S = self.nc, self.S
        with (nc.sbuf_tensor(self.un("ffw1"), [128, 8, 2 * DFF], BF16) as w1,
              nc.sbuf_tensor(self.un("ffw2"), [128, 22, D], BF16) as w2,
              nc.sbuf_tensor(self.un("fxt0"), [128, 8, TN], F32) as xt0,
              nc.sbuf_tensor(self.un("fxt1"), [128, 8, TN], F32) as xt1,
              nc.sbuf_tensor(self.un("fsq"), [128, 8, TN], BF16) as sq,
              nc.sbuf_tensor(self.un("fh0"), [128, 8, TN], BF16) as h0,
              nc.sbuf_tensor(self.un("fh1"), [128, 8, TN], BF16) as h1,
              nc.sbuf_tensor(self.un("fg"), [128, 22, TN], BF16) as g,
              nc.sbuf_tensor(self.un("fsl0"), [128, TN], F32) as sl0,
              nc.sbuf_tensor(self.un("fsl1"), [128, TN], F32) as sl1,
              nc.sbuf_tensor(self.un("frs"), [128, TN], F32) as rs,
              nc.sbuf_tensor(self.un("ftmp"), [128, 8, TN], F32) as tmp):
            xts, hs, sls = [xt0, xt1], [h0, h1], [sl0, sl1]
            src1 = self.ffn_w_in[l, which].rearrange("(k p) c -> p k c", p=128)
            for cb in range(11):
                S.dma("pool", w1[:, :, cb * 512:(cb + 1) * 512], src1[:, :, cb * 512:(cb + 1) * 512],
                      writes=[("w1", cb)])
            src2 = self.ffn_w_out[l, which].rearrange("(j p) c -> p j c", p=128)
            for jb in range(11):
                S.dma("pool", w2[:, 2 * jb:2 * jb + 2, :], src2[:, 2 * jb:2 * jb + 2, :], writes=[("w2", jb)])
            self.load_x(xts[0], 0, 0)
            for t in range(NT):
                par = t % 2
                xt, h = xts[par], hs[par]
                c = 1 if t == 0 else 0
                if t + 1 < NT:
                    self.load_x(xts[1 - par], t + 1, 1 - par)
                self.norm_mod(xt, par, sq, rs, tmp, h, sub, c)
                for j in range(22):
                    pa, ka = self.nextps()
                    pb, kb = self.nextps()
                    ca, cb_ = j * 128, DFF + j * 128
                    for k in range(8):
                        S.mm(pa[:, 0:TN], w1[:, k, ca:ca + 128], h[:, k, :], k == 0, k == 7,
                             reads=[("w1", ca // 512), ("h", par)], writes=[ka])
                    for k in range(8):
                        S.mm(pb[:, 0:TN], w1[:, k, cb_:cb_ + 128], h[:, k, :], k == 0, k == 7,
                             reads=[("w1", cb_ // 512), ("h", par)], writes=[kb])
                    sl = sls[j % 2]
                    S.act(sl[:], pa[:, 0:TN], AF.Silu, reads=[ka], writes=[("sl", j % 2)])
                    S.tt("dve", g[:, j, :], sl[:], pb[:, 0:TN], ALU.mult, reads=[("sl", j % 2), kb], writes=[("g", j)])
                for i in range(8):
                    po, ko = self.nextps()
                    for j in range(22):
                        S.mm(po[:, 0:TN], w2[:, j, i * 128:(i + 1) * 128], g[:, j, :], j == 0, j == 21,
                             reads=[("w2", j // 2), ("g", j)], writes=[ko])
                    S.stt(xt[:, i, :], po[:, 0:TN], self.modG[:, sub, i, c:c + 1], xt[:, i, :], ALU.mult, ALU.add,
                          reads=[ko, ("xt", par, i), ("modG", sub)], writes=[("xt", par, i)])
                self.store_x(xt, t, par)
            S.flush()

    def phase_final(self):
        nc, S = self.nc, self.S
        with (nc.sbuf_tensor(self.un("nxt0"), [128, 8, TN], F32) as xt0,
              nc.sbuf_tensor(self.un("nxt1"), [128, 8, TN], F32) as xt1,
              nc.sbuf_tensor(self.un("nsq"), [128, 8, TN], BF16) as sq,
              nc.sbuf_tensor(self.un("nrs"), [128, TN], F32) as rs,
              nc.sbuf_tensor(self.un("no0"), [128, 8, TN], F32) as o0,
              nc.sbuf_tensor(self.un("no1"), [128, 8, TN], F32) as o1):
            xts, os_ = [xt0, xt1], [o0, o1]
            fb = DEPTH * PL
            for t in range(1, NT):
                par = t % 2
                xt, o = xts[par], os_[par]
                self.load_x(xt, t, par)
                xk = [("xt", par, i) for i in range(8)]
                S.act(sq[:], xt[:], AF.Square, reads=xk, writes=["sq"])
                pn, kn = self.nextps()
                for k in range(8):
                    S.mm(pn[:, 0:TN], self.ones[:], sq[:, k, :], k == 0, k == 7, reads=["sq"], writes=[kn])
                S.act(rs[:], pn[:, 0:TN], AF.Sqrt, scale=1.0 / D, bias=self.epsb[:, 0:1], reads=[kn], writes=["rs"])
                S.recip(rs[:], rs[:], reads=["rs"], writes=["rs"])
                for k in range(8):
                    S.stt(o[:, k, :], xt[:, k, :], self.pv[:, fb + k:fb + k + 1], rs[:], ALU.mult, ALU.mult,
                          reads=xk + ["rs"], writes=[("o", par)])
                S.dma("sp", self.out[:, :, (t - 1) * TN:t * TN].rearrange("k p n -> p k n"), o[:],
                      reads=[("o", par)], writes=[("out", t)])
            S.flush()

    def phase_dbg(self):
        S = self.S
        for k in range(8):
            S.dma("sp", self.dbg[k], self.X[k], reads=[("X", k)], writes=[("dbg", k)])
        S.flush()

    def build(self):
        nc = self.nc
        self.declare()
        sem_names = ["s%d" % i for i in range(4 + 16 + 1)]
        from contextlib import ExitStack
        with ExitStack() as st:
            sems = [st.enter_context(nc.semaphore(n)) for n in sem_names]
            self.S = Sched(nc, sems)
            sb = lambda n, s, d=F32: st.enter_context(nc.sbuf_tensor(self.un("sb_") + n, s, d))
            self.pv = sb("pv", [128, NPV])
            self.ident = sb("ident", [128, 128], BF16)
            self.ones = sb("ones", [128, 128], BF16)
            self.bones = sb("bones", [128, 128], BF16)
            self.maskf = sb("maskf", [128, 128], BF16)
            self.maskb = sb("maskb", [128, 128], BF16)
            self.ones32 = sb("ones32", [128, 128])
            self.pos = sb("pos", [128, 2, TN])
            self.sb = sb("sb", [128, 8, 2], BF16)
            self.modA = sb("modA", [128, 3, 8, 2])
            self.modB = sb("modB", [128, 3, 8, 2])
            self.modG = sb("modG", [128, 3, 8, 2])
            self.lgam = sb("lgam", [128, 8])
            self.nlgam = sb("nlgam", [128, 8])
            self.gC = sb("gC", [128, 8])
            self.bc1 = sb("bc1", [128, 8])
            self.lcp = sb("lcp", [128, 8])
            self.lcp2 = sb("lcp2", [128, 8])
            self.epsb = sb("epsb", [128, 1])
            self.lnks = sb("lnks", [128, 1])
            self.ps = [st.enter_context(nc.psum_tensor("ps%d" % i, [128, 512], F32)) for i in range(6)]
            self.psb = [st.enter_context(nc.psum_tensor("psb%d" % i, [128, 1024], BF16)) for i in range(2)]
            self.S.memset("dve", self.epsb[:], EPS, writes=["epsb"])
            self.S.memset("dve", self.lnks[:], -0.5 * float(np.log(128.0)), writes=["lnks"])
            self.phase_init()
            stop = False
            for l in range(N_LAYERS):
                for name, fn in (("mod", lambda: self.phase_mod(l)),
                                 ("ffn1", lambda: self.phase_ffn(l, 0, 0)),
                                 ("mix", lambda: self.phase_mix(l)),
                                 ("ffn2", lambda: self.phase_ffn(l, 1, 2))):
                    r = fn()
                    if r or DEBUG_STOP == (l, name):
                        stop = True
                        break
                if stop:
                    break
            if DEBUG_STOP is not None:
                self.phase_dbg()
            else:
                self.phase_final()
            self.S.final_wait()


    def ex_w(self, off, cnt_):
        j, o = off // 65536, off % 65536
        assert o + cnt_ <= 65536
        return self.EXP[j].ap().rearrange("a b -> (a b)")[o:o + cnt_]

    def ex_r(self, slot, off, cnt_):
        j, o = off // 65536, off % 65536
        assert o + cnt_ <= 65536
        psz = 65536 if j < 10 else 2048
        lo = slot * psz + o
        return self.EXGP[j].ap().rearrange("a b -> (a b)")[lo:lo + cnt_]

    def phase_mix(self, l):
        def cc1():
            for j in range(11):
                self.S.cc(self.EXP[j].ap().opt(), self.EXGP[j].ap().opt())
            self.S.flush()

        def cc2():
            self.S.cc(self.EX2t.ap().opt(), self.EX2Gt.ap().opt())
            self.S.flush()

        for name, fn in (("m1", lambda: self.phase_m1(l)), ("m2", lambda: self.phase_m2(l)),
                         ("m2b", lambda: self.phase_m2b(l)), ("cc1", cc1), ("lruA", lambda: self.phase_lruA(l)),
                         ("cc2", cc2), ("lruB", lambda: self.phase_lruB(l)), ("ret", lambda: self.phase_ret(l)),
                         ("attn", lambda: self.phase_attn(l)), ("merge", lambda: self.phase_merge(l))):
            fn()
            if DEBUG_STOP == (l, name):
                return True
        return False

    ZSRC = list(range(0, 8)) + list(range(12, 29)) + list(range(30, 43))

    def phase_m1(self, l):
        nc, S = self.nc, self.S
        NWC = 43
        with (nc.sbuf_tensor(self.un("m1w"), [128, 8, NWC * 128], BF16) as wz,
              nc.sbuf_tensor(self.un("m1x0"), [128, 8, TN], F32) as xt0,
              nc.sbuf_tensor(self.un("m1x1"), [128, 8, TN], F32) as xt1,
              nc.sbuf_tensor(self.un("m1sq"), [128, 8, TN], BF16) as sq,
              nc.sbuf_tensor(self.un("m1h0"), [128, 8, TN], BF16) as h0,
              nc.sbuf_tensor(self.un("m1h1"), [128, 8, TN], BF16) as h1,
              nc.sbuf_tensor(self.un("m1rs"), [128, TN], F32) as rs,
              nc.sbuf_tensor(self.un("m1tmp"), [128, 8, TN], F32) as tmp,
              nc.sbuf_tensor(self.un("m1z0"), [128, 2, TN], F32) as zs0,
              nc.sbuf_tensor(self.un("m1z1"), [128, 2, TN], F32) as zs1,
              nc.sbuf_tensor(self.un("m1rv0"), [128, 512], BF16) as rv0,
              nc.sbuf_tensor(self.un("m1rv1"), [128, 512], BF16) as rv1,
              nc.sbuf_tensor(self.un("m1av0"), [128, 128], F32) as av0,
              nc.sbuf_tensor(self.un("m1av1"), [128, 128], F32) as av1):
            xts, hs, zss, rvs, avs = [xt0, xt1], [h0, h1], [zs0, zs1], [rv0, rv1], [av0, av1]
            src = self.w_in[l].rearrange("(k p) c -> p k c", p=128)
            srcp = self.w_inp[l].rearrange("(k p) c -> p k c", p=128)
            for pc in range(10):
                S.dma("pool", wz[:, :, pc * 384:(pc + 1) * 384], src[:, :, pc * 384:(pc + 1) * 384], writes=[("wz", pc)])
            for pc in range(5):
                lo, hi = pc * 384, min((pc + 1) * 384, 13 * 128)
                S.dma("pool", wz[:, :, 3840 + lo:3840 + hi], srcp[:, :, lo:hi], writes=[("wz", 10 + pc)])
            self.load_x(xts[0], 0, 0)
            for t in range(NT):
                par = t % 2
                xt, h = xts[par], hs[par]
                c = 1 if t == 0 else 0
                t0 = t * TN
                if t + 1 < NT:
                    self.load_x(xts[1 - par], t + 1, 1 - par)
                self.norm_mod(xt, par, sq, rs, tmp, h, 1, c)
                S.dma("sp", self.H[:, :, t0:t0 + TN].rearrange("k p n -> p k n"), h[:], reads=[("h", par)],
                      writes=[("H", t)])
                for zc in range(NZ):
                    wc = self.ZSRC[zc]
                    pz, kz = self.nextps()
                    for k in range(8):
                        S.mm(pz[:, 0:TN], wz[:, k, wc * 128:(wc + 1) * 128], h[:, k, :], k == 0, k == 7,
                             reads=[("wz", wc // 3), ("h", par)], writes=[kz])
                    zs = zss[(zc // 2) % 2]
                    zk = ("zs", (zc // 2) % 2, zc % 2)
                    S.cp("act" if zc % 2 == 0 else "dve", zs[:, zc % 2, :], pz[:, 0:TN], reads=[kz], writes=[zk])
                    if zc % 2 == 1:
                        S.dma("sp", self.Z[zc - 1:zc + 1, :, t0:t0 + TN].rearrange("c p n -> p c n"), zs[:],
                              reads=[("zs", (zc // 2) % 2, 0), ("zs", (zc // 2) % 2, 1)], writes=[("Z", zc // 2, t)])
                for hf in range(2):
                    i2 = (2 * t + hf) % 2
                    prv, krv = self.nextps()
                    for k in range(8):
                        S.mm(prv[:, 0:512], h[:, k, hf * 128:(hf + 1) * 128], wz[:, k, 1024:1536], k == 0, k == 7,
                             reads=[("wz", 2), ("wz", 3), ("h", par)], writes=[krv])
                    S.cp("act", rvs[i2][:], prv[:, 0:512], reads=[krv], writes=[("rvs", i2)])
                    S.dma("sp", self.RV[t0 + hf * 128:t0 + (hf + 1) * 128, :], rvs[i2][:], reads=[("rvs", i2)],
                          writes=[("RV", t, hf)])
                    pav, kav = self.nextps()
                    for k in range(8):
                        S.mm(pav[:, 0:128], h[:, k, hf * 128:(hf + 1) * 128], wz[:, k, 3712:3840], k == 0, k == 7,
                             reads=[("wz", 9), ("h", par)], writes=[kav])
                    S.cp("dve", avs[i2][:], pav[:, 0:128], reads=[kav], writes=[("avs", i2)])
                    if t == 0:
                        dst = self.AVC[hf * 128:(hf + 1) * 128, :]
                    else:
                        r0 = (t - 1) * TN + hf * 128
                        dst = self.ex_w(EX_AV + r0 * 128, 16384).rearrange("(t e) -> t e", e=128)
                    S.dma("sp", dst, avs[i2][:], reads=[("avs", i2)], writes=[("AV", t, hf)])
            S.flush()

    def phase_m2(self, l):
        nc, S = self.nc, self.S
        LNKS = -0.5 * float(np.log(128.0))
        with (nc.sbuf_tensor(self.un("m2G"), [128, 16, TN], F32) as G,
              nc.sbuf_tensor(self.un("m2rp0"), [128, 4, TN], F32) as rp0,
              nc.sbuf_tensor(self.un("m2rp1"), [128, 4, TN], F32) as rp1,
              nc.sbuf_tensor(self.un("m2z"), [128, 4, TN], F32) as zb,
              nc.sbuf_tensor(self.un("m2zp"), [128, 4, TN], F32) as zpb,
              nc.sbuf_tensor(self.un("m2r1"), [128, 2, TN], F32) as r1b,
              nc.sbuf_tensor(self.un("m2r2"), [128, 2, TN], F32) as r2b,
              nc.sbuf_tensor(self.un("m2qk0"), [128, 4, TN], BF16) as qk0,
              nc.sbuf_tensor(self.un("m2qk1"), [128, 4, TN], BF16) as qk1,
              nc.sbuf_tensor(self.un("m2sq"), [128, TN], BF16) as sq,
              nc.sbuf_tensor(self.un("m2rs"), [128, 2, TN], F32) as rsb,
              nc.sbuf_tensor(self.un("m2qa"), [128, 2, TN], BF16) as qab,
              nc.sbuf_tensor(self.un("m2kf"), [128, TN], F32) as kf32):
            rps, qks = [rp0, rp1], [qk0, qk1]
            for kind in range(2):
                for d in range(2):
                    for h in range(4):
                        gi = (kind * 2 + d) * 4 + h
                        sc = (self.lgam if kind == 0 else self.nlgam)[:, d * 4 + h:d * 4 + h + 1]
                        S.act(G[:, gi, :], self.pos[:, d, :], AF.Exp, scale=sc,
                              bias=(None if kind == 0 else self.lnks[:, 0:1]),
                              reads=["lgam", "nlgam", "pos"], writes=[("G", gi)])
            cnt = 0
            for t in range(NT):
                t0 = t * TN
                rp = rps[t % 2]
                S.dma("sp", rp[:], self.rope[:, :, t0:t0 + TN].rearrange("c p n -> p c n"), writes=[("rp", t % 2)])
                for h in range(4):
                    qk = qks[h % 2]
                    for kind in range(2):
                        zc, zpc = (h, 25 + h) if kind == 0 else (4 + h, 29 + h)
                        b4, b2 = cnt % 4, cnt % 2
                        cnt += 1
                        z, zp, r1, r2 = zb[:, b4, :], zpb[:, b4, :], r1b[:, b2, :], r2b[:, b2, :]
                        S.dma("sp", z, self.Z[zc, :, t0:t0 + TN], writes=[("z", b4)])
                        S.dma("sp", zp, self.Z[zpc, :, t0:t0 + TN], writes=[("zp", b4)])
                        S.tt("dve", r1, z, rp[:, 0, :], ALU.mult, reads=[("z", b4), ("rp", t % 2)], writes=[("r1", b2)])
                        S.tt("pool", r2, zp, rp[:, 1, :], ALU.mult, reads=[("zp", b4), ("rp", t % 2)], writes=[("r2", b2)])
                        S.tt("dve", r1, r1, r2, ALU.add, reads=[("r1", b2), ("r2", b2)], writes=[("r1", b2)])
                        gf = (kind * 2 + 0) * 4 + h
                        gb = (kind * 2 + 1) * 4 + h
                        S.tt("pool", qk[:, 2 * kind, :], r1, G[:, gf, :], ALU.mult, reads=[("r1", b2), ("G", gf)],
                             writes=[("qk", h % 2, 2 * kind)])
                        S.tt("dve", qk[:, 2 * kind + 1, :], r1, G[:, gb, :], ALU.mult, reads=[("r1", b2), ("G", gb)],
                             writes=[("qk", h % 2, 2 * kind + 1)])
                    S.dma("sp", self.QK[h, :, :, t0:t0 + TN].rearrange("s p n -> p s n"), qk[:],
                          reads=[("qk", h % 2, i) for i in range(4)], writes=[("QK", h, t)])
                for c in range(5):
                    zc, zpc = (20 + c, 33 + c) if c < 4 else (24, 37)
                    gcol = 144 if c < 4 else 146
                    b4, b2 = cnt % 4, cnt % 2
                    cnt += 1
                    z, zp, r1, r2, rs = zb[:, b4, :], zpb[:, b4, :], r1b[:, b2, :], r2b[:, b2, :], rsb[:, b2, :]
                    S.dma("sp", z, self.Z[zc, :, t0:t0 + TN], writes=[("z", b4)])
                    S.dma("sp", zp, self.Z[zpc, :, t0:t0 + TN], writes=[("zp", b4)])
                    S.act(sq[:], z, AF.Square, reads=[("z", b4)], writes=["sq"])
                    pn, kn = self.nextps()
                    S.mm(pn[:, 0:TN], self.bones[:], sq[:], True, True, reads=["sq"], writes=[kn])
                    S.act(rs, pn[:, 0:TN], AF.Sqrt, scale=1.0 / 64, bias=self.epsb[:, 0:1], reads=[kn], writes=[("rs", b2)])
                    S.recip(rs, rs, reads=[("rs", b2)], writes=[("rs", b2)])
                    S.stt(r1, z, self.pvl(l, gcol), rs, ALU.mult, ALU.mult, reads=[("z", b4), ("rs", b2)], writes=[("r1", b2)])
                    S.stt(r2, zp, self.pvl(l, gcol + 1), rs, ALU.mult, ALU.mult, reads=[("zp", b4), ("rs", b2)],
                          writes=[("r2", b2)])
                    S.tt("pool", r1, r1, rp[:, 2, :], ALU.mult, reads=[("r1", b2), ("rp", t % 2)], writes=[("r1", b2)])
                    S.tt("pool", r2, r2, rp[:, 3, :], ALU.mult, reads=[("r2", b2), ("rp", t % 2)], writes=[("r2", b2)])
                    if c < 4:
                        S.tt("dve", qab[:, c % 2, :], r1, r2, ALU.add, reads=[("r1", b2), ("r2", b2)], writes=[("qa", c % 2)])
                        S.dma("sp", self.QA[c, :, t0:t0 + TN], qab[:, c % 2, :], reads=[("qa", c % 2)], writes=[("QA", c, t)])
                    else:
                        S.tt("dve", kf32[:], r1, r2, ALU.add, reads=[("r1", b2), ("r2", b2)], writes=["kf32"])
                        if t == 0:
                            dst = self.KAC[:, :]
                        else:
                            dst = self.EXP[(t - 1) // 2].ap()[:, ((t - 1) % 2) * TN:((t - 1) % 2 + 1) * TN]
                        S.dma("sp", dst, kf32[:], reads=["kf32"], writes=[("KA", t)])
            EXLX = self.ex_w(EX_LX, 2048).rearrange("(c p n) -> c p n", p=128, n=4)
            for c in range(4):
                S.dma("sp", EXLX[c, :, 0:2], self.Z[12 + c, :, CTX:CTX + 2], writes=[("LXH", c, 0)], slow=True)
                S.dma("sp", EXLX[c, :, 2:4], self.Z[12 + c, :, T - 2:T], writes=[("LXH", c, 1)], slow=True)
            S.flush()

    def phase_m2b(self, l):
        nc, S = self.nc, self.S
        with (nc.sbuf_tensor(self.un("m2bS"), [128, 8, 128], F32) as S32,
              nc.sbuf_tensor(self.un("m2bv"), [128, 4, 512], BF16) as vb,
              nc.sbuf_tensor(self.un("m2bk"), [128, 4, 4, 128], BF16) as kb,
              nc.sbuf_tensor(self.un("m2bkt"), [128, 2, 4, 128], BF16) as ktb):
            cnt = 0
            kcnt = 0
            for step in range(16):
                for d in range(2):
                    ci = step if d == 0 else 15 - step
                    t0 = CTX + 128 * ci
                    b4 = cnt % 4
                    cnt += 1
                    S.dma("sp", vb[:, b4, :], self.RV[t0:t0 + 128, :], writes=[("vb", b4)])
                    S.dma("sp", kb[:, b4, :, :], self.QK[:, 2 + d, :, t0:t0 + 128].rearrange("h p n -> p h n"),
                          writes=[("kb", b4)])
                    bk = kcnt % 2
                    kcnt += 1
                    for h in range(4):
                        S.tr(self.psb[bk][:, h * 128:(h + 1) * 128], kb[:, b4, h, :], self.ident[:],
                             reads=[("kb", b4), "ident"], writes=[("psb", bk)])
                    S.cp("act" if bk else "dve", ktb[:, bk, :, :],
                         self.psb[bk][:, 0:512].rearrange("p (h n) -> p h n", n=128),
                         reads=[("psb", bk)], writes=[("ktb", bk)])
                    for h in range(4):
                        kv, kk = self.nextps()
                        S.mm(kv[:, 0:128], ktb[:, bk, h, :], vb[:, b4, h * 128:(h + 1) * 128], True, True,
                             reads=[("ktb", bk), ("vb", b4)], writes=[kk])
                        si = d * 4 + h
                        gc = self.gC[:, si:si + 1]
                        if step == 0:
                            S.ts("dve", S32[:, si, :], kv[:, 0:128], gc, None, ALU.mult, ALU.bypass,
                                 reads=[kk, "gC"], writes=[("S32", si)])
                        else:
                            S.ts("pool", S32[:, si, :], S32[:, si, :], gc, None, ALU.mult, ALU.bypass,
                                 reads=[("S32", si), "gC"], writes=[("S32", si)])
                            S.stt(S32[:, si, :], kv[:, 0:128], gc, S32[:, si, :], ALU.mult, ALU.add,
                                  reads=[kk, ("S32", si), "gC"], writes=[("S32", si)])
            for si in range(8):
                S.dma("sp", self.ex_w(EX_SL + si * 16384, 16384).rearrange("(p n) -> p n", n=128), S32[:, si, :],
                      reads=[("S32", si)], writes=[("EXSL", si)])
            S.flush()

    @staticmethod
    def rev(t, a, b):
        return t[:, slice(b - 1, a - 1 if a > 0 else None, -1)]

    def phase_lruA(self, l):
        nc, S = self.nc, self.S
        hb_ = DEPTH * PL + 24
        half, omh = self.pv[:, hb_:hb_ + 1], self.pv[:, hb_ + 1:hb_ + 2]
        big = lambda n: nc.sbuf_tensor(self.un(n), [128, T], F32)
        with (nc.sbuf_tensor(self.un("laxpc"), [128, CTX + 3], F32) as xpc,
              nc.sbuf_tensor(self.un("laxpl"), [128, LAT + 3], F32) as xpl,
              big("laxcv") as xcv, big("larg") as rg, big("laig") as ig, big("laa") as a_, big("lam") as m_,
              big("lau") as u_, big("lahf") as hf, big("lahb") as hb,
              nc.sbuf_tensor(self.un("laacf"), [128, LAT], F32) as acf,
              nc.sbuf_tensor(self.un("laacb"), [128, LAT], F32) as acb,
              nc.sbuf_tensor(self.un("lazero"), [128, LAT], F32) as zeros,
              nc.sbuf_tensor(self.un("laxb"), [128, T], BF16) as xb,
              nc.sbuf_tensor(self.un("lalw"), [128, 4, 128], BF16) as lw,
              nc.sbuf_tensor(self.un("lahl"), [128, 2, 4], F32) as hl,
              nc.sbuf_tensor(self.un("lah0"), [128, 2], F32) as h0,
              nc.sbuf_tensor(self.un("last"), [128, 2], F32) as stt_):
            S.memset("pool", zeros[:], 0.0, writes=["zeros"])
            slices = [(i * 512, min(512, T - i * 512)) for i in range(5)]
            for c in range(4):
                S.memset("dve", xpc[:, 0:1], 0.0, writes=["xpc"])
                S.memset("dve", xpc[:, CTX + 1:CTX + 3], 0.0, writes=["xpc"])
                S.dma("sp", xpc[:, 1:CTX + 1], self.Z[12 + c, :, 0:CTX], writes=["xpc"])
                S.dma("sp", xpl[:, 1:LAT + 1], self.Z[12 + c, :, CTX:T], writes=["xpl"])
                for r in range(2):
                    S.dma("sp", hl[:, r, :], self.ex_r(r, EX_LX + c * 512, 512).rearrange("(p n) -> p n", n=4),
                          writes=["hl"])
                S.ts("dve", xpl[:, 0:1], hl[:, 0, 3:4], half, None, ALU.mult, ALU.bypass, reads=["hl"], writes=["xpl"])
                S.ts("dve", xpl[:, LAT + 1:LAT + 3], hl[:, 1, 0:2], omh, None, ALU.mult, ALU.bypass, reads=["hl"],
                     writes=["xpl"])
                w = [self.pvl(l, 100 + j * 4 + c) for j in range(4)]
                bcv = self.pvl(l, 116 + c)
                for (src, sk, n, d0) in ((xpc, "xpc", CTX, 0), (xpl, "xpl", LAT, CTX)):
                    dst = xcv[:, d0:d0 + n]
                    S.ts("dve", dst, src[:, 0:n], w[0], bcv, ALU.mult, ALU.add, reads=[sk], writes=["xcv"])
                    for j in range(1, 4):
                        S.stt(dst, src[:, j:j + n], w[j], dst, ALU.mult, ALU.add, reads=[sk, "xcv"], writes=["xcv"])
                S.cp("act", xb[:], xcv[:], reads=["xcv"], writes=["xb"])
                for a in range(2):
                    for d in range(2):
                        S.dma("pool", lw[:, a * 2 + d, :], self.lbd[l, a, d, c], writes=[("lw", a * 2 + d)])
                for d in range(2):
                    for (s0, n) in slices:
                        p1, k1 = self.nextps()
                        S.mm(p1[:, 0:n], lw[:, d, :], xb[:, s0:s0 + n], True, True, reads=[("lw", d), "xb"], writes=[k1])
                        S.act(rg[:, s0:s0 + n], p1[:, 0:n], AF.Sigmoid, bias=self.pvl(l, 120 + d * 4 + c), reads=[k1],
                              writes=["rg"])
                        p2, k2 = self.nextps()
                        S.mm(p2[:, 0:n], lw[:, 2 + d, :], xb[:, s0:s0 + n], True, True, reads=[("lw", 2 + d), "xb"],
                             writes=[k2])
                        S.act(ig[:, s0:s0 + n], p2[:, 0:n], AF.Sigmoid, bias=self.pvl(l, 128 + d * 4 + c), reads=[k2],
                              writes=["ig"])
                    ci = d * 4 + c
                    S.act(a_[:], rg[:], AF.Exp, scale=self.lcp[:, ci:ci + 1], reads=["rg", "lcp"], writes=["a"])
                    S.act(m_[:], rg[:], AF.Exp, scale=self.lcp2[:, ci:ci + 1], reads=["rg", "lcp2"], writes=["m"])
                    S.act(m_[:], m_[:], AF.Sqrt, scale=-1.0, bias=1.0, reads=["m"], writes=["m"])
                    S.tt("dve", u_[:], ig[:], xcv[:], ALU.mult, reads=["ig", "xcv"], writes=["u"])
                    S.tt("pool", u_[:], u_[:], m_[:], ALU.mult, reads=["u", "m"], writes=["u"])
                    if d == 0:
                        S.scan(hf[:, 0:CTX], a_[:, 0:CTX], u_[:, 0:CTX], 0.0, reads=["a", "u"], writes=["hf"])
                        S.ts("dve", h0[:, 0:1], hf[:, CTX - 1:CTX], omh, None, ALU.mult, ALU.bypass, reads=["hf"],
                             writes=["h0f"])
                        S.scan(hf[:, CTX:T], a_[:, CTX:T], u_[:, CTX:T], h0[:, 0:1], reads=["a", "u", "h0f"], writes=["hf"])
                        S.scan(acf[:], a_[:, CTX:T], zeros[:], 1.0, reads=["a", "zeros"], writes=["acf"])
                        S.cp("dve", stt_[:, 0:1], hf[:, T - 1:T], reads=["hf"], writes=["st0"])
                    else:
                        rv = self.rev
                        S.scan(rv(hb, 0, CTX), rv(a_, 0, CTX), rv(u_, 0, CTX), 0.0, reads=["a", "u"], writes=["hb"])
                        S.ts("dve", h0[:, 1:2], hb[:, 0:1], half, None, ALU.mult, ALU.bypass, reads=["hb"], writes=["h0b"])
                        S.scan(rv(hb, CTX, T), rv(a_, CTX, T), rv(u_, CTX, T), h0[:, 1:2], reads=["a", "u", "h0b"],
                               writes=["hb"])
                        S.scan(rv(acb, 0, LAT), rv(a_, CTX, T), zeros[:], 1.0, reads=["a", "zeros"], writes=["acb"])
                        S.cp("dve", stt_[:, 1:2], hb[:, CTX:CTX + 1], reads=["hb"], writes=["st1"])
                S.tt("dve", hf[:], hf[:], hb[:], ALU.add, reads=["hf", "hb"], writes=["hf"])
                S.dma("sp", self.HS[c], hf[:], reads=["hf"], writes=[("HS", c)])
                S.dma("sp", self.ACF[c], acf[:], reads=["acf"], writes=[("ACF", c)])
                S.dma("sp", self.ACB[c], acb[:], reads=["acb"], writes=[("ACB", c)])
                for d in range(2):
                    lo = d * 512 + c * 128
                    S.dma("sp", self.EX2[lo:lo + 128].rearrange("(p o) -> p o", o=1), stt_[:, d:d + 1],
                          reads=["st%d" % d], writes=[("EX2", d, c)])
            S.flush()

    def phase_lruB(self, l):
        nc, S = self.nc, self.S
        hb_ = DEPTH * PL + 24
        half, omh = self.pv[:, hb_:hb_ + 1], self.pv[:, hb_ + 1:hb_ + 2]
        big = lambda n: nc.sbuf_tensor(self.un(n), [128, T], F32)
        with (big("lbhs") as hs, big("lblz") as lz, big("lbsq") as sq, big("lbin") as inn,
              nc.sbuf_tensor(self.un("lbacf"), [128, LAT], F32) as acf,
              nc.sbuf_tensor(self.un("lbacb"), [128, LAT], F32) as acb,
              nc.sbuf_tensor(self.un("lby"), [128, T], BF16) as y,
              nc.sbuf_tensor(self.un("lbdd"), [128, 4], F32) as dd):
            for c in range(4):
                S.dma("sp", hs[:], self.HS[c], writes=["hs"])
                S.dma("sp", acf[:], self.ACF[c], writes=["acf"])
                S.dma("sp", acb[:], self.ACB[c], writes=["acb"])
                S.dma("sp", lz[:], self.Z[16 + c], writes=["lz"])
                lo0 = 0 * EX2_N + 0 * 512 + c * 128
                lo1 = 1 * EX2_N + 1 * 512 + c * 128
                S.dma("sp", dd[:, 0:1], self.EX2G[lo0:lo0 + 128].rearrange("(p o) -> p o", o=1), writes=["dd0"])
                S.dma("sp", dd[:, 1:2], self.EX2G[lo1:lo1 + 128].rearrange("(p o) -> p o", o=1), writes=["dd1"])
                S.ts("dve", dd[:, 2:3], dd[:, 0:1], half, None, ALU.mult, ALU.bypass, reads=["dd0"], writes=["dd2"])
                S.ts("dve", dd[:, 3:4], dd[:, 1:2], omh, None, ALU.mult, ALU.bypass, reads=["dd1"], writes=["dd3"])
                S.stt(hs[:, CTX:T], acf[:], dd[:, 2:3], hs[:, CTX:T], ALU.mult, ALU.add, reads=["acf", "dd2", "hs"],
                      writes=["hs"])
                S.stt(hs[:, CTX:T], acb[:], dd[:, 3:4], hs[:, CTX:T], ALU.mult, ALU.add, reads=["acb", "dd3", "hs"],
                      writes=["hs"])
                S.act(sq[:], lz[:], AF.Square, reads=["lz"], writes=["sq"])
                S.ts("pool", sq[:], sq[:], 0.044715, 1.0, ALU.mult, ALU.add, reads=["sq"], writes=["sq"])
                S.tt("dve", inn[:], sq[:], lz[:], ALU.mult, reads=["sq", "lz"], writes=["inn"])
                S.act(inn[:], inn[:], AF.Sigmoid, scale=1.5957691216057308, reads=["inn"], writes=["inn"])
                S.tt("pool", inn[:], inn[:], lz[:], ALU.mult, reads=["inn", "lz"], writes=["inn"])
                S.tt("dve", y[:], inn[:], hs[:], ALU.mult, reads=["inn", "hs"], writes=["y"])
                S.dma("sp", self.YL[c], y[:], reads=["y"], writes=[("YL", c)])
            S.flush()

    def phase_ret(self, l):
        nc, S = self.nc, self.S
        hb_ = DEPTH * PL + 24
        half, omh = self.pv[:, hb_:hb_ + 1], self.pv[:, hb_ + 1:hb_ + 2]
        big = lambda n: nc.sbuf_tensor(self.un(n), [128, T], F32)
        with (nc.sbuf_tensor(self.un("rtqk"), [128, 4, T], BF16) as qk,
              nc.sbuf_tensor(self.un("rtv"), [128, 18, 128], BF16) as vt,
              nc.sbuf_tensor(self.un("rtkt"), [128, 2, 18, 128], BF16) as kt,
              big("rtacc") as acc, big("rtyc") as yc, big("rtsq") as sq32, big("rtrs") as rs, big("rtrg") as rgz,
              nc.sbuf_tensor(self.un("rtS"), [128, 2, 128], F32) as S32,
              nc.sbuf_tensor(self.un("rtSo"), [128, 2, 128], F32) as So,
              nc.sbuf_tensor(self.un("rtSb"), [128, 2, 2, 128], BF16) as Sb,
              nc.sbuf_tensor(self.un("rtpm"), [128, 4, 128], BF16) as pm,
              nc.sbuf_tensor(self.un("rty"), [128, T], BF16) as y):
            slices = [(i * 512, min(512, T - i * 512)) for i in range(5)]
            order = [[0, 1] + list(range(2, 18)), [1, 0] + list(range(17, 1, -1))]
            masks = [self.maskf, self.maskb]
            pcnt = 0
            for h in range(4):
                S.dma("sp", qk[:], self.QK[h].rearrange("s p n -> p s n"), writes=["qk"])
                S.dma("sp", vt[:], self.RV[:, h * 128:(h + 1) * 128].rearrange("(c p) e -> p c e", p=128), writes=["vt"])
                S.dma("sp", rgz[:], self.Z[8 + h], writes=["rgz"])
                for d in range(2):
                    r = d
                    S.dma("sp", So[:, d, :], self.ex_r(r, EX_SL + (d * 4 + h) * 16384, 16384).rearrange(
                        "(p n) -> p n", n=128), writes=[("So", d)])
                    S.memset("dve", S32[:, d, :], 0.0, writes=[("S32", d)])
                    S.memset("pool", Sb[:, d, 0, :], 0.0, writes=[("Sb", d, 0)])
                tcnt = 0
                for d in range(2):
                    for c0 in range(0, 18, 4):
                        ng = min(4, 18 - c0)
                        bk = tcnt % 2
                        tcnt += 1
                        for j in range(ng):
                            ci = c0 + j
                            S.tr(self.psb[bk][:, j * 128:(j + 1) * 128], qk[:, 2 + d, ci * 128:(ci + 1) * 128], self.ident[:],
                                 reads=["qk", "ident"], writes=[("psb", bk)])
                        S.cp("act" if bk else "dve", kt[:, d, c0:c0 + ng, :],
                             self.psb[bk][:, 0:ng * 128].rearrange("p (h n) -> p h n", n=128),
                             reads=[("psb", bk)], writes=[("kt", d, c0 + j) for j in range(ng)])
                written = set()
                sbi = [0, 0]
                for step in range(18):
                    for d in range(2):
                        ci = order[d][step]
                        cs = slice(ci * 128, (ci + 1) * 128)
                        si = d * 4 + h
                        if step == 2:
                            S.ts("dve", S32[:, d, :], S32[:, d, :], self.bc1[:, si:si + 1], None, ALU.mult, ALU.bypass,
                                 reads=[("S32", d), "bc1"], writes=[("S32", d)])
                            S.stt(S32[:, d, :], So[:, d, :], (half if d == 0 else omh), S32[:, d, :], ALU.mult, ALU.add,
                                  reads=[("So", d), ("S32", d)], writes=[("S32", d)])
                            nb = 1 - sbi[d]
                            S.cp("act", Sb[:, d, nb, :], S32[:, d, :], reads=[("S32", d)], writes=[("Sb", d, nb)])
                            sbi[d] = nb
                        sc, ksc = self.nextps()
                        S.mm(sc[:, 0:128], qk[:, 2 + d, cs], qk[:, d, cs], True, True, reads=["qk"], writes=[ksc])
                        p4 = pcnt % 4
                        pcnt += 1
                        S.tt("dve", pm[:, p4, :], sc[:, 0:128], masks[d][:], ALU.mult, reads=[ksc, "maskf", "maskb"],
                             writes=[("pm", p4)])
                        o, ko = self.nextps()
                        S.mm(o[:, 0:128], vt[:, ci, :], pm[:, p4, :], True, False, reads=["vt", ("pm", p4)], writes=[ko])
                        S.mm(o[:, 0:128], Sb[:, d, sbi[d], :], qk[:, d, cs], False, True, reads=[("Sb", d, sbi[d]), "qk"],
                             writes=[ko])
                        if ci not in written:
                            S.cp("act", acc[:, cs], o[:, 0:128], reads=[ko], writes=[("acc", ci)])
                            written.add(ci)
                        else:
                            S.tt("dve", acc[:, cs], o[:, 0:128], acc[:, cs], ALU.add, reads=[ko, ("acc", ci)],
                                 writes=[("acc", ci)])
                        kv, kkv = self.nextps()
                        S.mm(kv[:, 0:128], kt[:, d, ci, :], vt[:, ci, :], True, True, reads=[("kt", d, ci), "vt"],
                             writes=[kkv])
                        gc = self.gC[:, si:si + 1]
                        S.ts("pool", S32[:, d, :], S32[:, d, :], gc, None, ALU.mult, ALU.bypass,
                             reads=[("S32", d), "gC"], writes=[("S32", d)])
                        S.stt(S32[:, d, :], kv[:, 0:128], gc, S32[:, d, :], ALU.mult, ALU.add,
                              reads=[kkv, ("S32", d), "gC"], writes=[("S32", d)])
                        nb = 1 - sbi[d]
                        S.cp("act", Sb[:, d, nb, :], S32[:, d, :], reads=[("S32", d)], writes=[("Sb", d, nb)])
                        sbi[d] = nb
                acck = [("acc", ci) for ci in range(18)]
                S.act(rgz[:], rgz[:], AF.Silu, reads=["rgz"], writes=["rgz"])
                for (s0, n) in slices:
                    pmn, kmn = self.nextps()
                    S.mm(pmn[:, 0:n], self.ones32[:], acc[:, s0:s0 + n], True, True, reads=acck + ["ones32"], writes=[kmn])
                    S.stt(yc[:, s0:s0 + n], pmn[:, 0:n], -1.0 / 128, acc[:, s0:s0 + n], ALU.mult, ALU.add,
                          reads=[kmn] + acck, writes=[("yc", s0)])
                    S.act(sq32[:, s0:s0 + n], yc[:, s0:s0 + n], AF.Square, reads=[("yc", s0)], writes=[("sq32", s0)])
                    pvr, kvr = self.nextps()
                    S.mm(pvr[:, 0:n], self.ones32[:], sq32[:, s0:s0 + n], True, True, reads=[("sq32", s0)], writes=[kvr])
                    S.act(rs[:, s0:s0 + n], pvr[:, 0:n], AF.Sqrt, scale=1.0 / 128, bias=self.epsb[:, 0:1], reads=[kvr],
                          writes=[("rs", s0)])
                    S.recip(rs[:, s0:s0 + n], rs[:, s0:s0 + n], reads=[("rs", s0)], writes=[("rs", s0)])
                    S.tt("pool", yc[:, s0:s0 + n], yc[:, s0:s0 + n], rs[:, s0:s0 + n], ALU.mult,
                         reads=[("yc", s0), ("rs", s0)], writes=[("yc", s0)])
                    S.stt(y[:, s0:s0 + n], yc[:, s0:s0 + n], self.pvl(l, 96 + h), rgz[:, s0:s0 + n], ALU.mult, ALU.mult,
                          reads=[("yc", s0), "rgz"], writes=[("y", s0)])
                S.dma("sp", self.YR[h], y[:], reads=[("y", s0) for (s0, n) in slices], writes=[("YR", h)])
            S.flush()

    def phase_attn(self, l):
        nc, S = self.nc, self.S
        NK = CTX + 2 * LAT
        with (nc.sbuf_tensor(self.un("atk"), [128, 2, NK], BF16) as kT,
              nc.sbuf_tensor(self.un("atv"), [128, 2, 34, 64], BF16) as vv,
              nc.sbuf_tensor(self.un("atq"), [128, 2, T], BF16) as qa,
              nc.sbuf_tensor(self.un("aty"), [128, 2, T], BF16) as ya,
              nc.sbuf_tensor(self.un("atp"), [128, 4, 512], BF16) as pt,
              nc.sbuf_tensor(self.un("atr"), [128, 2, 512], F32) as rden):
            for g in range(2):
                for hh in range(2):
                    ps_ = slice(hh * 64, (hh + 1) * 64)
                    S.dma("pool", kT[ps_, g, 0:CTX], self.KAC[g * 64:(g + 1) * 64, :], writes=[("kT", g)])
                    for r in range(2):
                        for j in range(4):
                            src = self.EXGP[j].ap()[r * 128 + g * 64:r * 128 + (g + 1) * 64, :]
                            c0 = CTX + r * LAT + j * 512
                            S.dma("pool", kT[ps_, g, c0:c0 + 512], src, writes=[("kT", g)])
                S.dma("pool", vv[:, g, 0:2, :],
                      self.AVC.rearrange("(c p) e -> p c e", p=128)[:, :, g * 64:(g + 1) * 64], writes=[("vv", g)])
                for r in range(2):
                    for j in range(4):
                        src = self.ex_r(r, EX_AV + j * 65536, 65536).rearrange("(c p e) -> p c e", p=128, e=128)
                        c0 = 2 + r * 16 + j * 4
                        S.dma("pool", vv[:, g, c0:c0 + 4, :], src[:, :, g * 64:(g + 1) * 64], writes=[("vv", g)])
            ones64 = self.ones[:, 0:64]
            qtiles = [(0, CTX, [0, 1])] + [(CTX + 512 * i, 512, list(range(34))) for i in range(4)]
            scnt = 0
            qcnt = 0
            for c in range(4):
                g = c // 2
                S.dma("sp", qa[:, c % 2, :], self.QA[c], writes=[("qa", c % 2)])
                for (q0, nq, keys) in qtiles:
                    pq = qcnt % 2
                    qcnt += 1
                    num, knum = self.ps[2 + 2 * pq], ("ps", 2 + 2 * pq)
                    den, kden = self.ps[3 + 2 * pq], ("ps", 3 + 2 * pq)
                    for kc in keys:
                        for hh in range(2):
                            ps_ = slice(hh * 64, (hh + 1) * 64)
                            si = scnt % 2
                            p4 = scnt % 4
                            scnt += 1
                            sp_, ks = self.ps[si], ("ps", si)
                            S.mm(sp_[:, 0:nq], kT[ps_, g, kc * 128:(kc + 1) * 128], qa[ps_, c % 2, q0:q0 + nq], True, True,
                                 reads=[("kT", g), ("qa", c % 2)], writes=[ks])
                            S.act(pt[:, p4, 0:nq], sp_[:, 0:nq], AF.Exp, scale=0.125, reads=[ks], writes=[("pt", p4)])
                            S.mm(num[ps_, 0:nq], vv[:, g, kc, :], pt[:, p4, 0:nq], kc == keys[0], kc == keys[-1],
                                 reads=[("vv", g), ("pt", p4)], writes=[knum])
                            S.mm(den[ps_, 0:nq], ones64, pt[:, p4, 0:nq], kc == keys[0], kc == keys[-1],
                                 reads=["ones", ("pt", p4)], writes=[kden])
                    S.recip(rden[:, pq, 0:nq], den[:, 0:nq], reads=[kden], writes=[("rden", pq)])
                    S.tt("dve", ya[:, c % 2, q0:q0 + nq], num[:, 0:nq], rden[:, pq, 0:nq], ALU.mult,
                         reads=[knum, ("rden", pq)], writes=[("ya", c % 2)])
                S.dma("sp", self.YA[c], ya[:, c % 2, :], reads=[("ya", c % 2)], writes=[("YA", c)])
            S.flush()

    def phase_merge(self, l):
        nc, S = self.nc, self.S
        with (nc.sbuf_tensor(self.un("mgwg"), [128, 8, 3072], BF16) as wg,
              nc.sbuf_tensor(self.un("mgwb"), [128, 12, D], BF16) as wb,
              nc.sbuf_tensor(self.un("mgwo"), [128, 8, D], BF16) as wo,
              nc.sbuf_tensor(self.un("mgx0"), [128, 8, TN], F32) as xt0,
              nc.sbuf_tensor(self.un("mgx1"), [128, 8, TN], F32) as xt1,
              nc.sbuf_tensor(self.un("mgh0"), [128, 8, TN], BF16) as h0,
              nc.sbuf_tensor(self.un("mgh1"), [128, 8, TN], BF16) as h1,
              nc.sbuf_tensor(self.un("mgy0"), [128, 12, TN], BF16) as y0,
              nc.sbuf_tensor(self.un("mgy1"), [128, 12, TN], BF16) as y1,
              nc.sbuf_tensor(self.un("mgsg"), [128, 3, TN], F32) as sg,
              nc.sbuf_tensor(self.un("mgtm"), [128, 2, TN], F32) as tm,
              nc.sbuf_tensor(self.un("mgma"), [128, 2, TN], F32) as ma,
              nc.sbuf_tensor(self.un("mgm"), [128, 8, TN], BF16) as m):
            xts, hs, ys = [xt0, xt1], [h0, h1], [y0, y1]
            src = self.w_in[l].rearrange("(k p) c -> p k c", p=128)
            for pc in range(8):
                S.dma("pool", wg[:, :, pc * 384:(pc + 1) * 384], src[:, :, 3840 + pc * 384:3840 + (pc + 1) * 384],
                      writes=[("wg", pc)])
            for n in range(3):
                S.dma("pool", wb[:, n * 4:(n + 1) * 4, :], self.w_branch[l, n].rearrange("(k p) c -> p k c", p=128),
                      writes=[("wb", n)])
            srco = self.w_out[l].rearrange("(k p) c -> p k c", p=128)
            for kk in range(2):
                S.dma("pool", wo[:, kk * 4:(kk + 1) * 4, :], srco[:, kk * 4:(kk + 1) * 4, :], writes=[("wo", kk)])
            ysrc = [self.YR, self.YL, self.YA]

            def loads(t):
                par = t % 2
                t0 = t * TN
                self.load_x(xts[par], t, par)
                S.dma("sp", hs[par][:], self.H[:, :, t0:t0 + TN].rearrange("k p n -> p k n"), writes=[("h", par)])
                for n in range(3):
                    S.dma("sp", ys[par][:, n * 4:(n + 1) * 4, :], ysrc[n][:, :, t0:t0 + TN].rearrange("k p n -> p k n"),
                          writes=[("y3", par, n)])

            loads(0)
            scnt = 0
            for t in range(NT):
                par = t % 2
                xt, h, y3 = xts[par], hs[par], ys[par]
                c = 1 if t == 0 else 0
                if t + 1 < NT:
                    loads(t + 1)
                for i in range(8):
                    mi = i % 2
                    for n in range(3):
                        pg, kg = self.nextps()
                        wc = n * 8 + i
                        for k in range(8):
                            S.mm(pg[:, 0:TN], wg[:, k, wc * 128:(wc + 1) * 128], h[:, k, :], k == 0, k == 7,
                                 reads=[("wg", wc // 3), ("h", par)], writes=[kg])
                        pu, ku = self.nextps()
                        for kk in range(4):
                            S.mm(pu[:, 0:TN], wb[:, n * 4 + kk, i * 128:(i + 1) * 128], y3[:, n * 4 + kk, :], kk == 0, kk == 3,
                                 reads=[("wb", n), ("y3", par, n)], writes=[ku])
                        s3 = scnt % 3
                        scnt += 1
                        S.act(sg[:, s3, :], pg[:, 0:TN], AF.Sigmoid, reads=[kg], writes=[("sg", s3)])
                        if n == 0:
                            S.tt("dve", ma[:, mi, :], sg[:, s3, :], pu[:, 0:TN], ALU.mult, reads=[("sg", s3), ku],
                                 writes=[("ma", mi)])
                        else:
                            S.tt("dve", tm[:, n - 1, :], sg[:, s3, :], pu[:, 0:TN], ALU.mult, reads=[("sg", s3), ku],
                                 writes=[("tm", n - 1)])
                            if n == 1:
                                S.tt("pool", ma[:, mi, :], ma[:, mi, :], tm[:, 0, :], ALU.add,
                                     reads=[("ma", mi), ("tm", 0)], writes=[("ma", mi)])
                            else:
                                S.tt("pool", m[:, i, :], ma[:, mi, :], tm[:, 1, :], ALU.add,
                                     reads=[("ma", mi), ("tm", 1)], writes=[("m", i)])
                for i in range(8):
                    po, ko = self.nextps()
                    for k in range(8):
                        S.mm(po[:, 0:TN], wo[:, k, i * 128:(i + 1) * 128], m[:, k, :], k == 0, k == 7,
                             reads=[("wo", k // 4), ("m", k)], writes=[ko])
                    S.stt(xt[:, i, :], po[:, 0:TN], self.modG[:, 1, i, c:c + 1], xt[:, i, :], ALU.mult, ALU.add,
                          reads=[ko, ("xt", par, i), ("modG", 1)], writes=[("xt", par, i)])
                self.store_x(xt, t, par)
            S.flush()


def _perm128():
    return np.concatenate([np.arange(32, 64), np.arange(0, 32), np.arange(96, 128), np.arange(64, 96)])


def _perm64():
    return np.concatenate([np.arange(16, 32), np.arange(0, 16), np.arange(48, 64), np.arange(32, 48)])


def _fm(v):
    v = np.asarray(v, np.float32)
    lead = v.shape[:-1]
    n = v.shape[-1] // 128
    v = v.reshape(*lead, n, 128)
    return np.moveaxis(v, -1, 0)


def _rope_tables(half):
    theta = 10000.0
    tl = np.arange(LAT) + half * LAT
    rows = (tl // 64).astype(np.float32)
    cols = (tl % 64).astype(np.float32)
    tabs = np.zeros((4, 128, T), np.float32)
    tabs[0, :, :CTX] = 1.0
    tabs[2, :, :CTX] = 1.0
    f = (theta ** (-np.arange(0, 64, 2, dtype=np.float32) / 64)).astype(np.float32)
    ar = (rows[None, :] * f[:, None]).astype(np.float32)
    ac = (cols[None, :] * f[:, None]).astype(np.float32)
    C = np.concatenate([np.cos(ar), np.cos(ar), np.cos(ac), np.cos(ac)], 0)
    Sn = np.concatenate([-np.sin(ar), np.sin(ar), -np.sin(ac), np.sin(ac)], 0)
    tabs[0, :, CTX:] = C
    tabs[1, :, CTX:] = Sn
    f = (theta ** (-np.arange(0, 32, 2, dtype=np.float32) / 32)).astype(np.float32)
    ar = (rows[None, :] * f[:, None]).astype(np.float32)
    ac = (cols[None, :] * f[:, None]).astype(np.float32)
    C = np.concatenate([np.cos(ar), np.cos(ar), np.cos(ac), np.cos(ac)], 0)
    Sn = np.concatenate([-np.sin(ar), np.sin(ar), -np.sin(ac), np.sin(ac)], 0)
    tabs[2, :, CTX:] = np.concatenate([C, C], 0)
    tabs[3, :, CTX:] = np.concatenate([Sn, Sn], 0)
    return tabs


def _consts():
    cst = np.zeros((6, 128, 128), np.float32)
    cst[0] = np.eye(128)
    cst[1] = 1.0
    cst[2, :64, :64] = 1.0
    cst[2, 64:, 64:] = 1.0
    j = np.arange(128)[:, None]
    i = np.arange(128)[None, :]
    cst[3] = (i >= j)
    cst[4] = (i <= j)
    p = np.arange(TN) % 128
    pos = np.zeros((2, 128, TN), np.float32)
    pos[0] = (p + 1)[None, :]
    pos[1] = (128 - p)[None, :]
    return cst, pos


def _pack_pv(inp, b, half):
    pv = np.zeros((128, NPV), np.float32)
    p64 = _perm64()
    for l in range(DEPTH):
        o = l * PL
        pv[:, o:o + 24] = _fm(inp["norm_g"][l]).reshape(128, 24)
        pv[:, o + 24:o + 96] = _fm(inp["b_mod"][l]).reshape(128, 72)
        pv[:, o + 96:o + 100] = _fm(inp["ret_norm_g"][l])
        pv[:, o + 100:o + 116] = _fm(inp["lru_conv_w"][l]).reshape(128, 16)
        pv[:, o + 116:o + 120] = _fm(inp["lru_conv_b"][l])
        pv[:, o + 120:o + 128] = _fm(inp["lru_b_a"][l]).reshape(128, 8)
        pv[:, o + 128:o + 136] = _fm(inp["lru_b_x"][l]).reshape(128, 8)
        pv[:, o + 136:o + 144] = _fm(inp["lru_lambda"][l]).reshape(128, 8)
        qg = np.asarray(inp["attn_q_norm_g"][l], np.float32)
        kg = np.asarray(inp["attn_k_norm_g"][l], np.float32)
        pv[:, o + 144] = np.tile(qg, 2)
        pv[:, o + 145] = np.tile(qg[p64], 2)
        pv[:, o + 146] = np.tile(kg, 2)
        pv[:, o + 147] = np.tile(kg[p64], 2)
        pv[:, o + 148:o + 156] = np.asarray(inp["ret_decay_logit"][l], np.float32).reshape(1, 8)
    o = DEPTH * PL
    pv[:, o:o + 8] = _fm(inp["final_norm_g"])
    pv[:, o + 8:o + 16] = _fm(inp["c"][b])
    pv[:, o + 16:o + 24] = _fm(inp["c_ctx"])
    pv[:, o + 24] = float(half)
    pv[:, o + 25] = float(1 - half)
    return pv


def _host_inputs(inp):
    f = lambda a: np.ascontiguousarray(np.asarray(a, np.float32))
    cst, pos = _consts()
    w_in = f(inp["w_in"])
    p128, p64 = _perm128(), _perm64()
    idx = []
    for c in range(8):
        idx.append(c * 128 + p128)
    for c in range(5):
        for hh in range(2):
            idx.append(3072 + c * 128 + hh * 64 + p64)
    idx = np.concatenate(idx)
    w_inp = np.ascontiguousarray(w_in[:, :, idx])
    lbd = np.zeros((DEPTH, 2, 2, 4, 128, 128), np.float32)
    for a, name in enumerate(("lru_w_a", "lru_w_x")):
        w = f(inp[name])
        for c in range(4):
            lbd[:, a, :, c, :64, :64] = w[:, :, 2 * c]
            lbd[:, a, :, c, 64:, 64:] = w[:, :, 2 * c + 1]
    shared = {
        "cst": cst, "pos": pos, "w_mod": f(inp["w_mod"][:N_LAYERS]), "ffn_w_in": f(inp["ffn_w_in"][:N_LAYERS]),
        "ffn_w_out": f(inp["ffn_w_out"][:N_LAYERS]), "w_in": w_in[:N_LAYERS], "w_inp": w_inp[:N_LAYERS],
        "lbd": lbd[:N_LAYERS], "w_branch": f(inp["w_branch"][:N_LAYERS]), "w_out": f(inp["w_out"][:N_LAYERS]),
    }
    x = f(inp["x"])
    ctx = f(inp["ctx"])
    ropes = [_rope_tables(0), _rope_tables(1)]
    maps = []
    for core in range(8):
        b, half = core // 2, core % 2
        xt = np.concatenate([ctx[b], x[b, half * LAT:(half + 1) * LAT]], 0)
        xin = np.ascontiguousarray(xt.T.reshape(8, 128, T))
        m = dict(shared)
        m["xin"] = xin
        m["pv"] = _pack_pv(inp, b, half)
        m["rope"] = ropes[half]
        maps.append(m)
    return maps


_NC_CACHE = {}


def _get_nc():
    key = (DEBUG_STOP, N_LAYERS)
    if key not in _NC_CACHE:
        nc = bass.Bass("TRN2", target_bir_lowering=False)
        Builder(nc).build()
        _NC_CACHE[key] = nc
    return _NC_CACHE[key]


def kernel(**inputs):
    maps = _host_inputs(inputs)
    nc = _get_nc()
    if TRACE:
        res = run_bass_kernel_spmd(nc, maps, core_ids=list(range(8)), trace=True)
        print("exec_time_ns", res.exec_time_ns)
    else:
        res = run_bass_kernel_spmd(nc, maps, core_ids=list(range(8)))
    if DEBUG_STOP is not None:
        return [r["dbg"] for r in res.results]
    out = np.zeros((4, 2 * LAT, D), np.float32)
    for core in range(8):
        b, half = core // 2, core % 2
        o = res.results[core]["out"]
        out[b, half * LAT:(half + 1) * LAT] = o.reshape(D, LAT).T
    return out
```

```python
import numpy as np
import concourse.bass as bass
import concourse.mybir as mybir
from concourse.bass_utils import run_bass_kernel_spmd

F32 = mybir.dt.float32
BF16 = mybir.dt.bfloat16
AF = mybir.ActivationFunctionType
ALU = mybir.AluOpType

D = 1024
DEPTH = 4
CTX = 256
LAT = 2048
T = CTX + LAT
TN = 256
NT = T // TN
DFF = 2816
DIN = 6912
EPS = 1e-6
NZ = 38
PL = 156
NPV = DEPTH * PL + 8 + 16 + 2
EX_KA = 0
EX_AV = EX_KA + 128 * LAT
EX_SL = EX_AV + LAT * 128
EX_LX = EX_SL + 8 * 128 * 128
EX_N = EX_LX + 4 * 128 * 4
EX2_N = 2 * 512

DEBUG_STOP = None
N_LAYERS = DEPTH
TRACE = False


class _I:
    __slots__ = ("eng", "fn", "waits", "sig", "idx", "kind", "sem", "target")


class Sched:
    R = 8

    def __init__(self, nc, sems):
        self.nc = nc
        self.sems = sems
        nxt = iter(range(len(sems)))
        self.csem = {e: next(nxt) for e in ("pe", "act", "dve", "pool")}
        self.qsem = {q: [next(nxt) for _ in range(self.R)] for q in ("sp", "pool")}
        self.ccsem = next(nxt)
        self.ncc = 0
        self.qn = {"sp": 0, "pool": 0}
        self.cnt = {e: 0 for e in self.csem}
        self.lists = {e: [] for e in ("pe", "act", "dve", "pool", "sp")}
        self.lastw = {}
        self.rd = {}
        self.seen = {e: {} for e in self.lists}
        self.barrier = []
        self.ninstr = 0

    def add(self, eng, fn, reads=(), writes=(), kind="c"):
        I = _I()
        I.eng, I.fn, I.kind, I.sig, I.idx, I.waits = eng, fn, kind, False, None, []
        deps = []
        for b in reads:
            w = self.lastw.get(b)
            if w is not None:
                deps.append(w)
        for b in writes:
            w = self.lastw.get(b)
            if w is not None:
                deps.append(w)
            r = self.rd.get(b)
            if r:
                deps.extend(r[0].values())
                deps.extend(r[1])
        for J in deps:
            if J.kind in ("dma", "cc"):
                I.waits.append(J)
            elif J.eng == eng and kind == "c":
                if eng == "pe":
                    continue
                I.waits.append(J)
                J.sig = True
            else:
                I.waits.append(J)
                J.sig = True
        if kind == "dma":
            n = self.qn[eng]
            self.qn[eng] += 1
            I.sem = self.qsem[eng][n % self.R]
            I.target = 16 * (n // self.R + 1)
            if n >= self.R:
                I.waits.append((I.sem, 16 * (n // self.R)))
        elif kind == "cc":
            I.sem = self.ccsem
            self.ncc += 1
            I.target = self.ncc
        for b in reads:
            r = self.rd.setdefault(b, ({}, []))
            if kind == "c":
                r[0][eng] = I
            else:
                r[1].append(I)
        for b in writes:
            self.lastw[b] = I
            self.rd[b] = ({}, [])
        self.lists[eng].append(I)
        self.ninstr += 1
        return I

    def dma(self, q, out, in_, reads=(), writes=(), slow=False):
        if slow:
            return self.add(q, lambda e: e.dma_start(out=out, in_=in_, allow_slow_non_contiguous=True), reads, writes,
                            kind="dma")
        return self.add(q, lambda e: e.dma_start(out=out, in_=in_), reads, writes, kind="dma")

    def mm(self, out, lhsT, rhs, start, stop, reads=(), writes=()):
        return self.add("pe", lambda e: e.matmul(out, lhsT=lhsT, rhs=rhs, start=start, stop=stop), reads, writes)

    def tr(self, out, in_, ident, reads=(), writes=()):
        return self.add("pe", lambda e: e.transpose(out, in_, ident), reads, writes)

    def act(self, out, in_, func, reads=(), writes=(), bias=None, scale=None):
        kw = {}
        if bias is not None:
            kw["bias"] = bias
        if scale is not None:
            kw["scale"] = scale
        return self.add("act", lambda e: e.activation(out=out, in_=in_, func=func, **kw), reads, writes)

    def tt(self, eng, out, in0, in1, op, reads=(), writes=()):
        return self.add(eng, lambda e: e.tensor_tensor(out=out, in0=in0, in1=in1, op=op), reads, writes)

    def ts(self, eng, out, in0, s1, s2, op0, op1, reads=(), writes=()):
        return self.add(eng, lambda e: e.tensor_scalar(out=out, in0=in0, scalar1=s1, scalar2=s2, op0=op0, op1=op1),
                        reads, writes)

    def stt(self, out, in0, scalar, in1, op0, op1, reads=(), writes=()):
        return self.add("dve", lambda e: e.scalar_tensor_tensor(out=out, in0=in0, scalar=scalar, in1=in1,
                                                                op0=op0, op1=op1), reads, writes)

    def cp(self, eng, out, in_, reads=(), writes=()):
        if eng == "act":
            return self.add("act", lambda e: e.copy(out=out, in_=in_), reads, writes)
        return self.add(eng, lambda e: e.tensor_copy(out=out, in_=in_), reads, writes)

    def recip(self, out, in_, reads=(), writes=()):
        return self.add("dve", lambda e: e.reciprocal(out=out, in_=in_), reads, writes)

    def memset(self, eng, ap, val, reads=(), writes=()):
        return self.add(eng, lambda e: e.memset(ap, val), reads, writes)

    def scan(self, out, d0, d1, init, reads=(), writes=()):
        return self.add("dve", lambda e: e.tensor_tensor_scan(out=out, data0=d0, data1=d1, initial=init,
                                                              op0=ALU.mult, op1=ALU.add), reads, writes)

    def cc(self, ins_ap, outs_ap, reads=(), writes=()):
        groups = [[0, 1], [2, 3], [4, 5], [6, 7]]
        return self.add("pool", lambda e: e.collective_compute("AllGather", ALU.bypass, replica_groups=groups,
                                                               ins=[ins_ap], outs=[outs_ap]),
                        reads, writes, kind="cc")

    def flush(self):
        nc = self.nc
        for e in self.csem:
            last = None
            for I in self.lists[e]:
                if I.kind == "c":
                    last = I
            if last is not None:
                last.sig = True
            for I in self.lists[e]:
                if I.kind == "c" and I.sig:
                    self.cnt[e] += 1
                    I.idx = self.cnt[e]
        sems = self.sems

        def emit(eh, eng):
            seen = self.seen[eng]

            def wait(si, val):
                if seen.get(si, 0) < val:
                    eh.wait_ge(sems[si], val)
                    seen[si] = val

            for (si, val) in self.barrier:
                wait(si, val)
            for I in self.lists[eng]:
                for w in I.waits:
                    if isinstance(w, tuple):
                        wait(w[0], w[1])
                    elif w.kind in ("dma", "cc"):
                        wait(w.sem, w.target)
                    else:
                        wait(self.csem[w.eng], w.idx)
                ins = I.fn(eh)
                if I.kind == "dma":
                    ins.then_inc(sems[I.sem], 16)
                elif I.kind == "cc":
                    ins.then_inc(sems[I.sem])
                elif I.sig:
                    ins.then_inc(sems[self.csem[eng]], 1)

        with nc.Block() as block:
            @block.tensor
            def _(eh):
                emit(eh, "pe")

            @block.scalar
            def _(eh):
                emit(eh, "act")

            @block.vector
            def _(eh):
                emit(eh, "dve")

            @block.gpsimd
            def _(eh):
                emit(eh, "pool")

            @block.sync
            def _(eh):
                emit(eh, "sp")

        bar = [(self.csem[e], self.cnt[e]) for e in self.csem if self.cnt[e] > 0]
        for q in ("sp", "pool"):
            n = self.qn[q]
            for i in range(self.R):
                if n > i:
                    bar.append((self.qsem[q][i], 16 * ((n - 1 - i) // self.R + 1)))
        if self.ncc:
            bar.append((self.ccsem, self.ncc))
        self.barrier = bar
        self.lists = {e: [] for e in self.lists}
        self.lastw = {}
        self.rd = {}

    def final_wait(self):
        nc = self.nc
        sems = self.sems
        with nc.Block() as block:
            @block.sync
            def _(eh):
                for (si, val) in self.barrier:
                    eh.wait_ge(sems[si], val)

            @block.gpsimd
            def _(eh):
                for (si, val) in self.barrier:
                    eh.wait_ge(sems[si], val)


class Builder:
    def __init__(self, nc):
        self.nc = nc
        self.pscur = 0

    def declare(self):
        nc = self.nc
        di = lambda n, s: nc.dram_tensor(n, s, F32, kind="ExternalInput").ap()
        self.xin = di("xin", [8, 128, T])
        self.pvin = di("pv", [128, NPV])
        self.cst = di("cst", [6, 128, 128])
        self.posin = di("pos", [2, 128, TN])
        self.rope = di("rope", [4, 128, T])
        self.w_mod = di("w_mod", [N_LAYERS, D, 9 * D])
        self.ffn_w_in = di("ffn_w_in", [N_LAYERS, 2, D, 2 * DFF])
        self.ffn_w_out = di("ffn_w_out", [N_LAYERS, 2, DFF, D])
        self.w_in = di("w_in", [N_LAYERS, D, DIN])
        self.w_inp = di("w_inp", [N_LAYERS, D, 13 * 128])
        self.lbd = di("lbd", [N_LAYERS, 2, 2, 4, 128, 128])
        self.w_branch = di("w_branch", [N_LAYERS, 3, 512, D])
        self.w_out = di("w_out", [N_LAYERS, D, D])
        self.out = nc.dram_tensor("out", [8, 128, LAT], F32, kind="ExternalOutput").ap()
        if DEBUG_STOP is not None:
            self.dbg = nc.dram_tensor("dbg", [8, 128, T], F32, kind="ExternalOutput").ap()
        dt = lambda n, s, d=F32: nc.dram_tensor(n, s, d)
        self.X = dt("X", [8, 128, T]).ap()
        self.H = dt("H", [8, 128, T], BF16).ap()
        self.Z = dt("Z", [NZ, 128, T]).ap()
        self.RV = dt("RV", [T, 512], BF16).ap()
        self.AVC = dt("AVC", [CTX, 128]).ap()
        self.KAC = dt("KAC", [128, CTX]).ap()
        self.QA = dt("QA", [4, 128, T], BF16).ap()
        self.QK = dt("QK", [4, 4, 128, T], BF16).ap()
        self.EXP = [dt("EXP%d" % j, [128, 512]) for j in range(10)] + [dt("EXPL", [4, 512])]
        self.EXGP = [dt("EXGP%d" % j, [256, 512]) for j in range(10)] + [dt("EXGPL", [8, 512])]
        self.EX2t = dt("EX2", [EX2_N // 512, 512])
        self.EX2Gt = dt("EX2G", [2 * EX2_N // 512, 512])
        self.EX2 = self.EX2t.ap().rearrange("a b -> (a b)")
        self.EX2G = self.EX2Gt.ap().rearrange("a b -> (a b)")
        self.HS = dt("HS", [4, 128, T]).ap()
        self.ACF = dt("ACF", [4, 128, LAT]).ap()
        self.ACB = dt("ACB", [4, 128, LAT]).ap()
        self.YR = dt("YR", [4, 128, T], BF16).ap()
        self.YL = dt("YL", [4, 128, T], BF16).ap()
        self.YA = dt("YA", [4, 128, T], BF16).ap()

    def un(self, name):
        self.uid = getattr(self, "uid", 0) + 1
        return "%s_%d" % (name, self.uid)

    def nextps(self, n=6):
        i = self.pscur % n
        self.pscur += 1
        return self.ps[i], ("ps", i)

    def pvl(self, l, off, n=1):
        return self.pv[:, l * PL + off:l * PL + off + n]

    def phase_init(self):
        nc, S = self.nc, self.S
        with nc.sbuf_tensor(self.un("ini_c"), [128, 5, 128], F32) as cf, nc.sbuf_tensor(self.un("ini_s"), [128, 16], F32) as sv:
            S.dma("sp", self.pv[:], self.pvin, writes=["pv"])
            S.dma("sp", cf[:], self.cst[0:5].rearrange("c p n -> p c n"), writes=["cf"])
            S.dma("sp", self.pos[:], self.posin.rearrange("c p n -> p c n"), writes=["pos"])
            S.cp("dve", self.ident[:], cf[:, 0, :], reads=["cf"], writes=["ident"])
            S.cp("dve", self.ones[:], cf[:, 1, :], reads=["cf"], writes=["ones"])
            S.cp("dve", self.bones[:], cf[:, 2, :], reads=["cf"], writes=["bones"])
            S.cp("dve", self.maskf[:], cf[:, 3, :], reads=["cf"], writes=["maskf"])
            S.cp("dve", self.maskb[:], cf[:, 4, :], reads=["cf"], writes=["maskb"])
            S.cp("dve", self.ones32[:], cf[:, 1, :], reads=["cf"], writes=["ones32"])
            base = DEPTH * PL + 8
            S.act(sv[:], self.pv[:, base:base + 16], AF.Silu, reads=["pv"], writes=["sv"])
            S.cp("dve", self.sb[:], sv[:].rearrange("p (c k) -> p k c", c=2), reads=["sv"], writes=["sb"])
            for k in range(8):
                S.dma("sp", self.X[k], self.xin[k], writes=[("X", k)])
            S.flush()

    def phase_mod(self, l):
        nc, S = self.nc, self.S
        with (nc.sbuf_tensor(self.un("wm0"), [128, 8, 1024], BF16) as wm0,
              nc.sbuf_tensor(self.un("wm1"), [128, 8, 1024], BF16) as wm1,
              nc.sbuf_tensor(self.un("mraw"), [128, 9, 8, 2], F32) as mraw,
              nc.sbuf_tensor(self.un("lg"), [128, 8], F32) as lg):
            wms = [wm0, wm1]
            src = self.w_mod[l].rearrange("(k p) c -> p k c", p=128)
            for b in range(9):
                wm = wms[b % 2]
                for kk in range(0, 8, 4):
                    S.dma("pool", wm[:, kk:kk + 4, :], src[:, kk:kk + 4, b * 1024:(b + 1) * 1024],
                          writes=[("wm", b % 2, kk)])
                pm, km = self.nextps()
                for i in range(8):
                    for k in range(8):
                        S.mm(pm[:, i * 2:(i + 1) * 2], wm[:, k, i * 128:(i + 1) * 128], self.sb[:, k, :],
                             k == 0, k == 7, reads=[("wm", b % 2, (k // 4) * 4), "sb"], writes=[km])
                bm = self.pvl(l, 24 + b * 8, 8)
                S.tt("dve", mraw[:, b, :, :], pm[:, 0:16].rearrange("p (i c) -> p i c", c=2),
                     bm.unsqueeze(2).broadcast_to([128, 8, 2]), ALU.add, reads=[km, "pv"], writes=[("mraw", b)])
            for s in range(3):
                g = self.pvl(l, s * 8, 8).unsqueeze(2).broadcast_to([128, 8, 2])
                S.stt(self.modA[:, s, :, :], mraw[:, 3 * s + 1, :, :], 1.0, g, ALU.add, ALU.mult,
                      reads=[("mraw", 3 * s + 1), "pv"], writes=[("modA", s)])
                S.cp("dve", self.modB[:, s, :, :], mraw[:, 3 * s, :, :], reads=[("mraw", 3 * s)], writes=[("modB", s)])
                S.ts("dve", self.modG[:, s, :, :], mraw[:, 3 * s + 2, :, :], 0.5 if s != 1 else 1.0, None,
                     ALU.mult, ALU.bypass, reads=[("mraw", 3 * s + 2)], writes=[("modG", s)])
            S.act(lg[:], self.pvl(l, 148, 8), AF.Exp, scale=-1.0, reads=["pv"], writes=["lg"])
            S.act(lg[:], lg[:], AF.Ln, bias=1.0, reads=["lg"], writes=["lg"])
            S.ts("dve", self.lgam[:], lg[:], -1.0, None, ALU.mult, ALU.bypass, reads=["lg"], writes=["lgam"])
            S.ts("dve", self.nlgam[:], lg[:], 1.0, None, ALU.mult, ALU.bypass, reads=["lg"], writes=["nlgam"])
            S.act(self.gC[:], self.lgam[:], AF.Exp, scale=128.0, reads=["lgam"], writes=["gC"])
            hb = DEPTH * PL + 24
            S.ts("dve", lg[:, 0:4], self.lgam[:, 0:4], self.pv[:, hb:hb + 1], 2048.0, ALU.mult, ALU.mult,
                 reads=["lgam", "pv"], writes=["lg"])
            S.ts("dve", lg[:, 4:8], self.lgam[:, 4:8], self.pv[:, hb + 1:hb + 2], 2048.0, ALU.mult, ALU.mult,
                 reads=["lgam", "pv"], writes=["lg"])
            S.act(self.bc1[:], lg[:], AF.Exp, reads=["lg"], writes=["bc1"])
            S.act(self.lcp[:], self.pvl(l, 136, 8), AF.Exp, scale=-1.0, reads=["pv"], writes=["lcp"])
            S.act(self.lcp[:], self.lcp[:], AF.Ln, bias=1.0, reads=["lcp"], writes=["lcp"])
            S.ts("dve", self.lcp2[:], self.lcp[:], -16.0, None, ALU.mult, ALU.bypass, reads=["lcp"], writes=["lcp2"])
            S.ts("dve", self.lcp[:], self.lcp[:], -8.0, None, ALU.mult, ALU.bypass, reads=["lcp"], writes=["lcp"])
            S.flush()

    def load_x(self, xt, t, par):
        S = self.S
        S.dma("sp", xt[:], self.X[:, :, t * TN:(t + 1) * TN].rearrange("k p n -> p k n"),
              reads=[("X", t)], writes=[("xt", par, i) for i in range(8)])

    def store_x(self, xt, t, par, dst=None):
        S = self.S
        dst = self.X if dst is None else dst
        S.dma("sp", dst[:, :, t * TN:(t + 1) * TN].rearrange("k p n -> p k n"), xt[:],
              reads=[("xt", par, i) for i in range(8)], writes=[("X", t)])

    def norm_mod(self, xt, par, sq, rs, tmp, h, sub, c):
        S = self.S
        xk = [("xt", par, i) for i in range(8)]
        S.act(sq[:], xt[:], AF.Square, reads=xk, writes=["sq"])
        pn, kn = self.nextps()
        for k in range(8):
            S.mm(pn[:, 0:TN], self.ones[:], sq[:, k, :], k == 0, k == 7, reads=["sq", "ones"], writes=[kn])
        S.act(rs[:], pn[:, 0:TN], AF.Sqrt, scale=1.0 / D, bias=self.epsb[:, 0:1], reads=[kn], writes=["rs"])
        S.recip(rs[:], rs[:], reads=["rs"], writes=["rs"])
        S.tt("dve", tmp[:], xt[:], rs[:].unsqueeze(1).broadcast_to([128, 8, TN]), ALU.mult,
             reads=xk + ["rs"], writes=["tmp"])
        for k in range(8):
            S.act(h[:, k, :], tmp[:, k, :], AF.Identity, scale=self.modA[:, sub, k, c:c + 1],
                  bias=self.modB[:, sub, k, c:c + 1], reads=["tmp", ("modA", sub), ("modB", sub)],
                  writes=[("h", par, k)])

    def phase_ffn(self, l, which, sub):
        nc, S = self.nc, self.S
        with (nc.sbuf_tensor(self.un("ffw1"), [128, 8, 2 * DFF], BF16) as w1,
              nc.sbuf_tensor(self.un("ffw2"), [128, 22, D], BF16) as w2,
              nc.sbuf_tensor(self.un("fxt0"), [128, 8, TN], F32) as xt0,
              nc.sbuf_tensor(self.un("fxt1"), [128, 8, TN], F32) as xt1,
              nc.sbuf_tensor(self.un("fsq"), [128, 8, TN], BF16) as sq,
              nc.sbuf_tensor(self.un("fh0"), [128, 8, TN], BF16) as h0,
              nc.sbuf_tensor(self.un("fh1"), [128, 8, TN], BF16) as h1,
              nc.sbuf_tensor(self.un("fg"), [128, 22, TN], BF16) as g,
              nc.sbuf_tensor(self.un("fsl0"), [128, TN], F32) as sl0,
              nc.sbuf_tensor(self.un("fsl1"), [128, TN], F32) as sl1,
              nc.sbuf_tensor(self.un("frs"), [128, TN], F32) as rs,
              nc.sbuf_tensor(self.un("ftmp"), [128, 8, TN], F32) as tmp):
            xts, hs, sls = [xt0, xt1], [h0, h1], [sl0, sl1]
            src1 = self.ffn_w_in[l, which].rearrange("(k p) c -> p k c", p=128)
            for cb in range(11):
                S.dma("pool", w1[:, :, cb * 512:(cb + 1) * 512], src1[:, :, cb * 512:(cb + 1) * 512],
                      writes=[("w1", cb)])
            src2 = self.ffn_w_out[l, which].rearrange("(j p) c -> p j c", p=128)
            for jb in range(11):
                S.dma("pool", w2[:, 2 * jb:2 * jb + 2, :], src2[:, 2 * jb:2 * jb + 2, :], writes=[("w2", jb)])
            self.load_x(xts[0], 0, 0)
            self.norm_mod(xts[0], 0, sq, rs, tmp, hs[0], sub, 1)
            for t in range(NT):
                par = t % 2
                xt, h = xts[par], hs[par]
                c = 1 if t == 0 else 0
                if t + 1 < NT:
                    self.load_x(xts[1 - par], t + 1, 1 - par)
                for j in range(22):
                    pa, ka = self.nextps()
                    pb, kb = self.nextps()
                    ca, cb_ = j * 128, DFF + j * 128
                    for k in range(8):
                        S.mm(pa[:, 0:TN], w1[:, k, ca:ca + 128], h[:, k, :], k == 0, k == 7,
                             reads=[("w1", ca // 512), ("h", par, k)], writes=[ka])
                    for k in range(8):
                        S.mm(pb[:, 0:TN], w1[:, k, cb_:cb_ + 128], h[:, k, :], k == 0, k == 7,
                             reads=[("w1", cb_ // 512), ("h", par, k)], writes=[kb])
                    sl = sls[j % 2]
                    S.act(sl[:], pa[:, 0:TN], AF.Silu, reads=[ka], writes=[("sl", j % 2)])
                    S.tt("dve", g[:, j, :], sl[:], pb[:, 0:TN], ALU.mult, reads=[("sl", j % 2), kb], writes=[("g", j)])
                if t + 1 < NT:
                    self.norm_mod(xts[1 - par], 1 - par, sq, rs, tmp, hs[1 - par], sub, 0)
                for i in range(8):
                    po, ko = self.nextps()
                    for j in range(22):
                        S.mm(po[:, 0:TN], w2[:, j, i * 128:(i + 1) * 128], g[:, j, :], j == 0, j == 21,
                             reads=[("w2", j // 2), ("g", j)], writes=[ko])
                    S.stt(xt[:, i, :], po[:, 0:TN], self.modG[:, sub, i, c:c + 1], xt[:, i, :], ALU.mult, ALU.add,
                          reads=[ko, ("xt", par, i), ("modG", sub)], writes=[("xt", par, i)])
                self.store_x(xt, t, par)
            S.flush()

    def phase_final(self):
        nc, S = self.nc, self.S
        with (nc.sbuf_tensor(self.un("nxt0"), [128, 8, TN], F32) as xt0,
              nc.sbuf_tensor(self.un("nxt1"), [128, 8, TN], F32) as xt1,
              nc.sbuf_tensor(self.un("nsq"), [128, 8, TN], BF16) as sq,
              nc.sbuf_tensor(self.un("nrs"), [128, TN], F32) as rs,
              nc.sbuf_tensor(self.un("no0"), [128, 8, TN], F32) as o0,
              nc.sbuf_tensor(self.un("no1"), [128, 8, TN], F32) as o1):
            xts, os_ = [xt0, xt1], [o0, o1]
            fb = DEPTH * PL
            for t in range(1, NT):
                par = t % 2
                xt, o = xts[par], os_[par]
                self.load_x(xt, t, par)
                xk = [("xt", par, i) for i in range(8)]
                S.act(sq[:], xt[:], AF.Square, reads=xk, writes=["sq"])
                pn, kn = self.nextps()
                for k in range(8):
                    S.mm(pn[:, 0:TN], self.ones[:], sq[:, k, :], k == 0, k == 7, reads=["sq"], writes=[kn])
                S.act(rs[:], pn[:, 0:TN], AF.Sqrt, scale=1.0 / D, bias=self.epsb[:, 0:1], reads=[kn], writes=["rs"])
                S.recip(rs[:], rs[:], reads=["rs"], writes=["rs"])
                for k in range(8):
                    S.stt(o[:, k, :], xt[:, k, :], self.pv[:, fb + k:fb + k + 1], rs[:], ALU.mult, ALU.mult,
                          reads=xk + ["rs"], writes=[("o", par)])
                S.dma("sp", self.out[:, :, (t - 1) * TN:t * TN].rearrange("k p n -> p k n"), o[:],
                      reads=[("o", par)], writes=[("out", t)])
            S.flush()

    def phase_dbg(self):
        S = self.S
        for k in range(8):
            S.dma("sp", self.dbg[k], self.X[k], reads=[("X", k)], writes=[("dbg", k)])
        S.flush()

    def build(self):
        nc = self.nc
        self.declare()
        sem_names = ["s%d" % i for i in range(4 + 16 + 1)]
        from contextlib import ExitStack
        with ExitStack() as st:
            sems = [st.enter_context(nc.semaphore(n)) for n in sem_names]
            self.S = Sched(nc, sems)
            sb = lambda n, s, d=F32: st.enter_context(nc.sbuf_tensor(self.un("sb_") + n, s, d))
            self.pv = sb("pv", [128, NPV])
            self.ident = sb("ident", [128, 128], BF16)
            self.ones = sb("ones", [128, 128], BF16)
            self.bones = sb("bones", [128, 128], BF16)
            self.maskf = sb("maskf", [128, 128], BF16)
            self.maskb = sb("maskb", [128, 128], BF16)
            self.ones32 = sb("ones32", [128, 128])
            self.pos = sb("pos", [128, 2, TN])
            self.sb = sb("sb", [128, 8, 2], BF16)
            self.modA = sb("modA", [128, 3, 8, 2])
            self.modB = sb("modB", [128, 3, 8, 2])
            self.modG = sb("modG", [128, 3, 8, 2])
            self.lgam = sb("lgam", [128, 8])
            self.nlgam = sb("nlgam", [128, 8])
            self.gC = sb("gC", [128, 8])
            self.bc1 = sb("bc1", [128, 8])
            self.lcp = sb("lcp", [128, 8])
            self.lcp2 = sb("lcp2", [128, 8])
            self.epsb = sb("epsb", [128, 1])
            self.lnks = sb("lnks", [128, 1])
            self.ps = [st.enter_context(nc.psum_tensor("ps%d" % i, [128, 512], F32)) for i in range(6)]
            self.psb = [st.enter_context(nc.psum_tensor("psb%d" % i, [128, 1024], BF16)) for i in range(2)]
            self.S.memset("dve", self.epsb[:], EPS, writes=["epsb"])
            self.S.memset("dve", self.lnks[:], -0.5 * float(np.log(128.0)), writes=["lnks"])
            self.phase_init()
            stop = False
            for l in range(N_LAYERS):
                for name, fn in (("mod", lambda: self.phase_mod(l)),
                                 ("ffn1", lambda: self.phase_ffn(l, 0, 0)),
                                 ("mix", lambda: self.phase_mix(l)),
                                 ("ffn2", lambda: self.phase_ffn(l, 1, 2))):
                    r = fn()
                    if r or DEBUG_STOP == (l, name):
                        stop = True
                        break
                if stop:
                    break
            if DEBUG_STOP is not None:
                self.phase_dbg()
            else:
                self.phase_final()
            self.S.final_wait()


    def ex_w(self, off, cnt_):
        j, o = off // 65536, off % 65536
        assert o + cnt_ <= 65536
        return self.EXP[j].ap().rearrange("a b -> (a b)")[o:o + cnt_]

    def ex_r(self, slot, off, cnt_):
        j, o = off // 65536, off % 65536
        assert o + cnt_ <= 65536
        psz = 65536 if j < 10 else 2048
        lo = slot * psz + o
        return self.EXGP[j].ap().rearrange("a b -> (a b)")[lo:lo + cnt_]

    def phase_mix(self, l):
        def cc1():
            for j in range(11):
                self.S.cc(self.EXP[j].ap().opt(), self.EXGP[j].ap().opt())
            self.S.flush()

        def cc2():
            self.S.cc(self.EX2t.ap().opt(), self.EX2Gt.ap().opt())
            self.S.flush()

        for name, fn in (("m1", lambda: self.phase_m1(l)), ("m2", lambda: self.phase_m2(l)),
                         ("m2b", lambda: self.phase_m2b(l)), ("cc1", cc1), ("lruA", lambda: self.phase_lruA(l)),
                         ("cc2", cc2), ("lruB", lambda: self.phase_lruB(l)), ("ret", lambda: self.phase_ret(l)),
                         ("attn", lambda: self.phase_attn(l)), ("merge", lambda: self.phase_merge(l))):
            fn()
            if DEBUG_STOP == (l, name):
                return True
        return False

    ZSRC = list(range(0, 8)) + list(range(12, 29)) + list(range(30, 43))

    def phase_m1(self, l):
        nc, S = self.nc, self.S
        NWC = 43
        with (nc.sbuf_tensor(self.un("m1w"), [128, 8, NWC * 128], BF16) as wz,
              nc.sbuf_tensor(self.un("m1x0"), [128, 8, TN], F32) as xt0,
              nc.sbuf_tensor(self.un("m1x1"), [128, 8, TN], F32) as xt1,
              nc.sbuf_tensor(self.un("m1sq"), [128, 8, TN], BF16) as sq,
              nc.sbuf_tensor(self.un("m1h0"), [128, 8, TN], BF16) as h0,
              nc.sbuf_tensor(self.un("m1h1"), [128, 8, TN], BF16) as h1,
              nc.sbuf_tensor(self.un("m1rs"), [128, TN], F32) as rs,
              nc.sbuf_tensor(self.un("m1tmp"), [128, 8, TN], F32) as tmp,
              nc.sbuf_tensor(self.un("m1z0"), [128, 2, TN], F32) as zs0,
              nc.sbuf_tensor(self.un("m1z1"), [128, 2, TN], F32) as zs1,
              nc.sbuf_tensor(self.un("m1rv0"), [128, 512], BF16) as rv0,
              nc.sbuf_tensor(self.un("m1rv1"), [128, 512], BF16) as rv1,
              nc.sbuf_tensor(self.un("m1av0"), [128, 128], F32) as av0,
              nc.sbuf_tensor(self.un("m1av1"), [128, 128], F32) as av1):
            xts, hs, zss, rvs, avs = [xt0, xt1], [h0, h1], [zs0, zs1], [rv0, rv1], [av0, av1]
            src = self.w_in[l].rearrange("(k p) c -> p k c", p=128)
            srcp = self.w_inp[l].rearrange("(k p) c -> p k c", p=128)
            for pc in range(10):
                S.dma("pool", wz[:, :, pc * 384:(pc + 1) * 384], src[:, :, pc * 384:(pc + 1) * 384], writes=[("wz", pc)])
            for pc in range(5):
                lo, hi = pc * 384, min((pc + 1) * 384, 13 * 128)
                S.dma("pool", wz[:, :, 3840 + lo:3840 + hi], srcp[:, :, lo:hi], writes=[("wz", 10 + pc)])
            self.load_x(xts[0], 0, 0)
            self.norm_mod(xts[0], 0, sq, rs, tmp, hs[0], 1, 1)
            for t in range(NT):
                par = t % 2
                xt, h = xts[par], hs[par]
                c = 1 if t == 0 else 0
                t0 = t * TN
                if t + 1 < NT:
                    self.load_x(xts[1 - par], t + 1, 1 - par)
                S.dma("sp", self.H[:, :, t0:t0 + TN].rearrange("k p n -> p k n"), h[:], reads=[("h", par, k) for k in range(8)],
                      writes=[("H", t)])
                for zc in range(NZ):
                    wc = self.ZSRC[zc]
                    pz, kz = self.nextps()
                    for k in range(8):
                        S.mm(pz[:, 0:TN], wz[:, k, wc * 128:(wc + 1) * 128], h[:, k, :], k == 0, k == 7,
                             reads=[("wz", wc // 3), ("h", par, k)], writes=[kz])
                    zs = zss[(zc // 2) % 2]
                    zk = ("zs", (zc // 2) % 2, zc % 2)
                    S.cp("act" if zc % 2 == 0 else "dve", zs[:, zc % 2, :], pz[:, 0:TN], reads=[kz], writes=[zk])
                    if zc % 2 == 1:
                        S.dma("sp", self.Z[zc - 1:zc + 1, :, t0:t0 + TN].rearrange("c p n -> p c n"), zs[:],
                              reads=[("zs", (zc // 2) % 2, 0), ("zs", (zc // 2) % 2, 1)], writes=[("Z", zc // 2, t)])
                if t + 1 < NT:
                    self.norm_mod(xts[1 - par], 1 - par, sq, rs, tmp, hs[1 - par], 1, 0)
                for hf in range(2):
                    i2 = (2 * t + hf) % 2
                    prv, krv = self.nextps()
                    for k in range(8):
                        S.mm(prv[:, 0:512], h[:, k, hf * 128:(hf + 1) * 128], wz[:, k, 1024:1536], k == 0, k == 7,
                             reads=[("wz", 2), ("wz", 3), ("h", par, k)], writes=[krv])
                    S.cp("act", rvs[i2][:], prv[:, 0:512], reads=[krv], writes=[("rvs", i2)])
                    S.dma("sp", self.RV[t0 + hf * 128:t0 + (hf + 1) * 128, :], rvs[i2][:], reads=[("rvs", i2)],
                          writes=[("RV", t, hf)])
                    pav, kav = self.nextps()
                    for k in range(8):
                        S.mm(pav[:, 0:128], h[:, k, hf * 128:(hf + 1) * 128], wz[:, k, 3712:3840], k == 0, k == 7,
                             reads=[("wz", 9), ("h", par, k)], writes=[kav])
                    S.cp("dve", avs[i2][:], pav[:, 0:128], reads=[kav], writes=[("avs", i2)])
                    if t == 0:
                        dst = self.AVC[hf * 128:(hf + 1) * 128, :]
                    else:
                        r0 = (t - 1) * TN + hf * 128
                        dst = self.ex_w(EX_AV + r0 * 128, 16384).rearrange("(t e) -> t e", e=128)
                    S.dma("sp", dst, avs[i2][:], reads=[("avs", i2)], writes=[("AV", t, hf)])
            S.flush()

    def phase_m2(self, l):
        nc, S = self.nc, self.S
        LNKS = -0.5 * float(np.log(128.0))
        with (nc.sbuf_tensor(self.un("m2G"), [128, 16, TN], F32) as G,
              nc.sbuf_tensor(self.un("m2rp0"), [128, 4, TN], F32) as rp0,
              nc.sbuf_tensor(self.un("m2rp1"), [128, 4, TN], F32) as rp1,
              nc.sbuf_tensor(self.un("m2z"), [128, 4, TN], F32) as zb,
              nc.sbuf_tensor(self.un("m2zp"), [128, 4, TN], F32) as zpb,
              nc.sbuf_tensor(self.un("m2r1"), [128, 2, TN], F32) as r1b,
              nc.sbuf_tensor(self.un("m2r2"), [128, 2, TN], F32) as r2b,
              nc.sbuf_tensor(self.un("m2qk0"), [128, 4, TN], BF16) as qk0,
              nc.sbuf_tensor(self.un("m2qk1"), [128, 4, TN], BF16) as qk1,
              nc.sbuf_tensor(self.un("m2sq"), [128, TN], BF16) as sq,
              nc.sbuf_tensor(self.un("m2rs"), [128, 2, TN], F32) as rsb,
              nc.sbuf_tensor(self.un("m2qa"), [128, 2, TN], BF16) as qab,
              nc.sbuf_tensor(self.un("m2kf"), [128, TN], F32) as kf32):
            rps, qks = [rp0, rp1], [qk0, qk1]
            for kind in range(2):
                for d in range(2):
                    for h in range(4):
                        gi = (kind * 2 + d) * 4 + h
                        sc = (self.lgam if kind == 0 else self.nlgam)[:, d * 4 + h:d * 4 + h + 1]
                        S.act(G[:, gi, :], self.pos[:, d, :], AF.Exp, scale=sc,
                              bias=(None if kind == 0 else self.lnks[:, 0:1]),
                              reads=["lgam", "nlgam", "pos"], writes=[("G", gi)])
            cnt = 0
            for t in range(NT):
                t0 = t * TN
                rp = rps[t % 2]
                S.dma("sp", rp[:], self.rope[:, :, t0:t0 + TN].rearrange("c p n -> p c n"), writes=[("rp", t % 2)])
                for h in range(4):
                    qk = qks[h % 2]
                    for kind in range(2):
                        zc, zpc = (h, 25 + h) if kind == 0 else (4 + h, 29 + h)
                        b4, b2 = cnt % 4, cnt % 2
                        cnt += 1
                        z, zp, r1, r2 = zb[:, b4, :], zpb[:, b4, :], r1b[:, b2, :], r2b[:, b2, :]
                        S.dma("sp", z, self.Z[zc, :, t0:t0 + TN], writes=[("z", b4)])
                        S.dma("sp", zp, self.Z[zpc, :, t0:t0 + TN], writes=[("zp", b4)])
                        S.tt("dve", r1, z, rp[:, 0, :], ALU.mult, reads=[("z", b4), ("rp", t % 2)], writes=[("r1", b2)])
                        S.tt("pool", r2, zp, rp[:, 1, :], ALU.mult, reads=[("zp", b4), ("rp", t % 2)], writes=[("r2", b2)])
                        S.tt("dve", r1, r1, r2, ALU.add, reads=[("r1", b2), ("r2", b2)], writes=[("r1", b2)])
                        gf = (kind * 2 + 0) * 4 + h
                        gb = (kind * 2 + 1) * 4 + h
                        S.tt("pool", qk[:, 2 * kind, :], r1, G[:, gf, :], ALU.mult, reads=[("r1", b2), ("G", gf)],
                             writes=[("qk", h % 2, 2 * kind)])
                        S.tt("dve", qk[:, 2 * kind + 1, :], r1, G[:, gb, :], ALU.mult, reads=[("r1", b2), ("G", gb)],
                             writes=[("qk", h % 2, 2 * kind + 1)])
                    S.dma("sp", self.QK[h, :, :, t0:t0 + TN].rearrange("s p n -> p s n"), qk[:],
                          reads=[("qk", h % 2, i) for i in range(4)], writes=[("QK", h, t)])
                for c in range(5):
                    zc, zpc = (20 + c, 33 + c) if c < 4 else (24, 37)
                    gcol = 144 if c < 4 else 146
                    b4, b2 = cnt % 4, cnt % 2
                    cnt += 1
                    z, zp, r1, r2, rs = zb[:, b4, :], zpb[:, b4, :], r1b[:, b2, :], r2b[:, b2, :], rsb[:, b2, :]
                    S.dma("sp", z, self.Z[zc, :, t0:t0 + TN], writes=[("z", b4)])
                    S.dma("sp", zp, self.Z[zpc, :, t0:t0 + TN], writes=[("zp", b4)])
                    S.act(sq[:], z, AF.Square, reads=[("z", b4)], writes=["sq"])
                    pn, kn = self.nextps()
                    S.mm(pn[:, 0:TN], self.bones[:], sq[:], True, True, reads=["sq"], writes=[kn])
                    S.act(rs, pn[:, 0:TN], AF.Sqrt, scale=1.0 / 64, bias=self.epsb[:, 0:1], reads=[kn], writes=[("rs", b2)])
                    S.recip(rs, rs, reads=[("rs", b2)], writes=[("rs", b2)])
                    S.stt(r1, z, self.pvl(l, gcol), rs, ALU.mult, ALU.mult, reads=[("z", b4), ("rs", b2)], writes=[("r1", b2)])
                    S.stt(r2, zp, self.pvl(l, gcol + 1), rs, ALU.mult, ALU.mult, reads=[("zp", b4), ("rs", b2)],
                          writes=[("r2", b2)])
                    S.tt("pool", r1, r1, rp[:, 2, :], ALU.mult, reads=[("r1", b2), ("rp", t % 2)], writes=[("r1", b2)])
                    S.tt("pool", r2, r2, rp[:, 3, :], ALU.mult, reads=[("r2", b2), ("rp", t % 2)], writes=[("r2", b2)])
                    if c < 4:
                        S.tt("dve", qab[:, c % 2, :], r1, r2, ALU.add, reads=[("r1", b2), ("r2", b2)], writes=[("qa", c % 2)])
                        S.dma("sp", self.QA[c, :, t0:t0 + TN], qab[:, c % 2, :], reads=[("qa", c % 2)], writes=[("QA", c, t)])
                    else:
                        S.tt("dve", kf32[:], r1, r2, ALU.add, reads=[("r1", b2), ("r2", b2)], writes=["kf32"])
                        if t == 0:
                            dst = self.KAC[:, :]
                        else:
                            dst = self.EXP[(t - 1) // 2].ap()[:, ((t - 1) % 2) * TN:((t - 1) % 2 + 1) * TN]
                        S.dma("sp", dst, kf32[:], reads=["kf32"], writes=[("KA", t)])
            EXLX = self.ex_w(EX_LX, 2048).rearrange("(c p n) -> c p n", p=128, n=4)
            for c in range(4):
                S.dma("sp", EXLX[c, :, 0:2], self.Z[12 + c, :, CTX:CTX + 2], writes=[("LXH", c, 0)], slow=True)
                S.dma("sp", EXLX[c, :, 2:4], self.Z[12 + c, :, T - 2:T], writes=[("LXH", c, 1)], slow=True)
            S.flush()

    def phase_m2b(self, l):
        nc, S = self.nc, self.S
        with (nc.sbuf_tensor(self.un("m2bS"), [128, 8, 128], F32) as S32,
              nc.sbuf_tensor(self.un("m2bv"), [128, 4, 512], BF16) as vb,
              nc.sbuf_tensor(self.un("m2bk"), [128, 4, 4, 128], BF16) as kb,
              nc.sbuf_tensor(self.un("m2bkt"), [128, 2, 4, 128], BF16) as ktb):
            cnt = 0
            kcnt = 0
            for step in range(16):
                for d in range(2):
                    ci = step if d == 0 else 15 - step
                    t0 = CTX + 128 * ci
                    b4 = cnt % 4
                    cnt += 1
                    S.dma("sp", vb[:, b4, :], self.RV[t0:t0 + 128, :], writes=[("vb", b4)])
                    S.dma("sp", kb[:, b4, :, :], self.QK[:, 2 + d, :, t0:t0 + 128].rearrange("h p n -> p h n"),
                          writes=[("kb", b4)])
                    bk = kcnt % 2
                    kcnt += 1
                    for h in range(4):
                        S.tr(self.psb[bk][:, h * 128:(h + 1) * 128], kb[:, b4, h, :], self.ident[:],
                             reads=[("kb", b4), "ident"], writes=[("psb", bk)])
                    S.cp("act" if bk else "dve", ktb[:, bk, :, :],
                         self.psb[bk][:, 0:512].rearrange("p (h n) -> p h n", n=128),
                         reads=[("psb", bk)], writes=[("ktb", bk)])
                    for h in range(4):
                        kv, kk = self.nextps()
                        S.mm(kv[:, 0:128], ktb[:, bk, h, :], vb[:, b4, h * 128:(h + 1) * 128], True, True,
                             reads=[("ktb", bk), ("vb", b4)], writes=[kk])
                        si = d * 4 + h
                        gc = self.gC[:, si:si + 1]
                        if step == 0:
                            S.ts("dve", S32[:, si, :], kv[:, 0:128], gc, None, ALU.mult, ALU.bypass,
                                 reads=[kk, "gC"], writes=[("S32", si)])
                        else:
                            S.ts("pool", S32[:, si, :], S32[:, si, :], gc, None, ALU.mult, ALU.bypass,
                                 reads=[("S32", si), "gC"], writes=[("S32", si)])
                            S.stt(S32[:, si, :], kv[:, 0:128], gc, S32[:, si, :], ALU.mult, ALU.add,
                                  reads=[kk, ("S32", si), "gC"], writes=[("S32", si)])
            for si in range(8):
                S.dma("sp", self.ex_w(EX_SL + si * 16384, 16384).rearrange("(p n) -> p n", n=128), S32[:, si, :],
                      reads=[("S32", si)], writes=[("EXSL", si)])
            S.flush()

    @staticmethod
    def rev(t, a, b):
        return t[:, slice(b - 1, a - 1 if a > 0 else None, -1)]

    def phase_lruA(self, l):
        nc, S = self.nc, self.S
        hb_ = DEPTH * PL + 24
        half, omh = self.pv[:, hb_:hb_ + 1], self.pv[:, hb_ + 1:hb_ + 2]
        big = lambda n: nc.sbuf_tensor(self.un(n), [128, T], F32)
        with (nc.sbuf_tensor(self.un("laxpc"), [128, CTX + 3], F32) as xpc,
              nc.sbuf_tensor(self.un("laxpl"), [128, LAT + 3], F32) as xpl,
              big("laxcv") as xcv, big("larg") as rg, big("laig") as ig, big("laa") as a_, big("lam") as m_,
              big("lau") as u_, big("lahf") as hf, big("lahb") as hb,
              nc.sbuf_tensor(self.un("laacf"), [128, LAT], F32) as acf,
              nc.sbuf_tensor(self.un("laacb"), [128, LAT], F32) as acb,
              nc.sbuf_tensor(self.un("lazero"), [128, LAT], F32) as zeros,
              nc.sbuf_tensor(self.un("laxb"), [128, T], BF16) as xb,
              nc.sbuf_tensor(self.un("lalw"), [128, 4, 128], BF16) as lw,
              nc.sbuf_tensor(self.un("lahl"), [128, 2, 4], F32) as hl,
              nc.sbuf_tensor(self.un("lah0"), [128, 2], F32) as h0,
              nc.sbuf_tensor(self.un("last"), [128, 2], F32) as stt_):
            S.memset("pool", zeros[:], 0.0, writes=["zeros"])
            slices = [(i * 512, min(512, T - i * 512)) for i in range(5)]
            for c in range(4):
                S.memset("dve", xpc[:, 0:1], 0.0, writes=["xpc"])
                S.memset("dve", xpc[:, CTX + 1:CTX + 3], 0.0, writes=["xpc"])
                S.dma("sp", xpc[:, 1:CTX + 1], self.Z[12 + c, :, 0:CTX], writes=["xpc"])
                S.dma("sp", xpl[:, 1:LAT + 1], self.Z[12 + c, :, CTX:T], writes=["xpl"])
                for r in range(2):
                    S.dma("sp", hl[:, r, :], self.ex_r(r, EX_LX + c * 512, 512).rearrange("(p n) -> p n", n=4),
                          writes=["hl"])
                S.ts("dve", xpl[:, 0:1], hl[:, 0, 3:4], half, None, ALU.mult, ALU.bypass, reads=["hl"], writes=["xpl"])
                S.ts("dve", xpl[:, LAT + 1:LAT + 3], hl[:, 1, 0:2], omh, None, ALU.mult, ALU.bypass, reads=["hl"],
                     writes=["xpl"])
                w = [self.pvl(l, 100 + j * 4 + c) for j in range(4)]
                bcv = self.pvl(l, 116 + c)
                for (src, sk, n, d0) in ((xpc, "xpc", CTX, 0), (xpl, "xpl", LAT, CTX)):
                    dst = xcv[:, d0:d0 + n]
                    S.ts("dve", dst, src[:, 0:n], w[0], bcv, ALU.mult, ALU.add, reads=[sk], writes=["xcv"])
                    for j in range(1, 4):
                        S.stt(dst, src[:, j:j + n], w[j], dst, ALU.mult, ALU.add, reads=[sk, "xcv"], writes=["xcv"])
                S.cp("act", xb[:], xcv[:], reads=["xcv"], writes=["xb"])
                for a in range(2):
                    for d in range(2):
                        S.dma("pool", lw[:, a * 2 + d, :], self.lbd[l, a, d, c], writes=[("lw", a * 2 + d)])
                for d in range(2):
                    for (s0, n) in slices:
                        p1, k1 = self.nextps()
                        S.mm(p1[:, 0:n], lw[:, d, :], xb[:, s0:s0 + n], True, True, reads=[("lw", d), "xb"], writes=[k1])
                        S.act(rg[:, s0:s0 + n], p1[:, 0:n], AF.Sigmoid, bias=self.pvl(l, 120 + d * 4 + c), reads=[k1],
                              writes=["rg"])
                        p2, k2 = self.nextps()
                        S.mm(p2[:, 0:n], lw[:, 2 + d, :], xb[:, s0:s0 + n], True, True, reads=[("lw", 2 + d), "xb"],
                             writes=[k2])
                        S.act(ig[:, s0:s0 + n], p2[:, 0:n], AF.Sigmoid, bias=self.pvl(l, 128 + d * 4 + c), reads=[k2],
                              writes=["ig"])
                    ci = d * 4 + c
                    S.act(a_[:], rg[:], AF.Exp, scale=self.lcp[:, ci:ci + 1], reads=["rg", "lcp"], writes=["a"])
                    S.act(m_[:], rg[:], AF.Exp, scale=self.lcp2[:, ci:ci + 1], reads=["rg", "lcp2"], writes=["m"])
                    S.act(m_[:], m_[:], AF.Sqrt, scale=-1.0, bias=1.0, reads=["m"], writes=["m"])
                    S.tt("dve", u_[:], ig[:], xcv[:], ALU.mult, reads=["ig", "xcv"], writes=["u"])
                    S.tt("pool", u_[:], u_[:], m_[:], ALU.mult, reads=["u", "m"], writes=["u"])
                    if d == 0:
                        S.scan(hf[:, 0:CTX], a_[:, 0:CTX], u_[:, 0:CTX], 0.0, reads=["a", "u"], writes=["hf"])
                        S.ts("dve", h0[:, 0:1], hf[:, CTX - 1:CTX], omh, None, ALU.mult, ALU.bypass, reads=["hf"],
                             writes=["h0f"])
                        S.scan(hf[:, CTX:T], a_[:, CTX:T], u_[:, CTX:T], h0[:, 0:1], reads=["a", "u", "h0f"], writes=["hf"])
                        S.scan(acf[:], a_[:, CTX:T], zeros[:], 1.0, reads=["a", "zeros"], writes=["acf"])
                        S.cp("dve", stt_[:, 0:1], hf[:, T - 1:T], reads=["hf"], writes=["st0"])
                    else:
                        rv = self.rev
                        S.scan(rv(hb, 0, CTX), rv(a_, 0, CTX), rv(u_, 0, CTX), 0.0, reads=["a", "u"], writes=["hb"])
                        S.ts("dve", h0[:, 1:2], hb[:, 0:1], half, None, ALU.mult, ALU.bypass, reads=["hb"], writes=["h0b"])
                        S.scan(rv(hb, CTX, T), rv(a_, CTX, T), rv(u_, CTX, T), h0[:, 1:2], reads=["a", "u", "h0b"],
                               writes=["hb"])
                        S.scan(rv(acb, 0, LAT), rv(a_, CTX, T), zeros[:], 1.0, reads=["a", "zeros"], writes=["acb"])
                        S.cp("dve", stt_[:, 1:2], hb[:, CTX:CTX + 1], reads=["hb"], writes=["st1"])
                S.tt("dve", hf[:], hf[:], hb[:], ALU.add, reads=["hf", "hb"], writes=["hf"])
                S.dma("sp", self.HS[c], hf[:], reads=["hf"], writes=[("HS", c)])
                S.dma("sp", self.ACF[c], acf[:], reads=["acf"], writes=[("ACF", c)])
                S.dma("sp", self.ACB[c], acb[:], reads=["acb"], writes=[("ACB", c)])
                for d in range(2):
                    lo = d * 512 + c * 128
                    S.dma("sp", self.EX2[lo:lo + 128].rearrange("(p o) -> p o", o=1), stt_[:, d:d + 1],
                          reads=["st%d" % d], writes=[("EX2", d, c)])
            S.flush()

    def phase_lruB(self, l):
        nc, S = self.nc, self.S
        hb_ = DEPTH * PL + 24
        half, omh = self.pv[:, hb_:hb_ + 1], self.pv[:, hb_ + 1:hb_ + 2]
        big = lambda n: nc.sbuf_tensor(self.un(n), [128, T], F32)
        with (big("lbhs") as hs, big("lblz") as lz, big("lbsq") as sq, big("lbin") as inn,
              nc.sbuf_tensor(self.un("lbacf"), [128, LAT], F32) as acf,
              nc.sbuf_tensor(self.un("lbacb"), [128, LAT], F32) as acb,
              nc.sbuf_tensor(self.un("lby"), [128, T], BF16) as y,
              nc.sbuf_tensor(self.un("lbdd"), [128, 4], F32) as dd):
            for c in range(4):
                S.dma("sp", hs[:], self.HS[c], writes=["hs"])
                S.dma("sp", acf[:], self.ACF[c], writes=["acf"])
                S.dma("sp", acb[:], self.ACB[c], writes=["acb"])
                S.dma("sp", lz[:], self.Z[16 + c], writes=["lz"])
                lo0 = 0 * EX2_N + 0 * 512 + c * 128
                lo1 = 1 * EX2_N + 1 * 512 + c * 128
                S.dma("sp", dd[:, 0:1], self.EX2G[lo0:lo0 + 128].rearrange("(p o) -> p o", o=1), writes=["dd0"])
                S.dma("sp", dd[:, 1:2], self.EX2G[lo1:lo1 + 128].rearrange("(p o) -> p o", o=1), writes=["dd1"])
                S.ts("dve", dd[:, 2:3], dd[:, 0:1], half, None, ALU.mult, ALU.bypass, reads=["dd0"], writes=["dd2"])
                S.ts("dve", dd[:, 3:4], dd[:, 1:2], omh, None, ALU.mult, ALU.bypass, reads=["dd1"], writes=["dd3"])
                S.stt(hs[:, CTX:T], acf[:], dd[:, 2:3], hs[:, CTX:T], ALU.mult, ALU.add, reads=["acf", "dd2", "hs"],
                      writes=["hs"])
                S.stt(hs[:, CTX:T], acb[:], dd[:, 3:4], hs[:, CTX:T], ALU.mult, ALU.add, reads=["acb", "dd3", "hs"],
                      writes=["hs"])
                S.act(sq[:], lz[:], AF.Square, reads=["lz"], writes=["sq"])
                S.ts("pool", sq[:], sq[:], 0.044715, 1.0, ALU.mult, ALU.add, reads=["sq"], writes=["sq"])
                S.tt("dve", inn[:], sq[:], lz[:], ALU.mult, reads=["sq", "lz"], writes=["inn"])
                S.act(inn[:], inn[:], AF.Sigmoid, scale=1.5957691216057308, reads=["inn"], writes=["inn"])
                S.tt("pool", inn[:], inn[:], lz[:], ALU.mult, reads=["inn", "lz"], writes=["inn"])
                S.tt("dve", y[:], inn[:], hs[:], ALU.mult, reads=["inn", "hs"], writes=["y"])
                S.dma("sp", self.YL[c], y[:], reads=["y"], writes=[("YL", c)])
            S.flush()

    def phase_ret(self, l):
        nc, S = self.nc, self.S
        hb_ = DEPTH * PL + 24
        half, omh = self.pv[:, hb_:hb_ + 1], self.pv[:, hb_ + 1:hb_ + 2]
        big = lambda n: nc.sbuf_tensor(self.un(n), [128, T], F32)
        with (nc.sbuf_tensor(self.un("rtqk"), [128, 4, T], BF16) as qk,
              nc.sbuf_tensor(self.un("rtv"), [128, 18, 128], BF16) as vt,
              nc.sbuf_tensor(self.un("rtkt"), [128, 2, 18, 128], BF16) as kt,
              big("rtacc") as acc, big("rtyc") as yc, big("rtsq") as sq32, big("rtrs") as rs, big("rtrg") as rgz,
              nc.sbuf_tensor(self.un("rtS"), [128, 2, 128], F32) as S32,
              nc.sbuf_tensor(self.un("rtSo"), [128, 2, 128], F32) as So,
              nc.sbuf_tensor(self.un("rtSb"), [128, 2, 2, 128], BF16) as Sb,
              nc.sbuf_tensor(self.un("rtpm"), [128, 4, 128], BF16) as pm,
              nc.sbuf_tensor(self.un("rty"), [128, T], BF16) as y):
            slices = [(i * 512, min(512, T - i * 512)) for i in range(5)]
            order = [[0, 1] + list(range(2, 18)), [1, 0] + list(range(17, 1, -1))]
            masks = [self.maskf, self.maskb]
            pcnt = 0
            for h in range(4):
                S.dma("sp", qk[:], self.QK[h].rearrange("s p n -> p s n"), writes=["qk"])
                S.dma("sp", vt[:], self.RV[:, h * 128:(h + 1) * 128].rearrange("(c p) e -> p c e", p=128), writes=["vt"])
                S.dma("sp", rgz[:], self.Z[8 + h], writes=["rgz"])
                for d in range(2):
                    r = d
                    S.dma("sp", So[:, d, :], self.ex_r(r, EX_SL + (d * 4 + h) * 16384, 16384).rearrange(
                        "(p n) -> p n", n=128), writes=[("So", d)])
                    S.memset("dve", S32[:, d, :], 0.0, writes=[("S32", d)])
                    S.memset("pool", Sb[:, d, 0, :], 0.0, writes=[("Sb", d, 0)])
                tcnt = 0
                for d in range(2):
                    for c0 in range(0, 18, 4):
                        ng = min(4, 18 - c0)
                        bk = tcnt % 2
                        tcnt += 1
                        for j in range(ng):
                            ci = c0 + j
                            S.tr(self.psb[bk][:, j * 128:(j + 1) * 128], qk[:, 2 + d, ci * 128:(ci + 1) * 128], self.ident[:],
                                 reads=["qk", "ident"], writes=[("psb", bk)])
                        S.cp("act" if bk else "dve", kt[:, d, c0:c0 + ng, :],
                             self.psb[bk][:, 0:ng * 128].rearrange("p (h n) -> p h n", n=128),
                             reads=[("psb", bk)], writes=[("kt", d, c0 + j) for j in range(ng)])
                written = set()
                sbi = [0, 0]
                for step in range(18):
                    for d in range(2):
                        ci = order[d][step]
                        cs = slice(ci * 128, (ci + 1) * 128)
                        si = d * 4 + h
                        if step == 2:
                            S.ts("dve", S32[:, d, :], S32[:, d, :], self.bc1[:, si:si + 1], None, ALU.mult, ALU.bypass,
                                 reads=[("S32", d), "bc1"], writes=[("S32", d)])
                            S.stt(S32[:, d, :], So[:, d, :], (half if d == 0 else omh), S32[:, d, :], ALU.mult, ALU.add,
                                  reads=[("So", d), ("S32", d)], writes=[("S32", d)])
                            nb = 1 - sbi[d]
                            S.cp("act", Sb[:, d, nb, :], S32[:, d, :], reads=[("S32", d)], writes=[("Sb", d, nb)])
                            sbi[d] = nb
                        sc, ksc = self.nextps()
                        S.mm(sc[:, 0:128], qk[:, 2 + d, cs], qk[:, d, cs], True, True, reads=["qk"], writes=[ksc])
                        p4 = pcnt % 4
                        pcnt += 1
                        S.tt("dve", pm[:, p4, :], sc[:, 0:128], masks[d][:], ALU.mult, reads=[ksc, "maskf", "maskb"],
                             writes=[("pm", p4)])
                        o, ko = self.nextps()
                        S.mm(o[:, 0:128], vt[:, ci, :], pm[:, p4, :], True, False, reads=["vt", ("pm", p4)], writes=[ko])
                        S.mm(o[:, 0:128], Sb[:, d, sbi[d], :], qk[:, d, cs], False, True, reads=[("Sb", d, sbi[d]), "qk"],
                             writes=[ko])
                        if ci not in written:
                            S.cp("act", acc[:, cs], o[:, 0:128], reads=[ko], writes=[("acc", ci)])
                            written.add(ci)
                        else:
                            S.tt("dve", acc[:, cs], o[:, 0:128], acc[:, cs], ALU.add, reads=[ko, ("acc", ci)],
                                 writes=[("acc", ci)])
                        kv, kkv = self.nextps()
                        S.mm(kv[:, 0:128], kt[:, d, ci, :], vt[:, ci, :], True, True, reads=[("kt", d, ci), "vt"],
                             writes=[kkv])
                        gc = self.gC[:, si:si + 1]
                        S.ts("pool", S32[:, d, :], S32[:, d, :], gc, None, ALU.mult, ALU.bypass,
                             reads=[("S32", d), "gC"], writes=[("S32", d)])
                        S.stt(S32[:, d, :], kv[:, 0:128], gc, S32[:, d, :], ALU.mult, ALU.add,
                              reads=[kkv, ("S32", d), "gC"], writes=[("S32", d)])
                        nb = 1 - sbi[d]
                        S.cp("act", Sb[:, d, nb, :], S32[:, d, :], reads=[("S32", d)], writes=[("Sb", d, nb)])
                        sbi[d] = nb
                acck = [("acc", ci) for ci in range(18)]
                S.act(rgz[:], rgz[:], AF.Silu, reads=["rgz"], writes=["rgz"])
                for (s0, n) in slices:
                    pmn, kmn = self.nextps()
                    S.mm(pmn[:, 0:n], self.ones32[:], acc[:, s0:s0 + n], True, True, reads=acck + ["ones32"], writes=[kmn])
                    S.stt(yc[:, s0:s0 + n], pmn[:, 0:n], -1.0 / 128, acc[:, s0:s0 + n], ALU.mult, ALU.add,
                          reads=[kmn] + acck, writes=[("yc", s0)])
                    S.act(sq32[:, s0:s0 + n], yc[:, s0:s0 + n], AF.Square, reads=[("yc", s0)], writes=[("sq32", s0)])
                    pvr, kvr = self.nextps()
                    S.mm(pvr[:, 0:n], self.ones32[:], sq32[:, s0:s0 + n], True, True, reads=[("sq32", s0)], writes=[kvr])
                    S.act(rs[:, s0:s0 + n], pvr[:, 0:n], AF.Sqrt, scale=1.0 / 128, bias=self.epsb[:, 0:1], reads=[kvr],
                          writes=[("rs", s0)])
                    S.recip(rs[:, s0:s0 + n], rs[:, s0:s0 + n], reads=[("rs", s0)], writes=[("rs", s0)])
                    S.tt("pool", yc[:, s0:s0 + n], yc[:, s0:s0 + n], rs[:, s0:s0 + n], ALU.mult,
                         reads=[("yc", s0), ("rs", s0)], writes=[("yc", s0)])
                    S.stt(y[:, s0:s0 + n], yc[:, s0:s0 + n], self.pvl(l, 96 + h), rgz[:, s0:s0 + n], ALU.mult, ALU.mult,
                          reads=[("yc", s0), "rgz"], writes=[("y", s0)])
                S.dma("sp", self.YR[h], y[:], reads=[("y", s0) for (s0, n) in slices], writes=[("YR", h)])
            S.flush()

    def phase_attn(self, l):
        nc, S = self.nc, self.S
        NK = CTX + 2 * LAT
        with (nc.sbuf_tensor(self.un("atk"), [128, 2, NK], BF16) as kT,
              nc.sbuf_tensor(self.un("atv"), [128, 2, 34, 64], BF16) as vv,
              nc.sbuf_tensor(self.un("atq"), [128, 2, T], BF16) as qa,
              nc.sbuf_tensor(self.un("aty"), [128, 2, T], BF16) as ya,
              nc.sbuf_tensor(self.un("atp"), [128, 4, 512], BF16) as pt,
              nc.sbuf_tensor(self.un("atr"), [128, 2, 512], F32) as rden):
            for g in range(2):
                for hh in range(2):
                    ps_ = slice(hh * 64, (hh + 1) * 64)
                    S.dma("pool", kT[ps_, g, 0:CTX], self.KAC[g * 64:(g + 1) * 64, :], writes=[("kT", g)])
                    for r in range(2):
                        for j in range(4):
                            src = self.EXGP[j].ap()[r * 128 + g * 64:r * 128 + (g + 1) * 64, :]
                            c0 = CTX + r * LAT + j * 512
                            S.dma("pool", kT[ps_, g, c0:c0 + 512], src, writes=[("kT", g)])
                S.dma("pool", vv[:, g, 0:2, :],
                      self.AVC.rearrange("(c p) e -> p c e", p=128)[:, :, g * 64:(g + 1) * 64], writes=[("vv", g)])
                for r in range(2):
                    for j in range(4):
                        src = self.ex_r(r, EX_AV + j * 65536, 65536).rearrange("(c p e) -> p c e", p=128, e=128)
                        c0 = 2 + r * 16 + j * 4
                        S.dma("pool", vv[:, g, c0:c0 + 4, :], src[:, :, g * 64:(g + 1) * 64], writes=[("vv", g)])
            ones64 = self.ones[:, 0:64]
            qtiles = [(0, CTX, [0, 1])] + [(CTX + 512 * i, 512, list(range(34))) for i in range(4)]
            its = []
            qcnt = 0
            for c in range(4):
                for qi, (q0, nq, keys) in enumerate(qtiles):
                    pq = qcnt % 2
                    qcnt += 1
                    for kc in keys:
                        for hh in range(2):
                            its.append(dict(c=c, q0=q0, nq=nq, kc=kc, hh=hh, pq=pq, first=(kc == keys[0]),
                                            last=(kc == keys[-1]), qend=(kc == keys[-1] and hh == 1),
                                            cend=(kc == keys[-1] and hh == 1 and qi == len(qtiles) - 1),
                                            cstart=(kc == keys[0] and hh == 0 and qi == 0)))

            def emit_qk(i):
                it = its[i]
                c, g, hh, nq, q0, kc = it["c"], it["c"] // 2, it["hh"], it["nq"], it["q0"], it["kc"]
                if it["cstart"]:
                    S.dma("sp", qa[:, c % 2, :], self.QA[c], writes=[("qa", c % 2)])
                ps_ = slice(hh * 64, (hh + 1) * 64)
                si, p4 = i % 2, i % 4
                sp_, ks = self.ps[si], ("ps", si)
                S.mm(sp_[:, 0:nq], kT[ps_, g, kc * 128:(kc + 1) * 128], qa[ps_, c % 2, q0:q0 + nq], True, True,
                     reads=[("kT", g), ("qa", c % 2)], writes=[ks])
                S.act(pt[:, p4, 0:nq], sp_[:, 0:nq], AF.Exp, scale=0.125, reads=[ks], writes=[("pt", p4)])

            def emit_pv(i):
                it = its[i]
                c, g, hh, nq, q0, kc, pq = it["c"], it["c"] // 2, it["hh"], it["nq"], it["q0"], it["kc"], it["pq"]
                ps_ = slice(hh * 64, (hh + 1) * 64)
                p4 = i % 4
                num, knum = self.ps[2 + 2 * pq], ("ps", 2 + 2 * pq)
                den, kden = self.ps[3 + 2 * pq], ("ps", 3 + 2 * pq)
                S.mm(num[ps_, 0:nq], vv[:, g, kc, :], pt[:, p4, 0:nq], it["first"], it["last"],
                     reads=[("vv", g), ("pt", p4)], writes=[knum])
                S.mm(den[ps_, 0:nq], ones64, pt[:, p4, 0:nq], it["first"], it["last"],
                     reads=["ones", ("pt", p4)], writes=[kden])
                if it["qend"]:
                    S.recip(rden[:, pq, 0:nq], den[:, 0:nq], reads=[kden], writes=[("rden", pq)])
                    S.tt("dve", ya[:, c % 2, q0:q0 + nq], num[:, 0:nq], rden[:, pq, 0:nq], ALU.mult,
                         reads=[knum, ("rden", pq)], writes=[("ya", c % 2)])
                if it["cend"]:
                    S.dma("sp", self.YA[c], ya[:, c % 2, :], reads=[("ya", c % 2)], writes=[("YA", c)])

            emit_qk(0)
            for i in range(len(its)):
                if i + 1 < len(its):
                    emit_qk(i + 1)
                emit_pv(i)
            S.flush()

    def phase_merge(self, l):
        nc, S = self.nc, self.S
        with (nc.sbuf_tensor(self.un("mgwg"), [128, 8, 3072], BF16) as wg,
              nc.sbuf_tensor(self.un("mgwb"), [128, 12, D], BF16) as wb,
              nc.sbuf_tensor(self.un("mgwo"), [128, 8, D], BF16) as wo,
              nc.sbuf_tensor(self.un("mgx0"), [128, 8, TN], F32) as xt0,
              nc.sbuf_tensor(self.un("mgx1"), [128, 8, TN], F32) as xt1,
              nc.sbuf_tensor(self.un("mgh0"), [128, 8, TN], BF16) as h0,
              nc.sbuf_tensor(self.un("mgh1"), [128, 8, TN], BF16) as h1,
              nc.sbuf_tensor(self.un("mgy0"), [128, 12, TN], BF16) as y0,
              nc.sbuf_tensor(self.un("mgy1"), [128, 12, TN], BF16) as y1,
              nc.sbuf_tensor(self.un("mgsg"), [128, 3, TN], F32) as sg,
              nc.sbuf_tensor(self.un("mgtm"), [128, 2, TN], F32) as tm,
              nc.sbuf_tensor(self.un("mgma"), [128, 2, TN], F32) as ma,
              nc.sbuf_tensor(self.un("mgm"), [128, 8, TN], BF16) as m):
            xts, hs, ys = [xt0, xt1], [h0, h1], [y0, y1]
            src = self.w_in[l].rearrange("(k p) c -> p k c", p=128)
            for pc in range(8):
                S.dma("pool", wg[:, :, pc * 384:(pc + 1) * 384], src[:, :, 3840 + pc * 384:3840 + (pc + 1) * 384],
                      writes=[("wg", pc)])
            for n in range(3):
                S.dma("pool", wb[:, n * 4:(n + 1) * 4, :], self.w_branch[l, n].rearrange("(k p) c -> p k c", p=128),
                      writes=[("wb", n)])
            srco = self.w_out[l].rearrange("(k p) c -> p k c", p=128)
            for kk in range(2):
                S.dma("pool", wo[:, kk * 4:(kk + 1) * 4, :], srco[:, kk * 4:(kk + 1) * 4, :], writes=[("wo", kk)])
            ysrc = [self.YR, self.YL, self.YA]

            def loads(t):
                par = t % 2
                t0 = t * TN
                self.load_x(xts[par], t, par)
                S.dma("sp", hs[par][:], self.H[:, :, t0:t0 + TN].rearrange("k p n -> p k n"), writes=[("h", par)])
                for n in range(3):
                    S.dma("sp", ys[par][:, n * 4:(n + 1) * 4, :], ysrc[n][:, :, t0:t0 + TN].rearrange("k p n -> p k n"),
                          writes=[("y3", par, n)])

            loads(0)
            scnt = 0
            for t in range(NT):
                par = t % 2
                xt, h, y3 = xts[par], hs[par], ys[par]
                c = 1 if t == 0 else 0
                if t + 1 < NT:
                    loads(t + 1)
                for i in range(8):
                    mi = i % 2
                    for n in range(3):
                        pg, kg = self.nextps()
                        wc = n * 8 + i
                        for k in range(8):
                            S.mm(pg[:, 0:TN], wg[:, k, wc * 128:(wc + 1) * 128], h[:, k, :], k == 0, k == 7,
                                 reads=[("wg", wc // 3), ("h", par)], writes=[kg])
                        pu, ku = self.nextps()
                        for kk in range(4):
                            S.mm(pu[:, 0:TN], wb[:, n * 4 + kk, i * 128:(i + 1) * 128], y3[:, n * 4 + kk, :], kk == 0, kk == 3,
                                 reads=[("wb", n), ("y3", par, n)], writes=[ku])
                        s3 = scnt % 3
                        scnt += 1
                        S.act(sg[:, s3, :], pg[:, 0:TN], AF.Sigmoid, reads=[kg], writes=[("sg", s3)])
                        if n == 0:
                            S.tt("dve", ma[:, mi, :], sg[:, s3, :], pu[:, 0:TN], ALU.mult, reads=[("sg", s3), ku],
                                 writes=[("ma", mi)])
                        else:
                            S.tt("dve", tm[:, n - 1, :], sg[:, s3, :], pu[:, 0:TN], ALU.mult, reads=[("sg", s3), ku],
                                 writes=[("tm", n - 1)])
                            if n == 1:
                                S.tt("pool", ma[:, mi, :], ma[:, mi, :], tm[:, 0, :], ALU.add,
                                     reads=[("ma", mi), ("tm", 0)], writes=[("ma", mi)])
                            else:
                                S.tt("pool", m[:, i, :], ma[:, mi, :], tm[:, 1, :], ALU.add,
                                     reads=[("ma", mi), ("tm", 1)], writes=[("m", i)])
                for i in range(8):
                    po, ko = self.nextps()
                    for k in range(8):
                        S.mm(po[:, 0:TN], wo[:, k, i * 128:(i + 1) * 128], m[:, k, :], k == 0, k == 7,
                             reads=[("wo", k // 4), ("m", k)], writes=[ko])
                    S.stt(xt[:, i, :], po[:, 0:TN], self.modG[:, 1, i, c:c + 1], xt[:, i, :], ALU.mult, ALU.add,
                          reads=[ko, ("xt", par, i), ("modG", 1)], writes=[("xt", par, i)])
                self.store_x(xt, t, par)
            S.flush()


def _perm128():
    return np.concatenate([np.arange(32, 64), np.arange(0, 32), np.arange(96, 128), np.arange(64, 96)])


def _perm64():
    return np.concatenate([np.arange(16, 32), np.arange(0, 16), np.arange(48, 64), np.arange(32, 48)])


def _fm(v):
    v = np.asarray(v, np.float32)
    lead = v.shape[:-1]
    n = v.shape[-1] // 128
    v = v.reshape(*lead, n, 128)
    return np.moveaxis(v, -1, 0)


def _rope_tables(half):
    theta = 10000.0
    tl = np.arange(LAT) + half * LAT
    rows = (tl // 64).astype(np.float32)
    cols = (tl % 64).astype(np.float32)
    tabs = np.zeros((4, 128, T), np.float32)
    tabs[0, :, :CTX] = 1.0
    tabs[2, :, :CTX] = 1.0
    f = (theta ** (-np.arange(0, 64, 2, dtype=np.float32) / 64)).astype(np.float32)
    ar = (rows[None, :] * f[:, None]).astype(np.float32)
    ac = (cols[None, :] * f[:, None]).astype(np.float32)
    C = np.concatenate([np.cos(ar), np.cos(ar), np.cos(ac), np.cos(ac)], 0)
    Sn = np.concatenate([-np.sin(ar), np.sin(ar), -np.sin(ac), np.sin(ac)], 0)
    tabs[0, :, CTX:] = C
    tabs[1, :, CTX:] = Sn
    f = (theta ** (-np.arange(0, 32, 2, dtype=np.float32) / 32)).astype(np.float32)
    ar = (rows[None, :] * f[:, None]).astype(np.float32)
    ac = (cols[None, :] * f[:, None]).astype(np.float32)
    C = np.concatenate([np.cos(ar), np.cos(ar), np.cos(ac), np.cos(ac)], 0)
    Sn = np.concatenate([-np.sin(ar), np.sin(ar), -np.sin(ac), np.sin(ac)], 0)
    tabs[2, :, CTX:] = np.concatenate([C, C], 0)
    tabs[3, :, CTX:] = np.concatenate([Sn, Sn], 0)
    return tabs


def _consts():
    cst = np.zeros((6, 128, 128), np.float32)
    cst[0] = np.eye(128)
    cst[1] = 1.0
    cst[2, :64, :64] = 1.0
    cst[2, 64:, 64:] = 1.0
    j = np.arange(128)[:, None]
    i = np.arange(128)[None, :]
    cst[3] = (i >= j)
    cst[4] = (i <= j)
    p = np.arange(TN) % 128
    pos = np.zeros((2, 128, TN), np.float32)
    pos[0] = (p + 1)[None, :]
    pos[1] = (128 - p)[None, :]
    return cst, pos


def _pack_pv(inp, b, half):
    pv = np.zeros((128, NPV), np.float32)
    p64 = _perm64()
    for l in range(DEPTH):
        o = l * PL
        pv[:, o:o + 24] = _fm(inp["norm_g"][l]).reshape(128, 24)
        pv[:, o + 24:o + 96] = _fm(inp["b_mod"][l]).reshape(128, 72)
        pv[:, o + 96:o + 100] = _fm(inp["ret_norm_g"][l])
        pv[:, o + 100:o + 116] = _fm(inp["lru_conv_w"][l]).reshape(128, 16)
        pv[:, o + 116:o + 120] = _fm(inp["lru_conv_b"][l])
        pv[:, o + 120:o + 128] = _fm(inp["lru_b_a"][l]).reshape(128, 8)
        pv[:, o + 128:o + 136] = _fm(inp["lru_b_x"][l]).reshape(128, 8)
        pv[:, o + 136:o + 144] = _fm(inp["lru_lambda"][l]).reshape(128, 8)
        qg = np.asarray(inp["attn_q_norm_g"][l], np.float32)
        kg = np.asarray(inp["attn_k_norm_g"][l], np.float32)
        pv[:, o + 144] = np.tile(qg, 2)
        pv[:, o + 145] = np.tile(qg[p64], 2)
        pv[:, o + 146] = np.tile(kg, 2)
        pv[:, o + 147] = np.tile(kg[p64], 2)
        pv[:, o + 148:o + 156] = np.asarray(inp["ret_decay_logit"][l], np.float32).reshape(1, 8)
    o = DEPTH * PL
    pv[:, o:o + 8] = _fm(inp["final_norm_g"])
    pv[:, o + 8:o + 16] = _fm(inp["c"][b])
    pv[:, o + 16:o + 24] = _fm(inp["c_ctx"])
    pv[:, o + 24] = float(half)
    pv[:, o + 25] = float(1 - half)
    return pv


def _host_inputs(inp):
    f = lambda a: np.ascontiguousarray(np.asarray(a, np.float32))
    cst, pos = _consts()
    w_in = f(inp["w_in"])
    p128, p64 = _perm128(), _perm64()
    idx = []
    for c in range(8):
        idx.append(c * 128 + p128)
    for c in range(5):
        for hh in range(2):
            idx.append(3072 + c * 128 + hh * 64 + p64)
    idx = np.concatenate(idx)
    w_inp = np.ascontiguousarray(w_in[:, :, idx])
    lbd = np.zeros((DEPTH, 2, 2, 4, 128, 128), np.float32)
    for a, name in enumerate(("lru_w_a", "lru_w_x")):
        w = f(inp[name])
        for c in range(4):
            lbd[:, a, :, c, :64, :64] = w[:, :, 2 * c]
            lbd[:, a, :, c, 64:, 64:] = w[:, :, 2 * c + 1]
    shared = {
        "cst": cst, "pos": pos, "w_mod": f(inp["w_mod"][:N_LAYERS]), "ffn_w_in": f(inp["ffn_w_in"][:N_LAYERS]),
        "ffn_w_out": f(inp["ffn_w_out"][:N_LAYERS]), "w_in": w_in[:N_LAYERS], "w_inp": w_inp[:N_LAYERS],
        "lbd": lbd[:N_LAYERS], "w_branch": f(inp["w_branch"][:N_LAYERS]), "w_out": f(inp["w_out"][:N_LAYERS]),
    }
    x = f(inp["x"])
    ctx = f(inp["ctx"])
    ropes = [_rope_tables(0), _rope_tables(1)]
    maps = []
    for core in range(8):
        b, half = core // 2, core % 2
        xt = np.concatenate([ctx[b], x[b, half * LAT:(half + 1) * LAT]], 0)
        xin = np.ascontiguousarray(xt.T.reshape(8, 128, T))
        m = dict(shared)
        m["xin"] = xin
        m["pv"] = _pack_pv(inp, b, half)
        m["rope"] = ropes[half]
        maps.append(m)
    return maps


_NC_CACHE = {}


def _get_nc():
    key = (DEBUG_STOP, N_LAYERS)
    if key not in _NC_CACHE:
        nc = bass.Bass("TRN2", target_bir_lowering=False)
        Builder(nc).build()
        _NC_CACHE[key] = nc
    return _NC_CACHE[key]


def kernel(**inputs):
    maps = _host_inputs(inputs)
    nc = _get_nc()
    if TRACE:
        res = run_bass_kernel_spmd(nc, maps, core_ids=list(range(8)), trace=True)
        print("exec_time_ns", res.exec_time_ns)
    else:
        res = run_bass_kernel_spmd(nc, maps, core_ids=list(range(8)))
    if DEBUG_STOP is not None:
        return [r["dbg"] for r in res.results]
    out = np.zeros((4, 2 * LAT, D), np.float32)
    for core in range(8):
        b, half = core // 2, core % 2
        o = res.results[core]["out"]
        out[b, half * LAT:(half + 1) * LAT] = o.reshape(D, LAT).T
    return out
```

```python
import numpy as np
import concourse.bass as bass
import concourse.mybir as mybir
from concourse.bass_utils import run_bass_kernel_spmd

F32 = mybir.dt.float32
BF16 = mybir.dt.bfloat16
AF = mybir.ActivationFunctionType
ALU = mybir.AluOpType

D = 1024
DEPTH = 4
CTX = 256
LAT = 2048
T = CTX + LAT
TN = 256
NT = T // TN
DFF = 2816
DIN = 6912
EPS = 1e-6
NZ = 38
PL = 156
NPV = DEPTH * PL + 8 + 16 + 2
EX_KA = 0
EX_AV = EX_KA + 128 * LAT
EX_SL = EX_AV + LAT * 128
EX_LX = EX_SL + 8 * 128 * 128
EX_N = EX_LX + 4 * 128 * 4
EX2_N = 2 * 512

DEBUG_STOP = None
N_LAYERS = DEPTH
TRACE = False


class _I:
    __slots__ = ("eng", "fn", "waits", "sig", "idx", "kind", "sem", "target")


class Sched:
    R = 8

    def __init__(self, nc, sems):
        self.nc = nc
        self.sems = sems
        nxt = iter(range(len(sems)))
        self.csem = {e: next(nxt) for e in ("pe", "act", "dve", "pool")}
        self.qsem = {q: [next(nxt) for _ in range(self.R)] for q in ("sp", "pool")}
        self.ccsem = next(nxt)
        self.ncc = 0
        self.qn = {"sp": 0, "pool": 0}
        self.cnt = {e: 0 for e in self.csem}
        self.lists = {e: [] for e in ("pe", "act", "dve", "pool", "sp")}
        self.lastw = {}
        self.rd = {}
        self.seen = {e: {} for e in self.lists}
        self.barrier = []
        self.ninstr = 0

    def add(self, eng, fn, reads=(), writes=(), kind="c"):
        I = _I()
        I.eng, I.fn, I.kind, I.sig, I.idx, I.waits = eng, fn, kind, False, None, []
        deps = []
        for b in reads:
            w = self.lastw.get(b)
            if w is not None:
                deps.append(w)
        for b in writes:
            w = self.lastw.get(b)
            if w is not None:
                deps.append(w)
            r = self.rd.get(b)
            if r:
                deps.extend(r[0].values())
                deps.extend(r[1])
        for J in deps:
            if J.kind in ("dma", "cc"):
                I.waits.append(J)
            elif J.eng == eng and kind == "c":
                if eng == "pe":
                    continue
                I.waits.append(J)
                J.sig = True
            else:
                I.waits.append(J)
                J.sig = True
        if kind == "dma":
            n = self.qn[eng]
            self.qn[eng] += 1
            I.sem = self.qsem[eng][n % self.R]
            I.target = 16 * (n // self.R + 1)
            if n >= self.R:
                I.waits.append((I.sem, 16 * (n // self.R)))
        elif kind == "cc":
            I.sem = self.ccsem
            self.ncc += 1
            I.target = self.ncc
        for b in reads:
            r = self.rd.setdefault(b, ({}, []))
            if kind == "c":
                r[0][eng] = I
            else:
                r[1].append(I)
        for b in writes:
            self.lastw[b] = I
            self.rd[b] = ({}, [])
        self.lists[eng].append(I)
        self.ninstr += 1
        return I

    def dma(self, q, out, in_, reads=(), writes=(), slow=False):
        if slow:
            return self.add(q, lambda e: e.dma_start(out=out, in_=in_, allow_slow_non_contiguous=True), reads, writes,
                            kind="dma")
        return self.add(q, lambda e: e.dma_start(out=out, in_=in_), reads, writes, kind="dma")

    def mm(self, out, lhsT, rhs, start, stop, reads=(), writes=()):
        return self.add("pe", lambda e: e.matmul(out, lhsT=lhsT, rhs=rhs, start=start, stop=stop), reads, writes)

    def tr(self, out, in_, ident, reads=(), writes=()):
        return self.add("pe", lambda e: e.transpose(out, in_, ident), reads, writes)

    def act(self, out, in_, func, reads=(), writes=(), bias=None, scale=None):
        kw = {}
        if bias is not None:
            kw["bias"] = bias
        if scale is not None:
            kw["scale"] = scale
        return self.add("act", lambda e: e.activation(out=out, in_=in_, func=func, **kw), reads, writes)

    def tt(self, eng, out, in0, in1, op, reads=(), writes=()):
        return self.add(eng, lambda e: e.tensor_tensor(out=out, in0=in0, in1=in1, op=op), reads, writes)

    def ts(self, eng, out, in0, s1, s2, op0, op1, reads=(), writes=()):
        return self.add(eng, lambda e: e.tensor_scalar(out=out, in0=in0, scalar1=s1, scalar2=s2, op0=op0, op1=op1),
                        reads, writes)

    def stt(self, out, in0, scalar, in1, op0, op1, reads=(), writes=()):
        return self.add("dve", lambda e: e.scalar_tensor_tensor(out=out, in0=in0, scalar=scalar, in1=in1,
                                                                op0=op0, op1=op1), reads, writes)

    def cp(self, eng, out, in_, reads=(), writes=()):
        if eng == "act":
            return self.add("act", lambda e: e.copy(out=out, in_=in_), reads, writes)
        return self.add(eng, lambda e: e.tensor_copy(out=out, in_=in_), reads, writes)

    def recip(self, out, in_, reads=(), writes=()):
        return self.add("dve", lambda e: e.reciprocal(out=out, in_=in_), reads, writes)

    def memset(self, eng, ap, val, reads=(), writes=()):
        return self.add(eng, lambda e: e.memset(ap, val), reads, writes)

    def scan(self, out, d0, d1, init, reads=(), writes=()):
        return self.add("dve", lambda e: e.tensor_tensor_scan(out=out, data0=d0, data1=d1, initial=init,
                                                              op0=ALU.mult, op1=ALU.add), reads, writes)

    def cc(self, ins_ap, outs_ap, reads=(), writes=()):
        groups = [[0, 1], [2, 3], [4, 5], [6, 7]]
        return self.add("pool", lambda e: e.collective_compute("AllGather", ALU.bypass, replica_groups=groups,
                                                               ins=[ins_ap], outs=[outs_ap]),
                        reads, writes, kind="cc")

    def flush(self):
        nc = self.nc
        for e in self.csem:
            last = None
            for I in self.lists[e]:
                if I.kind == "c":
                    last = I
            if last is not None:
                last.sig = True
            for I in self.lists[e]:
                if I.kind == "c" and I.sig:
                    self.cnt[e] += 1
                    I.idx = self.cnt[e]
        sems = self.sems

        def emit(eh, eng):
            seen = self.seen[eng]

            def wait(si, val):
                if seen.get(si, 0) < val:
                    eh.wait_ge(sems[si], val)
                    seen[si] = val

            for (si, val) in self.barrier:
                wait(si, val)
            for I in self.lists[eng]:
                for w in I.waits:
                    if isinstance(w, tuple):
                        wait(w[0], w[1])
                    elif w.kind in ("dma", "cc"):
                        wait(w.sem, w.target)
                    else:
                        wait(self.csem[w.eng], w.idx)
                ins = I.fn(eh)
                if I.kind == "dma":
                    ins.then_inc(sems[I.sem], 16)
                elif I.kind == "cc":
                    ins.then_inc(sems[I.sem])
                elif I.sig:
                    ins.then_inc(sems[self.csem[eng]], 1)

        with nc.Block() as block:
            @block.tensor
            def _(eh):
                emit(eh, "pe")

            @block.scalar
            def _(eh):
                emit(eh, "act")

            @block.vector
            def _(eh):
                emit(eh, "dve")

            @block.gpsimd
            def _(eh):
                emit(eh, "pool")

            @block.sync
            def _(eh):
                emit(eh, "sp")

        bar = [(self.csem[e], self.cnt[e]) for e in self.csem if self.cnt[e] > 0]
        for q in ("sp", "pool"):
            n = self.qn[q]
            for i in range(self.R):
                if n > i:
                    bar.append((self.qsem[q][i], 16 * ((n - 1 - i) // self.R + 1)))
        if self.ncc:
            bar.append((self.ccsem, self.ncc))
        self.barrier = bar
        self.lists = {e: [] for e in self.lists}
        self.lastw = {}
        self.rd = {}

    def final_wait(self):
        nc = self.nc
        sems = self.sems
        with nc.Block() as block:
            @block.sync
            def _(eh):
                for (si, val) in self.barrier:
                    eh.wait_ge(sems[si], val)

            @block.gpsimd
            def _(eh):
                for (si, val) in self.barrier:
                    eh.wait_ge(sems[si], val)


class Builder:
    def __init__(self, nc):
        self.nc = nc
        self.pscur = 0

    def declare(self):
        nc = self.nc
        di = lambda n, s: nc.dram_tensor(n, s, F32, kind="ExternalInput").ap()
        self.xin = di("xin", [8, 128, T])
        self.pvin = di("pv", [128, NPV])
        self.cst = di("cst", [6, 128, 128])
        self.posin = di("pos", [2, 128, TN])
        self.rope = di("rope", [4, 128, T])
        self.w_mod = di("w_mod", [N_LAYERS, D, 9 * D])
        self.ffn_w_in = di("ffn_w_in", [N_LAYERS, 2, D, 2 * DFF])
        self.ffn_w_out = di("ffn_w_out", [N_LAYERS, 2, DFF, D])
        self.w_in = di("w_in", [N_LAYERS, D, DIN])
        self.w_inp = di("w_inp", [N_LAYERS, D, 13 * 128])
        self.lbd = di("lbd", [N_LAYERS, 2, 2, 4, 128, 128])
        self.w_branch = di("w_branch", [N_LAYERS, 3, 512, D])
        self.w_out = di("w_out", [N_LAYERS, D, D])
        self.out = nc.dram_tensor("out", [8, 128, LAT], F32, kind="ExternalOutput").ap()
        if DEBUG_STOP is not None:
            self.dbg = nc.dram_tensor("dbg", [8, 128, T], F32, kind="ExternalOutput").ap()
        dt = lambda n, s, d=F32: nc.dram_tensor(n, s, d)
        self.X = dt("X", [8, 128, T]).ap()
        self.H = dt("H", [8, 128, T], BF16).ap()
        self.Z = dt("Z", [NZ, 128, T]).ap()
        self.RV = dt("RV", [T, 512], BF16).ap()
        self.AVC = dt("AVC", [CTX, 128]).ap()
        self.KAC = dt("KAC", [128, CTX]).ap()
        self.QA = dt("QA", [4, 128, T], BF16).ap()
        self.QK = dt("QK", [4, 4, 128, T], BF16).ap()
        self.EXP = [dt("EXP%d" % j, [128, 512]) for j in range(10)] + [dt("EXPL", [4, 512])]
        self.EXGP = [dt("EXGP%d" % j, [256, 512]) for j in range(10)] + [dt("EXGPL", [8, 512])]
        self.EX2t = dt("EX2", [EX2_N // 512, 512])
        self.EX2Gt = dt("EX2G", [2 * EX2_N // 512, 512])
        self.EX2 = self.EX2t.ap().rearrange("a b -> (a b)")
        self.EX2G = self.EX2Gt.ap().rearrange("a b -> (a b)")
        self.HS = dt("HS", [4, 128, T]).ap()
        self.ACF = dt("ACF", [4, 128, LAT]).ap()
        self.ACB = dt("ACB", [4, 128, LAT]).ap()
        self.YR = dt("YR", [4, 128, T], BF16).ap()
        self.YL = dt("YL", [4, 128, T], BF16).ap()
        self.YA = dt("YA", [4, 128, T], BF16).ap()

    def un(self, name):
        self.uid = getattr(self, "uid", 0) + 1
        return "%s_%d" % (name, self.uid)

    def nextps(self, n=6):
        i = self.pscur % n
        self.pscur += 1
        return self.ps[i], ("ps", i)

    def pvl(self, l, off, n=1):
        return self.pv[:, l * PL + off:l * PL + off + n]

    def phase_init(self):
        nc, S = self.nc, self.S
        with nc.sbuf_tensor(self.un("ini_c"), [128, 5, 128], F32) as cf, nc.sbuf_tensor(self.un("ini_s"), [128, 16], F32) as sv:
            S.dma("sp", self.pv[:], self.pvin, writes=["pv"])
            S.dma("sp", cf[:], self.cst[0:5].rearrange("c p n -> p c n"), writes=["cf"])
            S.dma("sp", self.pos[:], self.posin.rearrange("c p n -> p c n"), writes=["pos"])
            S.cp("dve", self.ident[:], cf[:, 0, :], reads=["cf"], writes=["ident"])
            S.cp("dve", self.ones[:], cf[:, 1, :], reads=["cf"], writes=["ones"])
            S.cp("dve", self.bones[:], cf[:, 2, :], reads=["cf"], writes=["bones"])
            S.cp("dve", self.maskf[:], cf[:, 3, :], reads=["cf"], writes=["maskf"])
            S.cp("dve", self.maskb[:], cf[:, 4, :], reads=["cf"], writes=["maskb"])
            S.cp("dve", self.ones32[:], cf[:, 1, :], reads=["cf"], writes=["ones32"])
            base = DEPTH * PL + 8
            S.act(sv[:], self.pv[:, base:base + 16], AF.Silu, reads=["pv"], writes=["sv"])
            S.cp("dve", self.sb[:], sv[:].rearrange("p (c k) -> p k c", c=2), reads=["sv"], writes=["sb"])
            for k in range(8):
                S.dma("sp", self.X[k], self.xin[k], writes=[("X", k)])
            S.flush()

    def phase_mod(self, l):
        nc, S = self.nc, self.S
        with (nc.sbuf_tensor(self.un("wm0"), [128, 8, 1024], BF16) as wm0,
              nc.sbuf_tensor(self.un("wm1"), [128, 8, 1024], BF16) as wm1,
              nc.sbuf_tensor(self.un("mraw"), [128, 9, 8, 2], F32) as mraw,
              nc.sbuf_tensor(self.un("lg"), [128, 8], F32) as lg):
            wms = [wm0, wm1]
            src = self.w_mod[l].rearrange("(k p) c -> p k c", p=128)
            for b in range(9):
                wm = wms[b % 2]
                for kk in range(0, 8, 4):
                    S.dma("pool", wm[:, kk:kk + 4, :], src[:, kk:kk + 4, b * 1024:(b + 1) * 1024],
                          writes=[("wm", b % 2, kk)])
                pm, km = self.nextps()
                for i in range(8):
                    for k in range(8):
                        S.mm(pm[:, i * 2:(i + 1) * 2], wm[:, k, i * 128:(i + 1) * 128], self.sb[:, k, :],
                             k == 0, k == 7, reads=[("wm", b % 2, (k // 4) * 4), "sb"], writes=[km])
                bm = self.pvl(l, 24 + b * 8, 8)
                S.tt("dve", mraw[:, b, :, :], pm[:, 0:16].rearrange("p (i c) -> p i c", c=2),
                     bm.unsqueeze(2).broadcast_to([128, 8, 2]), ALU.add, reads=[km, "pv"], writes=[("mraw", b)])
            for s in range(3):
                g = self.pvl(l, s * 8, 8).unsqueeze(2).broadcast_to([128, 8, 2])
                S.stt(self.modA[:, s, :, :], mraw[:, 3 * s + 1, :, :], 1.0, g, ALU.add, ALU.mult,
                      reads=[("mraw", 3 * s + 1), "pv"], writes=[("modA", s)])
                S.cp("dve", self.modB[:, s, :, :], mraw[:, 3 * s, :, :], reads=[("mraw", 3 * s)], writes=[("modB", s)])
                S.ts("dve", self.modG[:, s, :, :], mraw[:, 3 * s + 2, :, :], 0.5 if s != 1 else 1.0, None,
                     ALU.mult, ALU.bypass, reads=[("mraw", 3 * s + 2)], writes=[("modG", s)])
            S.act(lg[:], self.pvl(l, 148, 8), AF.Exp, scale=-1.0, reads=["pv"], writes=["lg"])
            S.act(lg[:], lg[:], AF.Ln, bias=1.0, reads=["lg"], writes=["lg"])
            S.ts("dve", self.lgam[:], lg[:], -1.0, None, ALU.mult, ALU.bypass, reads=["lg"], writes=["lgam"])
            S.ts("dve", self.nlgam[:], lg[:], 1.0, None, ALU.mult, ALU.bypass, reads=["lg"], writes=["nlgam"])
            S.act(self.gC[:], self.lgam[:], AF.Exp, scale=128.0, reads=["lgam"], writes=["gC"])
            hb = DEPTH * PL + 24
            S.ts("dve", lg[:, 0:4], self.lgam[:, 0:4], self.pv[:, hb:hb + 1], 2048.0, ALU.mult, ALU.mult,
                 reads=["lgam", "pv"], writes=["lg"])
            S.ts("dve", lg[:, 4:8], self.lgam[:, 4:8], self.pv[:, hb + 1:hb + 2], 2048.0, ALU.mult, ALU.mult,
                 reads=["lgam", "pv"], writes=["lg"])
            S.act(self.bc1[:], lg[:], AF.Exp, reads=["lg"], writes=["bc1"])
            S.act(self.lcp[:], self.pvl(l, 136, 8), AF.Exp, scale=-1.0, reads=["pv"], writes=["lcp"])
            S.act(self.lcp[:], self.lcp[:], AF.Ln, bias=1.0, reads=["lcp"], writes=["lcp"])
            S.ts("dve", self.lcp2[:], self.lcp[:], -16.0, None, ALU.mult, ALU.bypass, reads=["lcp"], writes=["lcp2"])
            S.ts("dve", self.lcp[:], self.lcp[:], -8.0, None, ALU.mult, ALU.bypass, reads=["lcp"], writes=["lcp"])
            S.flush()

    def load_x(self, xt, t, par):
        S = self.S
        S.dma("sp", xt[:], self.X[:, :, t * TN:(t + 1) * TN].rearrange("k p n -> p k n"),
              reads=[("X", t)], writes=[("xt", par, i) for i in range(8)])

    def store_x(self, xt, t, par, dst=None):
        S = self.S
        dst = self.X if dst is None else dst
        S.dma("sp", dst[:, :, t * TN:(t + 1) * TN].rearrange("k p n -> p k n"), xt[:],
              reads=[("xt", par, i) for i in range(8)], writes=[("X", t)])

    def norm_mod(self, xt, par, sq, rs, tmp, h, sub, c):
        S = self.S
        xk = [("xt", par, i) for i in range(8)]
        S.act(sq[:], xt[:], AF.Square, reads=xk, writes=["sq"])
        pn, kn = self.nextps()
        for k in range(8):
            S.mm(pn[:, 0:TN], self.ones[:], sq[:, k, :], k == 0, k == 7, reads=["sq", "ones"], writes=[kn])
        S.act(rs[:], pn[:, 0:TN], AF.Sqrt, scale=1.0 / D, bias=self.epsb[:, 0:1], reads=[kn], writes=["rs"])
        S.recip(rs[:], rs[:], reads=["rs"], writes=["rs"])
        S.tt("dve", tmp[:], xt[:], rs[:].unsqueeze(1).broadcast_to([128, 8, TN]), ALU.mult,
             reads=xk + ["rs"], writes=["tmp"])
        for k in range(8):
            S.act(h[:, k, :], tmp[:, k, :], AF.Identity, scale=self.modA[:, sub, k, c:c + 1],
                  bias=self.modB[:, sub, k, c:c + 1], reads=["tmp", ("modA", sub), ("modB", sub)],
                  writes=[("h", par, k)])

    def phase_ffn(self, l, which, sub):
        nc, S = self.nc, self.S
        with (nc.sbuf_tensor(self.un("ffw1"), [128, 8, 2 * DFF], BF16) as w1,
              nc.sbuf_tensor(self.un("ffw2"), [128, 22, D], BF16) as w2,
              nc.sbuf_tensor(self.un("fxt0"), [128, 8, TN], F32) as xt0,
              nc.sbuf_tensor(self.un("fxt1"), [128, 8, TN], F32) as xt1,
              nc.sbuf_tensor(self.un("fsq"), [128, 8, TN], BF16) as sq,
              nc.sbuf_tensor(self.un("fh0"), [128, 8, TN], BF16) as h0,
              nc.sbuf_tensor(self.un("fh1"), [128, 8, TN], BF16) as h1,
              nc.sbuf_tensor(self.un("fg"), [128, 22, TN], BF16) as g,
              nc.sbuf_tensor(self.un("fsl0"), [128, TN], F32) as sl0,
              nc.sbuf_tensor(self.un("fsl1"), [128, TN], F32) as sl1,
              nc.sbuf_tensor(self.un("frs"), [128, TN], F32) as rs,
              nc.sbuf_tensor(self.un("ftmp"), [128, 8, TN], F32) as tmp):
            xts, hs, sls = [xt0, xt1], [h0, h1], [sl0, sl1]
            src1 = self.ffn_w_in[l, which].rearrange("(k p) c -> p k c", p=128)
            for cb in range(11):
                S.dma("pool", w1[:, :, cb * 512:(cb + 1) * 512], src1[:, :, cb * 512:(cb + 1) * 512],
                      writes=[("w1", cb)])
            src2 = self.ffn_w_out[l, which].rearrange("(j p) c -> p j c", p=128)
            for jb in range(11):
                S.dma("pool", w2[:, 2 * jb:2 * jb + 2, :], src2[:, 2 * jb:2 * jb + 2, :], writes=[("w2", jb)])
            self.load_x(xts[0], 0, 0)
            self.norm_mod(xts[0], 0, sq, rs, tmp, hs[0], sub, 1)
            for t in range(NT):
                par = t % 2
                xt, h = xts[par], hs[par]
                c = 1 if t == 0 else 0
                if t + 1 < NT:
                    self.load_x(xts[1 - par], t + 1, 1 - par)
                for j in range(22):
                    pa, ka = self.nextps()
                    pb, kb = self.nextps()
                    ca, cb_ = j * 128, DFF + j * 128
                    for k in range(8):
                        S.mm(pa[:, 0:TN], w1[:, k, ca:ca + 128], h[:, k, :], k == 0, k == 7,
                             reads=[("w1", ca // 512), ("h", par, k)], writes=[ka])
                    for k in range(8):
                        S.mm(pb[:, 0:TN], w1[:, k, cb_:cb_ + 128], h[:, k, :], k == 0, k == 7,
                             reads=[("w1", cb_ // 512), ("h", par, k)], writes=[kb])
                    sl = sls[j % 2]
                    S.act(sl[:], pa[:, 0:TN], AF.Silu, reads=[ka], writes=[("sl", j % 2)])
                    S.tt("dve", g[:, j, :], sl[:], pb[:, 0:TN], ALU.mult, reads=[("sl", j % 2), kb], writes=[("g", j)])
                if t + 1 < NT:
                    self.norm_mod(xts[1 - par], 1 - par, sq, rs, tmp, hs[1 - par], sub, 0)
                for i in range(8):
                    po, ko = self.nextps()
                    for j in range(22):
                        S.mm(po[:, 0:TN], w2[:, j, i * 128:(i + 1) * 128], g[:, j, :], j == 0, j == 21,
                             reads=[("w2", j // 2), ("g", j)], writes=[ko])
                    S.stt(xt[:, i, :], po[:, 0:TN], self.modG[:, sub, i, c:c + 1], xt[:, i, :], ALU.mult, ALU.add,
                          reads=[ko, ("xt", par, i), ("modG", sub)], writes=[("xt", par, i)])
                self.store_x(xt, t, par)
            S.flush()

    def phase_final(self):
        nc, S = self.nc, self.S
        with (nc.sbuf_tensor(self.un("nxt0"), [128, 8, TN], F32) as xt0,
              nc.sbuf_tensor(self.un("nxt1"), [128, 8, TN], F32) as xt1,
              nc.sbuf_tensor(self.un("nsq"), [128, 8, TN], BF16) as sq,
              nc.sbuf_tensor(self.un("nrs"), [128, TN], F32) as rs,
              nc.sbuf_tensor(self.un("no0"), [128, 8, TN], F32) as o0,
              nc.sbuf_tensor(self.un("no1"), [128, 8, TN], F32) as o1):
            xts, os_ = [xt0, xt1], [o0, o1]
            fb = DEPTH * PL
            for t in range(1, NT):
                par = t % 2
                xt, o = xts[par], os_[par]
                self.load_x(xt, t, par)
                xk = [("xt", par, i) for i in range(8)]
                S.act(sq[:], xt[:], AF.Square, reads=xk, writes=["sq"])
                pn, kn = self.nextps()
                for k in range(8):
                    S.mm(pn[:, 0:TN], self.ones[:], sq[:, k, :], k == 0, k == 7, reads=["sq"], writes=[kn])
                S.act(rs[:], pn[:, 0:TN], AF.Sqrt, scale=1.0 / D, bias=self.epsb[:, 0:1], reads=[kn], writes=["rs"])
                S.recip(rs[:], rs[:], reads=["rs"], writes=["rs"])
                for k in range(8):
                    S.stt(o[:, k, :], xt[:, k, :], self.pv[:, fb + k:fb + k + 1], rs[:], ALU.mult, ALU.mult,
                          reads=xk + ["rs"], writes=[("o", par)])
                S.dma("sp", self.out[:, :, (t - 1) * TN:t * TN].rearrange("k p n -> p k n"), o[:],
                      reads=[("o", par)], writes=[("out", t)])
            S.flush()

    def phase_dbg(self):
        S = self.S
        for k in range(8):
            S.dma("sp", self.dbg[k], self.X[k], reads=[("X", k)], writes=[("dbg", k)])
        S.flush()

    def build(self):
        nc = self.nc
        self.declare()
        sem_names = ["s%d" % i for i in range(4 + 16 + 1)]
        from contextlib import ExitStack
        with ExitStack() as st:
            sems = [st.enter_context(nc.semaphore(n)) for n in sem_names]
            self.S = Sched(nc, sems)
            sb = lambda n, s, d=F32: st.enter_context(nc.sbuf_tensor(self.un("sb_") + n, s, d))
            self.pv = sb("pv", [128, NPV])
            self.ident = sb("ident", [128, 128], BF16)
            self.ones = sb("ones", [128, 128], BF16)
            self.bones = sb("bones", [128, 128], BF16)
            self.maskf = sb("maskf", [128, 128], BF16)
            self.maskb = sb("maskb", [128, 128], BF16)
            self.ones32 = sb("ones32", [128, 128])
            self.pos = sb("pos", [128, 2, TN])
            self.sb = sb("sb", [128, 8, 2], BF16)
            self.modA = sb("modA", [128, 3, 8, 2])
            self.modB = sb("modB", [128, 3, 8, 2])
            self.modG = sb("modG", [128, 3, 8, 2])
            self.lgam = sb("lgam", [128, 8])
            self.nlgam = sb("nlgam", [128, 8])
            self.gC = sb("gC", [128, 8])
            self.bc1 = sb("bc1", [128, 8])
            self.lcp = sb("lcp", [128, 8])
            self.lcp2 = sb("lcp2", [128, 8])
            self.epsb = sb("epsb", [128, 1])
            self.lnks = sb("lnks", [128, 1])
            self.ps = [st.enter_context(nc.psum_tensor("ps%d" % i, [128, 512], F32)) for i in range(6)]
            self.psb = [st.enter_context(nc.psum_tensor("psb%d" % i, [128, 1024], BF16)) for i in range(2)]
            self.S.memset("dve", self.epsb[:], EPS, writes=["epsb"])
            self.S.memset("dve", self.lnks[:], -0.5 * float(np.log(128.0)), writes=["lnks"])
            self.phase_init()
            stop = False
            for l in range(N_LAYERS):
                for name, fn in (("mod", lambda: self.phase_mod(l)),
                                 ("ffn1", lambda: self.phase_ffn(l, 0, 0)),
                                 ("mix", lambda: self.phase_mix(l)),
                                 ("ffn2", lambda: self.phase_ffn(l, 1, 2))):
                    r = fn()
                    if r or DEBUG_STOP == (l, name):
                        stop = True
                        break
                if stop:
                    break
            if DEBUG_STOP is not None:
                self.phase_dbg()
            else:
                self.phase_final()
            self.S.final_wait()


    def ex_w(self, off, cnt_):
        j, o = off // 65536, off % 65536
        assert o + cnt_ <= 65536
        return self.EXP[j].ap().rearrange("a b -> (a b)")[o:o + cnt_]

    def ex_r(self, slot, off, cnt_):
        j, o = off // 65536, off % 65536
        assert o + cnt_ <= 65536
        psz = 65536 if j < 10 else 2048
        lo = slot * psz + o
        return self.EXGP[j].ap().rearrange("a b -> (a b)")[lo:lo + cnt_]

    def phase_mix(self, l):
        def cc1():
            for j in range(11):
                self.S.cc(self.EXP[j].ap().opt(), self.EXGP[j].ap().opt())
            self.S.flush()

        def cc2():
            self.S.cc(self.EX2t.ap().opt(), self.EX2Gt.ap().opt())
            self.S.flush()

        for name, fn in (("m1", lambda: self.phase_m1(l)), ("m2", lambda: self.phase_m2(l)),
                         ("m2b", lambda: self.phase_m2b(l)), ("cc1", cc1), ("lruA", lambda: self.phase_lruA(l)),
                         ("cc2", cc2), ("lruB", lambda: self.phase_lruB(l)), ("ret", lambda: self.phase_ret(l)),
                         ("attn", lambda: self.phase_attn(l)), ("merge", lambda: self.phase_merge(l))):
            fn()
            if DEBUG_STOP == (l, name):
                return True
        return False

    ZSRC = list(range(0, 8)) + list(range(12, 29)) + list(range(30, 43))

    def phase_m1(self, l):
        nc, S = self.nc, self.S
        NWC = 43
        with (nc.sbuf_tensor(self.un("m1w"), [128, 8, NWC * 128], BF16) as wz,
              nc.sbuf_tensor(self.un("m1x0"), [128, 8, TN], F32) as xt0,
              nc.sbuf_tensor(self.un("m1x1"), [128, 8, TN], F32) as xt1,
              nc.sbuf_tensor(self.un("m1sq"), [128, 8, TN], BF16) as sq,
              nc.sbuf_tensor(self.un("m1h0"), [128, 8, TN], BF16) as h0,
              nc.sbuf_tensor(self.un("m1h1"), [128, 8, TN], BF16) as h1,
              nc.sbuf_tensor(self.un("m1rs"), [128, TN], F32) as rs,
              nc.sbuf_tensor(self.un("m1tmp"), [128, 8, TN], F32) as tmp,
              nc.sbuf_tensor(self.un("m1z0"), [128, 2, TN], F32) as zs0,
              nc.sbuf_tensor(self.un("m1z1"), [128, 2, TN], F32) as zs1,
              nc.sbuf_tensor(self.un("m1rv0"), [128, 512], BF16) as rv0,
              nc.sbuf_tensor(self.un("m1rv1"), [128, 512], BF16) as rv1,
              nc.sbuf_tensor(self.un("m1av0"), [128, 128], F32) as av0,
              nc.sbuf_tensor(self.un("m1av1"), [128, 128], F32) as av1):
            xts, hs, zss, rvs, avs = [xt0, xt1], [h0, h1], [zs0, zs1], [rv0, rv1], [av0, av1]
            src = self.w_in[l].rearrange("(k p) c -> p k c", p=128)
            srcp = self.w_inp[l].rearrange("(k p) c -> p k c", p=128)
            for pc in range(10):
                S.dma("pool", wz[:, :, pc * 384:(pc + 1) * 384], src[:, :, pc * 384:(pc + 1) * 384], writes=[("wz", pc)])
            for pc in range(5):
                lo, hi = pc * 384, min((pc + 1) * 384, 13 * 128)
                S.dma("pool", wz[:, :, 3840 + lo:3840 + hi], srcp[:, :, lo:hi], writes=[("wz", 10 + pc)])
            self.load_x(xts[0], 0, 0)
            self.norm_mod(xts[0], 0, sq, rs, tmp, hs[0], 1, 1)
            for t in range(NT):
                par = t % 2
                xt, h = xts[par], hs[par]
                c = 1 if t == 0 else 0
                t0 = t * TN
                if t + 1 < NT:
                    self.load_x(xts[1 - par], t + 1, 1 - par)
                S.dma("sp", self.H[:, :, t0:t0 + TN].rearrange("k p n -> p k n"), h[:], reads=[("h", par, k) for k in range(8)],
                      writes=[("H", t)])
                for zc in range(NZ):
                    wc = self.ZSRC[zc]
                    pz, kz = self.nextps()
                    for k in range(8):
                        S.mm(pz[:, 0:TN], wz[:, k, wc * 128:(wc + 1) * 128], h[:, k, :], k == 0, k == 7,
                             reads=[("wz", wc // 3), ("h", par, k)], writes=[kz])
                    zs = zss[(zc // 2) % 2]
                    zk = ("zs", (zc // 2) % 2, zc % 2)
                    S.cp("act" if zc % 2 == 0 else "dve", zs[:, zc % 2, :], pz[:, 0:TN], reads=[kz], writes=[zk])
                    if zc % 2 == 1:
                        S.dma("sp", self.Z[zc - 1:zc + 1, :, t0:t0 + TN].rearrange("c p n -> p c n"), zs[:],
                              reads=[("zs", (zc // 2) % 2, 0), ("zs", (zc // 2) % 2, 1)], writes=[("Z", zc // 2, t)])
                if t + 1 < NT:
                    self.norm_mod(xts[1 - par], 1 - par, sq, rs, tmp, hs[1 - par], 1, 0)
                for hf in range(2):
                    i2 = (2 * t + hf) % 2
                    prv, krv = self.nextps()
                    for k in range(8):
                        S.mm(prv[:, 0:512], h[:, k, hf * 128:(hf + 1) * 128], wz[:, k, 1024:1536], k == 0, k == 7,
                             reads=[("wz", 2), ("wz", 3), ("h", par, k)], writes=[krv])
                    S.cp("act", rvs[i2][:], prv[:, 0:512], reads=[krv], writes=[("rvs", i2)])
                    S.dma("sp", self.RV[t0 + hf * 128:t0 + (hf + 1) * 128, :], rvs[i2][:], reads=[("rvs", i2)],
                          writes=[("RV", t, hf)])
                    pav, kav = self.nextps()
                    for k in range(8):
                        S.mm(pav[:, 0:128], h[:, k, hf * 128:(hf + 1) * 128], wz[:, k, 3712:3840], k == 0, k == 7,
                             reads=[("wz", 9), ("h", par, k)], writes=[kav])
                    S.cp("dve", avs[i2][:], pav[:, 0:128], reads=[kav], writes=[("avs", i2)])
                    if t == 0:
                        dst = self.AVC[hf * 128:(hf + 1) * 128, :]
                    else:
                        r0 = (t - 1) * TN + hf * 128
                        dst = self.ex_w(EX_AV + r0 * 128, 16384).rearrange("(t e) -> t e", e=128)
                    S.dma("sp", dst, avs[i2][:], reads=[("avs", i2)], writes=[("AV", t, hf)])
            S.flush()

    def phase_m2(self, l):
        nc, S = self.nc, self.S
        LNKS = -0.5 * float(np.log(128.0))
        with (nc.sbuf_tensor(self.un("m2G"), [128, 16, TN], F32) as G,
              nc.sbuf_tensor(self.un("m2rp0"), [128, 4, TN], F32) as rp0,
              nc.sbuf_tensor(self.un("m2rp1"), [128, 4, TN], F32) as rp1,
              nc.sbuf_tensor(self.un("m2z"), [128, 4, TN], F32) as zb,
              nc.sbuf_tensor(self.un("m2zp"), [128, 4, TN], F32) as zpb,
              nc.sbuf_tensor(self.un("m2r1"), [128, 2, TN], F32) as r1b,
              nc.sbuf_tensor(self.un("m2r2"), [128, 2, TN], F32) as r2b,
              nc.sbuf_tensor(self.un("m2qk0"), [128, 4, TN], BF16) as qk0,
              nc.sbuf_tensor(self.un("m2qk1"), [128, 4, TN], BF16) as qk1,
              nc.sbuf_tensor(self.un("m2sq"), [128, TN], BF16) as sq,
              nc.sbuf_tensor(self.un("m2rs"), [128, 2, TN], F32) as rsb,
              nc.sbuf_tensor(self.un("m2qa"), [128, 2, TN], BF16) as qab,
              nc.sbuf_tensor(self.un("m2kf"), [128, TN], F32) as kf32):
            rps, qks = [rp0, rp1], [qk0, qk1]
            for kind in range(2):
                for d in range(2):
                    for h in range(4):
                        gi = (kind * 2 + d) * 4 + h
                        sc = (self.lgam if kind == 0 else self.nlgam)[:, d * 4 + h:d * 4 + h + 1]
                        S.act(G[:, gi, :], self.pos[:, d, :], AF.Exp, scale=sc,
                              bias=(None if kind == 0 else self.lnks[:, 0:1]),
                              reads=["lgam", "nlgam", "pos"], writes=[("G", gi)])
            cnt = 0
            for t in range(NT):
                t0 = t * TN
                rp = rps[t % 2]
                S.dma("sp", rp[:], self.rope[:, :, t0:t0 + TN].rearrange("c p n -> p c n"), writes=[("rp", t % 2)])
                for h in range(4):
                    qk = qks[h % 2]
                    for kind in range(2):
                        zc, zpc = (h, 25 + h) if kind == 0 else (4 + h, 29 + h)
                        b4, b2 = cnt % 4, cnt % 2
                        cnt += 1
                        z, zp, r1, r2 = zb[:, b4, :], zpb[:, b4, :], r1b[:, b2, :], r2b[:, b2, :]
                        S.dma("sp", z, self.Z[zc, :, t0:t0 + TN], writes=[("z", b4)])
                        S.dma("sp", zp, self.Z[zpc, :, t0:t0 + TN], writes=[("zp", b4)])
                        S.tt("dve", r1, z, rp[:, 0, :], ALU.mult, reads=[("z", b4), ("rp", t % 2)], writes=[("r1", b2)])
                        S.tt("pool", r2, zp, rp[:, 1, :], ALU.mult, reads=[("zp", b4), ("rp", t % 2)], writes=[("r2", b2)])
                        S.tt("dve", r1, r1, r2, ALU.add, reads=[("r1", b2), ("r2", b2)], writes=[("r1", b2)])
                        gf = (kind * 2 + 0) * 4 + h
                        gb = (kind * 2 + 1) * 4 + h
                        S.tt("pool", qk[:, 2 * kind, :], r1, G[:, gf, :], ALU.mult, reads=[("r1", b2), ("G", gf)],
                             writes=[("qk", h % 2, 2 * kind)])
                        S.tt("dve", qk[:, 2 * kind + 1, :], r1, G[:, gb, :], ALU.mult, reads=[("r1", b2), ("G", gb)],
                             writes=[("qk", h % 2, 2 * kind + 1)])
                    S.dma("sp", self.QK[h, :, :, t0:t0 + TN].rearrange("s p n -> p s n"), qk[:],
                          reads=[("qk", h % 2, i) for i in range(4)], writes=[("QK", h, t)])
                for c in range(5):
                    zc, zpc = (20 + c, 33 + c) if c < 4 else (24, 37)
                    gcol = 144 if c < 4 else 146
                    b4, b2 = cnt % 4, cnt % 2
                    cnt += 1
                    z, zp, r1, r2, rs = zb[:, b4, :], zpb[:, b4, :], r1b[:, b2, :], r2b[:, b2, :], rsb[:, b2, :]
                    S.dma("sp", z, self.Z[zc, :, t0:t0 + TN], writes=[("z", b4)])
                    S.dma("sp", zp, self.Z[zpc, :, t0:t0 + TN], writes=[("zp", b4)])
                    S.act(sq[:], z, AF.Square, reads=[("z", b4)], writes=["sq"])
                    pn, kn = self.nextps()
                    S.mm(pn[:, 0:TN], self.bones[:], sq[:], True, True, reads=["sq"], writes=[kn])
                    S.act(rs, pn[:, 0:TN], AF.Sqrt, scale=1.0 / 64, bias=self.epsb[:, 0:1], reads=[kn], writes=[("rs", b2)])
                    S.recip(rs, rs, reads=[("rs", b2)], writes=[("rs", b2)])
                    S.stt(r1, z, self.pvl(l, gcol), rs, ALU.mult, ALU.mult, reads=[("z", b4), ("rs", b2)], writes=[("r1", b2)])
                    S.stt(r2, zp, self.pvl(l, gcol + 1), rs, ALU.mult, ALU.mult, reads=[("zp", b4), ("rs", b2)],
                          writes=[("r2", b2)])
                    S.tt("pool", r1, r1, rp[:, 2, :], ALU.mult, reads=[("r1", b2), ("rp", t % 2)], writes=[("r1", b2)])
                    S.tt("pool", r2, r2, rp[:, 3, :], ALU.mult, reads=[("r2", b2), ("rp", t % 2)], writes=[("r2", b2)])
                    if c < 4:
                        S.tt("dve", qab[:, c % 2, :], r1, r2, ALU.add, reads=[("r1", b2), ("r2", b2)], writes=[("qa", c % 2)])
                        S.dma("sp", self.QA[c, :, t0:t0 + TN], qab[:, c % 2, :], reads=[("qa", c % 2)], writes=[("QA", c, t)])
                    else:
                        S.tt("dve", kf32[:], r1, r2, ALU.add, reads=[("r1", b2), ("r2", b2)], writes=["kf32"])
                        if t == 0:
                            dst = self.KAC[:, :]
                        else:
                            dst = self.EXP[(t - 1) // 2].ap()[:, ((t - 1) % 2) * TN:((t - 1) % 2 + 1) * TN]
                        S.dma("sp", dst, kf32[:], reads=["kf32"], writes=[("KA", t)])
            EXLX = self.ex_w(EX_LX, 2048).rearrange("(c p n) -> c p n", p=128, n=4)
            for c in range(4):
                S.dma("sp", EXLX[c, :, 0:2], self.Z[12 + c, :, CTX:CTX + 2], writes=[("LXH", c, 0)], slow=True)
                S.dma("sp", EXLX[c, :, 2:4], self.Z[12 + c, :, T - 2:T], writes=[("LXH", c, 1)], slow=True)
            S.flush()

    def phase_m2b(self, l):
        nc, S = self.nc, self.S
        with (nc.sbuf_tensor(self.un("m2bS"), [128, 8, 128], F32) as S32,
              nc.sbuf_tensor(self.un("m2bv"), [128, 4, 512], BF16) as vb,
              nc.sbuf_tensor(self.un("m2bk"), [128, 4, 4, 128], BF16) as kb,
              nc.sbuf_tensor(self.un("m2bkt"), [128, 2, 4, 128], BF16) as ktb):
            cnt = 0
            kcnt = 0
            for step in range(16):
                for d in range(2):
                    ci = step if d == 0 else 15 - step
                    t0 = CTX + 128 * ci
                    b4 = cnt % 4
                    cnt += 1
                    S.dma("sp", vb[:, b4, :], self.RV[t0:t0 + 128, :], writes=[("vb", b4)])
                    S.dma("sp", kb[:, b4, :, :], self.QK[:, 2 + d, :, t0:t0 + 128].rearrange("h p n -> p h n"),
                          writes=[("kb", b4)])
                    bk = kcnt % 2
                    kcnt += 1
                    for h in range(4):
                        S.tr(self.psb[bk][:, h * 128:(h + 1) * 128], kb[:, b4, h, :], self.ident[:],
                             reads=[("kb", b4), "ident"], writes=[("psb", bk)])
                    S.cp("act" if bk else "dve", ktb[:, bk, :, :],
                         self.psb[bk][:, 0:512].rearrange("p (h n) -> p h n", n=128),
                         reads=[("psb", bk)], writes=[("ktb", bk)])
                    for h in range(4):
                        kv, kk = self.nextps()
                        S.mm(kv[:, 0:128], ktb[:, bk, h, :], vb[:, b4, h * 128:(h + 1) * 128], True, True,
                             reads=[("ktb", bk), ("vb", b4)], writes=[kk])
                        si = d * 4 + h
                        gc = self.gC[:, si:si + 1]
                        if step == 0:
                            S.ts("dve", S32[:, si, :], kv[:, 0:128], gc, None, ALU.mult, ALU.bypass,
                                 reads=[kk, "gC"], writes=[("S32", si)])
                        else:
                            S.act(S32[:, si, :], S32[:, si, :], AF.Identity, scale=gc,
                                  reads=[("S32", si), "gC"], writes=[("S32", si)])
                            S.stt(S32[:, si, :], kv[:, 0:128], gc, S32[:, si, :], ALU.mult, ALU.add,
                                  reads=[kk, ("S32", si), "gC"], writes=[("S32", si)])
            for si in range(8):
                S.dma("sp", self.ex_w(EX_SL + si * 16384, 16384).rearrange("(p n) -> p n", n=128), S32[:, si, :],
                      reads=[("S32", si)], writes=[("EXSL", si)])
            S.flush()

    @staticmethod
    def rev(t, a, b):
        return t[:, slice(b - 1, a - 1 if a > 0 else None, -1)]

    def phase_lruA(self, l):
        nc, S = self.nc, self.S
        hb_ = DEPTH * PL + 24
        half, omh = self.pv[:, hb_:hb_ + 1], self.pv[:, hb_ + 1:hb_ + 2]
        big = lambda n: nc.sbuf_tensor(self.un(n), [128, T], F32)
        with (nc.sbuf_tensor(self.un("laxpc"), [128, CTX + 3], F32) as xpc,
              nc.sbuf_tensor(self.un("laxpl"), [128, LAT + 3], F32) as xpl,
              big("laxcv") as xcv, big("larg") as rg, big("laig") as ig, big("laa") as a_, big("lam") as m_,
              big("lau") as u_, big("lahf") as hf, big("lahb") as hb,
              nc.sbuf_tensor(self.un("laacf"), [128, LAT], F32) as acf,
              nc.sbuf_tensor(self.un("laacb"), [128, LAT], F32) as acb,
              nc.sbuf_tensor(self.un("lazero"), [128, LAT], F32) as zeros,
              nc.sbuf_tensor(self.un("laxb"), [128, T], BF16) as xb,
              nc.sbuf_tensor(self.un("lalw"), [128, 4, 128], BF16) as lw,
              nc.sbuf_tensor(self.un("lahl"), [128, 2, 4], F32) as hl,
              nc.sbuf_tensor(self.un("lah0"), [128, 2], F32) as h0,
              nc.sbuf_tensor(self.un("last"), [128, 2], F32) as stt_):
            S.memset("pool", zeros[:], 0.0, writes=["zeros"])
            slices = [(i * 512, min(512, T - i * 512)) for i in range(5)]
            for c in range(4):
                S.memset("dve", xpc[:, 0:1], 0.0, writes=["xpc"])
                S.memset("dve", xpc[:, CTX + 1:CTX + 3], 0.0, writes=["xpc"])
                S.dma("sp", xpc[:, 1:CTX + 1], self.Z[12 + c, :, 0:CTX], writes=["xpc"])
                S.dma("sp", xpl[:, 1:LAT + 1], self.Z[12 + c, :, CTX:T], writes=["xpl"])
                for r in range(2):
                    S.dma("sp", hl[:, r, :], self.ex_r(r, EX_LX + c * 512, 512).rearrange("(p n) -> p n", n=4),
                          writes=["hl"])
                S.ts("dve", xpl[:, 0:1], hl[:, 0, 3:4], half, None, ALU.mult, ALU.bypass, reads=["hl"], writes=["xpl"])
                S.ts("dve", xpl[:, LAT + 1:LAT + 3], hl[:, 1, 0:2], omh, None, ALU.mult, ALU.bypass, reads=["hl"],
                     writes=["xpl"])
                w = [self.pvl(l, 100 + j * 4 + c) for j in range(4)]
                bcv = self.pvl(l, 116 + c)
                for (src, sk, n, d0) in ((xpc, "xpc", CTX, 0), (xpl, "xpl", LAT, CTX)):
                    dst = xcv[:, d0:d0 + n]
                    S.ts("dve", dst, src[:, 0:n], w[0], bcv, ALU.mult, ALU.add, reads=[sk], writes=["xcv"])
                    for j in range(1, 4):
                        S.stt(dst, src[:, j:j + n], w[j], dst, ALU.mult, ALU.add, reads=[sk, "xcv"], writes=["xcv"])
                S.cp("act", xb[:], xcv[:], reads=["xcv"], writes=["xb"])
                for a in range(2):
                    for d in range(2):
                        S.dma("pool", lw[:, a * 2 + d, :], self.lbd[l, a, d, c], writes=[("lw", a * 2 + d)])
                for d in range(2):
                    for (s0, n) in slices:
                        p1, k1 = self.nextps()
                        S.mm(p1[:, 0:n], lw[:, d, :], xb[:, s0:s0 + n], True, True, reads=[("lw", d), "xb"], writes=[k1])
                        S.act(rg[:, s0:s0 + n], p1[:, 0:n], AF.Sigmoid, bias=self.pvl(l, 120 + d * 4 + c), reads=[k1],
                              writes=["rg"])
                        p2, k2 = self.nextps()
                        S.mm(p2[:, 0:n], lw[:, 2 + d, :], xb[:, s0:s0 + n], True, True, reads=[("lw", 2 + d), "xb"],
                             writes=[k2])
                        S.act(ig[:, s0:s0 + n], p2[:, 0:n], AF.Sigmoid, bias=self.pvl(l, 128 + d * 4 + c), reads=[k2],
                              writes=["ig"])
                    ci = d * 4 + c
                    S.act(a_[:], rg[:], AF.Exp, scale=self.lcp[:, ci:ci + 1], reads=["rg", "lcp"], writes=["a"])
                    S.act(m_[:], rg[:], AF.Exp, scale=self.lcp2[:, ci:ci + 1], reads=["rg", "lcp2"], writes=["m"])
                    S.act(m_[:], m_[:], AF.Sqrt, scale=-1.0, bias=1.0, reads=["m"], writes=["m"])
                    S.tt("dve", u_[:], ig[:], xcv[:], ALU.mult, reads=["ig", "xcv"], writes=["u"])
                    S.tt("pool", u_[:], u_[:], m_[:], ALU.mult, reads=["u", "m"], writes=["u"])
                    if d == 0:
                        S.scan(hf[:, 0:CTX], a_[:, 0:CTX], u_[:, 0:CTX], 0.0, reads=["a", "u"], writes=["hf"])
                        S.ts("dve", h0[:, 0:1], hf[:, CTX - 1:CTX], omh, None, ALU.mult, ALU.bypass, reads=["hf"],
                             writes=["h0f"])
                        S.scan(hf[:, CTX:T], a_[:, CTX:T], u_[:, CTX:T], h0[:, 0:1], reads=["a", "u", "h0f"], writes=["hf"])
                        S.scan(acf[:], a_[:, CTX:T], zeros[:], 1.0, reads=["a", "zeros"], writes=["acf"])
                        S.cp("dve", stt_[:, 0:1], hf[:, T - 1:T], reads=["hf"], writes=["st0"])
                    else:
                        rv = self.rev
                        S.scan(rv(hb, 0, CTX), rv(a_, 0, CTX), rv(u_, 0, CTX), 0.0, reads=["a", "u"], writes=["hb"])
                        S.ts("dve", h0[:, 1:2], hb[:, 0:1], half, None, ALU.mult, ALU.bypass, reads=["hb"], writes=["h0b"])
                        S.scan(rv(hb, CTX, T), rv(a_, CTX, T), rv(u_, CTX, T), h0[:, 1:2], reads=["a", "u", "h0b"],
                               writes=["hb"])
                        S.scan(rv(acb, 0, LAT), rv(a_, CTX, T), zeros[:], 1.0, reads=["a", "zeros"], writes=["acb"])
                        S.cp("dve", stt_[:, 1:2], hb[:, CTX:CTX + 1], reads=["hb"], writes=["st1"])
                S.tt("dve", hf[:], hf[:], hb[:], ALU.add, reads=["hf", "hb"], writes=["hf"])
                S.dma("sp", self.HS[c], hf[:], reads=["hf"], writes=[("HS", c)])
                S.dma("sp", self.ACF[c], acf[:], reads=["acf"], writes=[("ACF", c)])
                S.dma("sp", self.ACB[c], acb[:], reads=["acb"], writes=[("ACB", c)])
                for d in range(2):
                    lo = d * 512 + c * 128
                    S.dma("sp", self.EX2[lo:lo + 128].rearrange("(p o) -> p o", o=1), stt_[:, d:d + 1],
                          reads=["st%d" % d], writes=[("EX2", d, c)])
            S.flush()

    def phase_lruB(self, l):
        nc, S = self.nc, self.S
        hb_ = DEPTH * PL + 24
        half, omh = self.pv[:, hb_:hb_ + 1], self.pv[:, hb_ + 1:hb_ + 2]
        big = lambda n: nc.sbuf_tensor(self.un(n), [128, T], F32)
        with (big("lbhs") as hs, big("lblz") as lz, big("lbsq") as sq, big("lbin") as inn,
              nc.sbuf_tensor(self.un("lbacf"), [128, LAT], F32) as acf,
              nc.sbuf_tensor(self.un("lbacb"), [128, LAT], F32) as acb,
              nc.sbuf_tensor(self.un("lby"), [128, T], BF16) as y,
              nc.sbuf_tensor(self.un("lbdd"), [128, 4], F32) as dd):
            for c in range(4):
                S.dma("sp", hs[:], self.HS[c], writes=["hs"])
                S.dma("sp", acf[:], self.ACF[c], writes=["acf"])
                S.dma("sp", acb[:], self.ACB[c], writes=["acb"])
                S.dma("sp", lz[:], self.Z[16 + c], writes=["lz"])
                lo0 = 0 * EX2_N + 0 * 512 + c * 128
                lo1 = 1 * EX2_N + 1 * 512 + c * 128
                S.dma("sp", dd[:, 0:1], self.EX2G[lo0:lo0 + 128].rearrange("(p o) -> p o", o=1), writes=["dd0"])
                S.dma("sp", dd[:, 1:2], self.EX2G[lo1:lo1 + 128].rearrange("(p o) -> p o", o=1), writes=["dd1"])
                S.ts("dve", dd[:, 2:3], dd[:, 0:1], half, None, ALU.mult, ALU.bypass, reads=["dd0"], writes=["dd2"])
                S.ts("dve", dd[:, 3:4], dd[:, 1:2], omh, None, ALU.mult, ALU.bypass, reads=["dd1"], writes=["dd3"])
                S.stt(hs[:, CTX:T], acf[:], dd[:, 2:3], hs[:, CTX:T], ALU.mult, ALU.add, reads=["acf", "dd2", "hs"],
                      writes=["hs"])
                S.stt(hs[:, CTX:T], acb[:], dd[:, 3:4], hs[:, CTX:T], ALU.mult, ALU.add, reads=["acb", "dd3", "hs"],
                      writes=["hs"])
                S.act(sq[:], lz[:], AF.Square, reads=["lz"], writes=["sq"])
                S.ts("pool", sq[:], sq[:], 0.044715, 1.0, ALU.mult, ALU.add, reads=["sq"], writes=["sq"])
                S.tt("dve", inn[:], sq[:], lz[:], ALU.mult, reads=["sq", "lz"], writes=["inn"])
                S.act(inn[:], inn[:], AF.Sigmoid, scale=1.5957691216057308, reads=["inn"], writes=["inn"])
                S.tt("pool", inn[:], inn[:], lz[:], ALU.mult, reads=["inn", "lz"], writes=["inn"])
                S.tt("dve", y[:], inn[:], hs[:], ALU.mult, reads=["inn", "hs"], writes=["y"])
                S.dma("sp", self.YL[c], y[:], reads=["y"], writes=[("YL", c)])
            S.flush()

    def phase_ret(self, l):
        nc, S = self.nc, self.S
        hb_ = DEPTH * PL + 24
        half, omh = self.pv[:, hb_:hb_ + 1], self.pv[:, hb_ + 1:hb_ + 2]
        big = lambda n: nc.sbuf_tensor(self.un(n), [128, T], F32)
        with (nc.sbuf_tensor(self.un("rtqk"), [128, 4, T], BF16) as qk,
              nc.sbuf_tensor(self.un("rtv"), [128, 18, 128], BF16) as vt,
              nc.sbuf_tensor(self.un("rtkt"), [128, 2, 18, 128], BF16) as kt,
              big("rtacc") as acc, big("rtyc") as yc, big("rtsq") as sq32, big("rtrs") as rs, big("rtrg") as rgz,
              nc.sbuf_tensor(self.un("rtS"), [128, 2, 128], F32) as S32,
              nc.sbuf_tensor(self.un("rtSo"), [128, 2, 128], F32) as So,
              nc.sbuf_tensor(self.un("rtSb"), [128, 2, 2, 128], BF16) as Sb,
              nc.sbuf_tensor(self.un("rtpm"), [128, 4, 128], BF16) as pm,
              nc.sbuf_tensor(self.un("rty"), [128, T], BF16) as y):
            slices = [(i * 512, min(512, T - i * 512)) for i in range(5)]
            order = [[0, 1] + list(range(2, 18)), [1, 0] + list(range(17, 1, -1))]
            masks = [self.maskf, self.maskb]
            pcnt = 0
            for h in range(4):
                S.dma("sp", qk[:], self.QK[h].rearrange("s p n -> p s n"), writes=["qk"])
                S.dma("sp", vt[:], self.RV[:, h * 128:(h + 1) * 128].rearrange("(c p) e -> p c e", p=128), writes=["vt"])
                S.dma("sp", rgz[:], self.Z[8 + h], writes=["rgz"])
                for d in range(2):
                    r = d
                    S.dma("sp", So[:, d, :], self.ex_r(r, EX_SL + (d * 4 + h) * 16384, 16384).rearrange(
                        "(p n) -> p n", n=128), writes=[("So", d)])
                    S.memset("dve", S32[:, d, :], 0.0, writes=[("S32", d)])
                    S.memset("pool", Sb[:, d, 0, :], 0.0, writes=[("Sb", d, 0)])
                tcnt = 0
                for d in range(2):
                    for c0 in range(0, 18, 4):
                        ng = min(4, 18 - c0)
                        bk = tcnt % 2
                        tcnt += 1
                        for j in range(ng):
                            ci = c0 + j
                            S.tr(self.psb[bk][:, j * 128:(j + 1) * 128], qk[:, 2 + d, ci * 128:(ci + 1) * 128], self.ident[:],
                                 reads=["qk", "ident"], writes=[("psb", bk)])
                        S.cp("act" if bk else "dve", kt[:, d, c0:c0 + ng, :],
                             self.psb[bk][:, 0:ng * 128].rearrange("p (h n) -> p h n", n=128),
                             reads=[("psb", bk)], writes=[("kt", d, c0 + j) for j in range(ng)])
                written = set()
                sbi = [0, 0]
                for step in range(18):
                    for d in range(2):
                        ci = order[d][step]
                        cs = slice(ci * 128, (ci + 1) * 128)
                        si = d * 4 + h
                        if step == 2:
                            S.ts("dve", S32[:, d, :], S32[:, d, :], self.bc1[:, si:si + 1], None, ALU.mult, ALU.bypass,
                                 reads=[("S32", d), "bc1"], writes=[("S32", d)])
                            S.stt(S32[:, d, :], So[:, d, :], (half if d == 0 else omh), S32[:, d, :], ALU.mult, ALU.add,
                                  reads=[("So", d), ("S32", d)], writes=[("S32", d)])
                            nb = 1 - sbi[d]
                            S.cp("act", Sb[:, d, nb, :], S32[:, d, :], reads=[("S32", d)], writes=[("Sb", d, nb)])
                            sbi[d] = nb
                        sc, ksc = self.nextps()
                        S.mm(sc[:, 0:128], qk[:, 2 + d, cs], qk[:, d, cs], True, True, reads=["qk"], writes=[ksc])
                        p4 = pcnt % 4
                        pcnt += 1
                        S.tt("dve", pm[:, p4, :], sc[:, 0:128], masks[d][:], ALU.mult, reads=[ksc, "maskf", "maskb"],
                             writes=[("pm", p4)])
                        o, ko = self.nextps()
                        S.mm(o[:, 0:128], vt[:, ci, :], pm[:, p4, :], True, False, reads=["vt", ("pm", p4)], writes=[ko])
                        S.mm(o[:, 0:128], Sb[:, d, sbi[d], :], qk[:, d, cs], False, True, reads=[("Sb", d, sbi[d]), "qk"],
                             writes=[ko])
                        if ci not in written:
                            S.cp("act", acc[:, cs], o[:, 0:128], reads=[ko], writes=[("acc", ci)])
                            written.add(ci)
                        else:
                            S.tt("dve", acc[:, cs], o[:, 0:128], acc[:, cs], ALU.add, reads=[ko, ("acc", ci)],
                                 writes=[("acc", ci)])
                        kv, kkv = self.nextps()
                        S.mm(kv[:, 0:128], kt[:, d, ci, :], vt[:, ci, :], True, True, reads=[("kt", d, ci), "vt"],
                             writes=[kkv])
                        gc = self.gC[:, si:si + 1]
                        S.act(S32[:, d, :], S32[:, d, :], AF.Identity, scale=gc,
                              reads=[("S32", d), "gC"], writes=[("S32", d)])
                        S.stt(S32[:, d, :], kv[:, 0:128], gc, S32[:, d, :], ALU.mult, ALU.add,
                              reads=[kkv, ("S32", d), "gC"], writes=[("S32", d)])
                        nb = 1 - sbi[d]
                        S.cp("act", Sb[:, d, nb, :], S32[:, d, :], reads=[("S32", d)], writes=[("Sb", d, nb)])
                        sbi[d] = nb
                acck = [("acc", ci) for ci in range(18)]
                S.act(rgz[:], rgz[:], AF.Silu, reads=["rgz"], writes=["rgz"])
                for (s0, n) in slices:
                    pmn, kmn = self.nextps()
                    S.mm(pmn[:, 0:n], self.ones32[:], acc[:, s0:s0 + n], True, True, reads=acck + ["ones32"], writes=[kmn])
                    S.stt(yc[:, s0:s0 + n], pmn[:, 0:n], -1.0 / 128, acc[:, s0:s0 + n], ALU.mult, ALU.add,
                          reads=[kmn] + acck, writes=[("yc", s0)])
                    S.act(sq32[:, s0:s0 + n], yc[:, s0:s0 + n], AF.Square, reads=[("yc", s0)], writes=[("sq32", s0)])
                    pvr, kvr = self.nextps()
                    S.mm(pvr[:, 0:n], self.ones32[:], sq32[:, s0:s0 + n], True, True, reads=[("sq32", s0)], writes=[kvr])
                    S.act(rs[:, s0:s0 + n], pvr[:, 0:n], AF.Sqrt, scale=1.0 / 128, bias=self.epsb[:, 0:1], reads=[kvr],
                          writes=[("rs", s0)])
                    S.recip(rs[:, s0:s0 + n], rs[:, s0:s0 + n], reads=[("rs", s0)], writes=[("rs", s0)])
                    S.tt("pool", yc[:, s0:s0 + n], yc[:, s0:s0 + n], rs[:, s0:s0 + n], ALU.mult,
                         reads=[("yc", s0), ("rs", s0)], writes=[("yc", s0)])
                    S.stt(y[:, s0:s0 + n], yc[:, s0:s0 + n], self.pvl(l, 96 + h), rgz[:, s0:s0 + n], ALU.mult, ALU.mult,
                          reads=[("yc", s0), "rgz"], writes=[("y", s0)])
                S.dma("sp", self.YR[h], y[:], reads=[("y", s0) for (s0, n) in slices], writes=[("YR", h)])
            S.flush()

    def phase_attn(self, l):
        nc, S = self.nc, self.S
        NK = CTX + 2 * LAT
        with (nc.sbuf_tensor(self.un("atk"), [128, 2, NK], BF16) as kT,
              nc.sbuf_tensor(self.un("atv"), [128, 2, 2, 34, 128], BF16) as vv,
              nc.sbuf_tensor(self.un("atq"), [128, 2, 2, T], BF16) as qa,
              nc.sbuf_tensor(self.un("aty"), [128, 2, T], BF16) as ya,
              nc.sbuf_tensor(self.un("atp"), [128, 4, 512], BF16) as pt,
              nc.sbuf_tensor(self.un("ato"), [128, 2, 128], BF16) as onz,
              nc.sbuf_tensor(self.un("atr"), [128, 2, 512], F32) as rden):
            S.memset("pool", qa[:], 0.0, writes=[("qa", 0), ("qa", 1)])
            S.memset("pool", vv[:], 0.0, writes=[("vv", 0), ("vv", 1)])
            S.memset("dve", onz[:], 0.0, writes=["onz"])
            S.memset("dve", onz[:, 0, 0:64], 1.0, writes=["onz"])
            S.memset("dve", onz[:, 1, 64:128], 1.0, writes=["onz"])
            for g in range(2):
                for hh in range(2):
                    ps_ = slice(hh * 64, (hh + 1) * 64)
                    S.dma("pool", kT[ps_, g, 0:CTX], self.KAC[g * 64:(g + 1) * 64, :], writes=[("kT", g)])
                    for r in range(2):
                        for j in range(4):
                            src = self.EXGP[j].ap()[r * 128 + g * 64:r * 128 + (g + 1) * 64, :]
                            c0 = CTX + r * LAT + j * 512
                            S.dma("pool", kT[ps_, g, c0:c0 + 512], src, writes=[("kT", g)])
                    cs_ = slice(hh * 64, (hh + 1) * 64)
                    S.dma("pool", vv[:, g, hh, 0:2, cs_],
                          self.AVC.rearrange("(c p) e -> p c e", p=128)[:, :, g * 64:(g + 1) * 64], writes=[("vv", g)])
                    for r in range(2):
                        for j in range(4):
                            src = self.ex_r(r, EX_AV + j * 65536, 65536).rearrange("(c p e) -> p c e", p=128, e=128)
                            c0 = 2 + r * 16 + j * 4
                            S.dma("pool", vv[:, g, hh, c0:c0 + 4, cs_], src[:, :, g * 64:(g + 1) * 64],
                                  writes=[("vv", g)])
            ones64 = self.ones[:, 0:64]
            qtiles = [(0, CTX, [0, 1])] + [(CTX + 512 * i, 512, list(range(34))) for i in range(4)]
            its = []
            qcnt = 0
            for c in range(4):
                for qi, (q0, nq, keys) in enumerate(qtiles):
                    pq = qcnt % 2
                    qcnt += 1
                    for kc in keys:
                        for hh in range(2):
                            its.append(dict(c=c, q0=q0, nq=nq, kc=kc, hh=hh, pq=pq, first=(kc == keys[0]),
                                            last=(kc == keys[-1]), qend=(kc == keys[-1] and hh == 1),
                                            cend=(kc == keys[-1] and hh == 1 and qi == len(qtiles) - 1),
                                            cstart=(kc == keys[0] and hh == 0 and qi == 0)))

            def emit_qk(i):
                it = its[i]
                c, g, hh, nq, q0, kc = it["c"], it["c"] // 2, it["hh"], it["nq"], it["q0"], it["kc"]
                if it["cstart"]:
                    for h2 in range(2):
                        S.dma("sp", qa[h2 * 64:(h2 + 1) * 64, c % 2, h2, :], self.QA[c, h2 * 64:(h2 + 1) * 64, :],
                              writes=[("qa", c % 2)])
                si, p4 = i % 2, i % 4
                sp_, ks = self.ps[si], ("ps", si)
                S.mm(sp_[:, 0:nq], kT[:, g, kc * 128:(kc + 1) * 128], qa[:, c % 2, hh, q0:q0 + nq], True, True,
                     reads=[("kT", g), ("qa", c % 2)], writes=[ks])
                S.act(pt[:, p4, 0:nq], sp_[:, 0:nq], AF.Exp, scale=0.125, reads=[ks], writes=[("pt", p4)])

            def emit_pv(i):
                it = its[i]
                c, g, hh, nq, q0, kc, pq = it["c"], it["c"] // 2, it["hh"], it["nq"], it["q0"], it["kc"], it["pq"]
                ps_ = slice(hh * 64, (hh + 1) * 64)
                p4 = i % 4
                num, knum = self.ps[2 + 2 * pq], ("ps", 2 + 2 * pq)
                den, kden = self.ps[3 + 2 * pq], ("ps", 3 + 2 * pq)
                st_ = it["first"] and hh == 0
                sp2 = it["last"] and hh == 1
                S.mm(num[:, 0:nq], vv[:, g, hh, kc, :], pt[:, p4, 0:nq], st_, sp2,
                     reads=[("vv", g), ("pt", p4)], writes=[knum])
                S.mm(den[:, 0:nq], onz[:, hh, :], pt[:, p4, 0:nq], st_, sp2,
                     reads=["onz", ("pt", p4)], writes=[kden])
                if it["qend"]:
                    S.recip(rden[:, pq, 0:nq], den[:, 0:nq], reads=[kden], writes=[("rden", pq)])
                    S.tt("dve", ya[:, c % 2, q0:q0 + nq], num[:, 0:nq], rden[:, pq, 0:nq], ALU.mult,
                         reads=[knum, ("rden", pq)], writes=[("ya", c % 2)])
                if it["cend"]:
                    S.dma("sp", self.YA[c], ya[:, c % 2, :], reads=[("ya", c % 2)], writes=[("YA", c)])

            emit_qk(0)
            for i in range(len(its)):
                if i + 1 < len(its):
                    emit_qk(i + 1)
                emit_pv(i)
            S.flush()

    def phase_merge(self, l):
        nc, S = self.nc, self.S
        with (nc.sbuf_tensor(self.un("mgwg"), [128, 8, 3072], BF16) as wg,
              nc.sbuf_tensor(self.un("mgwb"), [128, 12, D], BF16) as wb,
              nc.sbuf_tensor(self.un("mgwo"), [128, 8, D], BF16) as wo,
              nc.sbuf_tensor(self.un("mgx0"), [128, 8, TN], F32) as xt0,
              nc.sbuf_tensor(self.un("mgx1"), [128, 8, TN], F32) as xt1,
              nc.sbuf_tensor(self.un("mgh0"), [128, 8, TN], BF16) as h0,
              nc.sbuf_tensor(self.un("mgh1"), [128, 8, TN], BF16) as h1,
              nc.sbuf_tensor(self.un("mgy0"), [128, 12, TN], BF16) as y0,
              nc.sbuf_tensor(self.un("mgy1"), [128, 12, TN], BF16) as y1,
              nc.sbuf_tensor(self.un("mgsg"), [128, 3, TN], F32) as sg,
              nc.sbuf_tensor(self.un("mgtm"), [128, 2, TN], F32) as tm,
              nc.sbuf_tensor(self.un("mgma"), [128, 2, TN], F32) as ma,
              nc.sbuf_tensor(self.un("mgm"), [128, 8, TN], BF16) as m):
            xts, hs, ys = [xt0, xt1], [h0, h1], [y0, y1]
            src = self.w_in[l].rearrange("(k p) c -> p k c", p=128)
            for pc in range(8):
                S.dma("pool", wg[:, :, pc * 384:(pc + 1) * 384], src[:, :, 3840 + pc * 384:3840 + (pc + 1) * 384],
                      writes=[("wg", pc)])
            for n in range(3):
                S.dma("pool", wb[:, n * 4:(n + 1) * 4, :], self.w_branch[l, n].rearrange("(k p) c -> p k c", p=128),
                      writes=[("wb", n)])
            srco = self.w_out[l].rearrange("(k p) c -> p k c", p=128)
            for kk in range(2):
                S.dma("pool", wo[:, kk * 4:(kk + 1) * 4, :], srco[:, kk * 4:(kk + 1) * 4, :], writes=[("wo", kk)])
            ysrc = [self.YR, self.YL, self.YA]

            def loads(t):
                par = t % 2
                t0 = t * TN
                self.load_x(xts[par], t, par)
                S.dma("sp", hs[par][:], self.H[:, :, t0:t0 + TN].rearrange("k p n -> p k n"), writes=[("h", par)])
                for n in range(3):
                    S.dma("sp", ys[par][:, n * 4:(n + 1) * 4, :], ysrc[n][:, :, t0:t0 + TN].rearrange("k p n -> p k n"),
                          writes=[("y3", par, n)])

            loads(0)
            scnt = 0
            for t in range(NT):
                par = t % 2
                xt, h, y3 = xts[par], hs[par], ys[par]
                c = 1 if t == 0 else 0
                if t + 1 < NT:
                    loads(t + 1)
                for i in range(8):
                    mi = i % 2
                    for n in range(3):
                        pg, kg = self.nextps()
                        wc = n * 8 + i
                        for k in range(8):
                            S.mm(pg[:, 0:TN], wg[:, k, wc * 128:(wc + 1) * 128], h[:, k, :], k == 0, k == 7,
                                 reads=[("wg", wc // 3), ("h", par)], writes=[kg])
                        pu, ku = self.nextps()
                        for kk in range(4):
                            S.mm(pu[:, 0:TN], wb[:, n * 4 + kk, i * 128:(i + 1) * 128], y3[:, n * 4 + kk, :], kk == 0, kk == 3,
                                 reads=[("wb", n), ("y3", par, n)], writes=[ku])
                        s3 = scnt % 3
                        scnt += 1
                        S.act(sg[:, s3, :], pg[:, 0:TN], AF.Sigmoid, reads=[kg], writes=[("sg", s3)])
                        if n == 0:
                            S.tt("dve", ma[:, mi, :], sg[:, s3, :], pu[:, 0:TN], ALU.mult, reads=[("sg", s3), ku],
                                 writes=[("ma", mi)])
                        else:
                            S.tt("dve", tm[:, n - 1, :], sg[:, s3, :], pu[:, 0:TN], ALU.mult, reads=[("sg", s3), ku],
                                 writes=[("tm", n - 1)])
                            if n == 1:
                                S.tt("pool", ma[:, mi, :], ma[:, mi, :], tm[:, 0, :], ALU.add,
                                     reads=[("ma", mi), ("tm", 0)], writes=[("ma", mi)])
                            else:
                                S.tt("pool", m[:, i, :], ma[:, mi, :], tm[:, 1, :], ALU.add,
                                     reads=[("ma", mi), ("tm", 1)], writes=[("m", i)])
                for i in range(8):
                    po, ko = self.nextps()
                    for k in range(8):
                        S.mm(po[:, 0:TN], wo[:, k, i * 128:(i + 1) * 128], m[:, k, :], k == 0, k == 7,
                             reads=[("wo", k // 4), ("m", k)], writes=[ko])
                    S.stt(xt[:, i, :], po[:, 0:TN], self.modG[:, 1, i, c:c + 1], xt[:, i, :], ALU.mult, ALU.add,
                          reads=[ko, ("xt", par, i), ("modG", 1)], writes=[("xt", par, i)])
                self.store_x(xt, t, par)
            S.flush()


def _perm128():
    return np.concatenate([np.arange(32, 64), np.arange(0, 32), np.arange(96, 128), np.arange(64, 96)])


def _perm64():
    return np.concatenate([np.arange(16, 32), np.arange(0, 16), np.arange(48, 64), np.arange(32, 48)])


def _fm(v):
    v = np.asarray(v, np.float32)
    lead = v.shape[:-1]
    n = v.shape[-1] // 128
    v = v.reshape(*lead, n, 128)
    return np.moveaxis(v, -1, 0)


def _rope_tables(half):
    theta = 10000.0
    tl = np.arange(LAT) + half * LAT
    rows = (tl // 64).astype(np.float32)
    cols = (tl % 64).astype(np.float32)
    tabs = np.zeros((4, 128, T), np.float32)
    tabs[0, :, :CTX] = 1.0
    tabs[2, :, :CTX] = 1.0
    f = (theta ** (-np.arange(0, 64, 2, dtype=np.float32) / 64)).astype(np.float32)
    ar = (rows[None, :] * f[:, None]).astype(np.float32)
    ac = (cols[None, :] * f[:, None]).astype(np.float32)
    C = np.concatenate([np.cos(ar), np.cos(ar), np.cos(ac), np.cos(ac)], 0)
    Sn = np.concatenate([-np.sin(ar), np.sin(ar), -np.sin(ac), np.sin(ac)], 0)
    tabs[0, :, CTX:] = C
    tabs[1, :, CTX:] = Sn
    f = (theta ** (-np.arange(0, 32, 2, dtype=np.float32) / 32)).astype(np.float32)
    ar = (rows[None, :] * f[:, None]).astype(np.float32)
    ac = (cols[None, :] * f[:, None]).astype(np.float32)
    C = np.concatenate([np.cos(ar), np.cos(ar), np.cos(ac), np.cos(ac)], 0)
    Sn = np.concatenate([-np.sin(ar), np.sin(ar), -np.sin(ac), np.sin(ac)], 0)
    tabs[2, :, CTX:] = np.concatenate([C, C], 0)
    tabs[3, :, CTX:] = np.concatenate([Sn, Sn], 0)
    return tabs


def _consts():
    cst = np.zeros((6, 128, 128), np.float32)
    cst[0] = np.eye(128)
    cst[1] = 1.0
    cst[2, :64, :64] = 1.0
    cst[2, 64:, 64:] = 1.0
    j = np.arange(128)[:, None]
    i = np.arange(128)[None, :]
    cst[3] = (i >= j)
    cst[4] = (i <= j)
    p = np.arange(TN) % 128
    pos = np.zeros((2, 128, TN), np.float32)
    pos[0] = (p + 1)[None, :]
    pos[1] = (128 - p)[None, :]
    return cst, pos


def _pack_pv(inp, b, half):
    pv = np.zeros((128, NPV), np.float32)
    p64 = _perm64()
    for l in range(DEPTH):
        o = l * PL
        pv[:, o:o + 24] = _fm(inp["norm_g"][l]).reshape(128, 24)
        pv[:, o + 24:o + 96] = _fm(inp["b_mod"][l]).reshape(128, 72)
        pv[:, o + 96:o + 100] = _fm(inp["ret_norm_g"][l])
        pv[:, o + 100:o + 116] = _fm(inp["lru_conv_w"][l]).reshape(128, 16)
        pv[:, o + 116:o + 120] = _fm(inp["lru_conv_b"][l])
        pv[:, o + 120:o + 128] = _fm(inp["lru_b_a"][l]).reshape(128, 8)
        pv[:, o + 128:o + 136] = _fm(inp["lru_b_x"][l]).reshape(128, 8)
        pv[:, o + 136:o + 144] = _fm(inp["lru_lambda"][l]).reshape(128, 8)
        qg = np.asarray(inp["attn_q_norm_g"][l], np.float32)
        kg = np.asarray(inp["attn_k_norm_g"][l], np.float32)
        pv[:, o + 144] = np.tile(qg, 2)
        pv[:, o + 145] = np.tile(qg[p64], 2)
        pv[:, o + 146] = np.tile(kg, 2)
        pv[:, o + 147] = np.tile(kg[p64], 2)
        pv[:, o + 148:o + 156] = np.asarray(inp["ret_decay_logit"][l], np.float32).reshape(1, 8)
    o = DEPTH * PL
    pv[:, o:o + 8] = _fm(inp["final_norm_g"])
    pv[:, o + 8:o + 16] = _fm(inp["c"][b])
    pv[:, o + 16:o + 24] = _fm(inp["c_ctx"])
    pv[:, o + 24] = float(half)
    pv[:, o + 25] = float(1 - half)
    return pv


def _host_inputs(inp):
    f = lambda a: np.ascontiguousarray(np.asarray(a, np.float32))
    cst, pos = _consts()
    w_in = f(inp["w_in"])
    p128, p64 = _perm128(), _perm64()
    idx = []
    for c in range(8):
        idx.append(c * 128 + p128)
    for c in range(5):
        for hh in range(2):
            idx.append(3072 + c * 128 + hh * 64 + p64)
    idx = np.concatenate(idx)
    w_inp = np.ascontiguousarray(w_in[:, :, idx])
    lbd = np.zeros((DEPTH, 2, 2, 4, 128, 128), np.float32)
    for a, name in enumerate(("lru_w_a", "lru_w_x")):
        w = f(inp[name])
        for c in range(4):
            lbd[:, a, :, c, :64, :64] = w[:, :, 2 * c]
            lbd[:, a, :, c, 64:, 64:] = w[:, :, 2 * c + 1]
    shared = {
        "cst": cst, "pos": pos, "w_mod": f(inp["w_mod"][:N_LAYERS]), "ffn_w_in": f(inp["ffn_w_in"][:N_LAYERS]),
        "ffn_w_out": f(inp["ffn_w_out"][:N_LAYERS]), "w_in": w_in[:N_LAYERS], "w_inp": w_inp[:N_LAYERS],
        "lbd": lbd[:N_LAYERS], "w_branch": f(inp["w_branch"][:N_LAYERS]), "w_out": f(inp["w_out"][:N_LAYERS]),
    }
    x = f(inp["x"])
    ctx = f(inp["ctx"])
    ropes = [_rope_tables(0), _rope_tables(1)]
    maps = []
    for core in range(8):
        b, half = core // 2, core % 2
        xt = np.concatenate([ctx[b], x[b, half * LAT:(half + 1) * LAT]], 0)
        xin = np.ascontiguousarray(xt.T.reshape(8, 128, T))
        m = dict(shared)
        m["xin"] = xin
        m["pv"] = _pack_pv(inp, b, half)
        m["rope"] = ropes[half]
        maps.append(m)
    return maps


_NC_CACHE = {}


def _get_nc():
    key = (DEBUG_STOP, N_LAYERS)
    if key not in _NC_CACHE:
        nc = bass.Bass("TRN2", target_bir_lowering=False)
        Builder(nc).build()
        _NC_CACHE[key] = nc
    return _NC_CACHE[key]


def kernel(**inputs):
    maps = _host_inputs(inputs)
    nc = _get_nc()
    if TRACE:
        res = run_bass_kernel_spmd(nc, maps, core_ids=list(range(8)), trace=True)
        print("exec_time_ns", res.exec_time_ns)
    else:
        res = run_bass_kernel_spmd(nc, maps, core_ids=list(range(8)))
    if DEBUG_STOP is not None:
        return [r["dbg"] for r in res.results]
    out = np.zeros((4, 2 * LAT, D), np.float32)
    for core in range(8):
        b, half = core // 2, core % 2
        o = res.results[core]["out"]
        out[b, half * LAT:(half + 1) * LAT] = o.reshape(D, LAT).T
    return out
```

```python
import numpy as np
import concourse.bass as bass
import concourse.mybir as mybir
from concourse.bass_utils import run_bass_kernel_spmd

F32 = mybir.dt.float32
BF16 = mybir.dt.bfloat16
AF = mybir.ActivationFunctionType
ALU = mybir.AluOpType

D = 1024
DEPTH = 4
CTX = 256
LAT = 2048
T = CTX + LAT
TN = 256
NT = T // TN
DFF = 2816
DIN = 6912
EPS = 1e-6
NZ = 38
PL = 156
NPV = DEPTH * PL + 8 + 16 + 2
EX_KA = 0
EX_AV = EX_KA + 128 * LAT
EX_SL = EX_AV + LAT * 128
EX_LX = EX_SL + 8 * 128 * 128
EX_N = EX_LX + 4 * 128 * 4
EX2_N = 2 * 512

DEBUG_STOP = None
N_LAYERS = DEPTH
TRACE = False


class _I:
    __slots__ = ("eng", "fn", "waits", "sig", "idx", "kind", "sem", "target")


class Sched:
    R = 8

    def __init__(self, nc, sems):
        self.nc = nc
        self.sems = sems
        nxt = iter(range(len(sems)))
        self.csem = {e: next(nxt) for e in ("pe", "act", "dve", "pool")}
        self.qsem = {q: [next(nxt) for _ in range(self.R)] for q in ("sp", "pool")}
        self.ccsem = next(nxt)
        self.ncc = 0
        self.qn = {"sp": 0, "pool": 0}
        self.cnt = {e: 0 for e in self.csem}
        self.lists = {e: [] for e in ("pe", "act", "dve", "pool", "sp")}
        self.lastw = {}
        self.rd = {}
        self.seen = {e: {} for e in self.lists}
        self.barrier = []
        self.ninstr = 0

    def add(self, eng, fn, reads=(), writes=(), kind="c"):
        I = _I()
        I.eng, I.fn, I.kind, I.sig, I.idx, I.waits = eng, fn, kind, False, None, []
        deps = []
        for b in reads:
            w = self.lastw.get(b)
            if w is not None:
                deps.append(w)
        for b in writes:
            w = self.lastw.get(b)
            if w is not None:
                deps.append(w)
            r = self.rd.get(b)
            if r:
                deps.extend(r[0].values())
                deps.extend(r[1])
        for J in deps:
            if J.kind in ("dma", "cc"):
                I.waits.append(J)
            elif J.eng == eng and kind == "c":
                if eng == "pe":
                    continue
                I.waits.append(J)
                J.sig = True
            else:
                I.waits.append(J)
                J.sig = True
        if kind == "dma":
            n = self.qn[eng]
            self.qn[eng] += 1
            I.sem = self.qsem[eng][n % self.R]
            I.target = 16 * (n // self.R + 1)
            if n >= self.R:
                I.waits.append((I.sem, 16 * (n // self.R)))
        elif kind == "cc":
            I.sem = self.ccsem
            self.ncc += 1
            I.target = self.ncc
        for b in reads:
            r = self.rd.setdefault(b, ({}, []))
            if kind == "c":
                r[0][eng] = I
            else:
                r[1].append(I)
        for b in writes:
            self.lastw[b] = I
            self.rd[b] = ({}, [])
        self.lists[eng].append(I)
        self.ninstr += 1
        return I

    def dma(self, q, out, in_, reads=(), writes=(), slow=False):
        if slow:
            return self.add(q, lambda e: e.dma_start(out=out, in_=in_, allow_slow_non_contiguous=True), reads, writes,
                            kind="dma")
        return self.add(q, lambda e: e.dma_start(out=out, in_=in_), reads, writes, kind="dma")

    def mm(self, out, lhsT, rhs, start, stop, reads=(), writes=()):
        return self.add("pe", lambda e: e.matmul(out, lhsT=lhsT, rhs=rhs, start=start, stop=stop), reads, writes)

    def tr(self, out, in_, ident, reads=(), writes=()):
        return self.add("pe", lambda e: e.transpose(out, in_, ident), reads, writes)

    def act(self, out, in_, func, reads=(), writes=(), bias=None, scale=None):
        kw = {}
        if bias is not None:
            kw["bias"] = bias
        if scale is not None:
            kw["scale"] = scale
        return self.add("act", lambda e: e.activation(out=out, in_=in_, func=func, **kw), reads, writes)

    def tt(self, eng, out, in0, in1, op, reads=(), writes=()):
        return self.add(eng, lambda e: e.tensor_tensor(out=out, in0=in0, in1=in1, op=op), reads, writes)

    def ts(self, eng, out, in0, s1, s2, op0, op1, reads=(), writes=()):
        return self.add(eng, lambda e: e.tensor_scalar(out=out, in0=in0, scalar1=s1, scalar2=s2, op0=op0, op1=op1),
                        reads, writes)

    def stt(self, out, in0, scalar, in1, op0, op1, reads=(), writes=()):
        return self.add("dve", lambda e: e.scalar_tensor_tensor(out=out, in0=in0, scalar=scalar, in1=in1,
                                                                op0=op0, op1=op1), reads, writes)

    def cp(self, eng, out, in_, reads=(), writes=()):
        if eng == "act":
            return self.add("act", lambda e: e.copy(out=out, in_=in_), reads, writes)
        return self.add(eng, lambda e: e.tensor_copy(out=out, in_=in_), reads, writes)

    def recip(self, out, in_, reads=(), writes=()):
        return self.add("dve", lambda e: e.reciprocal(out=out, in_=in_), reads, writes)

    def memset(self, eng, ap, val, reads=(), writes=()):
        return self.add(eng, lambda e: e.memset(ap, val), reads, writes)

    def scan(self, out, d0, d1, init, reads=(), writes=()):
        return self.add("dve", lambda e: e.tensor_tensor_scan(out=out, data0=d0, data1=d1, initial=init,
                                                              op0=ALU.mult, op1=ALU.add), reads, writes)

    def cc(self, ins_ap, outs_ap, reads=(), writes=()):
        groups = [[0, 1], [2, 3], [4, 5], [6, 7]]
        return self.add("pool", lambda e: e.collective_compute("AllGather", ALU.bypass, replica_groups=groups,
                                                               ins=[ins_ap], outs=[outs_ap]),
                        reads, writes, kind="cc")

    def flush(self):
        nc = self.nc
        for e in self.csem:
            last = None
            for I in self.lists[e]:
                if I.kind == "c":
                    last = I
            if last is not None:
                last.sig = True
            for I in self.lists[e]:
                if I.kind == "c" and I.sig:
                    self.cnt[e] += 1
                    I.idx = self.cnt[e]
        sems = self.sems

        def emit(eh, eng):
            seen = self.seen[eng]

            def wait(si, val):
                if seen.get(si, 0) < val:
                    eh.wait_ge(sems[si], val)
                    seen[si] = val

            for (si, val) in self.barrier:
                wait(si, val)
            for I in self.lists[eng]:
                for w in I.waits:
                    if isinstance(w, tuple):
                        wait(w[0], w[1])
                    elif w.kind in ("dma", "cc"):
                        wait(w.sem, w.target)
                    else:
                        wait(self.csem[w.eng], w.idx)
                ins = I.fn(eh)
                if I.kind == "dma":
                    ins.then_inc(sems[I.sem], 16)
                elif I.kind == "cc":
                    ins.then_inc(sems[I.sem])
                elif I.sig:
                    ins.then_inc(sems[self.csem[eng]], 1)

        with nc.Block() as block:
            @block.tensor
            def _(eh):
                emit(eh, "pe")

            @block.scalar
            def _(eh):
                emit(eh, "act")

            @block.vector
            def _(eh):
                emit(eh, "dve")

            @block.gpsimd
            def _(eh):
                emit(eh, "pool")

            @block.sync
            def _(eh):
                emit(eh, "sp")

        bar = [(self.csem[e], self.cnt[e]) for e in self.csem if self.cnt[e] > 0]
        for q in ("sp", "pool"):
            n = self.qn[q]
            for i in range(self.R):
                if n > i:
                    bar.append((self.qsem[q][i], 16 * ((n - 1 - i) // self.R + 1)))
        if self.ncc:
            bar.append((self.ccsem, self.ncc))
        self.barrier = bar
        self.lists = {e: [] for e in self.lists}
        self.lastw = {}
        self.rd = {}

    def final_wait(self):
        nc = self.nc
        sems = self.sems
        with nc.Block() as block:
            @block.sync
            def _(eh):
                for (si, val) in self.barrier:
                    eh.wait_ge(sems[si], val)

            @block.gpsimd
            def _(eh):
                for (si, val) in self.barrier:
                    eh.wait_ge(sems[si], val)


class Builder:
    def __init__(self, nc):
        self.nc = nc
        self.pscur = 0

    def declare(self):
        nc = self.nc
        di = lambda n, s: nc.dram_tensor(n, s, F32, kind="ExternalInput").ap()
        self.xin = di("xin", [8, 128, T])
        self.pvin = di("pv", [128, NPV])
        self.cst = di("cst", [6, 128, 128])
        self.posin = di("pos", [2, 128, TN])
        self.rope = di("rope", [4, 128, T])
        self.w_mod = di("w_mod", [N_LAYERS, D, 9 * D])
        self.ffn_w_in = di("ffn_w_in", [N_LAYERS, 2, D, 2 * DFF])
        self.ffn_w_out = di("ffn_w_out", [N_LAYERS, 2, DFF, D])
        self.w_in = di("w_in", [N_LAYERS, D, DIN])
        self.w_inp = di("w_inp", [N_LAYERS, D, 13 * 128])
        self.lbd = di("lbd", [N_LAYERS, 2, 2, 4, 128, 128])
        self.w_branch = di("w_branch", [N_LAYERS, 3, 512, D])
        self.w_out = di("w_out", [N_LAYERS, D, D])
        self.out = nc.dram_tensor("out", [8, 128, LAT], F32, kind="ExternalOutput").ap()
        if DEBUG_STOP is not None:
            self.dbg = nc.dram_tensor("dbg", [8, 128, T], F32, kind="ExternalOutput").ap()
        dt = lambda n, s, d=F32: nc.dram_tensor(n, s, d)
        self.X = dt("X", [8, 128, T]).ap()
        self.H = dt("H", [8, 128, T], BF16).ap()
        self.Z = dt("Z", [NZ, 128, T]).ap()
        self.RV = dt("RV", [T, 512], BF16).ap()
        self.AVC = dt("AVC", [CTX, 128]).ap()
        self.KAC = dt("KAC", [128, CTX]).ap()
        self.QA = dt("QA", [4, 128, T], BF16).ap()
        self.QK = dt("QK", [4, 4, 128, T], BF16).ap()
        self.EXP = [dt("EXP%d" % j, [128, 512]) for j in range(10)] + [dt("EXPL", [4, 512])]
        self.EXGP = [dt("EXGP%d" % j, [256, 512]) for j in range(10)] + [dt("EXGPL", [8, 512])]
        self.EX2t = dt("EX2", [EX2_N // 512, 512])
        self.EX2Gt = dt("EX2G", [2 * EX2_N // 512, 512])
        self.EX2 = self.EX2t.ap().rearrange("a b -> (a b)")
        self.EX2G = self.EX2Gt.ap().rearrange("a b -> (a b)")
        self.HS = dt("HS", [4, 128, T]).ap()
        self.ACF = dt("ACF", [4, 128, LAT]).ap()
        self.ACB = dt("ACB", [4, 128, LAT]).ap()
        self.YR = dt("YR", [4, 128, T], BF16).ap()
        self.YL = dt("YL", [4, 128, T], BF16).ap()
        self.YA = dt("YA", [4, 128, T], BF16).ap()

    def un(self, name):
        self.uid = getattr(self, "uid", 0) + 1
        return "%s_%d" % (name, self.uid)

    def nextps(self, n=6):
        i = self.pscur % n
        self.pscur += 1
        return self.ps[i], ("ps", i)

    def pvl(self, l, off, n=1):
        return self.pv[:, l * PL + off:l * PL + off + n]

    def phase_init(self):
        nc, S = self.nc, self.S
        with nc.sbuf_tensor(self.un("ini_c"), [128, 5, 128], F32) as cf, nc.sbuf_tensor(self.un("ini_s"), [128, 16], F32) as sv:
            S.dma("sp", self.pv[:], self.pvin, writes=["pv"])
            S.dma("sp", cf[:], self.cst[0:5].rearrange("c p n -> p c n"), writes=["cf"])
            S.dma("sp", self.pos[:], self.posin.rearrange("c p n -> p c n"), writes=["pos"])
            S.cp("dve", self.ident[:], cf[:, 0, :], reads=["cf"], writes=["ident"])
            S.cp("dve", self.ones[:], cf[:, 1, :], reads=["cf"], writes=["ones"])
            S.cp("dve", self.bones[:], cf[:, 2, :], reads=["cf"], writes=["bones"])
            S.cp("dve", self.maskf[:], cf[:, 3, :], reads=["cf"], writes=["maskf"])
            S.cp("dve", self.maskb[:], cf[:, 4, :], reads=["cf"], writes=["maskb"])
            S.cp("dve", self.ones32[:], cf[:, 1, :], reads=["cf"], writes=["ones32"])
            base = DEPTH * PL + 8
            S.act(sv[:], self.pv[:, base:base + 16], AF.Silu, reads=["pv"], writes=["sv"])
            S.cp("dve", self.sb[:], sv[:].rearrange("p (c k) -> p k c", c=2), reads=["sv"], writes=["sb"])
            for k in range(8):
                S.dma("sp", self.X[k], self.xin[k], writes=[("X", k)])
            S.flush()

    def phase_mod(self, l):
        nc, S = self.nc, self.S
        with (nc.sbuf_tensor(self.un("wm0"), [128, 8, 1024], BF16) as wm0,
              nc.sbuf_tensor(self.un("wm1"), [128, 8, 1024], BF16) as wm1,
              nc.sbuf_tensor(self.un("mraw"), [128, 9, 8, 2], F32) as mraw,
              nc.sbuf_tensor(self.un("lg"), [128, 8], F32) as lg):
            wms = [wm0, wm1]
            src = self.w_mod[l].rearrange("(k p) c -> p k c", p=128)
            for b in range(9):
                wm = wms[b % 2]
                for kk in range(0, 8, 4):
                    S.dma("pool", wm[:, kk:kk + 4, :], src[:, kk:kk + 4, b * 1024:(b + 1) * 1024],
                          writes=[("wm", b % 2, kk)])
                pm, km = self.nextps()
                for i in range(8):
                    for k in range(8):
                        S.mm(pm[:, i * 2:(i + 1) * 2], wm[:, k, i * 128:(i + 1) * 128], self.sb[:, k, :],
                             k == 0, k == 7, reads=[("wm", b % 2, (k // 4) * 4), "sb"], writes=[km])
                bm = self.pvl(l, 24 + b * 8, 8)
                S.tt("dve", mraw[:, b, :, :], pm[:, 0:16].rearrange("p (i c) -> p i c", c=2),
                     bm.unsqueeze(2).broadcast_to([128, 8, 2]), ALU.add, reads=[km, "pv"], writes=[("mraw", b)])
            for s in range(3):
                g = self.pvl(l, s * 8, 8).unsqueeze(2).broadcast_to([128, 8, 2])
                S.stt(self.modA[:, s, :, :], mraw[:, 3 * s + 1, :, :], 1.0, g, ALU.add, ALU.mult,
                      reads=[("mraw", 3 * s + 1), "pv"], writes=[("modA", s)])
                S.cp("dve", self.modB[:, s, :, :], mraw[:, 3 * s, :, :], reads=[("mraw", 3 * s)], writes=[("modB", s)])
                S.ts("dve", self.modG[:, s, :, :], mraw[:, 3 * s + 2, :, :], 0.5 if s != 1 else 1.0, None,
                     ALU.mult, ALU.bypass, reads=[("mraw", 3 * s + 2)], writes=[("modG", s)])
            S.act(lg[:], self.pvl(l, 148, 8), AF.Exp, scale=-1.0, reads=["pv"], writes=["lg"])
            S.act(lg[:], lg[:], AF.Ln, bias=1.0, reads=["lg"], writes=["lg"])
            S.ts("dve", self.lgam[:], lg[:], -1.0, None, ALU.mult, ALU.bypass, reads=["lg"], writes=["lgam"])
            S.ts("dve", self.nlgam[:], lg[:], 1.0, None, ALU.mult, ALU.bypass, reads=["lg"], writes=["nlgam"])
            S.act(self.gC[:], self.lgam[:], AF.Exp, scale=128.0, reads=["lgam"], writes=["gC"])
            hb = DEPTH * PL + 24
            S.ts("dve", lg[:, 0:4], self.lgam[:, 0:4], self.pv[:, hb:hb + 1], 2048.0, ALU.mult, ALU.mult,
                 reads=["lgam", "pv"], writes=["lg"])
            S.ts("dve", lg[:, 4:8], self.lgam[:, 4:8], self.pv[:, hb + 1:hb + 2], 2048.0, ALU.mult, ALU.mult,
                 reads=["lgam", "pv"], writes=["lg"])
            S.act(self.bc1[:], lg[:], AF.Exp, reads=["lg"], writes=["bc1"])
            S.act(self.lcp[:], self.pvl(l, 136, 8), AF.Exp, scale=-1.0, reads=["pv"], writes=["lcp"])
            S.act(self.lcp[:], self.lcp[:], AF.Ln, bias=1.0, reads=["lcp"], writes=["lcp"])
            S.ts("dve", self.lcp2[:], self.lcp[:], -16.0, None, ALU.mult, ALU.bypass, reads=["lcp"], writes=["lcp2"])
            S.ts("dve", self.lcp[:], self.lcp[:], -8.0, None, ALU.mult, ALU.bypass, reads=["lcp"], writes=["lcp"])
            S.flush()

    def load_x(self, xt, t, par):
        S = self.S
        S.dma("sp", xt[:], self.X[:, :, t * TN:(t + 1) * TN].rearrange("k p n -> p k n"),
              reads=[("X", t)], writes=[("xt", par, i) for i in range(8)])

    def store_x(self, xt, t, par, dst=None):
        S = self.S
        dst = self.X if dst is None else dst
        S.dma("sp", dst[:, :, t * TN:(t + 1) * TN].rearrange("k p n -> p k n"), xt[:],
              reads=[("xt", par, i) for i in range(8)], writes=[("X", t)])

    def norm_mod(self, xt, par, sq, rs, tmp, h, sub, c):
        S = self.S
        xk = [("xt", par, i) for i in range(8)]
        S.act(sq[:], xt[:], AF.Square, reads=xk, writes=["sq"])
        pn, kn = self.nextps()
        for k in range(8):
            S.mm(pn[:, 0:TN], self.ones[:], sq[:, k, :], k == 0, k == 7, reads=["sq", "ones"], writes=[kn])
        S.act(rs[:], pn[:, 0:TN], AF.Sqrt, scale=1.0 / D, bias=self.epsb[:, 0:1], reads=[kn], writes=["rs"])
        S.recip(rs[:], rs[:], reads=["rs"], writes=["rs"])
        S.tt("dve", tmp[:], xt[:], rs[:].unsqueeze(1).broadcast_to([128, 8, TN]), ALU.mult,
             reads=xk + ["rs"], writes=["tmp"])
        for k in range(8):
            S.act(h[:, k, :], tmp[:, k, :], AF.Identity, scale=self.modA[:, sub, k, c:c + 1],
                  bias=self.modB[:, sub, k, c:c + 1], reads=["tmp", ("modA", sub), ("modB", sub)],
                  writes=[("h", par, k)])

    def phase_ffn(self, l, which, sub):
        nc, S = self.nc, self.S
        with (nc.sbuf_tensor(self.un("ffw1"), [128, 8, 2 * DFF], BF16) as w1,
              nc.sbuf_tensor(self.un("ffw2"), [128, 22, D], BF16) as w2,
              nc.sbuf_tensor(self.un("fxt0"), [128, 8, TN], F32) as xt0,
              nc.sbuf_tensor(self.un("fxt1"), [128, 8, TN], F32) as xt1,
              nc.sbuf_tensor(self.un("fsq"), [128, 8, TN], BF16) as sq,
              nc.sbuf_tensor(self.un("fh0"), [128, 8, TN], BF16) as h0,
              nc.sbuf_tensor(self.un("fh1"), [128, 8, TN], BF16) as h1,
              nc.sbuf_tensor(self.un("fg"), [128, 22, TN], BF16) as g,
              nc.sbuf_tensor(self.un("fsl0"), [128, TN], F32) as sl0,
              nc.sbuf_tensor(self.un("fsl1"), [128, TN], F32) as sl1,
              nc.sbuf_tensor(self.un("frs"), [128, TN], F32) as rs,
              nc.sbuf_tensor(self.un("ftmp"), [128, 8, TN], F32) as tmp):
            xts, hs, sls = [xt0, xt1], [h0, h1], [sl0, sl1]
            src1 = self.ffn_w_in[l, which].rearrange("(k p) c -> p k c", p=128)
            for cb in range(11):
                S.dma("pool", w1[:, :, cb * 512:(cb + 1) * 512], src1[:, :, cb * 512:(cb + 1) * 512],
                      writes=[("w1", cb)])
            src2 = self.ffn_w_out[l, which].rearrange("(j p) c -> p j c", p=128)
            for jb in range(11):
                S.dma("pool", w2[:, 2 * jb:2 * jb + 2, :], src2[:, 2 * jb:2 * jb + 2, :], writes=[("w2", jb)])
            self.load_x(xts[0], 0, 0)
            self.norm_mod(xts[0], 0, sq, rs, tmp, hs[0], sub, 1)
            for t in range(NT):
                par = t % 2
                xt, h = xts[par], hs[par]
                c = 1 if t == 0 else 0
                if t + 1 < NT:
                    self.load_x(xts[1 - par], t + 1, 1 - par)
                for j in range(22):
                    pa, ka = self.nextps()
                    pb, kb = self.nextps()
                    ca, cb_ = j * 128, DFF + j * 128
                    for k in range(8):
                        S.mm(pa[:, 0:TN], w1[:, k, ca:ca + 128], h[:, k, :], k == 0, k == 7,
                             reads=[("w1", ca // 512), ("h", par, k)], writes=[ka])
                    for k in range(8):
                        S.mm(pb[:, 0:TN], w1[:, k, cb_:cb_ + 128], h[:, k, :], k == 0, k == 7,
                             reads=[("w1", cb_ // 512), ("h", par, k)], writes=[kb])
                    sl = sls[j % 2]
                    S.act(sl[:], pa[:, 0:TN], AF.Silu, reads=[ka], writes=[("sl", j % 2)])
                    S.tt("dve", g[:, j, :], sl[:], pb[:, 0:TN], ALU.mult, reads=[("sl", j % 2), kb], writes=[("g", j)])
                if t + 1 < NT:
                    self.norm_mod(xts[1 - par], 1 - par, sq, rs, tmp, hs[1 - par], sub, 0)
                for i in range(8):
                    po, ko = self.nextps()
                    for j in range(22):
                        S.mm(po[:, 0:TN], w2[:, j, i * 128:(i + 1) * 128], g[:, j, :], j == 0, j == 21,
                             reads=[("w2", j // 2), ("g", j)], writes=[ko])
                    S.stt(xt[:, i, :], po[:, 0:TN], self.modG[:, sub, i, c:c + 1], xt[:, i, :], ALU.mult, ALU.add,
                          reads=[ko, ("xt", par, i), ("modG", sub)], writes=[("xt", par, i)])
                self.store_x(xt, t, par)
            S.flush()

    def phase_final(self):
        nc, S = self.nc, self.S
        with (nc.sbuf_tensor(self.un("nxt0"), [128, 8, TN], F32) as xt0,
              nc.sbuf_tensor(self.un("nxt1"), [128, 8, TN], F32) as xt1,
              nc.sbuf_tensor(self.un("nsq"), [128, 8, TN], BF16) as sq,
              nc.sbuf_tensor(self.un("nrs"), [128, TN], F32) as rs,
              nc.sbuf_tensor(self.un("no0"), [128, 8, TN], F32) as o0,
              nc.sbuf_tensor(self.un("no1"), [128, 8, TN], F32) as o1):
            xts, os_ = [xt0, xt1], [o0, o1]
            fb = DEPTH * PL
            for t in range(1, NT):
                par = t % 2
                xt, o = xts[par], os_[par]
                self.load_x(xt, t, par)
                xk = [("xt", par, i) for i in range(8)]
                S.act(sq[:], xt[:], AF.Square, reads=xk, writes=["sq"])
                pn, kn = self.nextps()
                for k in range(8):
                    S.mm(pn[:, 0:TN], self.ones[:], sq[:, k, :], k == 0, k == 7, reads=["sq"], writes=[kn])
                S.act(rs[:], pn[:, 0:TN], AF.Sqrt, scale=1.0 / D, bias=self.epsb[:, 0:1], reads=[kn], writes=["rs"])
                S.recip(rs[:], rs[:], reads=["rs"], writes=["rs"])
                for k in range(8):
                    S.stt(o[:, k, :], xt[:, k, :], self.pv[:, fb + k:fb + k + 1], rs[:], ALU.mult, ALU.mult,
                          reads=xk + ["rs"], writes=[("o", par)])
                S.dma("sp", self.out[:, :, (t - 1) * TN:t * TN].rearrange("k p n -> p k n"), o[:],
                      reads=[("o", par)], writes=[("out", t)])
            S.flush()

    def phase_dbg(self):
        S = self.S
        for k in range(8):
            S.dma("sp", self.dbg[k], self.X[k], reads=[("X", k)], writes=[("dbg", k)])
        S.flush()

    def build(self):
        nc = self.nc
        self.declare()
        sem_names = ["s%d" % i for i in range(4 + 16 + 1)]
        from contextlib import ExitStack
        with ExitStack() as st:
            sems = [st.enter_context(nc.semaphore(n)) for n in sem_names]
            self.S = Sched(nc, sems)
            sb = lambda n, s, d=F32: st.enter_context(nc.sbuf_tensor(self.un("sb_") + n, s, d))
            self.pv = sb("pv", [128, NPV])
            self.ident = sb("ident", [128, 128], BF16)
            self.ones = sb("ones", [128, 128], BF16)
            self.bones = sb("bones", [128, 128], BF16)
            self.maskf = sb("maskf", [128, 128], BF16)
            self.maskb = sb("maskb", [128, 128], BF16)
            self.ones32 = sb("ones32", [128, 128])
            self.pos = sb("pos", [128, 2, TN])
            self.sb = sb("sb", [128, 8, 2], BF16)
            self.modA = sb("modA", [128, 3, 8, 2])
            self.modB = sb("modB", [128, 3, 8, 2])
            self.modG = sb("modG", [128, 3, 8, 2])
            self.lgam = sb("lgam", [128, 8])
            self.nlgam = sb("nlgam", [128, 8])
            self.gC = sb("gC", [128, 8])
            self.bc1 = sb("bc1", [128, 8])
            self.lcp = sb("lcp", [128, 8])
            self.lcp2 = sb("lcp2", [128, 8])
            self.epsb = sb("epsb", [128, 1])
            self.lnks = sb("lnks", [128, 1])
            self.ps = [st.enter_context(nc.psum_tensor("ps%d" % i, [128, 512], F32)) for i in range(6)]
            self.psb = [st.enter_context(nc.psum_tensor("psb%d" % i, [128, 1024], BF16)) for i in range(2)]
            self.S.memset("dve", self.epsb[:], EPS, writes=["epsb"])
            self.S.memset("dve", self.lnks[:], -0.5 * float(np.log(128.0)), writes=["lnks"])
            self.phase_init()
            stop = False
            for l in range(N_LAYERS):
                for name, fn in (("mod", lambda: self.phase_mod(l)),
                                 ("ffn1", lambda: self.phase_ffn(l, 0, 0)),
                                 ("mix", lambda: self.phase_mix(l)),
                                 ("ffn2", lambda: self.phase_ffn(l, 1, 2))):
                    r = fn()
                    if r or DEBUG_STOP == (l, name):
                        stop = True
                        break
                if stop:
                    break
            if DEBUG_STOP is not None:
                self.phase_dbg()
            else:
                self.phase_final()
            self.S.final_wait()


    def ex_w(self, off, cnt_):
        j, o = off // 65536, off % 65536
        assert o + cnt_ <= 65536
        return self.EXP[j].ap().rearrange("a b -> (a b)")[o:o + cnt_]

    def ex_r(self, slot, off, cnt_):
        j, o = off // 65536, off % 65536
        assert o + cnt_ <= 65536
        psz = 65536 if j < 10 else 2048
        lo = slot * psz + o
        return self.EXGP[j].ap().rearrange("a b -> (a b)")[lo:lo + cnt_]

    def phase_mix(self, l):
        def cc1():
            for j in range(11):
                self.S.cc(self.EXP[j].ap().opt(), self.EXGP[j].ap().opt())
            self.S.flush()

        def cc2():
            self.S.cc(self.EX2t.ap().opt(), self.EX2Gt.ap().opt())
            self.S.flush()

        for name, fn in (("m1", lambda: self.phase_m1(l)), ("m2", lambda: self.phase_m2(l)),
                         ("m2b", lambda: self.phase_m2b(l)), ("cc1", cc1), ("lruA", lambda: self.phase_lruA(l)),
                         ("cc2", cc2), ("lruB", lambda: self.phase_lruB(l)), ("ret", lambda: self.phase_ret(l)),
                         ("attn", lambda: self.phase_attn(l)), ("merge", lambda: self.phase_merge(l))):
            fn()
            if DEBUG_STOP == (l, name):
                return True
        return False

    ZSRC = list(range(0, 8)) + list(range(12, 29)) + list(range(30, 43))

    def phase_m1(self, l):
        nc, S = self.nc, self.S
        NWC = 43
        with (nc.sbuf_tensor(self.un("m1w"), [128, 8, NWC * 128], BF16) as wz,
              nc.sbuf_tensor(self.un("m1x0"), [128, 8, TN], F32) as xt0,
              nc.sbuf_tensor(self.un("m1x1"), [128, 8, TN], F32) as xt1,
              nc.sbuf_tensor(self.un("m1sq"), [128, 8, TN], BF16) as sq,
              nc.sbuf_tensor(self.un("m1h0"), [128, 8, TN], BF16) as h0,
              nc.sbuf_tensor(self.un("m1h1"), [128, 8, TN], BF16) as h1,
              nc.sbuf_tensor(self.un("m1rs"), [128, TN], F32) as rs,
              nc.sbuf_tensor(self.un("m1tmp"), [128, 8, TN], F32) as tmp,
              nc.sbuf_tensor(self.un("m1z0"), [128, 2, TN], F32) as zs0,
              nc.sbuf_tensor(self.un("m1z1"), [128, 2, TN], F32) as zs1,
              nc.sbuf_tensor(self.un("m1rv0"), [128, 512], BF16) as rv0,
              nc.sbuf_tensor(self.un("m1rv1"), [128, 512], BF16) as rv1,
              nc.sbuf_tensor(self.un("m1av0"), [128, 128], F32) as av0,
              nc.sbuf_tensor(self.un("m1av1"), [128, 128], F32) as av1):
            xts, hs, zss, rvs, avs = [xt0, xt1], [h0, h1], [zs0, zs1], [rv0, rv1], [av0, av1]
            src = self.w_in[l].rearrange("(k p) c -> p k c", p=128)
            srcp = self.w_inp[l].rearrange("(k p) c -> p k c", p=128)
            for pc in range(10):
                S.dma("pool", wz[:, :, pc * 384:(pc + 1) * 384], src[:, :, pc * 384:(pc + 1) * 384], writes=[("wz", pc)])
            for pc in range(5):
                lo, hi = pc * 384, min((pc + 1) * 384, 13 * 128)
                S.dma("pool", wz[:, :, 3840 + lo:3840 + hi], srcp[:, :, lo:hi], writes=[("wz", 10 + pc)])
            self.load_x(xts[0], 0, 0)
            self.norm_mod(xts[0], 0, sq, rs, tmp, hs[0], 1, 1)
            for t in range(NT):
                par = t % 2
                xt, h = xts[par], hs[par]
                c = 1 if t == 0 else 0
                t0 = t * TN
                if t + 1 < NT:
                    self.load_x(xts[1 - par], t + 1, 1 - par)
                S.dma("sp", self.H[:, :, t0:t0 + TN].rearrange("k p n -> p k n"), h[:], reads=[("h", par, k) for k in range(8)],
                      writes=[("H", t)])
                for zc in range(NZ):
                    wc = self.ZSRC[zc]
                    pz, kz = self.nextps()
                    for k in range(8):
                        S.mm(pz[:, 0:TN], wz[:, k, wc * 128:(wc + 1) * 128], h[:, k, :], k == 0, k == 7,
                             reads=[("wz", wc // 3), ("h", par, k)], writes=[kz])
                    zs = zss[(zc // 2) % 2]
                    zk = ("zs", (zc // 2) % 2, zc % 2)
                    S.cp("act" if zc % 2 == 0 else "dve", zs[:, zc % 2, :], pz[:, 0:TN], reads=[kz], writes=[zk])
                    if zc % 2 == 1:
                        S.dma("sp", self.Z[zc - 1:zc + 1, :, t0:t0 + TN].rearrange("c p n -> p c n"), zs[:],
                              reads=[("zs", (zc // 2) % 2, 0), ("zs", (zc // 2) % 2, 1)], writes=[("Z", zc // 2, t)])
                if t + 1 < NT:
                    self.norm_mod(xts[1 - par], 1 - par, sq, rs, tmp, hs[1 - par], 1, 0)
                for hf in range(2):
                    i2 = (2 * t + hf) % 2
                    prv, krv = self.nextps()
                    for k in range(8):
                        S.mm(prv[:, 0:512], h[:, k, hf * 128:(hf + 1) * 128], wz[:, k, 1024:1536], k == 0, k == 7,
                             reads=[("wz", 2), ("wz", 3), ("h", par, k)], writes=[krv])
                    S.cp("act", rvs[i2][:], prv[:, 0:512], reads=[krv], writes=[("rvs", i2)])
                    S.dma("sp", self.RV[t0 + hf * 128:t0 + (hf + 1) * 128, :], rvs[i2][:], reads=[("rvs", i2)],
                          writes=[("RV", t, hf)])
                    pav, kav = self.nextps()
                    for k in range(8):
                        S.mm(pav[:, 0:128], h[:, k, hf * 128:(hf + 1) * 128], wz[:, k, 3712:3840], k == 0, k == 7,
                             reads=[("wz", 9), ("h", par, k)], writes=[kav])
                    S.cp("dve", avs[i2][:], pav[:, 0:128], reads=[kav], writes=[("avs", i2)])
                    if t == 0:
                        dst = self.AVC[hf * 128:(hf + 1) * 128, :]
                    else:
                        r0 = (t - 1) * TN + hf * 128
                        dst = self.ex_w(EX_AV + r0 * 128, 16384).rearrange("(t e) -> t e", e=128)
                    S.dma("sp", dst, avs[i2][:], reads=[("avs", i2)], writes=[("AV", t, hf)])
            S.flush()

    def phase_m2(self, l):
        nc, S = self.nc, self.S
        LNKS = -0.5 * float(np.log(128.0))
        with (nc.sbuf_tensor(self.un("m2G"), [128, 16, TN], F32) as G,
              nc.sbuf_tensor(self.un("m2rp0"), [128, 4, TN], F32) as rp0,
              nc.sbuf_tensor(self.un("m2rp1"), [128, 4, TN], F32) as rp1,
              nc.sbuf_tensor(self.un("m2z"), [128, 4, TN], F32) as zb,
              nc.sbuf_tensor(self.un("m2zp"), [128, 4, TN], F32) as zpb,
              nc.sbuf_tensor(self.un("m2r1"), [128, 2, TN], F32) as r1b,
              nc.sbuf_tensor(self.un("m2r2"), [128, 2, TN], F32) as r2b,
              nc.sbuf_tensor(self.un("m2qk0"), [128, 4, TN], BF16) as qk0,
              nc.sbuf_tensor(self.un("m2qk1"), [128, 4, TN], BF16) as qk1,
              nc.sbuf_tensor(self.un("m2sq"), [128, TN], BF16) as sq,
              nc.sbuf_tensor(self.un("m2rs"), [128, 2, TN], F32) as rsb,
              nc.sbuf_tensor(self.un("m2qa"), [128, 2, TN], BF16) as qab,
              nc.sbuf_tensor(self.un("m2kf"), [128, TN], F32) as kf32):
            rps, qks = [rp0, rp1], [qk0, qk1]
            for kind in range(2):
                for d in range(2):
                    for h in range(4):
                        gi = (kind * 2 + d) * 4 + h
                        sc = (self.lgam if kind == 0 else self.nlgam)[:, d * 4 + h:d * 4 + h + 1]
                        S.act(G[:, gi, :], self.pos[:, d, :], AF.Exp, scale=sc,
                              bias=(None if kind == 0 else self.lnks[:, 0:1]),
                              reads=["lgam", "nlgam", "pos"], writes=[("G", gi)])
            items = []
            for t in range(NT):
                for h in range(4):
                    for kind in range(2):
                        items.append(("ret", t, h, kind))
                for c in range(5):
                    items.append(("att", t, c, 0))

            def bufs(i):
                b4, b2 = i % 4, i % 2
                return b4, b2, zb[:, b4, :], zpb[:, b4, :], r1b[:, b2, :], r2b[:, b2, :], rsb[:, b2, :]

            def do_loads(i):
                typ, t, a, kind = items[i]
                t0 = t * TN
                b4, b2, z, zp, r1, r2, rs = bufs(i)
                if typ == "ret" and a == 0 and kind == 0:
                    S.dma("sp", rps[t % 2][:], self.rope[:, :, t0:t0 + TN].rearrange("c p n -> p c n"),
                          writes=[("rp", t % 2)])
                if typ == "ret":
                    zc, zpc = (a, 25 + a) if kind == 0 else (4 + a, 29 + a)
                else:
                    zc, zpc = (20 + a, 33 + a) if a < 4 else (24, 37)
                S.dma("sp", z, self.Z[zc, :, t0:t0 + TN], writes=[("z", b4)])
                S.dma("sp", zp, self.Z[zpc, :, t0:t0 + TN], writes=[("zp", b4)])

            def do_compute(i):
                typ, t, a, kind = items[i]
                t0 = t * TN
                rp = rps[t % 2]
                b4, b2, z, zp, r1, r2, rs = bufs(i)
                if typ == "ret":
                    h = a
                    qk = qks[h % 2]
                    S.tt("dve", r1, z, rp[:, 0, :], ALU.mult, reads=[("z", b4), ("rp", t % 2)], writes=[("r1", b2)])
                    S.tt("pool", r2, zp, rp[:, 1, :], ALU.mult, reads=[("zp", b4), ("rp", t % 2)], writes=[("r2", b2)])
                    S.tt("dve", r1, r1, r2, ALU.add, reads=[("r1", b2), ("r2", b2)], writes=[("r1", b2)])
                    gf = (kind * 2 + 0) * 4 + h
                    gb = (kind * 2 + 1) * 4 + h
                    S.tt("pool", qk[:, 2 * kind, :], r1, G[:, gf, :], ALU.mult, reads=[("r1", b2), ("G", gf)],
                         writes=[("qk", h % 2, 2 * kind)])
                    S.tt("dve", qk[:, 2 * kind + 1, :], r1, G[:, gb, :], ALU.mult, reads=[("r1", b2), ("G", gb)],
                         writes=[("qk", h % 2, 2 * kind + 1)])
                    if kind == 1:
                        S.dma("sp", self.QK[h, :, :, t0:t0 + TN].rearrange("s p n -> p s n"), qk[:],
                              reads=[("qk", h % 2, j) for j in range(4)], writes=[("QK", h, t)])
                else:
                    c = a
                    gcol = 144 if c < 4 else 146
                    S.act(sq[:], z, AF.Square, reads=[("z", b4)], writes=["sq"])
                    pn, kn = self.nextps()
                    S.mm(pn[:, 0:TN], self.bones[:], sq[:], True, True, reads=["sq"], writes=[kn])
                    S.act(rs, pn[:, 0:TN], AF.Sqrt, scale=1.0 / 64, bias=self.epsb[:, 0:1], reads=[kn], writes=[("rs", b2)])
                    S.recip(rs, rs, reads=[("rs", b2)], writes=[("rs", b2)])
                    S.stt(r1, z, self.pvl(l, gcol), rs, ALU.mult, ALU.mult, reads=[("z", b4), ("rs", b2)], writes=[("r1", b2)])
                    S.stt(r2, zp, self.pvl(l, gcol + 1), rs, ALU.mult, ALU.mult, reads=[("zp", b4), ("rs", b2)],
                          writes=[("r2", b2)])
                    S.tt("pool", r1, r1, rp[:, 2, :], ALU.mult, reads=[("r1", b2), ("rp", t % 2)], writes=[("r1", b2)])
                    S.tt("pool", r2, r2, rp[:, 3, :], ALU.mult, reads=[("r2", b2), ("rp", t % 2)], writes=[("r2", b2)])
                    if c < 4:
                        S.tt("dve", qab[:, c % 2, :], r1, r2, ALU.add, reads=[("r1", b2), ("r2", b2)], writes=[("qa", c % 2)])
                        S.dma("sp", self.QA[c, :, t0:t0 + TN], qab[:, c % 2, :], reads=[("qa", c % 2)], writes=[("QA", c, t)])
                    else:
                        S.tt("dve", kf32[:], r1, r2, ALU.add, reads=[("r1", b2), ("r2", b2)], writes=["kf32"])
                        if t == 0:
                            dst = self.KAC[:, :]
                        else:
                            dst = self.EXP[(t - 1) // 2].ap()[:, ((t - 1) % 2) * TN:((t - 1) % 2 + 1) * TN]
                        S.dma("sp", dst, kf32[:], reads=["kf32"], writes=[("KA", t)])

            do_loads(0)
            do_loads(1)
            for i in range(len(items)):
                if i + 2 < len(items):
                    do_loads(i + 2)
                do_compute(i)
            EXLX = self.ex_w(EX_LX, 2048).rearrange("(c p n) -> c p n", p=128, n=4)
            for c in range(4):
                S.dma("sp", EXLX[c, :, 0:2], self.Z[12 + c, :, CTX:CTX + 2], writes=[("LXH", c, 0)], slow=True)
                S.dma("sp", EXLX[c, :, 2:4], self.Z[12 + c, :, T - 2:T], writes=[("LXH", c, 1)], slow=True)
            S.flush()

    def phase_m2b(self, l):
        nc, S = self.nc, self.S
        with (nc.sbuf_tensor(self.un("m2bS"), [128, 8, 128], F32) as S32,
              nc.sbuf_tensor(self.un("m2bv"), [128, 4, 512], BF16) as vb,
              nc.sbuf_tensor(self.un("m2bk"), [128, 4, 4, 128], BF16) as kb,
              nc.sbuf_tensor(self.un("m2bkt"), [128, 2, 4, 128], BF16) as ktb):
            cnt = 0
            kcnt = 0
            for step in range(16):
                for d in range(2):
                    ci = step if d == 0 else 15 - step
                    t0 = CTX + 128 * ci
                    b4 = cnt % 4
                    cnt += 1
                    S.dma("sp", vb[:, b4, :], self.RV[t0:t0 + 128, :], writes=[("vb", b4)])
                    S.dma("sp", kb[:, b4, :, :], self.QK[:, 2 + d, :, t0:t0 + 128].rearrange("h p n -> p h n"),
                          writes=[("kb", b4)])
                    bk = kcnt % 2
                    kcnt += 1
                    for h in range(4):
                        S.tr(self.psb[bk][:, h * 128:(h + 1) * 128], kb[:, b4, h, :], self.ident[:],
                             reads=[("kb", b4), "ident"], writes=[("psb", bk)])
                    S.cp("act" if bk else "dve", ktb[:, bk, :, :],
                         self.psb[bk][:, 0:512].rearrange("p (h n) -> p h n", n=128),
                         reads=[("psb", bk)], writes=[("ktb", bk)])
                    for h in range(4):
                        kv, kk = self.nextps()
                        S.mm(kv[:, 0:128], ktb[:, bk, h, :], vb[:, b4, h * 128:(h + 1) * 128], True, True,
                             reads=[("ktb", bk), ("vb", b4)], writes=[kk])
                        si = d * 4 + h
                        gc = self.gC[:, si:si + 1]
                        if step == 0:
                            S.ts("dve", S32[:, si, :], kv[:, 0:128], gc, None, ALU.mult, ALU.bypass,
                                 reads=[kk, "gC"], writes=[("S32", si)])
                        else:
                            S.act(S32[:, si, :], S32[:, si, :], AF.Identity, scale=gc,
                                  reads=[("S32", si), "gC"], writes=[("S32", si)])
                            S.stt(S32[:, si, :], kv[:, 0:128], gc, S32[:, si, :], ALU.mult, ALU.add,
                                  reads=[kk, ("S32", si), "gC"], writes=[("S32", si)])
            for si in range(8):
                S.dma("sp", self.ex_w(EX_SL + si * 16384, 16384).rearrange("(p n) -> p n", n=128), S32[:, si, :],
                      reads=[("S32", si)], writes=[("EXSL", si)])
            S.flush()

    @staticmethod
    def rev(t, a, b):
        return t[:, slice(b - 1, a - 1 if a > 0 else None, -1)]

    def phase_lruA(self, l):
        nc, S = self.nc, self.S
        hb_ = DEPTH * PL + 24
        half, omh = self.pv[:, hb_:hb_ + 1], self.pv[:, hb_ + 1:hb_ + 2]
        big = lambda n: nc.sbuf_tensor(self.un(n), [128, T], F32)
        with (nc.sbuf_tensor(self.un("laxpc"), [128, CTX + 3], F32) as xpc,
              nc.sbuf_tensor(self.un("laxpl"), [128, LAT + 3], F32) as xpl,
              big("laxcv") as xcv, big("larg") as rg, big("laig") as ig, big("laa") as a_, big("lam") as m_,
              big("lau") as u_, big("lahf") as hf, big("lahb") as hb,
              nc.sbuf_tensor(self.un("laacf"), [128, LAT], F32) as acf,
              nc.sbuf_tensor(self.un("laacb"), [128, LAT], F32) as acb,
              nc.sbuf_tensor(self.un("lazero"), [128, LAT], F32) as zeros,
              nc.sbuf_tensor(self.un("laxb"), [128, T], BF16) as xb,
              nc.sbuf_tensor(self.un("lalw"), [128, 4, 128], BF16) as lw,
              nc.sbuf_tensor(self.un("lahl"), [128, 2, 4], F32) as hl,
              nc.sbuf_tensor(self.un("lah0"), [128, 2], F32) as h0,
              nc.sbuf_tensor(self.un("last"), [128, 2], F32) as stt_):
            S.memset("pool", zeros[:], 0.0, writes=["zeros"])
            slices = [(i * 512, min(512, T - i * 512)) for i in range(5)]
            for c in range(4):
                S.memset("dve", xpc[:, 0:1], 0.0, writes=["xpc"])
                S.memset("dve", xpc[:, CTX + 1:CTX + 3], 0.0, writes=["xpc"])
                S.dma("sp", xpc[:, 1:CTX + 1], self.Z[12 + c, :, 0:CTX], writes=["xpc"])
                S.dma("sp", xpl[:, 1:LAT + 1], self.Z[12 + c, :, CTX:T], writes=["xpl"])
                for r in range(2):
                    S.dma("sp", hl[:, r, :], self.ex_r(r, EX_LX + c * 512, 512).rearrange("(p n) -> p n", n=4),
                          writes=["hl"])
                S.ts("dve", xpl[:, 0:1], hl[:, 0, 3:4], half, None, ALU.mult, ALU.bypass, reads=["hl"], writes=["xpl"])
                S.ts("dve", xpl[:, LAT + 1:LAT + 3], hl[:, 1, 0:2], omh, None, ALU.mult, ALU.bypass, reads=["hl"],
                     writes=["xpl"])
                w = [self.pvl(l, 100 + j * 4 + c) for j in range(4)]
                bcv = self.pvl(l, 116 + c)
                for (src, sk, n, d0) in ((xpc, "xpc", CTX, 0), (xpl, "xpl", LAT, CTX)):
                    dst = xcv[:, d0:d0 + n]
                    S.ts("dve", dst, src[:, 0:n], w[0], bcv, ALU.mult, ALU.add, reads=[sk], writes=["xcv"])
                    for j in range(1, 4):
                        S.stt(dst, src[:, j:j + n], w[j], dst, ALU.mult, ALU.add, reads=[sk, "xcv"], writes=["xcv"])
                S.cp("act", xb[:], xcv[:], reads=["xcv"], writes=["xb"])
                for a in range(2):
                    for d in range(2):
                        S.dma("pool", lw[:, a * 2 + d, :], self.lbd[l, a, d, c], writes=[("lw", a * 2 + d)])
                for d in range(2):
                    for (s0, n) in slices:
                        p1, k1 = self.nextps()
                        S.mm(p1[:, 0:n], lw[:, d, :], xb[:, s0:s0 + n], True, True, reads=[("lw", d), "xb"], writes=[k1])
                        S.act(rg[:, s0:s0 + n], p1[:, 0:n], AF.Sigmoid, bias=self.pvl(l, 120 + d * 4 + c), reads=[k1],
                              writes=["rg"])
                        p2, k2 = self.nextps()
                        S.mm(p2[:, 0:n], lw[:, 2 + d, :], xb[:, s0:s0 + n], True, True, reads=[("lw", 2 + d), "xb"],
                             writes=[k2])
                        S.act(ig[:, s0:s0 + n], p2[:, 0:n], AF.Sigmoid, bias=self.pvl(l, 128 + d * 4 + c), reads=[k2],
                              writes=["ig"])
                    ci = d * 4 + c
                    S.act(a_[:], rg[:], AF.Exp, scale=self.lcp[:, ci:ci + 1], reads=["rg", "lcp"], writes=["a"])
                    S.act(m_[:], rg[:], AF.Exp, scale=self.lcp2[:, ci:ci + 1], reads=["rg", "lcp2"], writes=["m"])
                    S.act(m_[:], m_[:], AF.Sqrt, scale=-1.0, bias=1.0, reads=["m"], writes=["m"])
                    S.tt("dve", u_[:], ig[:], xcv[:], ALU.mult, reads=["ig", "xcv"], writes=["u"])
                    S.tt("pool", u_[:], u_[:], m_[:], ALU.mult, reads=["u", "m"], writes=["u"])
                    if d == 0:
                        S.scan(hf[:, 0:CTX], a_[:, 0:CTX], u_[:, 0:CTX], 0.0, reads=["a", "u"], writes=["hf"])
                        S.ts("dve", h0[:, 0:1], hf[:, CTX - 1:CTX], omh, None, ALU.mult, ALU.bypass, reads=["hf"],
                             writes=["h0f"])
                        S.scan(hf[:, CTX:T], a_[:, CTX:T], u_[:, CTX:T], h0[:, 0:1], reads=["a", "u", "h0f"], writes=["hf"])
                        S.scan(acf[:], a_[:, CTX:T], zeros[:], 1.0, reads=["a", "zeros"], writes=["acf"])
                        S.cp("dve", stt_[:, 0:1], hf[:, T - 1:T], reads=["hf"], writes=["st0"])
                    else:
                        rv = self.rev
                        S.scan(rv(hb, 0, CTX), rv(a_, 0, CTX), rv(u_, 0, CTX), 0.0, reads=["a", "u"], writes=["hb"])
                        S.ts("dve", h0[:, 1:2], hb[:, 0:1], half, None, ALU.mult, ALU.bypass, reads=["hb"], writes=["h0b"])
                        S.scan(rv(hb, CTX, T), rv(a_, CTX, T), rv(u_, CTX, T), h0[:, 1:2], reads=["a", "u", "h0b"],
                               writes=["hb"])
                        S.scan(rv(acb, 0, LAT), rv(a_, CTX, T), zeros[:], 1.0, reads=["a", "zeros"], writes=["acb"])
                        S.cp("dve", stt_[:, 1:2], hb[:, CTX:CTX + 1], reads=["hb"], writes=["st1"])
                S.tt("dve", hf[:], hf[:], hb[:], ALU.add, reads=["hf", "hb"], writes=["hf"])
                S.dma("sp", self.HS[c], hf[:], reads=["hf"], writes=[("HS", c)])
                S.dma("sp", self.ACF[c], acf[:], reads=["acf"], writes=[("ACF", c)])
                S.dma("sp", self.ACB[c], acb[:], reads=["acb"], writes=[("ACB", c)])
                for d in range(2):
                    lo = d * 512 + c * 128
                    S.dma("sp", self.EX2[lo:lo + 128].rearrange("(p o) -> p o", o=1), stt_[:, d:d + 1],
                          reads=["st%d" % d], writes=[("EX2", d, c)])
            S.flush()

    def phase_lruB(self, l):
        nc, S = self.nc, self.S
        hb_ = DEPTH * PL + 24
        half, omh = self.pv[:, hb_:hb_ + 1], self.pv[:, hb_ + 1:hb_ + 2]
        big = lambda n: nc.sbuf_tensor(self.un(n), [128, T], F32)
        with (big("lbhs") as hs, big("lblz") as lz, big("lbsq") as sq, big("lbin") as inn,
              nc.sbuf_tensor(self.un("lbacf"), [128, LAT], F32) as acf,
              nc.sbuf_tensor(self.un("lbacb"), [128, LAT], F32) as acb,
              nc.sbuf_tensor(self.un("lby"), [128, T], BF16) as y,
              nc.sbuf_tensor(self.un("lbdd"), [128, 4], F32) as dd):
            for c in range(4):
                S.dma("sp", hs[:], self.HS[c], writes=["hs"])
                S.dma("sp", acf[:], self.ACF[c], writes=["acf"])
                S.dma("sp", acb[:], self.ACB[c], writes=["acb"])
                S.dma("sp", lz[:], self.Z[16 + c], writes=["lz"])
                lo0 = 0 * EX2_N + 0 * 512 + c * 128
                lo1 = 1 * EX2_N + 1 * 512 + c * 128
                S.dma("sp", dd[:, 0:1], self.EX2G[lo0:lo0 + 128].rearrange("(p o) -> p o", o=1), writes=["dd0"])
                S.dma("sp", dd[:, 1:2], self.EX2G[lo1:lo1 + 128].rearrange("(p o) -> p o", o=1), writes=["dd1"])
                S.ts("dve", dd[:, 2:3], dd[:, 0:1], half, None, ALU.mult, ALU.bypass, reads=["dd0"], writes=["dd2"])
                S.ts("dve", dd[:, 3:4], dd[:, 1:2], omh, None, ALU.mult, ALU.bypass, reads=["dd1"], writes=["dd3"])
                S.stt(hs[:, CTX:T], acf[:], dd[:, 2:3], hs[:, CTX:T], ALU.mult, ALU.add, reads=["acf", "dd2", "hs"],
                      writes=["hs"])
                S.stt(hs[:, CTX:T], acb[:], dd[:, 3:4], hs[:, CTX:T], ALU.mult, ALU.add, reads=["acb", "dd3", "hs"],
                      writes=["hs"])
                S.act(sq[:], lz[:], AF.Square, reads=["lz"], writes=["sq"])
                S.ts("pool", sq[:], sq[:], 0.044715, 1.0, ALU.mult, ALU.add, reads=["sq"], writes=["sq"])
                S.tt("dve", inn[:], sq[:], lz[:], ALU.mult, reads=["sq", "lz"], writes=["inn"])
                S.act(inn[:], inn[:], AF.Sigmoid, scale=1.5957691216057308, reads=["inn"], writes=["inn"])
                S.tt("pool", inn[:], inn[:], lz[:], ALU.mult, reads=["inn", "lz"], writes=["inn"])
                S.tt("dve", y[:], inn[:], hs[:], ALU.mult, reads=["inn", "hs"], writes=["y"])
                S.dma("sp", self.YL[c], y[:], reads=["y"], writes=[("YL", c)])
            S.flush()

    def phase_ret(self, l):
        nc, S = self.nc, self.S
        hb_ = DEPTH * PL + 24
        half, omh = self.pv[:, hb_:hb_ + 1], self.pv[:, hb_ + 1:hb_ + 2]
        big = lambda n: nc.sbuf_tensor(self.un(n), [128, T], F32)
        with (nc.sbuf_tensor(self.un("rtqk"), [128, 4, T], BF16) as qk,
              nc.sbuf_tensor(self.un("rtv"), [128, 18, 128], BF16) as vt,
              nc.sbuf_tensor(self.un("rtkt"), [128, 2, 18, 128], BF16) as kt,
              big("rtacc") as acc, big("rtyc") as yc, big("rtsq") as sq32, big("rtrs") as rs, big("rtrg") as rgz,
              nc.sbuf_tensor(self.un("rtS"), [128, 2, 128], F32) as S32,
              nc.sbuf_tensor(self.un("rtSo"), [128, 2, 128], F32) as So,
              nc.sbuf_tensor(self.un("rtSb"), [128, 2, 2, 128], BF16) as Sb,
              nc.sbuf_tensor(self.un("rtpm"), [128, 4, 128], BF16) as pm,
              nc.sbuf_tensor(self.un("rty"), [128, T], BF16) as y):
            slices = [(i * 512, min(512, T - i * 512)) for i in range(5)]
            order = [[0, 1] + list(range(2, 18)), [1, 0] + list(range(17, 1, -1))]
            masks = [self.maskf, self.maskb]
            pcnt = 0
            for h in range(4):
                S.dma("sp", qk[:], self.QK[h].rearrange("s p n -> p s n"), writes=["qk"])
                S.dma("sp", vt[:], self.RV[:, h * 128:(h + 1) * 128].rearrange("(c p) e -> p c e", p=128), writes=["vt"])
                S.dma("sp", rgz[:], self.Z[8 + h], writes=["rgz"])
                for d in range(2):
                    r = d
                    S.dma("sp", So[:, d, :], self.ex_r(r, EX_SL + (d * 4 + h) * 16384, 16384).rearrange(
                        "(p n) -> p n", n=128), writes=[("So", d)])
                    S.memset("dve", S32[:, d, :], 0.0, writes=[("S32", d)])
                    S.memset("pool", Sb[:, d, 0, :], 0.0, writes=[("Sb", d, 0)])
                tcnt = 0
                for d in range(2):
                    for c0 in range(0, 18, 4):
                        ng = min(4, 18 - c0)
                        bk = tcnt % 2
                        tcnt += 1
                        for j in range(ng):
                            ci = c0 + j
                            S.tr(self.psb[bk][:, j * 128:(j + 1) * 128], qk[:, 2 + d, ci * 128:(ci + 1) * 128], self.ident[:],
                                 reads=["qk", "ident"], writes=[("psb", bk)])
                        S.cp("act" if bk else "dve", kt[:, d, c0:c0 + ng, :],
                             self.psb[bk][:, 0:ng * 128].rearrange("p (h n) -> p h n", n=128),
                             reads=[("psb", bk)], writes=[("kt", d, c0 + j) for j in range(ng)])
                written = set()
                sbi = [0, 0]
                for step in range(18):
                    for d in range(2):
                        ci = order[d][step]
                        cs = slice(ci * 128, (ci + 1) * 128)
                        si = d * 4 + h
                        if step == 2:
                            S.ts("dve", S32[:, d, :], S32[:, d, :], self.bc1[:, si:si + 1], None, ALU.mult, ALU.bypass,
                                 reads=[("S32", d), "bc1"], writes=[("S32", d)])
                            S.stt(S32[:, d, :], So[:, d, :], (half if d == 0 else omh), S32[:, d, :], ALU.mult, ALU.add,
                                  reads=[("So", d), ("S32", d)], writes=[("S32", d)])
                            nb = 1 - sbi[d]
                            S.cp("act", Sb[:, d, nb, :], S32[:, d, :], reads=[("S32", d)], writes=[("Sb", d, nb)])
                            sbi[d] = nb
                        sc, ksc = self.nextps()
                        S.mm(sc[:, 0:128], qk[:, 2 + d, cs], qk[:, d, cs], True, True, reads=["qk"], writes=[ksc])
                        p4 = pcnt % 4
                        pcnt += 1
                        S.tt("dve", pm[:, p4, :], sc[:, 0:128], masks[d][:], ALU.mult, reads=[ksc, "maskf", "maskb"],
                             writes=[("pm", p4)])
                        o, ko = self.nextps()
                        S.mm(o[:, 0:128], vt[:, ci, :], pm[:, p4, :], True, False, reads=["vt", ("pm", p4)], writes=[ko])
                        S.mm(o[:, 0:128], Sb[:, d, sbi[d], :], qk[:, d, cs], False, True, reads=[("Sb", d, sbi[d]), "qk"],
                             writes=[ko])
                        if ci not in written:
                            S.cp("act", acc[:, cs], o[:, 0:128], reads=[ko], writes=[("acc", ci)])
                            written.add(ci)
                        else:
                            S.tt("dve", acc[:, cs], o[:, 0:128], acc[:, cs], ALU.add, reads=[ko, ("acc", ci)],
                                 writes=[("acc", ci)])
                        kv, kkv = self.nextps()
                        S.mm(kv[:, 0:128], kt[:, d, ci, :], vt[:, ci, :], True, True, reads=[("kt", d, ci), "vt"],
                             writes=[kkv])
                        gc = self.gC[:, si:si + 1]
                        S.act(S32[:, d, :], S32[:, d, :], AF.Identity, scale=gc,
                              reads=[("S32", d), "gC"], writes=[("S32", d)])
                        S.stt(S32[:, d, :], kv[:, 0:128], gc, S32[:, d, :], ALU.mult, ALU.add,
                              reads=[kkv, ("S32", d), "gC"], writes=[("S32", d)])
                        nb = 1 - sbi[d]
                        S.cp("act", Sb[:, d, nb, :], S32[:, d, :], reads=[("S32", d)], writes=[("Sb", d, nb)])
                        sbi[d] = nb
                acck = [("acc", ci) for ci in range(18)]
                S.act(rgz[:], rgz[:], AF.Silu, reads=["rgz"], writes=["rgz"])
                for (s0, n) in slices:
                    pmn, kmn = self.nextps()
                    S.mm(pmn[:, 0:n], self.ones32[:], acc[:, s0:s0 + n], True, True, reads=acck + ["ones32"], writes=[kmn])
                    S.stt(yc[:, s0:s0 + n], pmn[:, 0:n], -1.0 / 128, acc[:, s0:s0 + n], ALU.mult, ALU.add,
                          reads=[kmn] + acck, writes=[("yc", s0)])
                    S.act(sq32[:, s0:s0 + n], yc[:, s0:s0 + n], AF.Square, reads=[("yc", s0)], writes=[("sq32", s0)])
                    pvr, kvr = self.nextps()
                    S.mm(pvr[:, 0:n], self.ones32[:], sq32[:, s0:s0 + n], True, True, reads=[("sq32", s0)], writes=[kvr])
                    S.act(rs[:, s0:s0 + n], pvr[:, 0:n], AF.Sqrt, scale=1.0 / 128, bias=self.epsb[:, 0:1], reads=[kvr],
                          writes=[("rs", s0)])
                    S.recip(rs[:, s0:s0 + n], rs[:, s0:s0 + n], reads=[("rs", s0)], writes=[("rs", s0)])
                    S.tt("pool", yc[:, s0:s0 + n], yc[:, s0:s0 + n], rs[:, s0:s0 + n], ALU.mult,
                         reads=[("yc", s0), ("rs", s0)], writes=[("yc", s0)])
                    S.stt(y[:, s0:s0 + n], yc[:, s0:s0 + n], self.pvl(l, 96 + h), rgz[:, s0:s0 + n], ALU.mult, ALU.mult,
                          reads=[("yc", s0), "rgz"], writes=[("y", s0)])
                S.dma("sp", self.YR[h], y[:], reads=[("y", s0) for (s0, n) in slices], writes=[("YR", h)])
            S.flush()

    def phase_attn(self, l):
        nc, S = self.nc, self.S
        NK = CTX + 2 * LAT
        with (nc.sbuf_tensor(self.un("atk"), [128, 2, NK], BF16) as kT,
              nc.sbuf_tensor(self.un("atv"), [128, 2, 2, 34, 128], BF16) as vv,
              nc.sbuf_tensor(self.un("atq"), [128, 2, 2, T], BF16) as qa,
              nc.sbuf_tensor(self.un("aty"), [128, 2, T], BF16) as ya,
              nc.sbuf_tensor(self.un("atp"), [128, 4, 512], BF16) as pt,
              nc.sbuf_tensor(self.un("ato"), [128, 2, 128], BF16) as onz,
              nc.sbuf_tensor(self.un("atr"), [128, 2, 512], F32) as rden):
            S.memset("pool", qa[:], 0.0, writes=[("qa", 0), ("qa", 1)])
            S.memset("pool", vv[:], 0.0, writes=[("vv", 0), ("vv", 1)])
            S.memset("dve", onz[:], 0.0, writes=["onz"])
            S.memset("dve", onz[:, 0, 0:64], 1.0, writes=["onz"])
            S.memset("dve", onz[:, 1, 64:128], 1.0, writes=["onz"])
            for g in range(2):
                for hh in range(2):
                    ps_ = slice(hh * 64, (hh + 1) * 64)
                    S.dma("pool", kT[ps_, g, 0:CTX], self.KAC[g * 64:(g + 1) * 64, :], writes=[("kT", g)])
                    for r in range(2):
                        for j in range(4):
                            src = self.EXGP[j].ap()[r * 128 + g * 64:r * 128 + (g + 1) * 64, :]
                            c0 = CTX + r * LAT + j * 512
                            S.dma("pool", kT[ps_, g, c0:c0 + 512], src, writes=[("kT", g)])
                    cs_ = slice(hh * 64, (hh + 1) * 64)
                    S.dma("pool", vv[:, g, hh, 0:2, cs_],
                          self.AVC.rearrange("(c p) e -> p c e", p=128)[:, :, g * 64:(g + 1) * 64], writes=[("vv", g)])
                    for r in range(2):
                        for j in range(4):
                            src = self.ex_r(r, EX_AV + j * 65536, 65536).rearrange("(c p e) -> p c e", p=128, e=128)
                            c0 = 2 + r * 16 + j * 4
                            S.dma("pool", vv[:, g, hh, c0:c0 + 4, cs_], src[:, :, g * 64:(g + 1) * 64],
                                  writes=[("vv", g)])
            ones64 = self.ones[:, 0:64]
            qtiles = [(0, CTX, [0, 1])] + [(CTX + 512 * i, 512, list(range(34))) for i in range(4)]
            its = []
            qcnt = 0
            for c in range(4):
                for qi, (q0, nq, keys) in enumerate(qtiles):
                    pq = qcnt % 2
                    qcnt += 1
                    for kc in keys:
                        for hh in range(2):
                            its.append(dict(c=c, q0=q0, nq=nq, kc=kc, hh=hh, pq=pq, first=(kc == keys[0]),
                                            last=(kc == keys[-1]), qend=(kc == keys[-1] and hh == 1),
                                            cend=(kc == keys[-1] and hh == 1 and qi == len(qtiles) - 1),
                                            cstart=(kc == keys[0] and hh == 0 and qi == 0)))

            def emit_qk(i):
                it = its[i]
                c, g, hh, nq, q0, kc = it["c"], it["c"] // 2, it["hh"], it["nq"], it["q0"], it["kc"]
                if it["cstart"]:
                    for h2 in range(2):
                        S.dma("sp", qa[h2 * 64:(h2 + 1) * 64, c % 2, h2, :], self.QA[c, h2 * 64:(h2 + 1) * 64, :],
                              writes=[("qa", c % 2)])
                si, p4 = i % 2, i % 4
                sp_, ks = self.ps[si], ("ps", si)
                S.mm(sp_[:, 0:nq], kT[:, g, kc * 128:(kc + 1) * 128], qa[:, c % 2, hh, q0:q0 + nq], True, True,
                     reads=[("kT", g), ("qa", c % 2)], writes=[ks])
                S.act(pt[:, p4, 0:nq], sp_[:, 0:nq], AF.Exp, scale=0.125, reads=[ks], writes=[("pt", p4)])

            def emit_pv(i):
                it = its[i]
                c, g, hh, nq, q0, kc, pq = it["c"], it["c"] // 2, it["hh"], it["nq"], it["q0"], it["kc"], it["pq"]
                ps_ = slice(hh * 64, (hh + 1) * 64)
                p4 = i % 4
                num, knum = self.ps[2 + 2 * pq], ("ps", 2 + 2 * pq)
                den, kden = self.ps[3 + 2 * pq], ("ps", 3 + 2 * pq)
                st_ = it["first"] and hh == 0
                sp2 = it["last"] and hh == 1
                S.mm(num[:, 0:nq], vv[:, g, hh, kc, :], pt[:, p4, 0:nq], st_, sp2,
                     reads=[("vv", g), ("pt", p4)], writes=[knum])
                S.mm(den[:, 0:nq], onz[:, hh, :], pt[:, p4, 0:nq], st_, sp2,
                     reads=["onz", ("pt", p4)], writes=[kden])
                if it["qend"]:
                    S.recip(rden[:, pq, 0:nq], den[:, 0:nq], reads=[kden], writes=[("rden", pq)])
                    S.tt("dve", ya[:, c % 2, q0:q0 + nq], num[:, 0:nq], rden[:, pq, 0:nq], ALU.mult,
                         reads=[knum, ("rden", pq)], writes=[("ya", c % 2)])
                if it["cend"]:
                    S.dma("sp", self.YA[c], ya[:, c % 2, :], reads=[("ya", c % 2)], writes=[("YA", c)])

            emit_qk(0)
            for i in range(len(its)):
                if i + 1 < len(its):
                    emit_qk(i + 1)
                emit_pv(i)
            S.flush()

    def phase_merge(self, l):
        nc, S = self.nc, self.S
        with (nc.sbuf_tensor(self.un("mgwg"), [128, 8, 3072], BF16) as wg,
              nc.sbuf_tensor(self.un("mgwb"), [128, 12, D], BF16) as wb,
              nc.sbuf_tensor(self.un("mgwo"), [128, 8, D], BF16) as wo,
              nc.sbuf_tensor(self.un("mgx0"), [128, 8, TN], F32) as xt0,
              nc.sbuf_tensor(self.un("mgx1"), [128, 8, TN], F32) as xt1,
              nc.sbuf_tensor(self.un("mgh0"), [128, 8, TN], BF16) as h0,
              nc.sbuf_tensor(self.un("mgh1"), [128, 8, TN], BF16) as h1,
              nc.sbuf_tensor(self.un("mgy0"), [128, 12, TN], BF16) as y0,
              nc.sbuf_tensor(self.un("mgy1"), [128, 12, TN], BF16) as y1,
              nc.sbuf_tensor(self.un("mgsg"), [128, 3, TN], F32) as sg,
              nc.sbuf_tensor(self.un("mgtm"), [128, 2, TN], F32) as tm,
              nc.sbuf_tensor(self.un("mgma"), [128, 2, TN], F32) as ma,
              nc.sbuf_tensor(self.un("mgm"), [128, 8, TN], BF16) as m):
            xts, hs, ys = [xt0, xt1], [h0, h1], [y0, y1]
            src = self.w_in[l].rearrange("(k p) c -> p k c", p=128)
            for pc in range(8):
                S.dma("pool", wg[:, :, pc * 384:(pc + 1) * 384], src[:, :, 3840 + pc * 384:3840 + (pc + 1) * 384],
                      writes=[("wg", pc)])
            for n in range(3):
                S.dma("pool", wb[:, n * 4:(n + 1) * 4, :], self.w_branch[l, n].rearrange("(k p) c -> p k c", p=128),
                      writes=[("wb", n)])
            srco = self.w_out[l].rearrange("(k p) c -> p k c", p=128)
            for kk in range(2):
                S.dma("pool", wo[:, kk * 4:(kk + 1) * 4, :], srco[:, kk * 4:(kk + 1) * 4, :], writes=[("wo", kk)])
            ysrc = [self.YR, self.YL, self.YA]

            def loads(t):
                par = t % 2
                t0 = t * TN
                self.load_x(xts[par], t, par)
                S.dma("sp", hs[par][:], self.H[:, :, t0:t0 + TN].rearrange("k p n -> p k n"), writes=[("h", par)])
                for n in range(3):
                    S.dma("sp", ys[par][:, n * 4:(n + 1) * 4, :], ysrc[n][:, :, t0:t0 + TN].rearrange("k p n -> p k n"),
                          writes=[("y3", par, n)])

            loads(0)
            scnt = 0
            for t in range(NT):
                par = t % 2
                xt, h, y3 = xts[par], hs[par], ys[par]
                c = 1 if t == 0 else 0
                if t + 1 < NT:
                    loads(t + 1)
                for i in range(8):
                    mi = i % 2
                    for n in range(3):
                        pg, kg = self.nextps()
                        wc = n * 8 + i
                        for k in range(8):
                            S.mm(pg[:, 0:TN], wg[:, k, wc * 128:(wc + 1) * 128], h[:, k, :], k == 0, k == 7,
                                 reads=[("wg", wc // 3), ("h", par)], writes=[kg])
                        pu, ku = self.nextps()
                        for kk in range(4):
                            S.mm(pu[:, 0:TN], wb[:, n * 4 + kk, i * 128:(i + 1) * 128], y3[:, n * 4 + kk, :], kk == 0, kk == 3,
                                 reads=[("wb", n), ("y3", par, n)], writes=[ku])
                        s3 = scnt % 3
                        scnt += 1
                        S.act(sg[:, s3, :], pg[:, 0:TN], AF.Sigmoid, reads=[kg], writes=[("sg", s3)])
                        if n == 0:
                            S.tt("dve", ma[:, mi, :], sg[:, s3, :], pu[:, 0:TN], ALU.mult, reads=[("sg", s3), ku],
                                 writes=[("ma", mi)])
                        else:
                            S.tt("dve", tm[:, n - 1, :], sg[:, s3, :], pu[:, 0:TN], ALU.mult, reads=[("sg", s3), ku],
                                 writes=[("tm", n - 1)])
                            if n == 1:
                                S.tt("pool", ma[:, mi, :], ma[:, mi, :], tm[:, 0, :], ALU.add,
                                     reads=[("ma", mi), ("tm", 0)], writes=[("ma", mi)])
                            else:
                                S.tt("pool", m[:, i, :], ma[:, mi, :], tm[:, 1, :], ALU.add,
                                     reads=[("ma", mi), ("tm", 1)], writes=[("m", i)])
                for i in range(8):
                    po, ko = self.nextps()
                    for k in range(8):
                        S.mm(po[:, 0:TN], wo[:, k, i * 128:(i + 1) * 128], m[:, k, :], k == 0, k == 7,
                             reads=[("wo", k // 4), ("m", k)], writes=[ko])
                    S.stt(xt[:, i, :], po[:, 0:TN], self.modG[:, 1, i, c:c + 1], xt[:, i, :], ALU.mult, ALU.add,
                          reads=[ko, ("xt", par, i), ("modG", 1)], writes=[("xt", par, i)])
                self.store_x(xt, t, par)
            S.flush()


def _perm128():
    return np.concatenate([np.arange(32, 64), np.arange(0, 32), np.arange(96, 128), np.arange(64, 96)])


def _perm64():
    return np.concatenate([np.arange(16, 32), np.arange(0, 16), np.arange(48, 64), np.arange(32, 48)])


def _fm(v):
    v = np.asarray(v, np.float32)
    lead = v.shape[:-1]
    n = v.shape[-1] // 128
    v = v.reshape(*lead, n, 128)
    return np.moveaxis(v, -1, 0)


def _rope_tables(half):
    theta = 10000.0
    tl = np.arange(LAT) + half * LAT
    rows = (tl // 64).astype(np.float32)
    cols = (tl % 64).astype(np.float32)
    tabs = np.zeros((4, 128, T), np.float32)
    tabs[0, :, :CTX] = 1.0
    tabs[2, :, :CTX] = 1.0
    f = (theta ** (-np.arange(0, 64, 2, dtype=np.float32) / 64)).astype(np.float32)
    ar = (rows[None, :] * f[:, None]).astype(np.float32)
    ac = (cols[None, :] * f[:, None]).astype(np.float32)
    C = np.concatenate([np.cos(ar), np.cos(ar), np.cos(ac), np.cos(ac)], 0)
    Sn = np.concatenate([-np.sin(ar), np.sin(ar), -np.sin(ac), np.sin(ac)], 0)
    tabs[0, :, CTX:] = C
    tabs[1, :, CTX:] = Sn
    f = (theta ** (-np.arange(0, 32, 2, dtype=np.float32) / 32)).astype(np.float32)
    ar = (rows[None, :] * f[:, None]).astype(np.float32)
    ac = (cols[None, :] * f[:, None]).astype(np.float32)
    C = np.concatenate([np.cos(ar), np.cos(ar), np.cos(ac), np.cos(ac)], 0)
    Sn = np.concatenate([-np.sin(ar), np.sin(ar), -np.sin(ac), np.sin(ac)], 0)
    tabs[2, :, CTX:] = np.concatenate([C, C], 0)
    tabs[3, :, CTX:] = np.concatenate([Sn, Sn], 0)
    return tabs


def _consts():
    cst = np.zeros((6, 128, 128), np.float32)
    cst[0] = np.eye(128)
    cst[1] = 1.0
    cst[2, :64, :64] = 1.0
    cst[2, 64:, 64:] = 1.0
    j = np.arange(128)[:, None]
    i = np.arange(128)[None, :]
    cst[3] = (i >= j)
    cst[4] = (i <= j)
    p = np.arange(TN) % 128
    pos = np.zeros((2, 128, TN), np.float32)
    pos[0] = (p + 1)[None, :]
    pos[1] = (128 - p)[None, :]
    return cst, pos


def _pack_pv(inp, b, half):
    pv = np.zeros((128, NPV), np.float32)
    p64 = _perm64()
    for l in range(DEPTH):
        o = l * PL
        pv[:, o:o + 24] = _fm(inp["norm_g"][l]).reshape(128, 24)
        pv[:, o + 24:o + 96] = _fm(inp["b_mod"][l]).reshape(128, 72)
        pv[:, o + 96:o + 100] = _fm(inp["ret_norm_g"][l])
        pv[:, o + 100:o + 116] = _fm(inp["lru_conv_w"][l]).reshape(128, 16)
        pv[:, o + 116:o + 120] = _fm(inp["lru_conv_b"][l])
        pv[:, o + 120:o + 128] = _fm(inp["lru_b_a"][l]).reshape(128, 8)
        pv[:, o + 128:o + 136] = _fm(inp["lru_b_x"][l]).reshape(128, 8)
        pv[:, o + 136:o + 144] = _fm(inp["lru_lambda"][l]).reshape(128, 8)
        qg = np.asarray(inp["attn_q_norm_g"][l], np.float32)
        kg = np.asarray(inp["attn_k_norm_g"][l], np.float32)
        pv[:, o + 144] = np.tile(qg, 2)
        pv[:, o + 145] = np.tile(qg[p64], 2)
        pv[:, o + 146] = np.tile(kg, 2)
        pv[:, o + 147] = np.tile(kg[p64], 2)
        pv[:, o + 148:o + 156] = np.asarray(inp["ret_decay_logit"][l], np.float32).reshape(1, 8)
    o = DEPTH * PL
    pv[:, o:o + 8] = _fm(inp["final_norm_g"])
    pv[:, o + 8:o + 16] = _fm(inp["c"][b])
    pv[:, o + 16:o + 24] = _fm(inp["c_ctx"])
    pv[:, o + 24] = float(half)
    pv[:, o + 25] = float(1 - half)
    return pv


def _host_inputs(inp):
    f = lambda a: np.ascontiguousarray(np.asarray(a, np.float32))
    cst, pos = _consts()
    w_in = f(inp["w_in"])
    p128, p64 = _perm128(), _perm64()
    idx = []
    for c in range(8):
        idx.append(c * 128 + p128)
    for c in range(5):
        for hh in range(2):
            idx.append(3072 + c * 128 + hh * 64 + p64)
    idx = np.concatenate(idx)
    w_inp = np.ascontiguousarray(w_in[:, :, idx])
    lbd = np.zeros((DEPTH, 2, 2, 4, 128, 128), np.float32)
    for a, name in enumerate(("lru_w_a", "lru_w_x")):
        w = f(inp[name])
        for c in range(4):
            lbd[:, a, :, c, :64, :64] = w[:, :, 2 * c]
            lbd[:, a, :, c, 64:, 64:] = w[:, :, 2 * c + 1]
    shared = {
        "cst": cst, "pos": pos, "w_mod": f(inp["w_mod"][:N_LAYERS]), "ffn_w_in": f(inp["ffn_w_in"][:N_LAYERS]),
        "ffn_w_out": f(inp["ffn_w_out"][:N_LAYERS]), "w_in": w_in[:N_LAYERS], "w_inp": w_inp[:N_LAYERS],
        "lbd": lbd[:N_LAYERS], "w_branch": f(inp["w_branch"][:N_LAYERS]), "w_out": f(inp["w_out"][:N_LAYERS]),
    }
    x = f(inp["x"])
    ctx = f(inp["ctx"])
    ropes = [_rope_tables(0), _rope_tables(1)]
    maps = []
    for core in range(8):
        b, half = core // 2, core % 2
        xt = np.concatenate([ctx[b], x[b, half * LAT:(half + 1) * LAT]], 0)
        xin = np.ascontiguousarray(xt.T.reshape(8, 128, T))
        m = dict(shared)
        m["xin"] = xin
        m["pv"] = _pack_pv(inp, b, half)
        m["rope"] = ropes[half]
        maps.append(m)
    return maps


_NC_CACHE = {}


def _get_nc():
    key = (DEBUG_STOP, N_LAYERS)
    if key not in _NC_CACHE:
        nc = bass.Bass("TRN2", target_bir_lowering=False)
        Builder(nc).build()
        _NC_CACHE[key] = nc
    return _NC_CACHE[key]


def kernel(**inputs):
    maps = _host_inputs(inputs)
    nc = _get_nc()
    if TRACE:
        res = run_bass_kernel_spmd(nc, maps, core_ids=list(range(8)), trace=True)
        print("exec_time_ns", res.exec_time_ns)
    else:
        res = run_bass_kernel_spmd(nc, maps, core_ids=list(range(8)))
    if DEBUG_STOP is not None:
        return [r["dbg"] for r in res.results]
    out = np.zeros((4, 2 * LAT, D), np.float32)
    for core in range(8):
        b, half = core // 2, core % 2
        o = res.results[core]["out"]
        out[b, half * LAT:(half + 1) * LAT] = o.reshape(D, LAT).T
    return out
```

```python
import numpy as np
import concourse.bass as bass
import concourse.mybir as mybir
from concourse.bass_utils import run_bass_kernel_spmd

F32 = mybir.dt.float32
BF16 = mybir.dt.bfloat16
AF = mybir.ActivationFunctionType
ALU = mybir.AluOpType

D = 1024
DEPTH = 4
CTX = 256
LAT = 2048
T = CTX + LAT
TN = 256
NT = T // TN
DFF = 2816
DIN = 6912
EPS = 1e-6
NZ = 38
PL = 156
NPV = DEPTH * PL + 8 + 16 + 2
EX_KA = 0
EX_AV = EX_KA + 128 * LAT
EX_SL = EX_AV + LAT * 128
EX_LX = EX_SL + 8 * 128 * 128
EX_N = EX_LX + 4 * 128 * 4
EX2_N = 2 * 512

DEBUG_STOP = None
N_LAYERS = DEPTH
TRACE = False


class _I:
    __slots__ = ("eng", "fn", "waits", "sig", "idx", "kind", "sem", "target")


class Sched:
    R = 8

    def __init__(self, nc, sems):
        self.nc = nc
        self.sems = sems
        nxt = iter(range(len(sems)))
        self.csem = {e: next(nxt) for e in ("pe", "act", "dve", "pool")}
        self.qsem = {q: [next(nxt) for _ in range(self.R)] for q in ("sp", "pool")}
        self.ccsem = next(nxt)
        self.ncc = 0
        self.qn = {"sp": 0, "pool": 0}
        self.cnt = {e: 0 for e in self.csem}
        self.lists = {e: [] for e in ("pe", "act", "dve", "pool", "sp")}
        self.lastw = {}
        self.rd = {}
        self.seen = {e: {} for e in self.lists}
        self.barrier = []
        self.ninstr = 0

    def add(self, eng, fn, reads=(), writes=(), kind="c"):
        I = _I()
        I.eng, I.fn, I.kind, I.sig, I.idx, I.waits = eng, fn, kind, False, None, []
        deps = []
        for b in reads:
            w = self.lastw.get(b)
            if w is not None:
                deps.append(w)
        for b in writes:
            w = self.lastw.get(b)
            if w is not None:
                deps.append(w)
            r = self.rd.get(b)
            if r:
                deps.extend(r[0].values())
                deps.extend(r[1])
        for J in deps:
            if J.kind in ("dma", "cc"):
                I.waits.append(J)
            elif J.eng == eng and kind == "c":
                if eng == "pe":
                    continue
                I.waits.append(J)
                J.sig = True
            else:
                I.waits.append(J)
                J.sig = True
        if kind == "dma":
            n = self.qn[eng]
            self.qn[eng] += 1
            I.sem = self.qsem[eng][n % self.R]
            I.target = 16 * (n // self.R + 1)
            if n >= self.R:
                I.waits.append((I.sem, 16 * (n // self.R)))
        elif kind == "cc":
            I.sem = self.ccsem
            self.ncc += 1
            I.target = self.ncc
        for b in reads:
            r = self.rd.setdefault(b, ({}, []))
            if kind == "c":
                r[0][eng] = I
            else:
                r[1].append(I)
        for b in writes:
            self.lastw[b] = I
            self.rd[b] = ({}, [])
        self.lists[eng].append(I)
        self.ninstr += 1
        return I

    def dma(self, q, out, in_, reads=(), writes=(), slow=False):
        if slow:
            return self.add(q, lambda e: e.dma_start(out=out, in_=in_, allow_slow_non_contiguous=True), reads, writes,
                            kind="dma")
        return self.add(q, lambda e: e.dma_start(out=out, in_=in_), reads, writes, kind="dma")

    def mm(self, out, lhsT, rhs, start, stop, reads=(), writes=()):
        return self.add("pe", lambda e: e.matmul(out, lhsT=lhsT, rhs=rhs, start=start, stop=stop), reads, writes)

    def tr(self, out, in_, ident, reads=(), writes=()):
        return self.add("pe", lambda e: e.transpose(out, in_, ident), reads, writes)

    def act(self, out, in_, func, reads=(), writes=(), bias=None, scale=None):
        kw = {}
        if bias is not None:
            kw["bias"] = bias
        if scale is not None:
            kw["scale"] = scale
        return self.add("act", lambda e: e.activation(out=out, in_=in_, func=func, **kw), reads, writes)

    def tt(self, eng, out, in0, in1, op, reads=(), writes=()):
        return self.add(eng, lambda e: e.tensor_tensor(out=out, in0=in0, in1=in1, op=op), reads, writes)

    def ts(self, eng, out, in0, s1, s2, op0, op1, reads=(), writes=()):
        return self.add(eng, lambda e: e.tensor_scalar(out=out, in0=in0, scalar1=s1, scalar2=s2, op0=op0, op1=op1),
                        reads, writes)

    def stt(self, out, in0, scalar, in1, op0, op1, reads=(), writes=()):
        return self.add("dve", lambda e: e.scalar_tensor_tensor(out=out, in0=in0, scalar=scalar, in1=in1,
                                                                op0=op0, op1=op1), reads, writes)

    def cp(self, eng, out, in_, reads=(), writes=()):
        if eng == "act":
            return self.add("act", lambda e: e.copy(out=out, in_=in_), reads, writes)
        return self.add(eng, lambda e: e.tensor_copy(out=out, in_=in_), reads, writes)

    def recip(self, out, in_, reads=(), writes=()):
        return self.add("dve", lambda e: e.reciprocal(out=out, in_=in_), reads, writes)

    def memset(self, eng, ap, val, reads=(), writes=()):
        return self.add(eng, lambda e: e.memset(ap, val), reads, writes)

    def scan(self, out, d0, d1, init, reads=(), writes=()):
        return self.add("dve", lambda e: e.tensor_tensor_scan(out=out, data0=d0, data1=d1, initial=init,
                                                              op0=ALU.mult, op1=ALU.add), reads, writes)

    def cc(self, ins_ap, outs_ap, reads=(), writes=()):
        groups = [[0, 1], [2, 3], [4, 5], [6, 7]]
        return self.add("pool", lambda e: e.collective_compute("AllGather", ALU.bypass, replica_groups=groups,
                                                               ins=[ins_ap], outs=[outs_ap]),
                        reads, writes, kind="cc")

    def flush(self):
        nc = self.nc
        for e in self.csem:
            last = None
            for I in self.lists[e]:
                if I.kind == "c":
                    last = I
            if last is not None:
                last.sig = True
            for I in self.lists[e]:
                if I.kind == "c" and I.sig:
                    self.cnt[e] += 1
                    I.idx = self.cnt[e]
        sems = self.sems

        def emit(eh, eng):
            seen = self.seen[eng]

            def wait(si, val):
                if seen.get(si, 0) < val:
                    eh.wait_ge(sems[si], val)
                    seen[si] = val

            for (si, val) in self.barrier:
                wait(si, val)
            for I in self.lists[eng]:
                for w in I.waits:
                    if isinstance(w, tuple):
                        wait(w[0], w[1])
                    elif w.kind in ("dma", "cc"):
                        wait(w.sem, w.target)
                    else:
                        wait(self.csem[w.eng], w.idx)
                ins = I.fn(eh)
                if I.kind == "dma":
                    ins.then_inc(sems[I.sem], 16)
                elif I.kind == "cc":
                    ins.then_inc(sems[I.sem])
                elif I.sig:
                    ins.then_inc(sems[self.csem[eng]], 1)

        with nc.Block() as block:
            @block.tensor
            def _(eh):
                emit(eh, "pe")

            @block.scalar
            def _(eh):
                emit(eh, "act")

            @block.vector
            def _(eh):
                emit(eh, "dve")

            @block.gpsimd
            def _(eh):
                emit(eh, "pool")

            @block.sync
            def _(eh):
                emit(eh, "sp")

        bar = [(self.csem[e], self.cnt[e]) for e in self.csem if self.cnt[e] > 0]
        for q in ("sp", "pool"):
            n = self.qn[q]
            for i in range(self.R):
                if n > i:
                    bar.append((self.qsem[q][i], 16 * ((n - 1 - i) // self.R + 1)))
        if self.ncc:
            bar.append((self.ccsem, self.ncc))
        self.barrier = bar
        self.lists = {e: [] for e in self.lists}
        self.lastw = {}
        self.rd = {}

    def final_wait(self):
        nc = self.nc
        sems = self.sems
        with nc.Block() as block:
            @block.sync
            def _(eh):
                for (si, val) in self.barrier:
                    eh.wait_ge(sems[si], val)

            @block.gpsimd
            def _(eh):
                for (si, val) in self.barrier:
                    eh.wait_ge(sems[si], val)


class Builder:
    def __init__(self, nc):
        self.nc = nc
        self.pscur = 0

    def declare(self):
        nc = self.nc
        di = lambda n, s: nc.dram_tensor(n, s, F32, kind="ExternalInput").ap()
        self.xin = di("xin", [8, 128, T])
        self.pvin = di("pv", [128, NPV])
        self.cst = di("cst", [6, 128, 128])
        self.posin = di("pos", [2, 128, TN])
        self.rope = di("rope", [4, 128, T])
        self.w_mod = di("w_mod", [N_LAYERS, D, 9 * D])
        self.ffn_w_in = di("ffn_w_in", [N_LAYERS, 2, D, 2 * DFF])
        self.ffn_w_out = di("ffn_w_out", [N_LAYERS, 2, DFF, D])
        self.w_in = di("w_in", [N_LAYERS, D, DIN])
        self.w_inp = di("w_inp", [N_LAYERS, D, 13 * 128])
        self.lbd = di("lbd", [N_LAYERS, 2, 2, 4, 128, 128])
        self.w_branch = di("w_branch", [N_LAYERS, 3, 512, D])
        self.w_out = di("w_out", [N_LAYERS, D, D])
        self.out = nc.dram_tensor("out", [8, 128, LAT], F32, kind="ExternalOutput").ap()
        if DEBUG_STOP is not None:
            self.dbg = nc.dram_tensor("dbg", [8, 128, T], F32, kind="ExternalOutput").ap()
        dt = lambda n, s, d=F32: nc.dram_tensor(n, s, d)
        self.X = dt("X", [8, 128, T]).ap()
        self.H = dt("H", [8, 128, T], BF16).ap()
        self.Z = dt("Z", [NZ, 128, T]).ap()
        self.RV = dt("RV", [T, 512], BF16).ap()
        self.AVC = dt("AVC", [CTX, 128]).ap()
        self.KAC = dt("KAC", [128, CTX]).ap()
        self.QA = dt("QA", [4, 128, T], BF16).ap()
        self.QK = dt("QK", [4, 4, 128, T], BF16).ap()
        self.EXP = [dt("EXP%d" % j, [128, 512]) for j in range(10)] + [dt("EXPL", [4, 512])]
        self.EXGP = [dt("EXGP%d" % j, [256, 512]) for j in range(10)] + [dt("EXGPL", [8, 512])]
        self.EX2t = dt("EX2", [EX2_N // 512, 512])
        self.EX2Gt = dt("EX2G", [2 * EX2_N // 512, 512])
        self.EX2 = self.EX2t.ap().rearrange("a b -> (a b)")
        self.EX2G = self.EX2Gt.ap().rearrange("a b -> (a b)")
        self.HS = dt("HS", [4, 128, T]).ap()
        self.ACF = dt("ACF", [4, 128, LAT]).ap()
        self.ACB = dt("ACB", [4, 128, LAT]).ap()
        self.YR = dt("YR", [4, 128, T], BF16).ap()
        self.YL = dt("YL", [4, 128, T], BF16).ap()
        self.YA = dt("YA", [4, 128, T], BF16).ap()

    def un(self, name):
        self.uid = getattr(self, "uid", 0) + 1
        return "%s_%d" % (name, self.uid)

    def nextps(self, n=6):
        i = self.pscur % n
        self.pscur += 1
        return self.ps[i], ("ps", i)

    def pvl(self, l, off, n=1):
        return self.pv[:, l * PL + off:l * PL + off + n]

    def phase_init(self):
        nc, S = self.nc, self.S
        with nc.sbuf_tensor(self.un("ini_c"), [128, 5, 128], F32) as cf, nc.sbuf_tensor(self.un("ini_s"), [128, 16], F32) as sv:
            S.dma("sp", self.pv[:], self.pvin, writes=["pv"])
            S.dma("sp", cf[:], self.cst[0:5].rearrange("c p n -> p c n"), writes=["cf"])
            S.dma("sp", self.pos[:], self.posin.rearrange("c p n -> p c n"), writes=["pos"])
            S.cp("dve", self.ident[:], cf[:, 0, :], reads=["cf"], writes=["ident"])
            S.cp("dve", self.ones[:], cf[:, 1, :], reads=["cf"], writes=["ones"])
            S.cp("dve", self.bones[:], cf[:, 2, :], reads=["cf"], writes=["bones"])
            S.cp("dve", self.maskf[:], cf[:, 3, :], reads=["cf"], writes=["maskf"])
            S.cp("dve", self.maskb[:], cf[:, 4, :], reads=["cf"], writes=["maskb"])
            S.cp("dve", self.ones32[:], cf[:, 1, :], reads=["cf"], writes=["ones32"])
            base = DEPTH * PL + 8
            S.act(sv[:], self.pv[:, base:base + 16], AF.Silu, reads=["pv"], writes=["sv"])
            S.cp("dve", self.sb[:], sv[:].rearrange("p (c k) -> p k c", c=2), reads=["sv"], writes=["sb"])
            for k in range(8):
                S.dma("sp", self.X[k], self.xin[k], writes=[("X", k)])
            S.flush()

    def phase_mod(self, l):
        nc, S = self.nc, self.S
        with (nc.sbuf_tensor(self.un("wm0"), [128, 8, 1024], BF16) as wm0,
              nc.sbuf_tensor(self.un("wm1"), [128, 8, 1024], BF16) as wm1,
              nc.sbuf_tensor(self.un("mraw"), [128, 9, 8, 2], F32) as mraw,
              nc.sbuf_tensor(self.un("lg"), [128, 8], F32) as lg):
            wms = [wm0, wm1]
            src = self.w_mod[l].rearrange("(k p) c -> p k c", p=128)
            for b in range(9):
                wm = wms[b % 2]
                for kk in range(0, 8, 4):
                    S.dma("pool", wm[:, kk:kk + 4, :], src[:, kk:kk + 4, b * 1024:(b + 1) * 1024],
                          writes=[("wm", b % 2, kk)])
                pm, km = self.nextps()
                for i in range(8):
                    for k in range(8):
                        S.mm(pm[:, i * 2:(i + 1) * 2], wm[:, k, i * 128:(i + 1) * 128], self.sb[:, k, :],
                             k == 0, k == 7, reads=[("wm", b % 2, (k // 4) * 4), "sb"], writes=[km])
                bm = self.pvl(l, 24 + b * 8, 8)
                S.tt("dve", mraw[:, b, :, :], pm[:, 0:16].rearrange("p (i c) -> p i c", c=2),
                     bm.unsqueeze(2).broadcast_to([128, 8, 2]), ALU.add, reads=[km, "pv"], writes=[("mraw", b)])
            for s in range(3):
                g = self.pvl(l, s * 8, 8).unsqueeze(2).broadcast_to([128, 8, 2])
                S.stt(self.modA[:, s, :, :], mraw[:, 3 * s + 1, :, :], 1.0, g, ALU.add, ALU.mult,
                      reads=[("mraw", 3 * s + 1), "pv"], writes=[("modA", s)])
                S.cp("dve", self.modB[:, s, :, :], mraw[:, 3 * s, :, :], reads=[("mraw", 3 * s)], writes=[("modB", s)])
                S.ts("dve", self.modG[:, s, :, :], mraw[:, 3 * s + 2, :, :], 0.5 if s != 1 else 1.0, None,
                     ALU.mult, ALU.bypass, reads=[("mraw", 3 * s + 2)], writes=[("modG", s)])
            S.act(lg[:], self.pvl(l, 148, 8), AF.Exp, scale=-1.0, reads=["pv"], writes=["lg"])
            S.act(lg[:], lg[:], AF.Ln, bias=1.0, reads=["lg"], writes=["lg"])
            S.ts("dve", self.lgam[:], lg[:], -1.0, None, ALU.mult, ALU.bypass, reads=["lg"], writes=["lgam"])
            S.ts("dve", self.nlgam[:], lg[:], 1.0, None, ALU.mult, ALU.bypass, reads=["lg"], writes=["nlgam"])
            S.act(self.gC[:], self.lgam[:], AF.Exp, scale=128.0, reads=["lgam"], writes=["gC"])
            hb = DEPTH * PL + 24
            S.ts("dve", lg[:, 0:4], self.lgam[:, 0:4], self.pv[:, hb:hb + 1], 2048.0, ALU.mult, ALU.mult,
                 reads=["lgam", "pv"], writes=["lg"])
            S.ts("dve", lg[:, 4:8], self.lgam[:, 4:8], self.pv[:, hb + 1:hb + 2], 2048.0, ALU.mult, ALU.mult,
                 reads=["lgam", "pv"], writes=["lg"])
            S.act(self.bc1[:], lg[:], AF.Exp, reads=["lg"], writes=["bc1"])
            S.act(self.lcp[:], self.pvl(l, 136, 8), AF.Exp, scale=-1.0, reads=["pv"], writes=["lcp"])
            S.act(self.lcp[:], self.lcp[:], AF.Ln, bias=1.0, reads=["lcp"], writes=["lcp"])
            S.ts("dve", self.lcp2[:], self.lcp[:], -16.0, None, ALU.mult, ALU.bypass, reads=["lcp"], writes=["lcp2"])
            S.ts("dve", self.lcp[:], self.lcp[:], -8.0, None, ALU.mult, ALU.bypass, reads=["lcp"], writes=["lcp"])
            S.flush()

    def load_x(self, xt, t, par):
        S = self.S
        S.dma("sp", xt[:], self.X[:, :, t * TN:(t + 1) * TN].rearrange("k p n -> p k n"),
              reads=[("X", t)], writes=[("xt", par, i) for i in range(8)])

    def store_x(self, xt, t, par, dst=None):
        S = self.S
        dst = self.X if dst is None else dst
        S.dma("sp", dst[:, :, t * TN:(t + 1) * TN].rearrange("k p n -> p k n"), xt[:],
              reads=[("xt", par, i) for i in range(8)], writes=[("X", t)])

    def norm_mod(self, xt, par, sq, rs, tmp, h, sub, c):
        S = self.S
        xk = [("xt", par, i) for i in range(8)]
        S.act(sq[:], xt[:], AF.Square, reads=xk, writes=["sq"])
        pn, kn = self.nextps()
        for k in range(8):
            S.mm(pn[:, 0:TN], self.ones[:], sq[:, k, :], k == 0, k == 7, reads=["sq", "ones"], writes=[kn])
        S.act(rs[:], pn[:, 0:TN], AF.Sqrt, scale=1.0 / D, bias=self.epsb[:, 0:1], reads=[kn], writes=["rs"])
        S.recip(rs[:], rs[:], reads=["rs"], writes=["rs"])
        S.tt("dve", tmp[:], xt[:], rs[:].unsqueeze(1).broadcast_to([128, 8, TN]), ALU.mult,
             reads=xk + ["rs"], writes=["tmp"])
        for k in range(8):
            S.act(h[:, k, :], tmp[:, k, :], AF.Identity, scale=self.modA[:, sub, k, c:c + 1],
                  bias=self.modB[:, sub, k, c:c + 1], reads=["tmp", ("modA", sub), ("modB", sub)],
                  writes=[("h", par, k)])

    def phase_ffn(self, l, which, sub):
        nc, S = self.nc, self.S
        with (nc.sbuf_tensor(self.un("ffw1"), [128, 8, 2 * DFF], BF16) as w1,
              nc.sbuf_tensor(self.un("ffw2"), [128, 22, D], BF16) as w2,
              nc.sbuf_tensor(self.un("fxt0"), [128, 8, TN], F32) as xt0,
              nc.sbuf_tensor(self.un("fxt1"), [128, 8, TN], F32) as xt1,
              nc.sbuf_tensor(self.un("fsq"), [128, 8, TN], BF16) as sq,
              nc.sbuf_tensor(self.un("fh0"), [128, 8, TN], BF16) as h0,
              nc.sbuf_tensor(self.un("fh1"), [128, 8, TN], BF16) as h1,
              nc.sbuf_tensor(self.un("fg"), [128, 22, TN], BF16) as g,
              nc.sbuf_tensor(self.un("fsl0"), [128, TN], F32) as sl0,
              nc.sbuf_tensor(self.un("fsl1"), [128, TN], F32) as sl1,
              nc.sbuf_tensor(self.un("frs"), [128, TN], F32) as rs,
              nc.sbuf_tensor(self.un("ftmp"), [128, 8, TN], F32) as tmp):
            xts, hs, sls = [xt0, xt1], [h0, h1], [sl0, sl1]
            src1 = self.ffn_w_in[l, which].rearrange("(k p) c -> p k c", p=128)
            for cb in range(11):
                S.dma("pool", w1[:, :, cb * 512:(cb + 1) * 512], src1[:, :, cb * 512:(cb + 1) * 512],
                      writes=[("w1", cb)])
            src2 = self.ffn_w_out[l, which].rearrange("(j p) c -> p j c", p=128)
            for jb in range(11):
                S.dma("pool", w2[:, 2 * jb:2 * jb + 2, :], src2[:, 2 * jb:2 * jb + 2, :], writes=[("w2", jb)])
            self.load_x(xts[0], 0, 0)
            self.norm_mod(xts[0], 0, sq, rs, tmp, hs[0], sub, 1)
            for t in range(NT):
                par = t % 2
                xt, h = xts[par], hs[par]
                c = 1 if t == 0 else 0
                if t + 1 < NT:
                    self.load_x(xts[1 - par], t + 1, 1 - par)
                for j in range(22):
                    pa, ka = self.nextps()
                    pb, kb = self.nextps()
                    ca, cb_ = j * 128, DFF + j * 128
                    for k in range(8):
                        S.mm(pa[:, 0:TN], w1[:, k, ca:ca + 128], h[:, k, :], k == 0, k == 7,
                             reads=[("w1", ca // 512), ("h", par, k)], writes=[ka])
                    for k in range(8):
                        S.mm(pb[:, 0:TN], w1[:, k, cb_:cb_ + 128], h[:, k, :], k == 0, k == 7,
                             reads=[("w1", cb_ // 512), ("h", par, k)], writes=[kb])
                    sl = sls[j % 2]
                    S.act(sl[:], pa[:, 0:TN], AF.Silu, reads=[ka], writes=[("sl", j % 2)])
                    S.tt("dve", g[:, j, :], sl[:], pb[:, 0:TN], ALU.mult, reads=[("sl", j % 2), kb], writes=[("g", j)])
                if t + 1 < NT:
                    self.norm_mod(xts[1 - par], 1 - par, sq, rs, tmp, hs[1 - par], sub, 0)
                for i in range(8):
                    po, ko = self.nextps()
                    for j in range(22):
                        S.mm(po[:, 0:TN], w2[:, j, i * 128:(i + 1) * 128], g[:, j, :], j == 0, j == 21,
                             reads=[("w2", j // 2), ("g", j)], writes=[ko])
                    S.stt(xt[:, i, :], po[:, 0:TN], self.modG[:, sub, i, c:c + 1], xt[:, i, :], ALU.mult, ALU.add,
                          reads=[ko, ("xt", par, i), ("modG", sub)], writes=[("xt", par, i)])
                self.store_x(xt, t, par)
            S.flush()

    def phase_final(self):
        nc, S = self.nc, self.S
        with (nc.sbuf_tensor(self.un("nxt0"), [128, 8, TN], F32) as xt0,
              nc.sbuf_tensor(self.un("nxt1"), [128, 8, TN], F32) as xt1,
              nc.sbuf_tensor(self.un("nsq"), [128, 8, TN], BF16) as sq,
              nc.sbuf_tensor(self.un("nrs"), [128, TN], F32) as rs,
              nc.sbuf_tensor(self.un("no0"), [128, 8, TN], F32) as o0,
              nc.sbuf_tensor(self.un("no1"), [128, 8, TN], F32) as o1):
            xts, os_ = [xt0, xt1], [o0, o1]
            fb = DEPTH * PL
            for t in range(1, NT):
                par = t % 2
                xt, o = xts[par], os_[par]
                self.load_x(xt, t, par)
                xk = [("xt", par, i) for i in range(8)]
                S.act(sq[:], xt[:], AF.Square, reads=xk, writes=["sq"])
                pn, kn = self.nextps()
                for k in range(8):
                    S.mm(pn[:, 0:TN], self.ones[:], sq[:, k, :], k == 0, k == 7, reads=["sq"], writes=[kn])
                S.act(rs[:], pn[:, 0:TN], AF.Sqrt, scale=1.0 / D, bias=self.epsb[:, 0:1], reads=[kn], writes=["rs"])
                S.recip(rs[:], rs[:], reads=["rs"], writes=["rs"])
                for k in range(8):
                    S.stt(o[:, k, :], xt[:, k, :], self.pv[:, fb + k:fb + k + 1], rs[:], ALU.mult, ALU.mult,
                          reads=xk + ["rs"], writes=[("o", par)])
                S.dma("sp", self.out[:, :, (t - 1) * TN:t * TN].rearrange("k p n -> p k n"), o[:],
                      reads=[("o", par)], writes=[("out", t)])
            S.flush()

    def phase_dbg(self):
        S = self.S
        for k in range(8):
            S.dma("sp", self.dbg[k], self.X[k], reads=[("X", k)], writes=[("dbg", k)])
        S.flush()

    def build(self):
        nc = self.nc
        self.declare()
        sem_names = ["s%d" % i for i in range(4 + 16 + 1)]
        from contextlib import ExitStack
        with ExitStack() as st:
            sems = [st.enter_context(nc.semaphore(n)) for n in sem_names]
            self.S = Sched(nc, sems)
            sb = lambda n, s, d=F32: st.enter_context(nc.sbuf_tensor(self.un("sb_") + n, s, d))
            self.pv = sb("pv", [128, NPV])
            self.ident = sb("ident", [128, 128], BF16)
            self.ones = sb("ones", [128, 128], BF16)
            self.bones = sb("bones", [128, 128], BF16)
            self.maskf = sb("maskf", [128, 128], BF16)
            self.maskb = sb("maskb", [128, 128], BF16)
            self.ones32 = sb("ones32", [128, 128])
            self.pos = sb("pos", [128, 2, TN])
            self.sb = sb("sb", [128, 8, 2], BF16)
            self.modA = sb("modA", [128, 3, 8, 2])
            self.modB = sb("modB", [128, 3, 8, 2])
            self.modG = sb("modG", [128, 3, 8, 2])
            self.lgam = sb("lgam", [128, 8])
            self.nlgam = sb("nlgam", [128, 8])
            self.gC = sb("gC", [128, 8])
            self.bc1 = sb("bc1", [128, 8])
            self.lcp = sb("lcp", [128, 8])
            self.lcp2 = sb("lcp2", [128, 8])
            self.epsb = sb("epsb", [128, 1])
            self.lnks = sb("lnks", [128, 1])
            self.ps = [st.enter_context(nc.psum_tensor("ps%d" % i, [128, 512], F32)) for i in range(6)]
            self.psb = [st.enter_context(nc.psum_tensor("psb%d" % i, [128, 1024], BF16)) for i in range(2)]
            self.S.memset("dve", self.epsb[:], EPS, writes=["epsb"])
            self.S.memset("dve", self.lnks[:], -0.5 * float(np.log(128.0)), writes=["lnks"])
            self.phase_init()
            stop = False
            for l in range(N_LAYERS):
                for name, fn in (("mod", lambda: self.phase_mod(l)),
                                 ("ffn1", lambda: self.phase_ffn(l, 0, 0)),
                                 ("mix", lambda: self.phase_mix(l)),
                                 ("ffn2", lambda: self.phase_ffn(l, 1, 2))):
                    r = fn()
                    if r or DEBUG_STOP == (l, name):
                        stop = True
                        break
                if stop:
                    break
            if DEBUG_STOP is not None:
                self.phase_dbg()
            else:
                self.phase_final()
            self.S.final_wait()


    def ex_w(self, off, cnt_):
        j, o = off // 65536, off % 65536
        assert o + cnt_ <= 65536
        return self.EXP[j].ap().rearrange("a b -> (a b)")[o:o + cnt_]

    def ex_r(self, slot, off, cnt_):
        j, o = off // 65536, off % 65536
        assert o + cnt_ <= 65536
        psz = 65536 if j < 10 else 2048
        lo = slot * psz + o
        return self.EXGP[j].ap().rearrange("a b -> (a b)")[lo:lo + cnt_]

    def phase_mix(self, l):
        def cc1():
            for j in range(11):
                self.S.cc(self.EXP[j].ap().opt(), self.EXGP[j].ap().opt())
            self.S.flush()

        def cc2():
            self.S.cc(self.EX2t.ap().opt(), self.EX2Gt.ap().opt())
            self.S.flush()

        for name, fn in (("m1", lambda: self.phase_m1(l)), ("m2", lambda: self.phase_m2(l)),
                         ("m2b", lambda: self.phase_m2b(l)), ("cc1", cc1), ("lruA", lambda: self.phase_lruA(l)),
                         ("cc2", cc2), ("lruB", lambda: self.phase_lruB(l)), ("ret", lambda: self.phase_ret(l)),
                         ("attn", lambda: self.phase_attn(l)), ("merge", lambda: self.phase_merge(l))):
            fn()
            if DEBUG_STOP == (l, name):
                return True
        return False

    ZSRC = list(range(0, 8)) + list(range(12, 29)) + list(range(30, 43))

    def phase_m1(self, l):
        nc, S = self.nc, self.S
        NWC = 43
        with (nc.sbuf_tensor(self.un("m1w"), [128, 8, NWC * 128], BF16) as wz,
              nc.sbuf_tensor(self.un("m1x0"), [128, 8, TN], F32) as xt0,
              nc.sbuf_tensor(self.un("m1x1"), [128, 8, TN], F32) as xt1,
              nc.sbuf_tensor(self.un("m1sq"), [128, 8, TN], BF16) as sq,
              nc.sbuf_tensor(self.un("m1h0"), [128, 8, TN], BF16) as h0,
              nc.sbuf_tensor(self.un("m1h1"), [128, 8, TN], BF16) as h1,
              nc.sbuf_tensor(self.un("m1rs"), [128, TN], F32) as rs,
              nc.sbuf_tensor(self.un("m1tmp"), [128, 8, TN], F32) as tmp,
              nc.sbuf_tensor(self.un("m1z0"), [128, 2, TN], F32) as zs0,
              nc.sbuf_tensor(self.un("m1z1"), [128, 2, TN], F32) as zs1,
              nc.sbuf_tensor(self.un("m1rv0"), [128, 512], BF16) as rv0,
              nc.sbuf_tensor(self.un("m1rv1"), [128, 512], BF16) as rv1,
              nc.sbuf_tensor(self.un("m1av0"), [128, 128], F32) as av0,
              nc.sbuf_tensor(self.un("m1av1"), [128, 128], F32) as av1):
            xts, hs, zss, rvs, avs = [xt0, xt1], [h0, h1], [zs0, zs1], [rv0, rv1], [av0, av1]
            src = self.w_in[l].rearrange("(k p) c -> p k c", p=128)
            srcp = self.w_inp[l].rearrange("(k p) c -> p k c", p=128)
            for pc in range(10):
                S.dma("pool", wz[:, :, pc * 384:(pc + 1) * 384], src[:, :, pc * 384:(pc + 1) * 384], writes=[("wz", pc)])
            for pc in range(5):
                lo, hi = pc * 384, min((pc + 1) * 384, 13 * 128)
                S.dma("pool", wz[:, :, 3840 + lo:3840 + hi], srcp[:, :, lo:hi], writes=[("wz", 10 + pc)])
            self.load_x(xts[0], 0, 0)
            self.norm_mod(xts[0], 0, sq, rs, tmp, hs[0], 1, 1)
            for t in range(NT):
                par = t % 2
                xt, h = xts[par], hs[par]
                c = 1 if t == 0 else 0
                t0 = t * TN
                if t + 1 < NT:
                    self.load_x(xts[1 - par], t + 1, 1 - par)
                S.dma("sp", self.H[:, :, t0:t0 + TN].rearrange("k p n -> p k n"), h[:], reads=[("h", par, k) for k in range(8)],
                      writes=[("H", t)])
                for zc in range(NZ):
                    wc = self.ZSRC[zc]
                    pz, kz = self.nextps()
                    for k in range(8):
                        S.mm(pz[:, 0:TN], wz[:, k, wc * 128:(wc + 1) * 128], h[:, k, :], k == 0, k == 7,
                             reads=[("wz", wc // 3), ("h", par, k)], writes=[kz])
                    zs = zss[(zc // 2) % 2]
                    zk = ("zs", (zc // 2) % 2, zc % 2)
                    S.cp("act" if zc % 2 == 0 else "dve", zs[:, zc % 2, :], pz[:, 0:TN], reads=[kz], writes=[zk])
                    if zc % 2 == 1:
                        S.dma("sp", self.Z[zc - 1:zc + 1, :, t0:t0 + TN].rearrange("c p n -> p c n"), zs[:],
                              reads=[("zs", (zc // 2) % 2, 0), ("zs", (zc // 2) % 2, 1)], writes=[("Z", zc // 2, t)])
                if t + 1 < NT:
                    self.norm_mod(xts[1 - par], 1 - par, sq, rs, tmp, hs[1 - par], 1, 0)
                for hf in range(2):
                    i2 = (2 * t + hf) % 2
                    prv, krv = self.nextps()
                    for k in range(8):
                        S.mm(prv[:, 0:512], h[:, k, hf * 128:(hf + 1) * 128], wz[:, k, 1024:1536], k == 0, k == 7,
                             reads=[("wz", 2), ("wz", 3), ("h", par, k)], writes=[krv])
                    S.cp("act", rvs[i2][:], prv[:, 0:512], reads=[krv], writes=[("rvs", i2)])
                    S.dma("sp", self.RV[t0 + hf * 128:t0 + (hf + 1) * 128, :], rvs[i2][:], reads=[("rvs", i2)],
                          writes=[("RV", t, hf)])
                    pav, kav = self.nextps()
                    for k in range(8):
                        S.mm(pav[:, 0:128], h[:, k, hf * 128:(hf + 1) * 128], wz[:, k, 3712:3840], k == 0, k == 7,
                             reads=[("wz", 9), ("h", par, k)], writes=[kav])
                    S.cp("dve", avs[i2][:], pav[:, 0:128], reads=[kav], writes=[("avs", i2)])
                    if t == 0:
                        dst = self.AVC[hf * 128:(hf + 1) * 128, :]
                    else:
                        r0 = (t - 1) * TN + hf * 128
                        dst = self.ex_w(EX_AV + r0 * 128, 16384).rearrange("(t e) -> t e", e=128)
                    S.dma("sp", dst, avs[i2][:], reads=[("avs", i2)], writes=[("AV", t, hf)])
            S.flush()

    def phase_m2(self, l):
        nc, S = self.nc, self.S
        LNKS = -0.5 * float(np.log(128.0))
        with (nc.sbuf_tensor(self.un("m2G"), [128, 16, TN], F32) as G,
              nc.sbuf_tensor(self.un("m2rp0"), [128, 4, TN], F32) as rp0,
              nc.sbuf_tensor(self.un("m2rp1"), [128, 4, TN], F32) as rp1,
              nc.sbuf_tensor(self.un("m2z"), [128, 4, TN], F32) as zb,
              nc.sbuf_tensor(self.un("m2zp"), [128, 4, TN], F32) as zpb,
              nc.sbuf_tensor(self.un("m2r1"), [128, 2, TN], F32) as r1b,
              nc.sbuf_tensor(self.un("m2r2"), [128, 2, TN], F32) as r2b,
              nc.sbuf_tensor(self.un("m2qk0"), [128, 4, TN], BF16) as qk0,
              nc.sbuf_tensor(self.un("m2qk1"), [128, 4, TN], BF16) as qk1,
              nc.sbuf_tensor(self.un("m2sq"), [128, TN], BF16) as sq,
              nc.sbuf_tensor(self.un("m2rs"), [128, 2, TN], F32) as rsb,
              nc.sbuf_tensor(self.un("m2qa"), [128, 2, TN], BF16) as qab,
              nc.sbuf_tensor(self.un("m2kf"), [128, TN], F32) as kf32):
            rps, qks = [rp0, rp1], [qk0, qk1]
            for kind in range(2):
                for d in range(2):
                    for h in range(4):
                        gi = (kind * 2 + d) * 4 + h
                        sc = (self.lgam if kind == 0 else self.nlgam)[:, d * 4 + h:d * 4 + h + 1]
                        S.act(G[:, gi, :], self.pos[:, d, :], AF.Exp, scale=sc,
                              bias=(None if kind == 0 else self.lnks[:, 0:1]),
                              reads=["lgam", "nlgam", "pos"], writes=[("G", gi)])
            items = []
            for t in range(NT):
                for h in range(4):
                    for kind in range(2):
                        items.append(("ret", t, h, kind))
                for c in range(5):
                    items.append(("att", t, c, 0))

            def bufs(i):
                b4, b2 = i % 4, i % 2
                return b4, b2, zb[:, b4, :], zpb[:, b4, :], r1b[:, b2, :], r2b[:, b2, :], rsb[:, b2, :]

            def do_loads(i):
                typ, t, a, kind = items[i]
                t0 = t * TN
                b4, b2, z, zp, r1, r2, rs = bufs(i)
                if typ == "ret" and a == 0 and kind == 0:
                    S.dma("sp", rps[t % 2][:], self.rope[:, :, t0:t0 + TN].rearrange("c p n -> p c n"),
                          writes=[("rp", t % 2)])
                if typ == "ret":
                    zc, zpc = (a, 25 + a) if kind == 0 else (4 + a, 29 + a)
                else:
                    zc, zpc = (20 + a, 33 + a) if a < 4 else (24, 37)
                S.dma("sp", z, self.Z[zc, :, t0:t0 + TN], writes=[("z", b4)])
                S.dma("sp", zp, self.Z[zpc, :, t0:t0 + TN], writes=[("zp", b4)])

            def do_compute(i):
                typ, t, a, kind = items[i]
                t0 = t * TN
                rp = rps[t % 2]
                b4, b2, z, zp, r1, r2, rs = bufs(i)
                if typ == "ret":
                    h = a
                    qk = qks[h % 2]
                    S.tt("dve", r1, z, rp[:, 0, :], ALU.mult, reads=[("z", b4), ("rp", t % 2)], writes=[("r1", b2)])
                    S.tt("pool", r2, zp, rp[:, 1, :], ALU.mult, reads=[("zp", b4), ("rp", t % 2)], writes=[("r2", b2)])
                    S.tt("dve", r1, r1, r2, ALU.add, reads=[("r1", b2), ("r2", b2)], writes=[("r1", b2)])
                    gf = (kind * 2 + 0) * 4 + h
                    gb = (kind * 2 + 1) * 4 + h
                    S.tt("pool", qk[:, 2 * kind, :], r1, G[:, gf, :], ALU.mult, reads=[("r1", b2), ("G", gf)],
                         writes=[("qk", h % 2, 2 * kind)])
                    S.tt("dve", qk[:, 2 * kind + 1, :], r1, G[:, gb, :], ALU.mult, reads=[("r1", b2), ("G", gb)],
                         writes=[("qk", h % 2, 2 * kind + 1)])
                    if kind == 1:
                        S.dma("sp", self.QK[h, :, :, t0:t0 + TN].rearrange("s p n -> p s n"), qk[:],
                              reads=[("qk", h % 2, j) for j in range(4)], writes=[("QK", h, t)])
                else:
                    c = a
                    gcol = 144 if c < 4 else 146
                    S.act(sq[:], z, AF.Square, reads=[("z", b4)], writes=["sq"])
                    pn, kn = self.nextps()
                    S.mm(pn[:, 0:TN], self.bones[:], sq[:], True, True, reads=["sq"], writes=[kn])
                    S.act(rs, pn[:, 0:TN], AF.Sqrt, scale=1.0 / 64, bias=self.epsb[:, 0:1], reads=[kn], writes=[("rs", b2)])
                    S.recip(rs, rs, reads=[("rs", b2)], writes=[("rs", b2)])
                    S.stt(r1, z, self.pvl(l, gcol), rs, ALU.mult, ALU.mult, reads=[("z", b4), ("rs", b2)], writes=[("r1", b2)])
                    S.stt(r2, zp, self.pvl(l, gcol + 1), rs, ALU.mult, ALU.mult, reads=[("zp", b4), ("rs", b2)],
                          writes=[("r2", b2)])
                    S.tt("pool", r1, r1, rp[:, 2, :], ALU.mult, reads=[("r1", b2), ("rp", t % 2)], writes=[("r1", b2)])
                    S.tt("pool", r2, r2, rp[:, 3, :], ALU.mult, reads=[("r2", b2), ("rp", t % 2)], writes=[("r2", b2)])
                    if c < 4:
                        S.tt("dve", qab[:, c % 2, :], r1, r2, ALU.add, reads=[("r1", b2), ("r2", b2)], writes=[("qa", c % 2)])
                        S.dma("sp", self.QA[c, :, t0:t0 + TN], qab[:, c % 2, :], reads=[("qa", c % 2)], writes=[("QA", c, t)])
                    else:
                        S.tt("dve", kf32[:], r1, r2, ALU.add, reads=[("r1", b2), ("r2", b2)], writes=["kf32"])
                        if t == 0:
                            dst = self.KAC[:, :]
                        else:
                            dst = self.EXP[(t - 1) // 2].ap()[:, ((t - 1) % 2) * TN:((t - 1) % 2 + 1) * TN]
                        S.dma("sp", dst, kf32[:], reads=["kf32"], writes=[("KA", t)])

            do_loads(0)
            do_loads(1)
            for i in range(len(items)):
                if i + 2 < len(items):
                    do_loads(i + 2)
                do_compute(i)
            EXLX = self.ex_w(EX_LX, 2048).rearrange("(c p n) -> c p n", p=128, n=4)
            for c in range(4):
                S.dma("sp", EXLX[c, :, 0:2], self.Z[12 + c, :, CTX:CTX + 2], writes=[("LXH", c, 0)], slow=True)
                S.dma("sp", EXLX[c, :, 2:4], self.Z[12 + c, :, T - 2:T], writes=[("LXH", c, 1)], slow=True)
            S.flush()

    def phase_m2b(self, l):
        nc, S = self.nc, self.S
        with (nc.sbuf_tensor(self.un("m2bS"), [128, 8, 128], F32) as S32,
              nc.sbuf_tensor(self.un("m2bv"), [128, 4, 512], BF16) as vb,
              nc.sbuf_tensor(self.un("m2bk"), [128, 4, 4, 128], BF16) as kb,
              nc.sbuf_tensor(self.un("m2bkt"), [128, 2, 4, 128], BF16) as ktb):
            cnt = 0
            kcnt = 0
            for step in range(16):
                for d in range(2):
                    ci = step if d == 0 else 15 - step
                    t0 = CTX + 128 * ci
                    b4 = cnt % 4
                    cnt += 1
                    S.dma("sp", vb[:, b4, :], self.RV[t0:t0 + 128, :], writes=[("vb", b4)])
                    S.dma("sp", kb[:, b4, :, :], self.QK[:, 2 + d, :, t0:t0 + 128].rearrange("h p n -> p h n"),
                          writes=[("kb", b4)])
                    bk = kcnt % 2
                    kcnt += 1
                    for h in range(4):
                        S.tr(self.psb[bk][:, h * 128:(h + 1) * 128], kb[:, b4, h, :], self.ident[:],
                             reads=[("kb", b4), "ident"], writes=[("psb", bk)])
                    S.cp("act" if bk else "dve", ktb[:, bk, :, :],
                         self.psb[bk][:, 0:512].rearrange("p (h n) -> p h n", n=128),
                         reads=[("psb", bk)], writes=[("ktb", bk)])
                    for h in range(4):
                        kv, kk = self.nextps()
                        S.mm(kv[:, 0:128], ktb[:, bk, h, :], vb[:, b4, h * 128:(h + 1) * 128], True, True,
                             reads=[("ktb", bk), ("vb", b4)], writes=[kk])
                        si = d * 4 + h
                        gc = self.gC[:, si:si + 1]
                        if step == 0:
                            S.ts("dve", S32[:, si, :], kv[:, 0:128], gc, None, ALU.mult, ALU.bypass,
                                 reads=[kk, "gC"], writes=[("S32", si)])
                        else:
                            S.act(S32[:, si, :], S32[:, si, :], AF.Identity, scale=gc,
                                  reads=[("S32", si), "gC"], writes=[("S32", si)])
                            S.stt(S32[:, si, :], kv[:, 0:128], gc, S32[:, si, :], ALU.mult, ALU.add,
                                  reads=[kk, ("S32", si), "gC"], writes=[("S32", si)])
            for si in range(8):
                S.dma("sp", self.ex_w(EX_SL + si * 16384, 16384).rearrange("(p n) -> p n", n=128), S32[:, si, :],
                      reads=[("S32", si)], writes=[("EXSL", si)])
            S.flush()

    @staticmethod
    def rev(t, a, b):
        return t[:, slice(b - 1, a - 1 if a > 0 else None, -1)]

    def phase_lruA(self, l):
        nc, S = self.nc, self.S
        hb_ = DEPTH * PL + 24
        half, omh = self.pv[:, hb_:hb_ + 1], self.pv[:, hb_ + 1:hb_ + 2]
        big = lambda n: nc.sbuf_tensor(self.un(n), [128, T], F32)
        with (nc.sbuf_tensor(self.un("laxpc"), [128, CTX + 3], F32) as xpc,
              nc.sbuf_tensor(self.un("laxpl"), [128, LAT + 3], F32) as xpl,
              big("laxcv") as xcv, big("larg") as rg, big("laig") as ig, big("laa") as a_, big("lam") as m_,
              big("lau") as u_, big("lahf") as hf, big("lahb") as hb,
              nc.sbuf_tensor(self.un("laacf"), [128, LAT], F32) as acf,
              nc.sbuf_tensor(self.un("laacb"), [128, LAT], F32) as acb,
              nc.sbuf_tensor(self.un("lazero"), [128, LAT], F32) as zeros,
              nc.sbuf_tensor(self.un("laxb"), [128, T], BF16) as xb,
              nc.sbuf_tensor(self.un("lalw"), [128, 4, 128], BF16) as lw,
              nc.sbuf_tensor(self.un("lahl"), [128, 2, 4], F32) as hl,
              nc.sbuf_tensor(self.un("lah0"), [128, 2], F32) as h0,
              nc.sbuf_tensor(self.un("last"), [128, 2], F32) as stt_):
            S.memset("pool", zeros[:], 0.0, writes=["zeros"])
            slices = [(i * 512, min(512, T - i * 512)) for i in range(5)]
            for c in range(4):
                S.memset("dve", xpc[:, 0:1], 0.0, writes=["xpc"])
                S.memset("dve", xpc[:, CTX + 1:CTX + 3], 0.0, writes=["xpc"])
                S.dma("sp", xpc[:, 1:CTX + 1], self.Z[12 + c, :, 0:CTX], writes=["xpc"])
                S.dma("sp", xpl[:, 1:LAT + 1], self.Z[12 + c, :, CTX:T], writes=["xpl"])
                for r in range(2):
                    S.dma("sp", hl[:, r, :], self.ex_r(r, EX_LX + c * 512, 512).rearrange("(p n) -> p n", n=4),
                          writes=["hl"])
                S.ts("dve", xpl[:, 0:1], hl[:, 0, 3:4], half, None, ALU.mult, ALU.bypass, reads=["hl"], writes=["xpl"])
                S.ts("dve", xpl[:, LAT + 1:LAT + 3], hl[:, 1, 0:2], omh, None, ALU.mult, ALU.bypass, reads=["hl"],
                     writes=["xpl"])
                w = [self.pvl(l, 100 + j * 4 + c) for j in range(4)]
                bcv = self.pvl(l, 116 + c)
                for (src, sk, n, d0) in ((xpc, "xpc", CTX, 0), (xpl, "xpl", LAT, CTX)):
                    dst = xcv[:, d0:d0 + n]
                    S.ts("dve", dst, src[:, 0:n], w[0], bcv, ALU.mult, ALU.add, reads=[sk], writes=["xcv"])
                    for j in range(1, 4):
                        S.stt(dst, src[:, j:j + n], w[j], dst, ALU.mult, ALU.add, reads=[sk, "xcv"], writes=["xcv"])
                S.cp("act", xb[:], xcv[:], reads=["xcv"], writes=["xb"])
                for a in range(2):
                    for d in range(2):
                        S.dma("pool", lw[:, a * 2 + d, :], self.lbd[l, a, d, c], writes=[("lw", a * 2 + d)])
                for d in range(2):
                    for (s0, n) in slices:
                        p1, k1 = self.nextps()
                        S.mm(p1[:, 0:n], lw[:, d, :], xb[:, s0:s0 + n], True, True, reads=[("lw", d), "xb"], writes=[k1])
                        S.act(rg[:, s0:s0 + n], p1[:, 0:n], AF.Sigmoid, bias=self.pvl(l, 120 + d * 4 + c), reads=[k1],
                              writes=["rg"])
                        p2, k2 = self.nextps()
                        S.mm(p2[:, 0:n], lw[:, 2 + d, :], xb[:, s0:s0 + n], True, True, reads=[("lw", 2 + d), "xb"],
                             writes=[k2])
                        S.act(ig[:, s0:s0 + n], p2[:, 0:n], AF.Sigmoid, bias=self.pvl(l, 128 + d * 4 + c), reads=[k2],
                              writes=["ig"])
                    ci = d * 4 + c
                    S.act(a_[:], rg[:], AF.Exp, scale=self.lcp[:, ci:ci + 1], reads=["rg", "lcp"], writes=["a"])
                    S.act(m_[:], rg[:], AF.Exp, scale=self.lcp2[:, ci:ci + 1], reads=["rg", "lcp2"], writes=["m"])
                    S.act(m_[:], m_[:], AF.Sqrt, scale=-1.0, bias=1.0, reads=["m"], writes=["m"])
                    S.tt("dve", u_[:], ig[:], xcv[:], ALU.mult, reads=["ig", "xcv"], writes=["u"])
                    S.tt("pool", u_[:], u_[:], m_[:], ALU.mult, reads=["u", "m"], writes=["u"])
                    if d == 0:
                        S.scan(hf[:, 0:CTX], a_[:, 0:CTX], u_[:, 0:CTX], 0.0, reads=["a", "u"], writes=["hf"])
                        S.ts("dve", h0[:, 0:1], hf[:, CTX - 1:CTX], omh, None, ALU.mult, ALU.bypass, reads=["hf"],
                             writes=["h0f"])
                        S.scan(hf[:, CTX:T], a_[:, CTX:T], u_[:, CTX:T], h0[:, 0:1], reads=["a", "u", "h0f"], writes=["hf"])
                        S.scan(acf[:], a_[:, CTX:T], zeros[:], 1.0, reads=["a", "zeros"], writes=["acf"])
                        S.cp("dve", stt_[:, 0:1], hf[:, T - 1:T], reads=["hf"], writes=["st0"])
                    else:
                        rv = self.rev
                        S.scan(rv(hb, 0, CTX), rv(a_, 0, CTX), rv(u_, 0, CTX), 0.0, reads=["a", "u"], writes=["hb"])
                        S.ts("dve", h0[:, 1:2], hb[:, 0:1], half, None, ALU.mult, ALU.bypass, reads=["hb"], writes=["h0b"])
                        S.scan(rv(hb, CTX, T), rv(a_, CTX, T), rv(u_, CTX, T), h0[:, 1:2], reads=["a", "u", "h0b"],
                               writes=["hb"])
                        S.scan(rv(acb, 0, LAT), rv(a_, CTX, T), zeros[:], 1.0, reads=["a", "zeros"], writes=["acb"])
                        S.cp("dve", stt_[:, 1:2], hb[:, CTX:CTX + 1], reads=["hb"], writes=["st1"])
                S.tt("dve", hf[:], hf[:], hb[:], ALU.add, reads=["hf", "hb"], writes=["hf"])
                S.dma("sp", self.HS[c], hf[:], reads=["hf"], writes=[("HS", c)])
                S.dma("sp", self.ACF[c], acf[:], reads=["acf"], writes=[("ACF", c)])
                S.dma("sp", self.ACB[c], acb[:], reads=["acb"], writes=[("ACB", c)])
                for d in range(2):
                    lo = d * 512 + c * 128
                    S.dma("sp", self.EX2[lo:lo + 128].rearrange("(p o) -> p o", o=1), stt_[:, d:d + 1],
                          reads=["st%d" % d], writes=[("EX2", d, c)])
            S.flush()

    def phase_lruB(self, l):
        nc, S = self.nc, self.S
        hb_ = DEPTH * PL + 24
        half, omh = self.pv[:, hb_:hb_ + 1], self.pv[:, hb_ + 1:hb_ + 2]
        big = lambda n: nc.sbuf_tensor(self.un(n), [128, T], F32)
        with (big("lbhs") as hs, big("lblz") as lz, big("lbsq") as sq, big("lbin") as inn,
              nc.sbuf_tensor(self.un("lbacf"), [128, LAT], F32) as acf,
              nc.sbuf_tensor(self.un("lbacb"), [128, LAT], F32) as acb,
              nc.sbuf_tensor(self.un("lby"), [128, T], BF16) as y,
              nc.sbuf_tensor(self.un("lbdd"), [128, 4], F32) as dd):
            for c in range(4):
                S.dma("sp", hs[:], self.HS[c], writes=["hs"])
                S.dma("sp", acf[:], self.ACF[c], writes=["acf"])
                S.dma("sp", acb[:], self.ACB[c], writes=["acb"])
                S.dma("sp", lz[:], self.Z[16 + c], writes=["lz"])
                lo0 = 0 * EX2_N + 0 * 512 + c * 128
                lo1 = 1 * EX2_N + 1 * 512 + c * 128
                S.dma("sp", dd[:, 0:1], self.EX2G[lo0:lo0 + 128].rearrange("(p o) -> p o", o=1), writes=["dd0"])
                S.dma("sp", dd[:, 1:2], self.EX2G[lo1:lo1 + 128].rearrange("(p o) -> p o", o=1), writes=["dd1"])
                S.ts("dve", dd[:, 2:3], dd[:, 0:1], half, None, ALU.mult, ALU.bypass, reads=["dd0"], writes=["dd2"])
                S.ts("dve", dd[:, 3:4], dd[:, 1:2], omh, None, ALU.mult, ALU.bypass, reads=["dd1"], writes=["dd3"])
                S.stt(hs[:, CTX:T], acf[:], dd[:, 2:3], hs[:, CTX:T], ALU.mult, ALU.add, reads=["acf", "dd2", "hs"],
                      writes=["hs"])
                S.stt(hs[:, CTX:T], acb[:], dd[:, 3:4], hs[:, CTX:T], ALU.mult, ALU.add, reads=["acb", "dd3", "hs"],
                      writes=["hs"])
                S.act(sq[:], lz[:], AF.Square, reads=["lz"], writes=["sq"])
                S.ts("pool", sq[:], sq[:], 0.044715, 1.0, ALU.mult, ALU.add, reads=["sq"], writes=["sq"])
                S.tt("dve", inn[:], sq[:], lz[:], ALU.mult, reads=["sq", "lz"], writes=["inn"])
                S.act(inn[:], inn[:], AF.Sigmoid, scale=1.5957691216057308, reads=["inn"], writes=["inn"])
                S.tt("pool", inn[:], inn[:], lz[:], ALU.mult, reads=["inn", "lz"], writes=["inn"])
                S.tt("dve", y[:], inn[:], hs[:], ALU.mult, reads=["inn", "hs"], writes=["y"])
                S.dma("sp", self.YL[c], y[:], reads=["y"], writes=[("YL", c)])
            S.flush()

    def phase_ret(self, l):
        nc, S = self.nc, self.S
        hb_ = DEPTH * PL + 24
        half, omh = self.pv[:, hb_:hb_ + 1], self.pv[:, hb_ + 1:hb_ + 2]
        big = lambda n: nc.sbuf_tensor(self.un(n), [128, T], F32)
        with (nc.sbuf_tensor(self.un("rtqk"), [128, 4, T], BF16) as qk,
              nc.sbuf_tensor(self.un("rtv"), [128, 18, 128], BF16) as vt,
              nc.sbuf_tensor(self.un("rtkt"), [128, 2, 18, 128], BF16) as kt,
              big("rtacc") as acc, big("rtyc") as yc, big("rtsq") as sq32, big("rtrs") as rs, big("rtrg") as rgz,
              nc.sbuf_tensor(self.un("rtS"), [128, 2, 128], F32) as S32,
              nc.sbuf_tensor(self.un("rtSo"), [128, 2, 128], F32) as So,
              nc.sbuf_tensor(self.un("rtSb"), [128, 2, 2, 128], BF16) as Sb,
              nc.sbuf_tensor(self.un("rtpm"), [128, 4, 128], BF16) as pm,
              nc.sbuf_tensor(self.un("rty"), [128, T], BF16) as y):
            slices = [(i * 512, min(512, T - i * 512)) for i in range(5)]
            order = [[0, 1] + list(range(2, 18)), [1, 0] + list(range(17, 1, -1))]
            masks = [self.maskf, self.maskb]
            pcnt = 0
            for h in range(4):
                S.dma("sp", qk[:], self.QK[h].rearrange("s p n -> p s n"), writes=["qk"])
                S.dma("sp", vt[:], self.RV[:, h * 128:(h + 1) * 128].rearrange("(c p) e -> p c e", p=128), writes=["vt"])
                S.dma("sp", rgz[:], self.Z[8 + h], writes=["rgz"])
                for d in range(2):
                    r = d
                    S.dma("sp", So[:, d, :], self.ex_r(r, EX_SL + (d * 4 + h) * 16384, 16384).rearrange(
                        "(p n) -> p n", n=128), writes=[("So", d)])
                    S.memset("dve", S32[:, d, :], 0.0, writes=[("S32", d)])
                    S.memset("pool", Sb[:, d, 0, :], 0.0, writes=[("Sb", d, 0)])
                tcnt = 0
                for d in range(2):
                    for c0 in range(0, 18, 4):
                        ng = min(4, 18 - c0)
                        bk = tcnt % 2
                        tcnt += 1
                        for j in range(ng):
                            ci = c0 + j
                            S.tr(self.psb[bk][:, j * 128:(j + 1) * 128], qk[:, 2 + d, ci * 128:(ci + 1) * 128], self.ident[:],
                                 reads=["qk", "ident"], writes=[("psb", bk)])
                        S.cp("act" if bk else "dve", kt[:, d, c0:c0 + ng, :],
                             self.psb[bk][:, 0:ng * 128].rearrange("p (h n) -> p h n", n=128),
                             reads=[("psb", bk)], writes=[("kt", d, c0 + j) for j in range(ng)])
                written = set()
                sbi = [0, 0]
                for step in range(18):
                    for d in range(2):
                        ci = order[d][step]
                        cs = slice(ci * 128, (ci + 1) * 128)
                        si = d * 4 + h
                        if step == 2:
                            S.ts("dve", S32[:, d, :], S32[:, d, :], self.bc1[:, si:si + 1], None, ALU.mult, ALU.bypass,
                                 reads=[("S32", d), "bc1"], writes=[("S32", d)])
                            S.stt(S32[:, d, :], So[:, d, :], (half if d == 0 else omh), S32[:, d, :], ALU.mult, ALU.add,
                                  reads=[("So", d), ("S32", d)], writes=[("S32", d)])
                            nb = 1 - sbi[d]
                            S.cp("act", Sb[:, d, nb, :], S32[:, d, :], reads=[("S32", d)], writes=[("Sb", d, nb)])
                            sbi[d] = nb
                        sc, ksc = self.nextps()
                        S.mm(sc[:, 0:128], qk[:, 2 + d, cs], qk[:, d, cs], True, True, reads=["qk"], writes=[ksc])
                        p4 = pcnt % 4
                        pcnt += 1
                        S.tt("dve", pm[:, p4, :], sc[:, 0:128], masks[d][:], ALU.mult, reads=[ksc, "maskf", "maskb"],
                             writes=[("pm", p4)])
                        o, ko = self.nextps()
                        S.mm(o[:, 0:128], vt[:, ci, :], pm[:, p4, :], True, False, reads=["vt", ("pm", p4)], writes=[ko])
                        S.mm(o[:, 0:128], Sb[:, d, sbi[d], :], qk[:, d, cs], False, True, reads=[("Sb", d, sbi[d]), "qk"],
                             writes=[ko])
                        if ci not in written:
                            S.cp("act", acc[:, cs], o[:, 0:128], reads=[ko], writes=[("acc", ci)])
                            written.add(ci)
                        else:
                            S.tt("dve", acc[:, cs], o[:, 0:128], acc[:, cs], ALU.add, reads=[ko, ("acc", ci)],
                                 writes=[("acc", ci)])
                        kv, kkv = self.nextps()
                        S.mm(kv[:, 0:128], kt[:, d, ci, :], vt[:, ci, :], True, True, reads=[("kt", d, ci), "vt"],
                             writes=[kkv])
                        gc = self.gC[:, si:si + 1]
                        S.act(S32[:, d, :], S32[:, d, :], AF.Identity, scale=gc,
                              reads=[("S32", d), "gC"], writes=[("S32", d)])
                        S.stt(S32[:, d, :], kv[:, 0:128], gc, S32[:, d, :], ALU.mult, ALU.add,
                              reads=[kkv, ("S32", d), "gC"], writes=[("S32", d)])
                        nb = 1 - sbi[d]
                        S.cp("act", Sb[:, d, nb, :], S32[:, d, :], reads=[("S32", d)], writes=[("Sb", d, nb)])
                        sbi[d] = nb
                acck = [("acc", ci) for ci in range(18)]
                S.act(rgz[:], rgz[:], AF.Silu, reads=["rgz"], writes=["rgz"])
                pms, pvs = {}, {}
                for (s0, n) in slices:
                    pms[s0] = self.nextps()
                    S.mm(pms[s0][0][:, 0:n], self.ones32[:], acc[:, s0:s0 + n], True, True, reads=acck + ["ones32"],
                         writes=[pms[s0][1]])
                    S.stt(yc[:, s0:s0 + n], pms[s0][0][:, 0:n], -1.0 / 128, acc[:, s0:s0 + n], ALU.mult, ALU.add,
                          reads=[pms[s0][1]] + acck, writes=[("yc", s0)])
                for (s0, n) in slices:
                    S.act(sq32[:, s0:s0 + n], yc[:, s0:s0 + n], AF.Square, reads=[("yc", s0)], writes=[("sq32", s0)])
                for (s0, n) in slices:
                    pvs[s0] = self.nextps()
                    S.mm(pvs[s0][0][:, 0:n], self.ones32[:], sq32[:, s0:s0 + n], True, True, reads=[("sq32", s0)],
                         writes=[pvs[s0][1]])
                    S.act(rs[:, s0:s0 + n], pvs[s0][0][:, 0:n], AF.Sqrt, scale=1.0 / 128, bias=self.epsb[:, 0:1],
                          reads=[pvs[s0][1]], writes=[("rs", s0)])
                for (s0, n) in slices:
                    S.recip(rs[:, s0:s0 + n], rs[:, s0:s0 + n], reads=[("rs", s0)], writes=[("rs", s0)])
                for (s0, n) in slices:
                    S.tt("pool", yc[:, s0:s0 + n], yc[:, s0:s0 + n], rs[:, s0:s0 + n], ALU.mult,
                         reads=[("yc", s0), ("rs", s0)], writes=[("yc", s0)])
                for (s0, n) in slices:
                    S.stt(y[:, s0:s0 + n], yc[:, s0:s0 + n], self.pvl(l, 96 + h), rgz[:, s0:s0 + n], ALU.mult, ALU.mult,
                          reads=[("yc", s0), "rgz"], writes=[("y", s0)])
                S.dma("sp", self.YR[h], y[:], reads=[("y", s0) for (s0, n) in slices], writes=[("YR", h)])
            S.flush()

    def phase_attn(self, l):
        nc, S = self.nc, self.S
        NK = CTX + 2 * LAT
        with (nc.sbuf_tensor(self.un("atk"), [128, 2, NK], BF16) as kT,
              nc.sbuf_tensor(self.un("atv"), [128, 2, 2, 34, 128], BF16) as vv,
              nc.sbuf_tensor(self.un("atq"), [128, 2, 2, T], BF16) as qa,
              nc.sbuf_tensor(self.un("aty"), [128, 2, T], BF16) as ya,
              nc.sbuf_tensor(self.un("atp"), [128, 4, 512], BF16) as pt,
              nc.sbuf_tensor(self.un("ato"), [128, 2, 128], BF16) as onz,
              nc.sbuf_tensor(self.un("atr"), [128, 2, 512], F32) as rden):
            S.memset("pool", qa[:], 0.0, writes=[("qa", 0), ("qa", 1)])
            S.memset("pool", vv[:], 0.0, writes=[("vv", 0), ("vv", 1)])
            S.memset("dve", onz[:], 0.0, writes=["onz"])
            S.memset("dve", onz[:, 0, 0:64], 1.0, writes=["onz"])
            S.memset("dve", onz[:, 1, 64:128], 1.0, writes=["onz"])
            for g in range(2):
                for hh in range(2):
                    ps_ = slice(hh * 64, (hh + 1) * 64)
                    S.dma("pool", kT[ps_, g, 0:CTX], self.KAC[g * 64:(g + 1) * 64, :], writes=[("kT", g)])
                    for r in range(2):
                        for j in range(4):
                            src = self.EXGP[j].ap()[r * 128 + g * 64:r * 128 + (g + 1) * 64, :]
                            c0 = CTX + r * LAT + j * 512
                            S.dma("pool", kT[ps_, g, c0:c0 + 512], src, writes=[("kT", g)])
                    cs_ = slice(hh * 64, (hh + 1) * 64)
                    S.dma("pool", vv[:, g, hh, 0:2, cs_],
                          self.AVC.rearrange("(c p) e -> p c e", p=128)[:, :, g * 64:(g + 1) * 64], writes=[("vv", g)])
                    for r in range(2):
                        for j in range(4):
                            src = self.ex_r(r, EX_AV + j * 65536, 65536).rearrange("(c p e) -> p c e", p=128, e=128)
                            c0 = 2 + r * 16 + j * 4
                            S.dma("pool", vv[:, g, hh, c0:c0 + 4, cs_], src[:, :, g * 64:(g + 1) * 64],
                                  writes=[("vv", g)])
            ones64 = self.ones[:, 0:64]
            qtiles = [(0, CTX, [0, 1])] + [(CTX + 512 * i, 512, list(range(34))) for i in range(4)]
            its = []
            qcnt = 0
            for c in range(4):
                for qi, (q0, nq, keys) in enumerate(qtiles):
                    pq = qcnt % 2
                    qcnt += 1
                    for kc in keys:
                        for hh in range(2):
                            its.append(dict(c=c, q0=q0, nq=nq, kc=kc, hh=hh, pq=pq, first=(kc == keys[0]),
                                            last=(kc == keys[-1]), qend=(kc == keys[-1] and hh == 1),
                                            cend=(kc == keys[-1] and hh == 1 and qi == len(qtiles) - 1),
                                            cstart=(kc == keys[0] and hh == 0 and qi == 0)))

            def emit_qk(i):
                it = its[i]
                c, g, hh, nq, q0, kc = it["c"], it["c"] // 2, it["hh"], it["nq"], it["q0"], it["kc"]
                if it["cstart"]:
                    for h2 in range(2):
                        S.dma("sp", qa[h2 * 64:(h2 + 1) * 64, c % 2, h2, :], self.QA[c, h2 * 64:(h2 + 1) * 64, :],
                              writes=[("qa", c % 2)])
                si, p4 = i % 2, i % 4
                sp_, ks = self.ps[si], ("ps", si)
                S.mm(sp_[:, 0:nq], kT[:, g, kc * 128:(kc + 1) * 128], qa[:, c % 2, hh, q0:q0 + nq], True, True,
                     reads=[("kT", g), ("qa", c % 2)], writes=[ks])
                S.act(pt[:, p4, 0:nq], sp_[:, 0:nq], AF.Exp, scale=0.125, reads=[ks], writes=[("pt", p4)])

            def emit_pv(i):
                it = its[i]
                c, g, hh, nq, q0, kc, pq = it["c"], it["c"] // 2, it["hh"], it["nq"], it["q0"], it["kc"], it["pq"]
                ps_ = slice(hh * 64, (hh + 1) * 64)
                p4 = i % 4
                num, knum = self.ps[2 + 2 * pq], ("ps", 2 + 2 * pq)
                den, kden = self.ps[3 + 2 * pq], ("ps", 3 + 2 * pq)
                st_ = it["first"] and hh == 0
                sp2 = it["last"] and hh == 1
                S.mm(num[:, 0:nq], vv[:, g, hh, kc, :], pt[:, p4, 0:nq], st_, sp2,
                     reads=[("vv", g), ("pt", p4)], writes=[knum])
                S.mm(den[:, 0:nq], onz[:, hh, :], pt[:, p4, 0:nq], st_, sp2,
                     reads=["onz", ("pt", p4)], writes=[kden])
                if it["qend"]:
                    S.recip(rden[:, pq, 0:nq], den[:, 0:nq], reads=[kden], writes=[("rden", pq)])
                    S.tt("dve", ya[:, c % 2, q0:q0 + nq], num[:, 0:nq], rden[:, pq, 0:nq], ALU.mult,
                         reads=[knum, ("rden", pq)], writes=[("ya", c % 2)])
                if it["cend"]:
                    S.dma("sp", self.YA[c], ya[:, c % 2, :], reads=[("ya", c % 2)], writes=[("YA", c)])

            emit_qk(0)
            for i in range(len(its)):
                if i + 1 < len(its):
                    emit_qk(i + 1)
                emit_pv(i)
            S.flush()

    def phase_merge(self, l):
        nc, S = self.nc, self.S
        with (nc.sbuf_tensor(self.un("mgwg"), [128, 8, 3072], BF16) as wg,
              nc.sbuf_tensor(self.un("mgwb"), [128, 12, D], BF16) as wb,
              nc.sbuf_tensor(self.un("mgwo"), [128, 8, D], BF16) as wo,
              nc.sbuf_tensor(self.un("mgx0"), [128, 8, TN], F32) as xt0,
              nc.sbuf_tensor(self.un("mgx1"), [128, 8, TN], F32) as xt1,
              nc.sbuf_tensor(self.un("mgh0"), [128, 8, TN], BF16) as h0,
              nc.sbuf_tensor(self.un("mgh1"), [128, 8, TN], BF16) as h1,
              nc.sbuf_tensor(self.un("mgy0"), [128, 12, TN], BF16) as y0,
              nc.sbuf_tensor(self.un("mgy1"), [128, 12, TN], BF16) as y1,
              nc.sbuf_tensor(self.un("mgsg"), [128, 3, TN], F32) as sg,
              nc.sbuf_tensor(self.un("mgtm"), [128, 2, TN], F32) as tm,
              nc.sbuf_tensor(self.un("mgma"), [128, 2, TN], F32) as ma,
              nc.sbuf_tensor(self.un("mgm"), [128, 8, TN], BF16) as m):
            xts, hs, ys = [xt0, xt1], [h0, h1], [y0, y1]
            src = self.w_in[l].rearrange("(k p) c -> p k c", p=128)
            for pc in range(8):
                S.dma("pool", wg[:, :, pc * 384:(pc + 1) * 384], src[:, :, 3840 + pc * 384:3840 + (pc + 1) * 384],
                      writes=[("wg", pc)])
            for n in range(3):
                S.dma("pool", wb[:, n * 4:(n + 1) * 4, :], self.w_branch[l, n].rearrange("(k p) c -> p k c", p=128),
                      writes=[("wb", n)])
            srco = self.w_out[l].rearrange("(k p) c -> p k c", p=128)
            for kk in range(2):
                S.dma("pool", wo[:, kk * 4:(kk + 1) * 4, :], srco[:, kk * 4:(kk + 1) * 4, :], writes=[("wo", kk)])
            ysrc = [self.YR, self.YL, self.YA]

            def loads(t):
                par = t % 2
                t0 = t * TN
                self.load_x(xts[par], t, par)
                S.dma("sp", hs[par][:], self.H[:, :, t0:t0 + TN].rearrange("k p n -> p k n"), writes=[("h", par)])
                for n in range(3):
                    S.dma("sp", ys[par][:, n * 4:(n + 1) * 4, :], ysrc[n][:, :, t0:t0 + TN].rearrange("k p n -> p k n"),
                          writes=[("y3", par, n)])

            loads(0)
            scnt = 0
            for t in range(NT):
                par = t % 2
                xt, h, y3 = xts[par], hs[par], ys[par]
                c = 1 if t == 0 else 0
                if t + 1 < NT:
                    loads(t + 1)
                for i in range(8):
                    mi = i % 2
                    for n in range(3):
                        pg, kg = self.nextps()
                        wc = n * 8 + i
                        for k in range(8):
                            S.mm(pg[:, 0:TN], wg[:, k, wc * 128:(wc + 1) * 128], h[:, k, :], k == 0, k == 7,
                                 reads=[("wg", wc // 3), ("h", par)], writes=[kg])
                        pu, ku = self.nextps()
                        for kk in range(4):
                            S.mm(pu[:, 0:TN], wb[:, n * 4 + kk, i * 128:(i + 1) * 128], y3[:, n * 4 + kk, :], kk == 0, kk == 3,
                                 reads=[("wb", n), ("y3", par, n)], writes=[ku])
                        s3 = scnt % 3
                        scnt += 1
                        S.act(sg[:, s3, :], pg[:, 0:TN], AF.Sigmoid, reads=[kg], writes=[("sg", s3)])
                        if n == 0:
                            S.tt("dve", ma[:, mi, :], sg[:, s3, :], pu[:, 0:TN], ALU.mult, reads=[("sg", s3), ku],
                                 writes=[("ma", mi)])
                        else:
                            S.tt("dve", tm[:, n - 1, :], sg[:, s3, :], pu[:, 0:TN], ALU.mult, reads=[("sg", s3), ku],
                                 writes=[("tm", n - 1)])
                            if n == 1:
                                S.tt("pool", ma[:, mi, :], ma[:, mi, :], tm[:, 0, :], ALU.add,
                                     reads=[("ma", mi), ("tm", 0)], writes=[("ma", mi)])
                            else:
                                S.tt("pool", m[:, i, :], ma[:, mi, :], tm[:, 1, :], ALU.add,
                                     reads=[("ma", mi), ("tm", 1)], writes=[("m", i)])
                for i in range(8):
                    po, ko = self.nextps()
                    for k in range(8):
                        S.mm(po[:, 0:TN], wo[:, k, i * 128:(i + 1) * 128], m[:, k, :], k == 0, k == 7,
                             reads=[("wo", k // 4), ("m", k)], writes=[ko])
                    S.stt(xt[:, i, :], po[:, 0:TN], self.modG[:, 1, i, c:c + 1], xt[:, i, :], ALU.mult, ALU.add,
                          reads=[ko, ("xt", par, i), ("modG", 1)], writes=[("xt", par, i)])
                self.store_x(xt, t, par)
            S.flush()


def _perm128():
    return np.concatenate([np.arange(32, 64), np.arange(0, 32), np.arange(96, 128), np.arange(64, 96)])


def _perm64():
    return np.concatenate([np.arange(16, 32), np.arange(0, 16), np.arange(48, 64), np.arange(32, 48)])


def _fm(v):
    v = np.asarray(v, np.float32)
    lead = v.shape[:-1]
    n = v.shape[-1] // 128
    v = v.reshape(*lead, n, 128)
    return np.moveaxis(v, -1, 0)


def _rope_tables(half):
    theta = 10000.0
    tl = np.arange(LAT) + half * LAT
    rows = (tl // 64).astype(np.float32)
    cols = (tl % 64).astype(np.float32)
    tabs = np.zeros((4, 128, T), np.float32)
    tabs[0, :, :CTX] = 1.0
    tabs[2, :, :CTX] = 1.0
    f = (theta ** (-np.arange(0, 64, 2, dtype=np.float32) / 64)).astype(np.float32)
    ar = (rows[None, :] * f[:, None]).astype(np.float32)
    ac = (cols[None, :] * f[:, None]).astype(np.float32)
    C = np.concatenate([np.cos(ar), np.cos(ar), np.cos(ac), np.cos(ac)], 0)
    Sn = np.concatenate([-np.sin(ar), np.sin(ar), -np.sin(ac), np.sin(ac)], 0)
    tabs[0, :, CTX:] = C
    tabs[1, :, CTX:] = Sn
    f = (theta ** (-np.arange(0, 32, 2, dtype=np.float32) / 32)).astype(np.float32)
    ar = (rows[None, :] * f[:, None]).astype(np.float32)
    ac = (cols[None, :] * f[:, None]).astype(np.float32)
    C = np.concatenate([np.cos(ar), np.cos(ar), np.cos(ac), np.cos(ac)], 0)
    Sn = np.concatenate([-np.sin(ar), np.sin(ar), -np.sin(ac), np.sin(ac)], 0)
    tabs[2, :, CTX:] = np.concatenate([C, C], 0)
    tabs[3, :, CTX:] = np.concatenate([Sn, Sn], 0)
    return tabs


def _consts():
    cst = np.zeros((6, 128, 128), np.float32)
    cst[0] = np.eye(128)
    cst[1] = 1.0
    cst[2, :64, :64] = 1.0
    cst[2, 64:, 64:] = 1.0
    j = np.arange(128)[:, None]
    i = np.arange(128)[None, :]
    cst[3] = (i >= j)
    cst[4] = (i <= j)
    p = np.arange(TN) % 128
    pos = np.zeros((2, 128, TN), np.float32)
    pos[0] = (p + 1)[None, :]
    pos[1] = (128 - p)[None, :]
    return cst, pos


def _pack_pv(inp, b, half):
    pv = np.zeros((128, NPV), np.float32)
    p64 = _perm64()
    for l in range(DEPTH):
        o = l * PL
        pv[:, o:o + 24] = _fm(inp["norm_g"][l]).reshape(128, 24)
        pv[:, o + 24:o + 96] = _fm(inp["b_mod"][l]).reshape(128, 72)
        pv[:, o + 96:o + 100] = _fm(inp["ret_norm_g"][l])
        pv[:, o + 100:o + 116] = _fm(inp["lru_conv_w"][l]).reshape(128, 16)
        pv[:, o + 116:o + 120] = _fm(inp["lru_conv_b"][l])
        pv[:, o + 120:o + 128] = _fm(inp["lru_b_a"][l]).reshape(128, 8)
        pv[:, o + 128:o + 136] = _fm(inp["lru_b_x"][l]).reshape(128, 8)
        pv[:, o + 136:o + 144] = _fm(inp["lru_lambda"][l]).reshape(128, 8)
        qg = np.asarray(inp["attn_q_norm_g"][l], np.float32)
        kg = np.asarray(inp["attn_k_norm_g"][l], np.float32)
        pv[:, o + 144] = np.tile(qg, 2)
        pv[:, o + 145] = np.tile(qg[p64], 2)
        pv[:, o + 146] = np.tile(kg, 2)
        pv[:, o + 147] = np.tile(kg[p64], 2)
        pv[:, o + 148:o + 156] = np.asarray(inp["ret_decay_logit"][l], np.float32).reshape(1, 8)
    o = DEPTH * PL
    pv[:, o:o + 8] = _fm(inp["final_norm_g"])
    pv[:, o + 8:o + 16] = _fm(inp["c"][b])
    pv[:, o + 16:o + 24] = _fm(inp["c_ctx"])
    pv[:, o + 24] = float(half)
    pv[:, o + 25] = float(1 - half)
    return pv


def _host_inputs(inp):
    f = lambda a: np.ascontiguousarray(np.asarray(a, np.float32))
    cst, pos = _consts()
    w_in = f(inp["w_in"])
    p128, p64 = _perm128(), _perm64()
    idx = []
    for c in range(8):
        idx.append(c * 128 + p128)
    for c in range(5):
        for hh in range(2):
            idx.append(3072 + c * 128 + hh * 64 + p64)
    idx = np.concatenate(idx)
    w_inp = np.ascontiguousarray(w_in[:, :, idx])
    lbd = np.zeros((DEPTH, 2, 2, 4, 128, 128), np.float32)
    for a, name in enumerate(("lru_w_a", "lru_w_x")):
        w = f(inp[name])
        for c in range(4):
            lbd[:, a, :, c, :64, :64] = w[:, :, 2 * c]
            lbd[:, a, :, c, 64:, 64:] = w[:, :, 2 * c + 1]
    shared = {
        "cst": cst, "pos": pos, "w_mod": f(inp["w_mod"][:N_LAYERS]), "ffn_w_in": f(inp["ffn_w_in"][:N_LAYERS]),
        "ffn_w_out": f(inp["ffn_w_out"][:N_LAYERS]), "w_in": w_in[:N_LAYERS], "w_inp": w_inp[:N_LAYERS],
        "lbd": lbd[:N_LAYERS], "w_branch": f(inp["w_branch"][:N_LAYERS]), "w_out": f(inp["w_out"][:N_LAYERS]),
    }
    x = f(inp["x"])
    ctx = f(inp["ctx"])
    ropes = [_rope_tables(0), _rope_tables(1)]
    maps = []
    for core in range(8):
        b, half = core // 2, core % 2
        xt = np.concatenate([ctx[b], x[b, half * LAT:(half + 1) * LAT]], 0)
        xin = np.ascontiguousarray(xt.T.reshape(8, 128, T))
        m = dict(shared)
        m["xin"] = xin
        m["pv"] = _pack_pv(inp, b, half)
        m["rope"] = ropes[half]
        maps.append(m)
    return maps


_NC_CACHE = {}


def _get_nc():
    key = (DEBUG_STOP, N_LAYERS)
    if key not in _NC_CACHE:
        nc = bass.Bass("TRN2", target_bir_lowering=False)
        Builder(nc).build()
        _NC_CACHE[key] = nc
    return _NC_CACHE[key]


def kernel(**inputs):
    maps = _host_inputs(inputs)
    nc = _get_nc()
    if TRACE:
        res = run_bass_kernel_spmd(nc, maps, core_ids=list(range(8)), trace=True)
        print("exec_time_ns", res.exec_time_ns)
    else:
        res = run_bass_kernel_spmd(nc, maps, core_ids=list(range(8)))
    if DEBUG_STOP is not None:
        return [r["dbg"] for r in res.results]
    out = np.zeros((4, 2 * LAT, D), np.float32)
    for core in range(8):
        b, half = core // 2, core % 2
        o = res.results[core]["out"]
        out[b, half * LAT:(half + 1) * LAT] = o.reshape(D, LAT).T
    return out
```

```python
import numpy as np
import concourse.bass as bass
import concourse.mybir as mybir
from concourse.bass_utils import run_bass_kernel_spmd

F32 = mybir.dt.float32
BF16 = mybir.dt.bfloat16
AF = mybir.ActivationFunctionType
ALU = mybir.AluOpType

D = 1024
DEPTH = 4
CTX = 256
LAT = 2048
T = CTX + LAT
TN = 256
NT = T // TN
DFF = 2816
DIN = 6912
EPS = 1e-6
NZ = 38
PL = 156
NPV = DEPTH * PL + 8 + 16 + 2
EX_KA = 0
EX_AV = EX_KA + 128 * LAT
EX_SL = EX_AV + LAT * 128
EX_LX = EX_SL + 8 * 128 * 128
EX_N = EX_LX + 4 * 128 * 4
EX2_N = 2 * 512

DEBUG_STOP = None
N_LAYERS = DEPTH
TRACE = False


class _I:
    __slots__ = ("eng", "fn", "waits", "sig", "idx", "kind", "sem", "target")


class Sched:
    R = 8

    def __init__(self, nc, sems):
        self.nc = nc
        self.sems = sems
        nxt = iter(range(len(sems)))
        self.csem = {e: next(nxt) for e in ("pe", "act", "dve", "pool")}
        self.qsem = {q: [next(nxt) for _ in range(self.R)] for q in ("sp", "pool")}
        self.ccsem = next(nxt)
        self.ncc = 0
        self.qn = {"sp": 0, "pool": 0}
        self.cnt = {e: 0 for e in self.csem}
        self.lists = {e: [] for e in ("pe", "act", "dve", "pool", "sp")}
        self.lastw = {}
        self.rd = {}
        self.seen = {e: {} for e in self.lists}
        self.barrier = []
        self.ninstr = 0

    def add(self, eng, fn, reads=(), writes=(), kind="c"):
        I = _I()
        I.eng, I.fn, I.kind, I.sig, I.idx, I.waits = eng, fn, kind, False, None, []
        deps = []
        for b in reads:
            w = self.lastw.get(b)
            if w is not None:
                deps.append(w)
        for b in writes:
            w = self.lastw.get(b)
            if w is not None:
                deps.append(w)
            r = self.rd.get(b)
            if r:
                deps.extend(r[0].values())
                deps.extend(r[1])
        for J in deps:
            if J.kind in ("dma", "cc"):
                I.waits.append(J)
            elif J.eng == eng and kind == "c":
                if eng == "pe":
                    continue
                I.waits.append(J)
                J.sig = True
            else:
                I.waits.append(J)
                J.sig = True
        if kind == "dma":
            n = self.qn[eng]
            self.qn[eng] += 1
            I.sem = self.qsem[eng][n % self.R]
            I.target = 16 * (n // self.R + 1)
            if n >= self.R:
                I.waits.append((I.sem, 16 * (n // self.R)))
        elif kind == "cc":
            I.sem = self.ccsem
            self.ncc += 1
            I.target = self.ncc
        for b in reads:
            r = self.rd.setdefault(b, ({}, []))
            if kind == "c":
                r[0][eng] = I
            else:
                r[1].append(I)
        for b in writes:
            self.lastw[b] = I
            self.rd[b] = ({}, [])
        self.lists[eng].append(I)
        self.ninstr += 1
        return I

    def dma(self, q, out, in_, reads=(), writes=(), slow=False):
        if slow:
            return self.add(q, lambda e: e.dma_start(out=out, in_=in_, allow_slow_non_contiguous=True), reads, writes,
                            kind="dma")
        return self.add(q, lambda e: e.dma_start(out=out, in_=in_), reads, writes, kind="dma")

    def mm(self, out, lhsT, rhs, start, stop, reads=(), writes=()):
        return self.add("pe", lambda e: e.matmul(out, lhsT=lhsT, rhs=rhs, start=start, stop=stop), reads, writes)

    def tr(self, out, in_, ident, reads=(), writes=()):
        return self.add("pe", lambda e: e.transpose(out, in_, ident), reads, writes)

    def act(self, out, in_, func, reads=(), writes=(), bias=None, scale=None):
        kw = {}
        if bias is not None:
            kw["bias"] = bias
        if scale is not None:
            kw["scale"] = scale
        return self.add("act", lambda e: e.activation(out=out, in_=in_, func=func, **kw), reads, writes)

    def tt(self, eng, out, in0, in1, op, reads=(), writes=()):
        return self.add(eng, lambda e: e.tensor_tensor(out=out, in0=in0, in1=in1, op=op), reads, writes)

    def ts(self, eng, out, in0, s1, s2, op0, op1, reads=(), writes=()):
        return self.add(eng, lambda e: e.tensor_scalar(out=out, in0=in0, scalar1=s1, scalar2=s2, op0=op0, op1=op1),
                        reads, writes)

    def stt(self, out, in0, scalar, in1, op0, op1, reads=(), writes=()):
        return self.add("dve", lambda e: e.scalar_tensor_tensor(out=out, in0=in0, scalar=scalar, in1=in1,
                                                                op0=op0, op1=op1), reads, writes)

    def cp(self, eng, out, in_, reads=(), writes=()):
        if eng == "act":
            return self.add("act", lambda e: e.copy(out=out, in_=in_), reads, writes)
        return self.add(eng, lambda e: e.tensor_copy(out=out, in_=in_), reads, writes)

    def recip(self, out, in_, reads=(), writes=()):
        return self.add("dve", lambda e: e.reciprocal(out=out, in_=in_), reads, writes)

    def memset(self, eng, ap, val, reads=(), writes=()):
        return self.add(eng, lambda e: e.memset(ap, val), reads, writes)

    def scan(self, out, d0, d1, init, reads=(), writes=()):
        return self.add("dve", lambda e: e.tensor_tensor_scan(out=out, data0=d0, data1=d1, initial=init,
                                                              op0=ALU.mult, op1=ALU.add), reads, writes)

    def cc(self, ins_ap, outs_ap, reads=(), writes=()):
        groups = [[0, 1], [2, 3], [4, 5], [6, 7]]
        return self.add("pool", lambda e: e.collective_compute("AllGather", ALU.bypass, replica_groups=groups,
                                                               ins=[ins_ap], outs=[outs_ap]),
                        reads, writes, kind="cc")

    def flush(self):
        nc = self.nc
        for e in self.csem:
            last = None
            for I in self.lists[e]:
                if I.kind == "c":
                    last = I
            if last is not None:
                last.sig = True
            for I in self.lists[e]:
                if I.kind == "c" and I.sig:
                    self.cnt[e] += 1
                    I.idx = self.cnt[e]
        sems = self.sems

        def emit(eh, eng):
            seen = self.seen[eng]

            def wait(si, val):
                if seen.get(si, 0) < val:
                    eh.wait_ge(sems[si], val)
                    seen[si] = val

            for (si, val) in self.barrier:
                wait(si, val)
            for I in self.lists[eng]:
                for w in I.waits:
                    if isinstance(w, tuple):
                        wait(w[0], w[1])
                    elif w.kind in ("dma", "cc"):
                        wait(w.sem, w.target)
                    else:
                        wait(self.csem[w.eng], w.idx)
                ins = I.fn(eh)
                if I.kind == "dma":
                    ins.then_inc(sems[I.sem], 16)
                elif I.kind == "cc":
                    ins.then_inc(sems[I.sem])
                elif I.sig:
                    ins.then_inc(sems[self.csem[eng]], 1)

        with nc.Block() as block:
            @block.tensor
            def _(eh):
                emit(eh, "pe")

            @block.scalar
            def _(eh):
                emit(eh, "act")

            @block.vector
            def _(eh):
                emit(eh, "dve")

            @block.gpsimd
            def _(eh):
                emit(eh, "pool")

            @block.sync
            def _(eh):
                emit(eh, "sp")

        bar = [(self.csem[e], self.cnt[e]) for e in self.csem if self.cnt[e] > 0]
        for q in ("sp", "pool"):
            n = self.qn[q]
            for i in range(self.R):
                if n > i:
                    bar.append((self.qsem[q][i], 16 * ((n - 1 - i) // self.R + 1)))
        if self.ncc:
            bar.append((self.ccsem, self.ncc))
        self.barrier = bar
        self.lists = {e: [] for e in self.lists}
        self.lastw = {}
        self.rd = {}

    def final_wait(self):
        nc = self.nc
        sems = self.sems
        with nc.Block() as block:
            @block.sync
            def _(eh):
                for (si, val) in self.barrier:
                    eh.wait_ge(sems[si], val)

            @block.gpsimd
            def _(eh):
                for (si, val) in self.barrier:
                    eh.wait_ge(sems[si], val)


class Builder:
    def __init__(self, nc):
        self.nc = nc
        self.pscur = 0

    def declare(self):
        nc = self.nc
        di = lambda n, s: nc.dram_tensor(n, s, F32, kind="ExternalInput").ap()
        self.xin = di("xin", [8, 128, T])
        self.pvin = di("pv", [128, NPV])
        self.cst = di("cst", [6, 128, 128])
        self.posin = di("pos", [2, 128, TN])
        self.rope = di("rope", [4, 128, T])
        self.w_mod = di("w_mod", [N_LAYERS, D, 9 * D])
        self.ffn_w_in = di("ffn_w_in", [N_LAYERS, 2, D, 2 * DFF])
        self.ffn_w_out = di("ffn_w_out", [N_LAYERS, 2, DFF, D])
        self.w_in = di("w_in", [N_LAYERS, D, DIN])
        self.w_inp = di("w_inp", [N_LAYERS, D, 13 * 128])
        self.lbd = di("lbd", [N_LAYERS, 2, 2, 4, 128, 128])
        self.w_branch = di("w_branch", [N_LAYERS, 3, 512, D])
        self.w_out = di("w_out", [N_LAYERS, D, D])
        self.out = nc.dram_tensor("out", [8, 128, LAT], F32, kind="ExternalOutput").ap()
        if DEBUG_STOP is not None:
            self.dbg = nc.dram_tensor("dbg", [8, 128, T], F32, kind="ExternalOutput").ap()
        dt = lambda n, s, d=F32: nc.dram_tensor(n, s, d)
        self.X = dt("X", [8, 128, T]).ap()
        self.H = dt("H", [8, 128, T], BF16).ap()
        self.Z = dt("Z", [NZ, 128, T]).ap()
        self.RV = dt("RV", [T, 512], BF16).ap()
        self.AVC = dt("AVC", [CTX, 128]).ap()
        self.KAC = dt("KAC", [128, CTX]).ap()
        self.QA = dt("QA", [4, 128, T], BF16).ap()
        self.QK = dt("QK", [4, 4, 128, T], BF16).ap()
        self.EXP = [dt("EXP%d" % j, [128, 512]) for j in range(10)] + [dt("EXPL", [4, 512])]
        self.EXGP = [dt("EXGP%d" % j, [256, 512]) for j in range(10)] + [dt("EXGPL", [8, 512])]
        self.EX2t = dt("EX2", [EX2_N // 512, 512])
        self.EX2Gt = dt("EX2G", [2 * EX2_N // 512, 512])
        self.EX2 = self.EX2t.ap().rearrange("a b -> (a b)")
        self.EX2G = self.EX2Gt.ap().rearrange("a b -> (a b)")
        self.HS = dt("HS", [4, 128, T]).ap()
        self.ACF = dt("ACF", [4, 128, LAT]).ap()
        self.ACB = dt("ACB", [4, 128, LAT]).ap()
        self.YR = dt("YR", [4, 128, T], BF16).ap()
        self.YL = dt("YL", [4, 128, T], BF16).ap()
        self.YA = dt("YA", [4, 128, T], BF16).ap()

    def un(self, name):
        self.uid = getattr(self, "uid", 0) + 1
        return "%s_%d" % (name, self.uid)

    def nextps(self, n=6):
        i = self.pscur % n
        self.pscur += 1
        return self.ps[i], ("ps", i)

    def pvl(self, l, off, n=1):
        return self.pv[:, l * PL + off:l * PL + off + n]

    def phase_init(self):
        nc, S = self.nc, self.S
        with nc.sbuf_tensor(self.un("ini_c"), [128, 5, 128], F32) as cf, nc.sbuf_tensor(self.un("ini_s"), [128, 16], F32) as sv:
            S.dma("sp", self.pv[:], self.pvin, writes=["pv"])
            S.dma("sp", cf[:], self.cst[0:5].rearrange("c p n -> p c n"), writes=["cf"])
            S.dma("sp", self.pos[:], self.posin.rearrange("c p n -> p c n"), writes=["pos"])
            S.cp("dve", self.ident[:], cf[:, 0, :], reads=["cf"], writes=["ident"])
            S.cp("dve", self.ones[:], cf[:, 1, :], reads=["cf"], writes=["ones"])
            S.cp("dve", self.bones[:], cf[:, 2, :], reads=["cf"], writes=["bones"])
            S.cp("dve", self.maskf[:], cf[:, 3, :], reads=["cf"], writes=["maskf"])
            S.cp("dve", self.maskb[:], cf[:, 4, :], reads=["cf"], writes=["maskb"])
            S.cp("dve", self.ones32[:], cf[:, 1, :], reads=["cf"], writes=["ones32"])
            base = DEPTH * PL + 8
            S.act(sv[:], self.pv[:, base:base + 16], AF.Silu, reads=["pv"], writes=["sv"])
            S.cp("dve", self.sb[:], sv[:].rearrange("p (c k) -> p k c", c=2), reads=["sv"], writes=["sb"])
            for k in range(8):
                S.dma("sp", self.X[k], self.xin[k], writes=[("X", k)])
            S.flush()

    def phase_mod(self, l):
        nc, S = self.nc, self.S
        with (nc.sbuf_tensor(self.un("wm0"), [128, 8, 1024], BF16) as wm0,
              nc.sbuf_tensor(self.un("wm1"), [128, 8, 1024], BF16) as wm1,
              nc.sbuf_tensor(self.un("mraw"), [128, 9, 8, 2], F32) as mraw,
              nc.sbuf_tensor(self.un("lg"), [128, 8], F32) as lg):
            wms = [wm0, wm1]
            src = self.w_mod[l].rearrange("(k p) c -> p k c", p=128)
            for b in range(9):
                wm = wms[b % 2]
                for kk in range(0, 8, 4):
                    S.dma("pool", wm[:, kk:kk + 4, :], src[:, kk:kk + 4, b * 1024:(b + 1) * 1024],
                          writes=[("wm", b % 2, kk)])
                pm, km = self.nextps()
                for i in range(8):
                    for k in range(8):
                        S.mm(pm[:, i * 2:(i + 1) * 2], wm[:, k, i * 128:(i + 1) * 128], self.sb[:, k, :],
                             k == 0, k == 7, reads=[("wm", b % 2, (k // 4) * 4), "sb"], writes=[km])
                bm = self.pvl(l, 24 + b * 8, 8)
                S.tt("dve", mraw[:, b, :, :], pm[:, 0:16].rearrange("p (i c) -> p i c", c=2),
                     bm.unsqueeze(2).broadcast_to([128, 8, 2]), ALU.add, reads=[km, "pv"], writes=[("mraw", b)])
            for s in range(3):
                g = self.pvl(l, s * 8, 8).unsqueeze(2).broadcast_to([128, 8, 2])
                S.stt(self.modA[:, s, :, :], mraw[:, 3 * s + 1, :, :], 1.0, g, ALU.add, ALU.mult,
                      reads=[("mraw", 3 * s + 1), "pv"], writes=[("modA", s)])
                S.cp("dve", self.modB[:, s, :, :], mraw[:, 3 * s, :, :], reads=[("mraw", 3 * s)], writes=[("modB", s)])
                S.ts("dve", self.modG[:, s, :, :], mraw[:, 3 * s + 2, :, :], 0.5 if s != 1 else 1.0, None,
                     ALU.mult, ALU.bypass, reads=[("mraw", 3 * s + 2)], writes=[("modG", s)])
            S.act(lg[:], self.pvl(l, 148, 8), AF.Exp, scale=-1.0, reads=["pv"], writes=["lg"])
            S.act(lg[:], lg[:], AF.Ln, bias=1.0, reads=["lg"], writes=["lg"])
            S.ts("dve", self.lgam[:], lg[:], -1.0, None, ALU.mult, ALU.bypass, reads=["lg"], writes=["lgam"])
            S.ts("dve", self.nlgam[:], lg[:], 1.0, None, ALU.mult, ALU.bypass, reads=["lg"], writes=["nlgam"])
            S.act(self.gC[:], self.lgam[:], AF.Exp, scale=128.0, reads=["lgam"], writes=["gC"])
            hb = DEPTH * PL + 24
            S.ts("dve", lg[:, 0:4], self.lgam[:, 0:4], self.pv[:, hb:hb + 1], 2048.0, ALU.mult, ALU.mult,
                 reads=["lgam", "pv"], writes=["lg"])
            S.ts("dve", lg[:, 4:8], self.lgam[:, 4:8], self.pv[:, hb + 1:hb + 2], 2048.0, ALU.mult, ALU.mult,
                 reads=["lgam", "pv"], writes=["lg"])
            S.act(self.bc1[:], lg[:], AF.Exp, reads=["lg"], writes=["bc1"])
            S.act(self.lcp[:], self.pvl(l, 136, 8), AF.Exp, scale=-1.0, reads=["pv"], writes=["lcp"])
            S.act(self.lcp[:], self.lcp[:], AF.Ln, bias=1.0, reads=["lcp"], writes=["lcp"])
            S.ts("dve", self.lcp2[:], self.lcp[:], -16.0, None, ALU.mult, ALU.bypass, reads=["lcp"], writes=["lcp2"])
            S.ts("dve", self.lcp[:], self.lcp[:], -8.0, None, ALU.mult, ALU.bypass, reads=["lcp"], writes=["lcp"])
            S.flush()

    def load_x(self, xt, t, par):
        S = self.S
        S.dma("sp", xt[:], self.X[:, :, t * TN:(t + 1) * TN].rearrange("k p n -> p k n"),
              reads=[("X", t)], writes=[("xt", par, i) for i in range(8)])

    def store_x(self, xt, t, par, dst=None):
        S = self.S
        dst = self.X if dst is None else dst
        S.dma("sp", dst[:, :, t * TN:(t + 1) * TN].rearrange("k p n -> p k n"), xt[:],
              reads=[("xt", par, i) for i in range(8)], writes=[("X", t)])

    def norm_mod(self, xt, par, sq, rs, tmp, h, sub, c):
        S = self.S
        xk = [("xt", par, i) for i in range(8)]
        S.act(sq[:], xt[:], AF.Square, reads=xk, writes=["sq"])
        pn, kn = self.nextps()
        for k in range(8):
            S.mm(pn[:, 0:TN], self.ones[:], sq[:, k, :], k == 0, k == 7, reads=["sq", "ones"], writes=[kn])
        S.act(rs[:], pn[:, 0:TN], AF.Sqrt, scale=1.0 / D, bias=self.epsb[:, 0:1], reads=[kn], writes=["rs"])
        S.recip(rs[:], rs[:], reads=["rs"], writes=["rs"])
        S.tt("dve", tmp[:], xt[:], rs[:].unsqueeze(1).broadcast_to([128, 8, TN]), ALU.mult,
             reads=xk + ["rs"], writes=["tmp"])
        for k in range(8):
            S.act(h[:, k, :], tmp[:, k, :], AF.Identity, scale=self.modA[:, sub, k, c:c + 1],
                  bias=self.modB[:, sub, k, c:c + 1], reads=["tmp", ("modA", sub), ("modB", sub)],
                  writes=[("h", par, k)])

    def phase_ffn(self, l, which, sub):
        nc, S = self.nc, self.S
        with (nc.sbuf_tensor(self.un("ffw1"), [128, 8, 2 * DFF], BF16) as w1,
              nc.sbuf_tensor(self.un("ffw2"), [128, 22, D], BF16) as w2,
              nc.sbuf_tensor(self.un("fxt0"), [128, 8, TN], F32) as xt0,
              nc.sbuf_tensor(self.un("fxt1"), [128, 8, TN], F32) as xt1,
              nc.sbuf_tensor(self.un("fsq"), [128, 8, TN], BF16) as sq,
              nc.sbuf_tensor(self.un("fh0"), [128, 8, TN], BF16) as h0,
              nc.sbuf_tensor(self.un("fh1"), [128, 8, TN], BF16) as h1,
              nc.sbuf_tensor(self.un("fg"), [128, 22, TN], BF16) as g,
              nc.sbuf_tensor(self.un("fsl0"), [128, TN], F32) as sl0,
              nc.sbuf_tensor(self.un("fsl1"), [128, TN], F32) as sl1,
              nc.sbuf_tensor(self.un("frs"), [128, TN], F32) as rs,
              nc.sbuf_tensor(self.un("ftmp"), [128, 8, TN], F32) as tmp):
            xts, hs, sls = [xt0, xt1], [h0, h1], [sl0, sl1]
            src1 = self.ffn_w_in[l, which].rearrange("(k p) c -> p k c", p=128)
            for cb in range(11):
                S.dma("pool", w1[:, :, cb * 512:(cb + 1) * 512], src1[:, :, cb * 512:(cb + 1) * 512],
                      writes=[("w1", cb)])
            src2 = self.ffn_w_out[l, which].rearrange("(j p) c -> p j c", p=128)
            for jb in range(11):
                S.dma("pool", w2[:, 2 * jb:2 * jb + 2, :], src2[:, 2 * jb:2 * jb + 2, :], writes=[("w2", jb)])
            self.load_x(xts[0], 0, 0)
            self.norm_mod(xts[0], 0, sq, rs, tmp, hs[0], sub, 1)
            for t in range(NT):
                par = t % 2
                xt, h = xts[par], hs[par]
                c = 1 if t == 0 else 0
                if t + 1 < NT:
                    self.load_x(xts[1 - par], t + 1, 1 - par)
                for j in range(22):
                    pa, ka = self.nextps()
                    pb, kb = self.nextps()
                    ca, cb_ = j * 128, DFF + j * 128
                    for k in range(8):
                        S.mm(pa[:, 0:TN], w1[:, k, ca:ca + 128], h[:, k, :], k == 0, k == 7,
                             reads=[("w1", ca // 512), ("h", par, k)], writes=[ka])
                    for k in range(8):
                        S.mm(pb[:, 0:TN], w1[:, k, cb_:cb_ + 128], h[:, k, :], k == 0, k == 7,
                             reads=[("w1", cb_ // 512), ("h", par, k)], writes=[kb])
                    sl = sls[j % 2]
                    S.act(sl[:], pa[:, 0:TN], AF.Silu, reads=[ka], writes=[("sl", j % 2)])
                    S.tt("dve", g[:, j, :], sl[:], pb[:, 0:TN], ALU.mult, reads=[("sl", j % 2), kb], writes=[("g", j)])
                if t + 1 < NT:
                    self.norm_mod(xts[1 - par], 1 - par, sq, rs, tmp, hs[1 - par], sub, 0)
                for i in range(8):
                    po, ko = self.nextps()
                    for j in range(22):
                        S.mm(po[:, 0:TN], w2[:, j, i * 128:(i + 1) * 128], g[:, j, :], j == 0, j == 21,
                             reads=[("w2", j // 2), ("g", j)], writes=[ko])
                    S.stt(xt[:, i, :], po[:, 0:TN], self.modG[:, sub, i, c:c + 1], xt[:, i, :], ALU.mult, ALU.add,
                          reads=[ko, ("xt", par, i), ("modG", sub)], writes=[("xt", par, i)])
                self.store_x(xt, t, par)
            S.flush()

    def phase_final(self):
        nc, S = self.nc, self.S
        with (nc.sbuf_tensor(self.un("nxt0"), [128, 8, TN], F32) as xt0,
              nc.sbuf_tensor(self.un("nxt1"), [128, 8, TN], F32) as xt1,
              nc.sbuf_tensor(self.un("nsq"), [128, 8, TN], BF16) as sq,
              nc.sbuf_tensor(self.un("nrs"), [128, TN], F32) as rs,
              nc.sbuf_tensor(self.un("no0"), [128, 8, TN], F32) as o0,
              nc.sbuf_tensor(self.un("no1"), [128, 8, TN], F32) as o1):
            xts, os_ = [xt0, xt1], [o0, o1]
            fb = DEPTH * PL
            for t in range(1, NT):
                par = t % 2
                xt, o = xts[par], os_[par]
                self.load_x(xt, t, par)
                xk = [("xt", par, i) for i in range(8)]
                S.act(sq[:], xt[:], AF.Square, reads=xk, writes=["sq"])
                pn, kn = self.nextps()
                for k in range(8):
                    S.mm(pn[:, 0:TN], self.ones[:], sq[:, k, :], k == 0, k == 7, reads=["sq"], writes=[kn])
                S.act(rs[:], pn[:, 0:TN], AF.Sqrt, scale=1.0 / D, bias=self.epsb[:, 0:1], reads=[kn], writes=["rs"])
                S.recip(rs[:], rs[:], reads=["rs"], writes=["rs"])
                for k in range(8):
                    S.stt(o[:, k, :], xt[:, k, :], self.pv[:, fb + k:fb + k + 1], rs[:], ALU.mult, ALU.mult,
                          reads=xk + ["rs"], writes=[("o", par)])
                S.dma("sp", self.out[:, :, (t - 1) * TN:t * TN].rearrange("k p n -> p k n"), o[:],
                      reads=[("o", par)], writes=[("out", t)])
            S.flush()

    def phase_dbg(self):
        S = self.S
        for k in range(8):
            S.dma("sp", self.dbg[k], self.X[k], reads=[("X", k)], writes=[("dbg", k)])
        S.flush()

    def build(self):
        nc = self.nc
        self.declare()
        sem_names = ["s%d" % i for i in range(4 + 16 + 1)]
        from contextlib import ExitStack
        with ExitStack() as st:
            sems = [st.enter_context(nc.semaphore(n)) for n in sem_names]
            self.S = Sched(nc, sems)
            sb = lambda n, s, d=F32: st.enter_context(nc.sbuf_tensor(self.un("sb_") + n, s, d))
            self.pv = sb("pv", [128, NPV])
            self.ident = sb("ident", [128, 128], BF16)
            self.ones = sb("ones", [128, 128], BF16)
            self.bones = sb("bones", [128, 128], BF16)
            self.maskf = sb("maskf", [128, 128], BF16)
            self.maskb = sb("maskb", [128, 128], BF16)
            self.ones32 = sb("ones32", [128, 128])
            self.pos = sb("pos", [128, 2, TN])
            self.sb = sb("sb", [128, 8, 2], BF16)
            self.modA = sb("modA", [128, 3, 8, 2])
            self.modB = sb("modB", [128, 3, 8, 2])
            self.modG = sb("modG", [128, 3, 8, 2])
            self.lgam = sb("lgam", [128, 8])
            self.nlgam = sb("nlgam", [128, 8])
            self.gC = sb("gC", [128, 8])
            self.bc1 = sb("bc1", [128, 8])
            self.lcp = sb("lcp", [128, 8])
            self.lcp2 = sb("lcp2", [128, 8])
            self.epsb = sb("epsb", [128, 1])
            self.lnks = sb("lnks", [128, 1])
            self.ps = [st.enter_context(nc.psum_tensor("ps%d" % i, [128, 512], F32)) for i in range(6)]
            self.psb = [st.enter_context(nc.psum_tensor("psb%d" % i, [128, 1024], BF16)) for i in range(2)]
            self.S.memset("dve", self.epsb[:], EPS, writes=["epsb"])
            self.S.memset("dve", self.lnks[:], -0.5 * float(np.log(128.0)), writes=["lnks"])
            self.phase_init()
            stop = False
            for l in range(N_LAYERS):
                for name, fn in (("mod", lambda: self.phase_mod(l)),
                                 ("ffn1", lambda: self.phase_ffn(l, 0, 0)),
                                 ("mix", lambda: self.phase_mix(l)),
                                 ("ffn2", lambda: self.phase_ffn(l, 1, 2))):
                    r = fn()
                    if r or DEBUG_STOP == (l, name):
                        stop = True
                        break
                if stop:
                    break
            if DEBUG_STOP is not None:
                self.phase_dbg()
            else:
                self.phase_final()
            self.S.final_wait()


    def ex_w(self, off, cnt_):
        j, o = off // 65536, off % 65536
        assert o + cnt_ <= 65536
        return self.EXP[j].ap().rearrange("a b -> (a b)")[o:o + cnt_]

    def ex_r(self, slot, off, cnt_):
        j, o = off // 65536, off % 65536
        assert o + cnt_ <= 65536
        psz = 65536 if j < 10 else 2048
        lo = slot * psz + o
        return self.EXGP[j].ap().rearrange("a b -> (a b)")[lo:lo + cnt_]

    def phase_mix(self, l):
        def cc1():
            for j in range(11):
                self.S.cc(self.EXP[j].ap().opt(), self.EXGP[j].ap().opt())
            self.S.flush()

        def cc2():
            self.S.cc(self.EX2t.ap().opt(), self.EX2Gt.ap().opt())
            self.S.flush()

        for name, fn in (("m1", lambda: self.phase_m1(l)), ("m2", lambda: self.phase_m2(l)),
                         ("m2b", lambda: self.phase_m2b(l)), ("cc1", cc1), ("lruA", lambda: self.phase_lruA(l)),
                         ("cc2", cc2), ("lruB", lambda: self.phase_lruB(l)), ("ret", lambda: self.phase_ret(l)),
                         ("attn", lambda: self.phase_attn(l)), ("merge", lambda: self.phase_merge(l))):
            fn()
            if DEBUG_STOP == (l, name):
                return True
        return False

    ZSRC = list(range(0, 8)) + list(range(12, 29)) + list(range(30, 43))

    def phase_m1(self, l):
        nc, S = self.nc, self.S
        NWC = 43
        with (nc.sbuf_tensor(self.un("m1w"), [128, 8, NWC * 128], BF16) as wz,
              nc.sbuf_tensor(self.un("m1x0"), [128, 8, TN], F32) as xt0,
              nc.sbuf_tensor(self.un("m1x1"), [128, 8, TN], F32) as xt1,
              nc.sbuf_tensor(self.un("m1sq"), [128, 8, TN], BF16) as sq,
              nc.sbuf_tensor(self.un("m1h0"), [128, 8, TN], BF16) as h0,
              nc.sbuf_tensor(self.un("m1h1"), [128, 8, TN], BF16) as h1,
              nc.sbuf_tensor(self.un("m1rs"), [128, TN], F32) as rs,
              nc.sbuf_tensor(self.un("m1tmp"), [128, 8, TN], F32) as tmp,
              nc.sbuf_tensor(self.un("m1z0"), [128, 2, TN], F32) as zs0,
              nc.sbuf_tensor(self.un("m1z1"), [128, 2, TN], F32) as zs1,
              nc.sbuf_tensor(self.un("m1rv0"), [128, 512], BF16) as rv0,
              nc.sbuf_tensor(self.un("m1rv1"), [128, 512], BF16) as rv1,
              nc.sbuf_tensor(self.un("m1av0"), [128, 128], F32) as av0,
              nc.sbuf_tensor(self.un("m1av1"), [128, 128], F32) as av1):
            xts, hs, zss, rvs, avs = [xt0, xt1], [h0, h1], [zs0, zs1], [rv0, rv1], [av0, av1]
            src = self.w_in[l].rearrange("(k p) c -> p k c", p=128)
            srcp = self.w_inp[l].rearrange("(k p) c -> p k c", p=128)
            for pc in range(10):
                S.dma("pool", wz[:, :, pc * 384:(pc + 1) * 384], src[:, :, pc * 384:(pc + 1) * 384], writes=[("wz", pc)])
            for pc in range(5):
                lo, hi = pc * 384, min((pc + 1) * 384, 13 * 128)
                S.dma("pool", wz[:, :, 3840 + lo:3840 + hi], srcp[:, :, lo:hi], writes=[("wz", 10 + pc)])
            self.load_x(xts[0], 0, 0)
            self.norm_mod(xts[0], 0, sq, rs, tmp, hs[0], 1, 1)
            for t in range(NT):
                par = t % 2
                xt, h = xts[par], hs[par]
                c = 1 if t == 0 else 0
                t0 = t * TN
                if t + 1 < NT:
                    self.load_x(xts[1 - par], t + 1, 1 - par)
                S.dma("sp", self.H[:, :, t0:t0 + TN].rearrange("k p n -> p k n"), h[:], reads=[("h", par, k) for k in range(8)],
                      writes=[("H", t)])
                for zc in range(NZ):
                    wc = self.ZSRC[zc]
                    pz, kz = self.nextps()
                    for k in range(8):
                        S.mm(pz[:, 0:TN], wz[:, k, wc * 128:(wc + 1) * 128], h[:, k, :], k == 0, k == 7,
                             reads=[("wz", wc // 3), ("h", par, k)], writes=[kz])
                    zs = zss[(zc // 2) % 2]
                    zk = ("zs", (zc // 2) % 2, zc % 2)
                    S.cp("act" if zc % 2 == 0 else "dve", zs[:, zc % 2, :], pz[:, 0:TN], reads=[kz], writes=[zk])
                    if zc % 2 == 1:
                        S.dma("sp", self.Z[zc - 1:zc + 1, :, t0:t0 + TN].rearrange("c p n -> p c n"), zs[:],
                              reads=[("zs", (zc // 2) % 2, 0), ("zs", (zc // 2) % 2, 1)], writes=[("Z", zc // 2, t)])
                if t + 1 < NT:
                    self.norm_mod(xts[1 - par], 1 - par, sq, rs, tmp, hs[1 - par], 1, 0)
                for hf in range(2):
                    i2 = (2 * t + hf) % 2
                    prv, krv = self.nextps()
                    for k in range(8):
                        S.mm(prv[:, 0:512], h[:, k, hf * 128:(hf + 1) * 128], wz[:, k, 1024:1536], k == 0, k == 7,
                             reads=[("wz", 2), ("wz", 3), ("h", par, k)], writes=[krv])
                    S.cp("act", rvs[i2][:], prv[:, 0:512], reads=[krv], writes=[("rvs", i2)])
                    S.dma("sp", self.RV[t0 + hf * 128:t0 + (hf + 1) * 128, :], rvs[i2][:], reads=[("rvs", i2)],
                          writes=[("RV", t, hf)])
                    pav, kav = self.nextps()
                    for k in range(8):
                        S.mm(pav[:, 0:128], h[:, k, hf * 128:(hf + 1) * 128], wz[:, k, 3712:3840], k == 0, k == 7,
                             reads=[("wz", 9), ("h", par, k)], writes=[kav])
                    S.cp("dve", avs[i2][:], pav[:, 0:128], reads=[kav], writes=[("avs", i2)])
                    if t == 0:
                        dst = self.AVC[hf * 128:(hf + 1) * 128, :]
                    else:
                        r0 = (t - 1) * TN + hf * 128
                        dst = self.ex_w(EX_AV + r0 * 128, 16384).rearrange("(t e) -> t e", e=128)
                    S.dma("sp", dst, avs[i2][:], reads=[("avs", i2)], writes=[("AV", t, hf)])
            S.flush()

    def phase_m2(self, l):
        nc, S = self.nc, self.S
        LNKS = -0.5 * float(np.log(128.0))
        with (nc.sbuf_tensor(self.un("m2G"), [128, 16, TN], F32) as G,
              nc.sbuf_tensor(self.un("m2rp0"), [128, 4, TN], F32) as rp0,
              nc.sbuf_tensor(self.un("m2rp1"), [128, 4, TN], F32) as rp1,
              nc.sbuf_tensor(self.un("m2z"), [128, 4, TN], F32) as zb,
              nc.sbuf_tensor(self.un("m2zp"), [128, 4, TN], F32) as zpb,
              nc.sbuf_tensor(self.un("m2r1"), [128, 2, TN], F32) as r1b,
              nc.sbuf_tensor(self.un("m2r2"), [128, 2, TN], F32) as r2b,
              nc.sbuf_tensor(self.un("m2qk0"), [128, 4, TN], BF16) as qk0,
              nc.sbuf_tensor(self.un("m2qk1"), [128, 4, TN], BF16) as qk1,
              nc.sbuf_tensor(self.un("m2sq"), [128, TN], BF16) as sq,
              nc.sbuf_tensor(self.un("m2rs"), [128, 2, TN], F32) as rsb,
              nc.sbuf_tensor(self.un("m2qa"), [128, 2, TN], BF16) as qab,
              nc.sbuf_tensor(self.un("m2kf"), [128, TN], F32) as kf32):
            rps, qks = [rp0, rp1], [qk0, qk1]
            for kind in range(2):
                for d in range(2):
                    for h in range(4):
                        gi = (kind * 2 + d) * 4 + h
                        sc = (self.lgam if kind == 0 else self.nlgam)[:, d * 4 + h:d * 4 + h + 1]
                        S.act(G[:, gi, :], self.pos[:, d, :], AF.Exp, scale=sc,
                              bias=(None if kind == 0 else self.lnks[:, 0:1]),
                              reads=["lgam", "nlgam", "pos"], writes=[("G", gi)])
            items = []
            for t in range(NT):
                for h in range(4):
                    for kind in range(2):
                        items.append(("ret", t, h, kind))
                for c in range(5):
                    items.append(("att", t, c, 0))

            def bufs(i):
                b4, b2 = i % 4, i % 2
                return b4, b2, zb[:, b4, :], zpb[:, b4, :], r1b[:, b2, :], r2b[:, b2, :], rsb[:, b2, :]

            def do_loads(i):
                typ, t, a, kind = items[i]
                t0 = t * TN
                b4, b2, z, zp, r1, r2, rs = bufs(i)
                if typ == "ret" and a == 0 and kind == 0:
                    S.dma("sp", rps[t % 2][:], self.rope[:, :, t0:t0 + TN].rearrange("c p n -> p c n"),
                          writes=[("rp", t % 2)])
                if typ == "ret":
                    zc, zpc = (a, 25 + a) if kind == 0 else (4 + a, 29 + a)
                else:
                    zc, zpc = (20 + a, 33 + a) if a < 4 else (24, 37)
                S.dma("sp", z, self.Z[zc, :, t0:t0 + TN], writes=[("z", b4)])
                S.dma("sp", zp, self.Z[zpc, :, t0:t0 + TN], writes=[("zp", b4)])

            def do_compute(i):
                typ, t, a, kind = items[i]
                t0 = t * TN
                rp = rps[t % 2]
                b4, b2, z, zp, r1, r2, rs = bufs(i)
                if typ == "ret":
                    h = a
                    qk = qks[h % 2]
                    S.tt("dve", r1, z, rp[:, 0, :], ALU.mult, reads=[("z", b4), ("rp", t % 2)], writes=[("r1", b2)])
                    S.tt("pool", r2, zp, rp[:, 1, :], ALU.mult, reads=[("zp", b4), ("rp", t % 2)], writes=[("r2", b2)])
                    S.tt("dve", r1, r1, r2, ALU.add, reads=[("r1", b2), ("r2", b2)], writes=[("r1", b2)])
                    gf = (kind * 2 + 0) * 4 + h
                    gb = (kind * 2 + 1) * 4 + h
                    S.tt("pool", qk[:, 2 * kind, :], r1, G[:, gf, :], ALU.mult, reads=[("r1", b2), ("G", gf)],
                         writes=[("qk", h % 2, 2 * kind)])
                    S.tt("dve", qk[:, 2 * kind + 1, :], r1, G[:, gb, :], ALU.mult, reads=[("r1", b2), ("G", gb)],
                         writes=[("qk", h % 2, 2 * kind + 1)])
                    if kind == 1:
                        S.dma("sp", self.QK[h, :, :, t0:t0 + TN].rearrange("s p n -> p s n"), qk[:],
                              reads=[("qk", h % 2, j) for j in range(4)], writes=[("QK", h, t)])
                else:
                    c = a
                    gcol = 144 if c < 4 else 146
                    S.act(sq[:], z, AF.Square, reads=[("z", b4)], writes=["sq"])
                    pn, kn = self.nextps()
                    S.mm(pn[:, 0:TN], self.bones[:], sq[:], True, True, reads=["sq"], writes=[kn])
                    S.act(rs, pn[:, 0:TN], AF.Sqrt, scale=1.0 / 64, bias=self.epsb[:, 0:1], reads=[kn], writes=[("rs", b2)])
                    S.recip(rs, rs, reads=[("rs", b2)], writes=[("rs", b2)])
                    S.stt(r1, z, self.pvl(l, gcol), rs, ALU.mult, ALU.mult, reads=[("z", b4), ("rs", b2)], writes=[("r1", b2)])
                    S.stt(r2, zp, self.pvl(l, gcol + 1), rs, ALU.mult, ALU.mult, reads=[("zp", b4), ("rs", b2)],
                          writes=[("r2", b2)])
                    S.tt("pool", r1, r1, rp[:, 2, :], ALU.mult, reads=[("r1", b2), ("rp", t % 2)], writes=[("r1", b2)])
                    S.tt("pool", r2, r2, rp[:, 3, :], ALU.mult, reads=[("r2", b2), ("rp", t % 2)], writes=[("r2", b2)])
                    if c < 4:
                        S.tt("dve", qab[:, c % 2, :], r1, r2, ALU.add, reads=[("r1", b2), ("r2", b2)], writes=[("qa", c % 2)])
                        S.dma("sp", self.QA[c, :, t0:t0 + TN], qab[:, c % 2, :], reads=[("qa", c % 2)], writes=[("QA", c, t)])
                    else:
                        S.tt("dve", kf32[:], r1, r2, ALU.add, reads=[("r1", b2), ("r2", b2)], writes=["kf32"])
                        if t == 0:
                            dst = self.KAC[:, :]
                        else:
                            dst = self.EXP[(t - 1) // 2].ap()[:, ((t - 1) % 2) * TN:((t - 1) % 2 + 1) * TN]
                        S.dma("sp", dst, kf32[:], reads=["kf32"], writes=[("KA", t)])

            do_loads(0)
            do_loads(1)
            for i in range(len(items)):
                if i + 2 < len(items):
                    do_loads(i + 2)
                do_compute(i)
            EXLX = self.ex_w(EX_LX, 2048).rearrange("(c p n) -> c p n", p=128, n=4)
            for c in range(4):
                S.dma("sp", EXLX[c, :, 0:2], self.Z[12 + c, :, CTX:CTX + 2], writes=[("LXH", c, 0)], slow=True)
                S.dma("sp", EXLX[c, :, 2:4], self.Z[12 + c, :, T - 2:T], writes=[("LXH", c, 1)], slow=True)
            S.flush()

    def phase_m2b(self, l):
        nc, S = self.nc, self.S
        with (nc.sbuf_tensor(self.un("m2bS"), [128, 8, 128], F32) as S32,
              nc.sbuf_tensor(self.un("m2bv"), [128, 4, 512], BF16) as vb,
              nc.sbuf_tensor(self.un("m2bk"), [128, 4, 4, 128], BF16) as kb,
              nc.sbuf_tensor(self.un("m2bkt"), [128, 2, 4, 128], BF16) as ktb):
            cnt = 0
            kcnt = 0
            for step in range(16):
                for d in range(2):
                    ci = step if d == 0 else 15 - step
                    t0 = CTX + 128 * ci
                    b4 = cnt % 4
                    cnt += 1
                    S.dma("sp", vb[:, b4, :], self.RV[t0:t0 + 128, :], writes=[("vb", b4)])
                    S.dma("sp", kb[:, b4, :, :], self.QK[:, 2 + d, :, t0:t0 + 128].rearrange("h p n -> p h n"),
                          writes=[("kb", b4)])
                    bk = kcnt % 2
                    kcnt += 1
                    for h in range(4):
                        S.tr(self.psb[bk][:, h * 128:(h + 1) * 128], kb[:, b4, h, :], self.ident[:],
                             reads=[("kb", b4), "ident"], writes=[("psb", bk)])
                    S.cp("act" if bk else "dve", ktb[:, bk, :, :],
                         self.psb[bk][:, 0:512].rearrange("p (h n) -> p h n", n=128),
                         reads=[("psb", bk)], writes=[("ktb", bk)])
                    for h in range(4):
                        kv, kk = self.nextps()
                        S.mm(kv[:, 0:128], ktb[:, bk, h, :], vb[:, b4, h * 128:(h + 1) * 128], True, True,
                             reads=[("ktb", bk), ("vb", b4)], writes=[kk])
                        si = d * 4 + h
                        gc = self.gC[:, si:si + 1]
                        if step == 0:
                            S.ts("dve", S32[:, si, :], kv[:, 0:128], gc, None, ALU.mult, ALU.bypass,
                                 reads=[kk, "gC"], writes=[("S32", si)])
                        else:
                            S.act(S32[:, si, :], S32[:, si, :], AF.Identity, scale=gc,
                                  reads=[("S32", si), "gC"], writes=[("S32", si)])
                            S.stt(S32[:, si, :], kv[:, 0:128], gc, S32[:, si, :], ALU.mult, ALU.add,
                                  reads=[kk, ("S32", si), "gC"], writes=[("S32", si)])
            for si in range(8):
                S.dma("sp", self.ex_w(EX_SL + si * 16384, 16384).rearrange("(p n) -> p n", n=128), S32[:, si, :],
                      reads=[("S32", si)], writes=[("EXSL", si)])
            S.flush()

    @staticmethod
    def rev(t, a, b):
        return t[:, slice(b - 1, a - 1 if a > 0 else None, -1)]

    def phase_lruA(self, l):
        nc, S = self.nc, self.S
        hb_ = DEPTH * PL + 24
        half, omh = self.pv[:, hb_:hb_ + 1], self.pv[:, hb_ + 1:hb_ + 2]
        big = lambda n: nc.sbuf_tensor(self.un(n), [128, T], F32)
        from contextlib import ExitStack
        es = ExitStack()
        rg1, ig1, a1_, m1_, u1_ = [es.enter_context(big(n)) for n in ("larg1", "laig1", "laa1", "lam1", "lau1")]
        with (nc.sbuf_tensor(self.un("laxpc"), [128, CTX + 3], F32) as xpc,
              nc.sbuf_tensor(self.un("laxpl"), [128, LAT + 3], F32) as xpl,
              big("laxcv") as xcv, big("larg") as rg0, big("laig") as ig0, big("laa") as a0_, big("lam") as m0_,
              big("lau") as u0_, big("lahf") as hf, big("lahb") as hb,
              nc.sbuf_tensor(self.un("laacf"), [128, LAT], F32) as acf,
              nc.sbuf_tensor(self.un("laacb"), [128, LAT], F32) as acb,
              nc.sbuf_tensor(self.un("lazero"), [128, LAT], F32) as zeros,
              nc.sbuf_tensor(self.un("laxb"), [128, T], BF16) as xb,
              nc.sbuf_tensor(self.un("lalw"), [128, 4, 128], BF16) as lw,
              nc.sbuf_tensor(self.un("lahl"), [128, 2, 4], F32) as hl,
              nc.sbuf_tensor(self.un("lah0"), [128, 2], F32) as h0,
              nc.sbuf_tensor(self.un("last"), [128, 2], F32) as stt_):
            S.memset("pool", zeros[:], 0.0, writes=["zeros"])
            slices = [(i * 512, min(512, T - i * 512)) for i in range(5)]
            for c in range(4):
                S.memset("dve", xpc[:, 0:1], 0.0, writes=["xpc"])
                S.memset("dve", xpc[:, CTX + 1:CTX + 3], 0.0, writes=["xpc"])
                S.dma("sp", xpc[:, 1:CTX + 1], self.Z[12 + c, :, 0:CTX], writes=["xpc"])
                S.dma("sp", xpl[:, 1:LAT + 1], self.Z[12 + c, :, CTX:T], writes=["xpl"])
                for r in range(2):
                    S.dma("sp", hl[:, r, :], self.ex_r(r, EX_LX + c * 512, 512).rearrange("(p n) -> p n", n=4),
                          writes=["hl"])
                S.ts("dve", xpl[:, 0:1], hl[:, 0, 3:4], half, None, ALU.mult, ALU.bypass, reads=["hl"], writes=["xpl"])
                S.ts("dve", xpl[:, LAT + 1:LAT + 3], hl[:, 1, 0:2], omh, None, ALU.mult, ALU.bypass, reads=["hl"],
                     writes=["xpl"])
                w = [self.pvl(l, 100 + j * 4 + c) for j in range(4)]
                bcv = self.pvl(l, 116 + c)
                for (src, sk, n, d0) in ((xpc, "xpc", CTX, 0), (xpl, "xpl", LAT, CTX)):
                    dst = xcv[:, d0:d0 + n]
                    S.ts("dve", dst, src[:, 0:n], w[0], bcv, ALU.mult, ALU.add, reads=[sk], writes=["xcv"])
                    for j in range(1, 4):
                        S.stt(dst, src[:, j:j + n], w[j], dst, ALU.mult, ALU.add, reads=[sk, "xcv"], writes=["xcv"])
                S.cp("act", xb[:], xcv[:], reads=["xcv"], writes=["xb"])
                for a in range(2):
                    for d in range(2):
                        S.dma("pool", lw[:, a * 2 + d, :], self.lbd[l, a, d, c], writes=[("lw", a * 2 + d)])
                rgs, igs, as_, ms_, us_ = [rg0, rg1], [ig0, ig1], [a0_, a1_], [m0_, m1_], [u0_, u1_]
                for d in range(2):
                    rg, ig, a_, m_, u_ = rgs[d], igs[d], as_[d], ms_[d], us_[d]
                    for (s0, n) in slices:
                        p1, k1 = self.nextps()
                        S.mm(p1[:, 0:n], lw[:, d, :], xb[:, s0:s0 + n], True, True, reads=[("lw", d), "xb"], writes=[k1])
                        S.act(rg[:, s0:s0 + n], p1[:, 0:n], AF.Sigmoid, bias=self.pvl(l, 120 + d * 4 + c), reads=[k1],
                              writes=[("rg", d)])
                        p2, k2 = self.nextps()
                        S.mm(p2[:, 0:n], lw[:, 2 + d, :], xb[:, s0:s0 + n], True, True, reads=[("lw", 2 + d), "xb"],
                             writes=[k2])
                        S.act(ig[:, s0:s0 + n], p2[:, 0:n], AF.Sigmoid, bias=self.pvl(l, 128 + d * 4 + c), reads=[k2],
                              writes=[("ig", d)])
                    ci = d * 4 + c
                    S.act(a_[:], rg[:], AF.Exp, scale=self.lcp[:, ci:ci + 1], reads=[("rg", d), "lcp"], writes=[("a", d)])
                    S.act(m_[:], rg[:], AF.Exp, scale=self.lcp2[:, ci:ci + 1], reads=[("rg", d), "lcp2"], writes=[("m", d)])
                    S.act(m_[:], m_[:], AF.Sqrt, scale=-1.0, bias=1.0, reads=[("m", d)], writes=[("m", d)])
                for d in range(2):
                    rg, ig, a_, m_, u_ = rgs[d], igs[d], as_[d], ms_[d], us_[d]
                    ak, uk = ("a", d), ("u", d)
                    S.tt("dve", u_[:], ig[:], xcv[:], ALU.mult, reads=[("ig", d), "xcv"], writes=[uk])
                    S.tt("pool", u_[:], u_[:], m_[:], ALU.mult, reads=[uk, ("m", d)], writes=[uk])
                    if d == 0:
                        S.scan(hf[:, 0:CTX], a_[:, 0:CTX], u_[:, 0:CTX], 0.0, reads=[ak, uk], writes=["hf"])
                        S.ts("dve", h0[:, 0:1], hf[:, CTX - 1:CTX], omh, None, ALU.mult, ALU.bypass, reads=["hf"],
                             writes=["h0f"])
                        S.scan(hf[:, CTX:T], a_[:, CTX:T], u_[:, CTX:T], h0[:, 0:1], reads=[ak, uk, "h0f"], writes=["hf"])
                        S.scan(acf[:], a_[:, CTX:T], zeros[:], 1.0, reads=[ak, "zeros"], writes=["acf"])
                        S.cp("dve", stt_[:, 0:1], hf[:, T - 1:T], reads=["hf"], writes=["st0"])
                    else:
                        rv = self.rev
                        S.scan(rv(hb, 0, CTX), rv(a_, 0, CTX), rv(u_, 0, CTX), 0.0, reads=[ak, uk], writes=["hb"])
                        S.ts("dve", h0[:, 1:2], hb[:, 0:1], half, None, ALU.mult, ALU.bypass, reads=["hb"], writes=["h0b"])
                        S.scan(rv(hb, CTX, T), rv(a_, CTX, T), rv(u_, CTX, T), h0[:, 1:2], reads=[ak, uk, "h0b"],
                               writes=["hb"])
                        S.scan(rv(acb, 0, LAT), rv(a_, CTX, T), zeros[:], 1.0, reads=[ak, "zeros"], writes=["acb"])
                        S.cp("dve", stt_[:, 1:2], hb[:, CTX:CTX + 1], reads=["hb"], writes=["st1"])
                S.tt("dve", hf[:], hf[:], hb[:], ALU.add, reads=["hf", "hb"], writes=["hf"])
                S.dma("sp", self.HS[c], hf[:], reads=["hf"], writes=[("HS", c)])
                S.dma("sp", self.ACF[c], acf[:], reads=["acf"], writes=[("ACF", c)])
                S.dma("sp", self.ACB[c], acb[:], reads=["acb"], writes=[("ACB", c)])
                for d in range(2):
                    lo = d * 512 + c * 128
                    S.dma("sp", self.EX2[lo:lo + 128].rearrange("(p o) -> p o", o=1), stt_[:, d:d + 1],
                          reads=["st%d" % d], writes=[("EX2", d, c)])
            S.flush()
        es.close()

    def phase_lruB(self, l):
        nc, S = self.nc, self.S
        hb_ = DEPTH * PL + 24
        half, omh = self.pv[:, hb_:hb_ + 1], self.pv[:, hb_ + 1:hb_ + 2]
        big = lambda n: nc.sbuf_tensor(self.un(n), [128, T], F32)
        with (big("lbhs") as hs, big("lblz") as lz, big("lbsq") as sq, big("lbin") as inn,
              nc.sbuf_tensor(self.un("lbacf"), [128, LAT], F32) as acf,
              nc.sbuf_tensor(self.un("lbacb"), [128, LAT], F32) as acb,
              nc.sbuf_tensor(self.un("lby"), [128, T], BF16) as y,
              nc.sbuf_tensor(self.un("lbdd"), [128, 4], F32) as dd):
            for c in range(4):
                S.dma("sp", hs[:], self.HS[c], writes=["hs"])
                S.dma("sp", acf[:], self.ACF[c], writes=["acf"])
                S.dma("sp", acb[:], self.ACB[c], writes=["acb"])
                S.dma("sp", lz[:], self.Z[16 + c], writes=["lz"])
                lo0 = 0 * EX2_N + 0 * 512 + c * 128
                lo1 = 1 * EX2_N + 1 * 512 + c * 128
                S.dma("sp", dd[:, 0:1], self.EX2G[lo0:lo0 + 128].rearrange("(p o) -> p o", o=1), writes=["dd0"])
                S.dma("sp", dd[:, 1:2], self.EX2G[lo1:lo1 + 128].rearrange("(p o) -> p o", o=1), writes=["dd1"])
                S.ts("dve", dd[:, 2:3], dd[:, 0:1], half, None, ALU.mult, ALU.bypass, reads=["dd0"], writes=["dd2"])
                S.ts("dve", dd[:, 3:4], dd[:, 1:2], omh, None, ALU.mult, ALU.bypass, reads=["dd1"], writes=["dd3"])
                S.stt(hs[:, CTX:T], acf[:], dd[:, 2:3], hs[:, CTX:T], ALU.mult, ALU.add, reads=["acf", "dd2", "hs"],
                      writes=["hs"])
                S.stt(hs[:, CTX:T], acb[:], dd[:, 3:4], hs[:, CTX:T], ALU.mult, ALU.add, reads=["acb", "dd3", "hs"],
                      writes=["hs"])
                S.act(sq[:], lz[:], AF.Square, reads=["lz"], writes=["sq"])
                S.ts("pool", sq[:], sq[:], 0.044715, 1.0, ALU.mult, ALU.add, reads=["sq"], writes=["sq"])
                S.tt("dve", inn[:], sq[:], lz[:], ALU.mult, reads=["sq", "lz"], writes=["inn"])
                S.act(inn[:], inn[:], AF.Sigmoid, scale=1.5957691216057308, reads=["inn"], writes=["inn"])
                S.tt("pool", inn[:], inn[:], lz[:], ALU.mult, reads=["inn", "lz"], writes=["inn"])
                S.tt("dve", y[:], inn[:], hs[:], ALU.mult, reads=["inn", "hs"], writes=["y"])
                S.dma("sp", self.YL[c], y[:], reads=["y"], writes=[("YL", c)])
            S.flush()

    def phase_ret(self, l):
        nc, S = self.nc, self.S
        hb_ = DEPTH * PL + 24
        half, omh = self.pv[:, hb_:hb_ + 1], self.pv[:, hb_ + 1:hb_ + 2]
        big = lambda n: nc.sbuf_tensor(self.un(n), [128, T], F32)
        with (nc.sbuf_tensor(self.un("rtqk"), [128, 4, T], BF16) as qk,
              nc.sbuf_tensor(self.un("rtv"), [128, 18, 128], BF16) as vt,
              nc.sbuf_tensor(self.un("rtkt"), [128, 2, 18, 128], BF16) as kt,
              big("rtacc") as acc, big("rtyc") as yc, big("rtsq") as sq32, big("rtrs") as rs, big("rtrg") as rgz,
              nc.sbuf_tensor(self.un("rtS"), [128, 2, 128], F32) as S32,
              nc.sbuf_tensor(self.un("rtSo"), [128, 2, 128], F32) as So,
              nc.sbuf_tensor(self.un("rtSb"), [128, 2, 2, 128], BF16) as Sb,
              nc.sbuf_tensor(self.un("rtpm"), [128, 4, 128], BF16) as pm,
              nc.sbuf_tensor(self.un("rty"), [128, T], BF16) as y):
            slices = [(i * 512, min(512, T - i * 512)) for i in range(5)]
            order = [[0, 1] + list(range(2, 18)), [1, 0] + list(range(17, 1, -1))]
            masks = [self.maskf, self.maskb]
            pcnt = 0
            for h in range(4):
                S.dma("sp", qk[:], self.QK[h].rearrange("s p n -> p s n"), writes=["qk"])
                S.dma("sp", vt[:], self.RV[:, h * 128:(h + 1) * 128].rearrange("(c p) e -> p c e", p=128), writes=["vt"])
                S.dma("sp", rgz[:], self.Z[8 + h], writes=["rgz"])
                for d in range(2):
                    r = d
                    S.dma("sp", So[:, d, :], self.ex_r(r, EX_SL + (d * 4 + h) * 16384, 16384).rearrange(
                        "(p n) -> p n", n=128), writes=[("So", d)])
                    S.memset("dve", S32[:, d, :], 0.0, writes=[("S32", d)])
                    S.memset("pool", Sb[:, d, 0, :], 0.0, writes=[("Sb", d, 0)])
                tcnt = 0
                for d in range(2):
                    for c0 in range(0, 18, 4):
                        ng = min(4, 18 - c0)
                        bk = tcnt % 2
                        tcnt += 1
                        for j in range(ng):
                            ci = c0 + j
                            S.tr(self.psb[bk][:, j * 128:(j + 1) * 128], qk[:, 2 + d, ci * 128:(ci + 1) * 128], self.ident[:],
                                 reads=["qk", "ident"], writes=[("psb", bk)])
                        S.cp("act" if bk else "dve", kt[:, d, c0:c0 + ng, :],
                             self.psb[bk][:, 0:ng * 128].rearrange("p (h n) -> p h n", n=128),
                             reads=[("psb", bk)], writes=[("kt", d, c0 + j) for j in range(ng)])
                written = set()
                sbi = [0, 0]
                for step in range(18):
                    for d in range(2):
                        ci = order[d][step]
                        cs = slice(ci * 128, (ci + 1) * 128)
                        si = d * 4 + h
                        if step == 2:
                            S.ts("dve", S32[:, d, :], S32[:, d, :], self.bc1[:, si:si + 1], None, ALU.mult, ALU.bypass,
                                 reads=[("S32", d), "bc1"], writes=[("S32", d)])
                            S.stt(S32[:, d, :], So[:, d, :], (half if d == 0 else omh), S32[:, d, :], ALU.mult, ALU.add,
                                  reads=[("So", d), ("S32", d)], writes=[("S32", d)])
                            nb = 1 - sbi[d]
                            S.cp("act", Sb[:, d, nb, :], S32[:, d, :], reads=[("S32", d)], writes=[("Sb", d, nb)])
                            sbi[d] = nb
                        sc, ksc = self.nextps()
                        S.mm(sc[:, 0:128], qk[:, 2 + d, cs], qk[:, d, cs], True, True, reads=["qk"], writes=[ksc])
                        p4 = pcnt % 4
                        pcnt += 1
                        S.tt("dve", pm[:, p4, :], sc[:, 0:128], masks[d][:], ALU.mult, reads=[ksc, "maskf", "maskb"],
                             writes=[("pm", p4)])
                        o, ko = self.nextps()
                        S.mm(o[:, 0:128], vt[:, ci, :], pm[:, p4, :], True, False, reads=["vt", ("pm", p4)], writes=[ko])
                        S.mm(o[:, 0:128], Sb[:, d, sbi[d], :], qk[:, d, cs], False, True, reads=[("Sb", d, sbi[d]), "qk"],
                             writes=[ko])
                        if ci not in written:
                            S.cp("act", acc[:, cs], o[:, 0:128], reads=[ko], writes=[("acc", ci)])
                            written.add(ci)
                        else:
                            S.tt("dve", acc[:, cs], o[:, 0:128], acc[:, cs], ALU.add, reads=[ko, ("acc", ci)],
                                 writes=[("acc", ci)])
                        kv, kkv = self.nextps()
                        S.mm(kv[:, 0:128], kt[:, d, ci, :], vt[:, ci, :], True, True, reads=[("kt", d, ci), "vt"],
                             writes=[kkv])
                        gc = self.gC[:, si:si + 1]
                        S.act(S32[:, d, :], S32[:, d, :], AF.Identity, scale=gc,
                              reads=[("S32", d), "gC"], writes=[("S32", d)])
                        S.stt(S32[:, d, :], kv[:, 0:128], gc, S32[:, d, :], ALU.mult, ALU.add,
                              reads=[kkv, ("S32", d), "gC"], writes=[("S32", d)])
                        nb = 1 - sbi[d]
                        S.cp("act", Sb[:, d, nb, :], S32[:, d, :], reads=[("S32", d)], writes=[("Sb", d, nb)])
                        sbi[d] = nb
                acck = [("acc", ci) for ci in range(18)]
                S.act(rgz[:], rgz[:], AF.Silu, reads=["rgz"], writes=["rgz"])
                pms, pvs = {}, {}
                for (s0, n) in slices:
                    pms[s0] = self.nextps()
                    S.mm(pms[s0][0][:, 0:n], self.ones32[:], acc[:, s0:s0 + n], True, True, reads=acck + ["ones32"],
                         writes=[pms[s0][1]])
                    S.stt(yc[:, s0:s0 + n], pms[s0][0][:, 0:n], -1.0 / 128, acc[:, s0:s0 + n], ALU.mult, ALU.add,
                          reads=[pms[s0][1]] + acck, writes=[("yc", s0)])
                for (s0, n) in slices:
                    S.act(sq32[:, s0:s0 + n], yc[:, s0:s0 + n], AF.Square, reads=[("yc", s0)], writes=[("sq32", s0)])
                for (s0, n) in slices:
                    pvs[s0] = self.nextps()
                    S.mm(pvs[s0][0][:, 0:n], self.ones32[:], sq32[:, s0:s0 + n], True, True, reads=[("sq32", s0)],
                         writes=[pvs[s0][1]])
                    S.act(rs[:, s0:s0 + n], pvs[s0][0][:, 0:n], AF.Sqrt, scale=1.0 / 128, bias=self.epsb[:, 0:1],
                          reads=[pvs[s0][1]], writes=[("rs", s0)])
                for (s0, n) in slices:
                    S.recip(rs[:, s0:s0 + n], rs[:, s0:s0 + n], reads=[("rs", s0)], writes=[("rs", s0)])
                for (s0, n) in slices:
                    S.tt("pool", yc[:, s0:s0 + n], yc[:, s0:s0 + n], rs[:, s0:s0 + n], ALU.mult,
                         reads=[("yc", s0), ("rs", s0)], writes=[("yc", s0)])
                for (s0, n) in slices:
                    S.stt(y[:, s0:s0 + n], yc[:, s0:s0 + n], self.pvl(l, 96 + h), rgz[:, s0:s0 + n], ALU.mult, ALU.mult,
                          reads=[("yc", s0), "rgz"], writes=[("y", s0)])
                S.dma("sp", self.YR[h], y[:], reads=[("y", s0) for (s0, n) in slices], writes=[("YR", h)])
            S.flush()

    def phase_attn(self, l):
        nc, S = self.nc, self.S
        NK = CTX + 2 * LAT
        with (nc.sbuf_tensor(self.un("atk"), [128, 2, NK], BF16) as kT,
              nc.sbuf_tensor(self.un("atv"), [128, 2, 2, 34, 128], BF16) as vv,
              nc.sbuf_tensor(self.un("atq"), [128, 2, 2, T], BF16) as qa,
              nc.sbuf_tensor(self.un("aty"), [128, 2, T], BF16) as ya,
              nc.sbuf_tensor(self.un("atp"), [128, 4, 512], BF16) as pt,
              nc.sbuf_tensor(self.un("ato"), [128, 2, 128], BF16) as onz,
              nc.sbuf_tensor(self.un("atr"), [128, 2, 512], F32) as rden):
            S.memset("pool", qa[:], 0.0, writes=[("qa", 0), ("qa", 1)])
            S.memset("pool", vv[:], 0.0, writes=[("vv", 0), ("vv", 1)])
            S.memset("dve", onz[:], 0.0, writes=["onz"])
            S.memset("dve", onz[:, 0, 0:64], 1.0, writes=["onz"])
            S.memset("dve", onz[:, 1, 64:128], 1.0, writes=["onz"])
            for g in range(2):
                for hh in range(2):
                    ps_ = slice(hh * 64, (hh + 1) * 64)
                    S.dma("pool", kT[ps_, g, 0:CTX], self.KAC[g * 64:(g + 1) * 64, :], writes=[("kT", g)])
                    for r in range(2):
                        for j in range(4):
                            src = self.EXGP[j].ap()[r * 128 + g * 64:r * 128 + (g + 1) * 64, :]
                            c0 = CTX + r * LAT + j * 512
                            S.dma("pool", kT[ps_, g, c0:c0 + 512], src, writes=[("kT", g)])
                    cs_ = slice(hh * 64, (hh + 1) * 64)
                    S.dma("pool", vv[:, g, hh, 0:2, cs_],
                          self.AVC.rearrange("(c p) e -> p c e", p=128)[:, :, g * 64:(g + 1) * 64], writes=[("vv", g)])
                    for r in range(2):
                        for j in range(4):
                            src = self.ex_r(r, EX_AV + j * 65536, 65536).rearrange("(c p e) -> p c e", p=128, e=128)
                            c0 = 2 + r * 16 + j * 4
                            S.dma("pool", vv[:, g, hh, c0:c0 + 4, cs_], src[:, :, g * 64:(g + 1) * 64],
                                  writes=[("vv", g)])
            ones64 = self.ones[:, 0:64]
            qtiles = [(0, CTX, [0, 1])] + [(CTX + 512 * i, 512, list(range(34))) for i in range(4)]
            its = []
            qcnt = 0
            for c in range(4):
                for qi, (q0, nq, keys) in enumerate(qtiles):
                    pq = qcnt % 2
                    qcnt += 1
                    for kc in keys:
                        for hh in range(2):
                            its.append(dict(c=c, q0=q0, nq=nq, kc=kc, hh=hh, pq=pq, first=(kc == keys[0]),
                                            last=(kc == keys[-1]), qend=(kc == keys[-1] and hh == 1),
                                            cend=(kc == keys[-1] and hh == 1 and qi == len(qtiles) - 1),
                                            cstart=(kc == keys[0] and hh == 0 and qi == 0)))

            def emit_qk(i):
                it = its[i]
                c, g, hh, nq, q0, kc = it["c"], it["c"] // 2, it["hh"], it["nq"], it["q0"], it["kc"]
                if it["cstart"]:
                    for h2 in range(2):
                        S.dma("sp", qa[h2 * 64:(h2 + 1) * 64, c % 2, h2, :], self.QA[c, h2 * 64:(h2 + 1) * 64, :],
                              writes=[("qa", c % 2)])
                si, p4 = i % 2, i % 4
                sp_, ks = self.ps[si], ("ps", si)
                S.mm(sp_[:, 0:nq], kT[:, g, kc * 128:(kc + 1) * 128], qa[:, c % 2, hh, q0:q0 + nq], True, True,
                     reads=[("kT", g), ("qa", c % 2)], writes=[ks])
                S.act(pt[:, p4, 0:nq], sp_[:, 0:nq], AF.Exp, scale=0.125, reads=[ks], writes=[("pt", p4)])

            def emit_pv(i):
                it = its[i]
                c, g, hh, nq, q0, kc, pq = it["c"], it["c"] // 2, it["hh"], it["nq"], it["q0"], it["kc"], it["pq"]
                ps_ = slice(hh * 64, (hh + 1) * 64)
                p4 = i % 4
                num, knum = self.ps[2 + 2 * pq], ("ps", 2 + 2 * pq)
                den, kden = self.ps[3 + 2 * pq], ("ps", 3 + 2 * pq)
                st_ = it["first"] and hh == 0
                sp2 = it["last"] and hh == 1
                S.mm(num[:, 0:nq], vv[:, g, hh, kc, :], pt[:, p4, 0:nq], st_, sp2,
                     reads=[("vv", g), ("pt", p4)], writes=[knum])
                S.mm(den[:, 0:nq], onz[:, hh, :], pt[:, p4, 0:nq], st_, sp2,
                     reads=["onz", ("pt", p4)], writes=[kden])
                if it["qend"]:
                    S.recip(rden[:, pq, 0:nq], den[:, 0:nq], reads=[kden], writes=[("rden", pq)])
                    S.tt("dve", ya[:, c % 2, q0:q0 + nq], num[:, 0:nq], rden[:, pq, 0:nq], ALU.mult,
                         reads=[knum, ("rden", pq)], writes=[("ya", c % 2)])
                if it["cend"]:
                    S.dma("sp", self.YA[c], ya[:, c % 2, :], reads=[("ya", c % 2)], writes=[("YA", c)])

            emit_qk(0)
            for i in range(len(its)):
                if i + 1 < len(its):
                    emit_qk(i + 1)
                emit_pv(i)
            S.flush()

    def phase_merge(self, l):
        nc, S = self.nc, self.S
        with (nc.sbuf_tensor(self.un("mgwg"), [128, 8, 3072], BF16) as wg,
              nc.sbuf_tensor(self.un("mgwb"), [128, 12, D], BF16) as wb,
              nc.sbuf_tensor(self.un("mgwo"), [128, 8, D], BF16) as wo,
              nc.sbuf_tensor(self.un("mgx0"), [128, 8, TN], F32) as xt0,
              nc.sbuf_tensor(self.un("mgx1"), [128, 8, TN], F32) as xt1,
              nc.sbuf_tensor(self.un("mgh0"), [128, 8, TN], BF16) as h0,
              nc.sbuf_tensor(self.un("mgh1"), [128, 8, TN], BF16) as h1,
              nc.sbuf_tensor(self.un("mgy0"), [128, 12, TN], BF16) as y0,
              nc.sbuf_tensor(self.un("mgy1"), [128, 12, TN], BF16) as y1,
              nc.sbuf_tensor(self.un("mgsg"), [128, 3, TN], F32) as sg,
              nc.sbuf_tensor(self.un("mgtm"), [128, 2, TN], F32) as tm,
              nc.sbuf_tensor(self.un("mgma"), [128, 2, TN], F32) as ma,
              nc.sbuf_tensor(self.un("mgm"), [128, 8, TN], BF16) as m):
            xts, hs, ys = [xt0, xt1], [h0, h1], [y0, y1]
            src = self.w_in[l].rearrange("(k p) c -> p k c", p=128)
            for pc in range(8):
                S.dma("pool", wg[:, :, pc * 384:(pc + 1) * 384], src[:, :, 3840 + pc * 384:3840 + (pc + 1) * 384],
                      writes=[("wg", pc)])
            for n in range(3):
                S.dma("pool", wb[:, n * 4:(n + 1) * 4, :], self.w_branch[l, n].rearrange("(k p) c -> p k c", p=128),
                      writes=[("wb", n)])
            srco = self.w_out[l].rearrange("(k p) c -> p k c", p=128)
            for kk in range(2):
                S.dma("pool", wo[:, kk * 4:(kk + 1) * 4, :], srco[:, kk * 4:(kk + 1) * 4, :], writes=[("wo", kk)])
            ysrc = [self.YR, self.YL, self.YA]

            def loads(t):
                par = t % 2
                t0 = t * TN
                self.load_x(xts[par], t, par)
                S.dma("sp", hs[par][:], self.H[:, :, t0:t0 + TN].rearrange("k p n -> p k n"), writes=[("h", par)])
                for n in range(3):
                    S.dma("sp", ys[par][:, n * 4:(n + 1) * 4, :], ysrc[n][:, :, t0:t0 + TN].rearrange("k p n -> p k n"),
                          writes=[("y3", par, n)])

            loads(0)
            scnt = 0
            for t in range(NT):
                par = t % 2
                xt, h, y3 = xts[par], hs[par], ys[par]
                c = 1 if t == 0 else 0
                if t + 1 < NT:
                    loads(t + 1)
                for i in range(8):
                    mi = i % 2
                    for n in range(3):
                        pg, kg = self.nextps()
                        wc = n * 8 + i
                        for k in range(8):
                            S.mm(pg[:, 0:TN], wg[:, k, wc * 128:(wc + 1) * 128], h[:, k, :], k == 0, k == 7,
                                 reads=[("wg", wc // 3), ("h", par)], writes=[kg])
                        pu, ku = self.nextps()
                        for kk in range(4):
                            S.mm(pu[:, 0:TN], wb[:, n * 4 + kk, i * 128:(i + 1) * 128], y3[:, n * 4 + kk, :], kk == 0, kk == 3,
                                 reads=[("wb", n), ("y3", par, n)], writes=[ku])
                        s3 = scnt % 3
                        scnt += 1
                        S.act(sg[:, s3, :], pg[:, 0:TN], AF.Sigmoid, reads=[kg], writes=[("sg", s3)])
                        if n == 0:
                            S.tt("dve", ma[:, mi, :], sg[:, s3, :], pu[:, 0:TN], ALU.mult, reads=[("sg", s3), ku],
                                 writes=[("ma", mi)])
                        else:
                            S.tt("dve", tm[:, n - 1, :], sg[:, s3, :], pu[:, 0:TN], ALU.mult, reads=[("sg", s3), ku],
                                 writes=[("tm", n - 1)])
                            if n == 1:
                                S.tt("pool", ma[:, mi, :], ma[:, mi, :], tm[:, 0, :], ALU.add,
                                     reads=[("ma", mi), ("tm", 0)], writes=[("ma", mi)])
                            else:
                                S.tt("pool", m[:, i, :], ma[:, mi, :], tm[:, 1, :], ALU.add,
                                     reads=[("ma", mi), ("tm", 1)], writes=[("m", i)])
                for i in range(8):
                    po, ko = self.nextps()
                    for k in range(8):
                        S.mm(po[:, 0:TN], wo[:, k, i * 128:(i + 1) * 128], m[:, k, :], k == 0, k == 7,
                             reads=[("wo", k // 4), ("m", k)], writes=[ko])
                    S.stt(xt[:, i, :], po[:, 0:TN], self.modG[:, 1, i, c:c + 1], xt[:, i, :], ALU.mult, ALU.add,
                          reads=[ko, ("xt", par, i), ("modG", 1)], writes=[("xt", par, i)])
                self.store_x(xt, t, par)
            S.flush()


def _perm128():
    return np.concatenate([np.arange(32, 64), np.arange(0, 32), np.arange(96, 128), np.arange(64, 96)])


def _perm64():
    return np.concatenate([np.arange(16, 32), np.arange(0, 16), np.arange(48, 64), np.arange(32, 48)])


def _fm(v):
    v = np.asarray(v, np.float32)
    lead = v.shape[:-1]
    n = v.shape[-1] // 128
    v = v.reshape(*lead, n, 128)
    return np.moveaxis(v, -1, 0)


def _rope_tables(half):
    theta = 10000.0
    tl = np.arange(LAT) + half * LAT
    rows = (tl // 64).astype(np.float32)
    cols = (tl % 64).astype(np.float32)
    tabs = np.zeros((4, 128, T), np.float32)
    tabs[0, :, :CTX] = 1.0
    tabs[2, :, :CTX] = 1.0
    f = (theta ** (-np.arange(0, 64, 2, dtype=np.float32) / 64)).astype(np.float32)
    ar = (rows[None, :] * f[:, None]).astype(np.float32)
    ac = (cols[None, :] * f[:, None]).astype(np.float32)
    C = np.concatenate([np.cos(ar), np.cos(ar), np.cos(ac), np.cos(ac)], 0)
    Sn = np.concatenate([-np.sin(ar), np.sin(ar), -np.sin(ac), np.sin(ac)], 0)
    tabs[0, :, CTX:] = C
    tabs[1, :, CTX:] = Sn
    f = (theta ** (-np.arange(0, 32, 2, dtype=np.float32) / 32)).astype(np.float32)
    ar = (rows[None, :] * f[:, None]).astype(np.float32)
    ac = (cols[None, :] * f[:, None]).astype(np.float32)
    C = np.concatenate([np.cos(ar), np.cos(ar), np.cos(ac), np.cos(ac)], 0)
    Sn = np.concatenate([-np.sin(ar), np.sin(ar), -np.sin(ac), np.sin(ac)], 0)
    tabs[2, :, CTX:] = np.concatenate([C, C], 0)
    tabs[3, :, CTX:] = np.concatenate([Sn, Sn], 0)
    return tabs


def _consts():
    cst = np.zeros((6, 128, 128), np.float32)
    cst[0] = np.eye(128)
    cst[1] = 1.0
    cst[2, :64, :64] = 1.0
    cst[2, 64:, 64:] = 1.0
    j = np.arange(128)[:, None]
    i = np.arange(128)[None, :]
    cst[3] = (i >= j)
    cst[4] = (i <= j)
    p = np.arange(TN) % 128
    pos = np.zeros((2, 128, TN), np.float32)
    pos[0] = (p + 1)[None, :]
    pos[1] = (128 - p)[None, :]
    return cst, pos


def _pack_pv(inp, b, half):
    pv = np.zeros((128, NPV), np.float32)
    p64 = _perm64()
    for l in range(DEPTH):
        o = l * PL
        pv[:, o:o + 24] = _fm(inp["norm_g"][l]).reshape(128, 24)
        pv[:, o + 24:o + 96] = _fm(inp["b_mod"][l]).reshape(128, 72)
        pv[:, o + 96:o + 100] = _fm(inp["ret_norm_g"][l])
        pv[:, o + 100:o + 116] = _fm(inp["lru_conv_w"][l]).reshape(128, 16)
        pv[:, o + 116:o + 120] = _fm(inp["lru_conv_b"][l])
        pv[:, o + 120:o + 128] = _fm(inp["lru_b_a"][l]).reshape(128, 8)
        pv[:, o + 128:o + 136] = _fm(inp["lru_b_x"][l]).reshape(128, 8)
        pv[:, o + 136:o + 144] = _fm(inp["lru_lambda"][l]).reshape(128, 8)
        qg = np.asarray(inp["attn_q_norm_g"][l], np.float32)
        kg = np.asarray(inp["attn_k_norm_g"][l], np.float32)
        pv[:, o + 144] = np.tile(qg, 2)
        pv[:, o + 145] = np.tile(qg[p64], 2)
        pv[:, o + 146] = np.tile(kg, 2)
        pv[:, o + 147] = np.tile(kg[p64], 2)
        pv[:, o + 148:o + 156] = np.asarray(inp["ret_decay_logit"][l], np.float32).reshape(1, 8)
    o = DEPTH * PL
    pv[:, o:o + 8] = _fm(inp["final_norm_g"])
    pv[:, o + 8:o + 16] = _fm(inp["c"][b])
    pv[:, o + 16:o + 24] = _fm(inp["c_ctx"])
    pv[:, o + 24] = float(half)
    pv[:, o + 25] = float(1 - half)
    return pv


def _host_inputs(inp):
    f = lambda a: np.ascontiguousarray(np.asarray(a, np.float32))
    cst, pos = _consts()
    w_in = f(inp["w_in"])
    p128, p64 = _perm128(), _perm64()
    idx = []
    for c in range(8):
        idx.append(c * 128 + p128)
    for c in range(5):
        for hh in range(2):
            idx.append(3072 + c * 128 + hh * 64 + p64)
    idx = np.concatenate(idx)
    w_inp = np.ascontiguousarray(w_in[:, :, idx])
    lbd = np.zeros((DEPTH, 2, 2, 4, 128, 128), np.float32)
    for a, name in enumerate(("lru_w_a", "lru_w_x")):
        w = f(inp[name])
        for c in range(4):
            lbd[:, a, :, c, :64, :64] = w[:, :, 2 * c]
            lbd[:, a, :, c, 64:, 64:] = w[:, :, 2 * c + 1]
    shared = {
        "cst": cst, "pos": pos, "w_mod": f(inp["w_mod"][:N_LAYERS]), "ffn_w_in": f(inp["ffn_w_in"][:N_LAYERS]),
        "ffn_w_out": f(inp["ffn_w_out"][:N_LAYERS]), "w_in": w_in[:N_LAYERS], "w_inp": w_inp[:N_LAYERS],
        "lbd": lbd[:N_LAYERS], "w_branch": f(inp["w_branch"][:N_LAYERS]), "w_out": f(inp["w_out"][:N_LAYERS]),
    }
    x = f(inp["x"])
    ctx = f(inp["ctx"])
    ropes = [_rope_tables(0), _rope_tables(1)]
    maps = []
    for core in range(8):
        b, half = core // 2, core % 2
        xt = np.concatenate([ctx[b], x[b, half * LAT:(half + 1) * LAT]], 0)
        xin = np.ascontiguousarray(xt.T.reshape(8, 128, T))
        m = dict(shared)
        m["xin"] = xin
        m["pv"] = _pack_pv(inp, b, half)
        m["rope"] = ropes[half]
        maps.append(m)
    return maps


_NC_CACHE = {}


def _get_nc():
    key = (DEBUG_STOP, N_LAYERS)
    if key not in _NC_CACHE:
        nc = bass.Bass("TRN2", target_bir_lowering=False)
        Builder(nc).build()
        _NC_CACHE[key] = nc
    return _NC_CACHE[key]


def kernel(**inputs):
    maps = _host_inputs(inputs)
    nc = _get_nc()
    if TRACE:
        res = run_bass_kernel_spmd(nc, maps, core_ids=list(range(8)), trace=True)
        print("exec_time_ns", res.exec_time_ns)
    else:
        res = run_bass_kernel_spmd(nc, maps, core_ids=list(range(8)))
    if DEBUG_STOP is not None:
        return [r["dbg"] for r in res.results]
    out = np.zeros((4, 2 * LAT, D), np.float32)
    for core in range(8):
        b, half = core // 2, core % 2
        o = res.results[core]["out"]
        out[b, half * LAT:(half + 1) * LAT] = o.reshape(D, LAT).T
    return out
```
